# Optimizing a Trainium2 kernel written in Bass

```python
import jax, jax.numpy as jnp
from jax import lax
import numpy as np

D_MODEL = 1024
BATCH = 8
SEQ = 2048
DEPTH = 2
DEC_BATCH = 128
DEC_SEQ = 8
PAST_LEN = 16384
PAGE_SIZE = 128

N_META = 16
N_MIXERS = 2
N_GDN_LAYERS = (DEPTH + 1) // 2
N_RWKV_LAYERS = DEPTH // 2
GDN_QK_HEADS = 8
GDN_V_HEADS = 16
GDN_HEAD_K = 128
GDN_HEAD_V = 128
GDN_KEY_DIM = GDN_QK_HEADS * GDN_HEAD_K
GDN_VALUE_DIM = GDN_V_HEADS * GDN_HEAD_V
GDN_CONV_DIM = 2 * GDN_KEY_DIM + GDN_VALUE_DIM
GDN_IN_DIM = GDN_CONV_DIM + GDN_VALUE_DIM + 2 * GDN_V_HEADS
CONV_W = 4
CHUNK = 64
RWKV_HEAD = 64
RWKV_HEADS = D_MODEL // RWKV_HEAD
DECAY_LORA = 64
A_LORA = 64
RMS_EPS = 1e-6
GN_EPS = 64e-5

kernel_name = "gdn_rwkv7_hybrid_step"


def rmsnorm(x, w):
    xf = x.astype(jnp.float32)
    return xf * lax.rsqrt(jnp.mean(xf * xf, -1, keepdims=True) + RMS_EPS) * w.astype(jnp.float32)


def l2norm(x, eps=1e-6):
    return x * lax.rsqrt(jnp.sum(x * x, -1, keepdims=True) + eps)


def causal_conv(u, buf, w):
    T = u.shape[1]
    up = jnp.concatenate([buf.astype(u.dtype), u], 1)
    y = up[:, 0:T] * w[:, 0]
    for j in range(1, CONV_W):
        y = y + up[:, j:j + T] * w[:, j]
    return jax.nn.silu(y), up[:, up.shape[1] - (CONV_W - 1):]


def gdn_chunked(q, k, v, g, beta, S0, chunk):
    B, T, H, DK = q.shape
    DV = v.shape[-1]
    n = T // chunk

    def blocks(t):
        t = t.reshape((B, n, chunk) + t.shape[2:])
        return t.transpose((1, 0, 3, 2) + tuple(range(4, t.ndim)))

    q, k, v, g, beta = blocks(q), blocks(k), blocks(v), blocks(g), blocks(beta)
    gc = jnp.cumsum(g, axis=-1)
    causal = jnp.tril(jnp.ones((chunk, chunk), bool))
    strict = jnp.tril(jnp.ones((chunk, chunk), bool), -1)
    decay = jnp.exp(jnp.where(causal, gc[..., :, None] - gc[..., None, :], -jnp.inf))
    kb = k * beta[..., None]
    vb = v * beta[..., None]
    L = jnp.where(strict, jnp.einsum('nbhik,nbhjk->nbhij', kb, k) * decay, 0.0)
    A = L + jnp.eye(chunk, dtype=L.dtype)
    rhs = jnp.concatenate([vb, kb * jnp.exp(gc)[..., None]], -1)
    sol = lax.linalg.triangular_solve(A, rhs, left_side=True, lower=True, unit_diagonal=True)
    u, wk = sol[..., :DV], sol[..., DV:]
    attn = jnp.einsum('nbhik,nbhjk->nbhij', q, k) * decay
    g_last = gc[..., -1]
    k_dec = k * jnp.exp(g_last[..., None] - gc)[..., None]
    q_dec = q * jnp.exp(gc)[..., None]

    def step(S, inp):
        u_c, w_c, a_c, qd_c, kd_c, gl_c = inp
        v_new = u_c - jnp.einsum('bhck,bhkv->bhcv', w_c, S)
        o = jnp.einsum('bhck,bhkv->bhcv', qd_c, S) + jnp.einsum('bhij,bhjv->bhiv', a_c, v_new)
        S = S * jnp.exp(gl_c)[..., None, None] + jnp.einsum('bhck,bhcv->bhkv', kd_c, v_new)
        return S, o

    S, o = lax.scan(step, S0.astype(jnp.float32), (u, wk, attn, q_dec, k_dec, g_last))
    return o.transpose(1, 0, 3, 2, 4).reshape(B, T, H, DV), S


def gdn_mixer(xn, S0, conv_buf, w_in, conv_w, a_log, dt_bias, gn_w, w_out, segments):
    B, T, _ = xn.shape
    proj = xn @ w_in
    c1 = GDN_CONV_DIM
    c2 = c1 + GDN_VALUE_DIM
    c3 = c2 + GDN_V_HEADS
    qkv, z, b, a = proj[..., :c1], proj[..., c1:c2], proj[..., c2:c3], proj[..., c3:]
    qkv, new_buf = causal_conv(qkv, conv_buf, conv_w)
    rep = GDN_V_HEADS // GDN_QK_HEADS
    q = jnp.repeat(l2norm(qkv[..., :GDN_KEY_DIM].reshape(B, T, GDN_QK_HEADS, GDN_HEAD_K)), rep, axis=2) * (GDN_HEAD_K ** -0.5)
    k = jnp.repeat(l2norm(qkv[..., GDN_KEY_DIM:2 * GDN_KEY_DIM].reshape(B, T, GDN_QK_HEADS, GDN_HEAD_K)), rep, axis=2)
    v = qkv[..., 2 * GDN_KEY_DIM:].reshape(B, T, GDN_V_HEADS, GDN_HEAD_V)
    beta = jax.nn.sigmoid(b)
    g = -jnp.exp(a_log.astype(jnp.float32)) * jax.nn.softplus(a + dt_bias)
    S = S0
    outs = []
    start = 0
    for length, chunk in segments:
        o_seg, S = gdn_chunked(q[:, start:start + length], k[:, start:start + length], v[:, start:start + length],
                               g[:, start:start + length], beta[:, start:start + length], S, chunk)
        outs.append(o_seg)
        start += length
    o = jnp.concatenate(outs, 1)
    o = rmsnorm(o, gn_w) * jax.nn.silu(z.reshape(B, T, GDN_V_HEADS, GDN_HEAD_V))
    return o.reshape(B, T, GDN_VALUE_DIM) @ w_out, S, new_buf


def rwkv_mixer(xn, S0, shift_prev, mu, w_rkvz, w0, w1, w2, a0, a1, a2, k_k, k_a, r_k, lnx_w, lnx_b, w_o):
    B, T, D = xn.shape
    H, N = RWKV_HEADS, RWKV_HEAD
    xprev = jnp.concatenate([shift_prev[:, None].astype(jnp.float32), xn[:, :-1]], 1)
    xx = xprev - xn
    xs = xn[None] + xx[None] * mu[:, None, None, :]
    rkvz = jnp.einsum('sbtd,sde->sbte', xs[:4], w_rkvz)
    r, k, v, z = rkvz[0], rkvz[1], rkvz[2], rkvz[3]
    w = -jax.nn.softplus(-(w0 + jnp.tanh(xs[4] @ w1) @ w2)) - 0.5
    a = jax.nn.sigmoid(a0 + (xs[5] @ a1) @ a2)
    kk = l2norm((k * k_k).reshape(B, T, H, N))
    k = k * (1.0 + (a - 1.0) * k_a)
    decay = jnp.exp(-jnp.exp(w))
    hs = lambda t: t.reshape(B, T, H, N)
    r_h, k_h, v_h, a_h, d_h = hs(r), hs(k), hs(v), hs(a), hs(decay)
    b_h = kk * a_h
    tm = lambda t: t.transpose(1, 0, 2, 3)

    def step(S, inp):
        r_t, k_t, v_t, d_t, kk_t, b_t = inp
        S = (S * d_t[:, :, None, :]
             + jnp.einsum('bhvk,bhk->bhv', S, -kk_t)[..., None] * b_t[:, :, None, :]
             + v_t[..., None] * k_t[:, :, None, :])
        return S, jnp.einsum('bhvk,bhk->bhv', S, r_t)

    S, y = lax.scan(step, S0.astype(jnp.float32), (tm(r_h), tm(k_h), tm(v_h), tm(d_h), tm(kk), tm(b_h)))
    y = y.transpose(1, 0, 2, 3)
    mean = jnp.mean(y, -1, keepdims=True)
    var = jnp.mean(jnp.square(y - mean), -1, keepdims=True)
    y = (y - mean) * lax.rsqrt(var + GN_EPS) * lnx_w.reshape(H, N) + lnx_b.reshape(H, N)
    y = y + jnp.sum(r_h * k_h * r_k, -1, keepdims=True) * v_h
    y = y.reshape(B, T, D) * jax.nn.silu(z)
    return y @ w_o, S, xn[:, -1]


def trunk(x, gdn_S, gdn_conv, rwkv_S, rwkv_shift, norm_w, final_norm_w, gdn_p, rwkv_p, segments):
    new_gdn_S, new_gdn_conv, new_rwkv_S, new_rwkv_shift = [], [], [], []
    for i in range(DEPTH):
        xn = rmsnorm(x, norm_w[i])
        j = i // N_MIXERS
        if i % N_MIXERS == 0:
            out, S, buf = gdn_mixer(xn, gdn_S[j], gdn_conv[j], *[p[j] for p in gdn_p], segments)
            new_gdn_S.append(S)
            new_gdn_conv.append(buf)
        else:
            out, S, sh = rwkv_mixer(xn, rwkv_S[j], rwkv_shift[j], *[p[j] for p in rwkv_p])
            new_rwkv_S.append(S)
            new_rwkv_shift.append(sh)
        x = x + out.astype(x.dtype)
    y = rmsnorm(x, final_norm_w).astype(x.dtype)
    return y, jnp.stack(new_gdn_S), jnp.stack(new_gdn_conv), jnp.stack(new_rwkv_S), jnp.stack(new_rwkv_shift)


def setup_inputs(seed: int = 0) -> dict:
    key = jax.random.key(seed)
    ks = jax.random.split(key, 32)
    nrm = jax.random.normal
    uni = jax.random.uniform
    D = D_MODEL
    NA, NB = N_GDN_LAYERS, N_RWKV_LAYERS
    dt = jnp.exp(uni(ks[12], (NA, GDN_V_HEADS), minval=float(np.log(1e-3)), maxval=float(np.log(1e-1))))
    return {
        'x_prompt': nrm(ks[0], (BATCH, SEQ, D)),
        'x_sample': nrm(ks[1], (DEC_BATCH, DEC_SEQ, D)),
        'state_gdn': 0.05 * nrm(ks[2], (NA, DEC_BATCH, GDN_V_HEADS, GDN_HEAD_K, GDN_HEAD_V)),
        'state_gdn_conv': nrm(ks[3], (NA, DEC_BATCH, CONV_W - 1, GDN_CONV_DIM)),
        'state_rwkv': 0.1 * nrm(ks[4], (NB, DEC_BATCH, RWKV_HEADS, RWKV_HEAD, RWKV_HEAD)),
        'state_rwkv_shift': nrm(ks[5], (NB, DEC_BATCH, D)),
        'meta_tokens': nrm(ks[6], (N_META, D)),
        'norm_w': 1.0 + 0.01 * nrm(ks[7], (DEPTH, D)),
        'final_norm_w': 1.0 + 0.01 * nrm(ks[8], (D,)),
        'gdn_w_in': nrm(ks[9], (NA, D, GDN_IN_DIM)) * D ** -0.5,
        'gdn_conv_w': nrm(ks[10], (NA, GDN_CONV_DIM, CONV_W)) * CONV_W ** -0.5,
        'gdn_a_log': jnp.log(uni(ks[11], (NA, GDN_V_HEADS), minval=1.0, maxval=16.0)),
        'gdn_dt_bias': dt + jnp.log(-jnp.expm1(-dt)),
        'gdn_norm_w': 1.0 + 0.01 * nrm(ks[13], (NA, GDN_HEAD_V)),
        'gdn_w_out': nrm(ks[14], (NA, GDN_VALUE_DIM, D)) * GDN_VALUE_DIM ** -0.5,
        'rwkv_mu': uni(ks[15], (NB, 6, D)),
        'rwkv_w_rkvz': nrm(ks[16], (NB, 4, D, D)) * D ** -0.5,
        'rwkv_w0': uni(ks[17], (NB, D), minval=-6.0, maxval=-1.0),
        'rwkv_w1': 0.1 * nrm(ks[18], (NB, D, DECAY_LORA)) * D ** -0.5,
        'rwkv_w2': 0.1 * nrm(ks[19], (NB, DECAY_LORA, D)) * DECAY_LORA ** -0.5,
        'rwkv_a0': 0.1 * nrm(ks[20], (NB, D)),
        'rwkv_a1': nrm(ks[21], (NB, D, A_LORA)) * D ** -0.5,
        'rwkv_a2': 0.1 * nrm(ks[22], (NB, A_LORA, D)) * A_LORA ** -0.5,
        'rwkv_k_k': 0.85 + 0.05 * nrm(ks[23], (NB, D)),
        'rwkv_k_a': 1.0 + 0.05 * nrm(ks[24], (NB, D)),
        'rwkv_r_k': 0.1 * nrm(ks[25], (NB, RWKV_HEADS, RWKV_HEAD)),
        'rwkv_lnx_w': 1.0 + 0.01 * nrm(ks[26], (NB, D)),
        'rwkv_lnx_b': 0.01 * nrm(ks[27], (NB, D)),
        'rwkv_w_o': nrm(ks[28], (NB, D, D)) * D ** -0.5,
    }


def reference(x_prompt, x_sample, state_gdn, state_gdn_conv, state_rwkv, state_rwkv_shift,
              meta_tokens, norm_w, final_norm_w,
              gdn_w_in, gdn_conv_w, gdn_a_log, gdn_dt_bias, gdn_norm_w, gdn_w_out,
              rwkv_mu, rwkv_w_rkvz, rwkv_w0, rwkv_w1, rwkv_w2, rwkv_a0, rwkv_a1, rwkv_a2,
              rwkv_k_k, rwkv_k_a, rwkv_r_k, rwkv_lnx_w, rwkv_lnx_b, rwkv_w_o):
    gdn_p = (gdn_w_in, gdn_conv_w, gdn_a_log, gdn_dt_bias, gdn_norm_w, gdn_w_out)
    rwkv_p = (rwkv_mu, rwkv_w_rkvz, rwkv_w0, rwkv_w1, rwkv_w2, rwkv_a0, rwkv_a1, rwkv_a2,
              rwkv_k_k, rwkv_k_a, rwkv_r_k, rwkv_lnx_w, rwkv_lnx_b, rwkv_w_o)
    f32 = jnp.float32
    Bp, Tp, _ = x_prompt.shape
    meta = jnp.broadcast_to(meta_tokens.astype(x_prompt.dtype)[None], (Bp, N_META, D_MODEL))
    xp = jnp.concatenate([meta, x_prompt], 1)
    z_gdn = jnp.zeros((N_GDN_LAYERS, Bp, GDN_V_HEADS, GDN_HEAD_K, GDN_HEAD_V), f32)
    z_conv = jnp.zeros((N_GDN_LAYERS, Bp, CONV_W - 1, GDN_CONV_DIM), f32)
    z_rwkv = jnp.zeros((N_RWKV_LAYERS, Bp, RWKV_HEADS, RWKV_HEAD, RWKV_HEAD), f32)
    z_shift = jnp.zeros((N_RWKV_LAYERS, Bp, D_MODEL), f32)
    y_p, p_gdn, p_gdn_conv, p_rwkv, p_rwkv_shift = trunk(
        xp, z_gdn, z_conv, z_rwkv, z_shift, norm_w, final_norm_w, gdn_p, rwkv_p,
        ((N_META, N_META), (Tp, CHUNK)))
    y_prompt = y_p[:, N_META:]
    Ts = x_sample.shape[1]
    y_sample, s_gdn, s_gdn_conv, s_rwkv, s_rwkv_shift = trunk(
        x_sample, state_gdn, state_gdn_conv, state_rwkv, state_rwkv_shift, norm_w, final_norm_w,
        gdn_p, rwkv_p, ((Ts, Ts),))
    return (y_prompt, y_sample, p_gdn, p_gdn_conv, p_rwkv, p_rwkv_shift, s_gdn, s_gdn_conv, s_rwkv, s_rwkv_shift)
```

```python
from contextlib import ExitStack
import numpy as np
import concourse.bass as bass
import concourse.mybir as mybir
from concourse.bass_utils import run_bass_kernel_spmd

F32 = mybir.dt.float32
BF16 = mybir.dt.bfloat16
AF = mybir.ActivationFunctionType
ALU = mybir.AluOpType
AX = mybir.AxisListType
ENGS = ("pe", "act", "dve", "pool", "sp")
NT = 18
BIG = 30000.0


class Prog:
    CE = ("pe", "act", "dve", "pool")

    def __init__(self, nc):
        self.nc = nc
        self.es = ExitStack()
        self.q = {e: [] for e in ENGS}
        self.cnt = {}
        self.waited = {e: {} for e in ENGS}
        self.lastw = {}
        self.reads = {}
        self.semkeys = set(ENGS)
        self.ntens = 0
        self.nops = 0
        self.nidx = {e: 0 for e in self.CE}
        self.marks = {e: [] for e in self.CE}
        self.entry = {e: {} for e in self.CE}

    def sb(self, shape, dt=F32, name=None):
        self.ntens += 1
        name = name or f"t{self.ntens}"
        return self.es.enter_context(self.nc.sbuf_tensor(name, list(shape), dt))

    def ps(self, shape, dt=F32, name=None):
        self.ntens += 1
        name = name or f"p{self.ntens}"
        return self.es.enter_context(self.nc.psum_tensor(name, list(shape), dt))

    @staticmethod
    def _k(x):
        return x if isinstance(x, str) else x.name

    def _deps(self, r, w):
        toks = []
        for x in r:
            t = self.lastw.get(x)
            if t is not None:
                toks.append(t)
        for x in w:
            t = self.lastw.get(x)
            if t is not None:
                toks.append(t)
            toks.extend(self.reads.get(x, {}).values())
        return toks

    def _record(self, r, w, tok):
        key = tok[1]
        for x in r:
            d = self.reads.setdefault(x, {})
            o = d.get(key)
            if o is None or o[2] < tok[2]:
                d[key] = tok
        for x in w:
            self.lastw[x] = tok
            self.reads[x] = {}

    def _resolve(self, tok, peek=False):
        if tok[0] == "D":
            return tok[1], tok[2]
        _, eng, idx = tok
        m = self.marks[eng]
        if m and m[-1][0] >= idx:
            lo, hi = 0, len(m) - 1
            while lo < hi:
                mid = (lo + hi) // 2
                if m[mid][0] >= idx:
                    hi = mid
                else:
                    lo = mid + 1
            return eng, m[lo][1]
        ent = self.entry[eng][idx]
        cnt = len(m) + 1
        m.append((idx, cnt))
        ent[2] = (eng, 1)
        return eng, cnt

    def _waits(self, eng, toks, pe_self=False):
        need = {}
        for tok in toks:
            if tok[0] == "E" and tok[1] == eng and eng == "pe" and not pe_self:
                continue
            k, v = self._resolve(tok)
            if self.waited[eng].get(k, 0) >= v:
                continue
            need[k] = max(need.get(k, 0), v)
        for k, v in need.items():
            self.waited[eng][k] = v
        return list(need.items())

    def op(self, eng, fn, r=(), w=(), pe_self=False):
        r = [self._k(x) for x in r]
        w = [self._k(x) for x in w]
        waits = self._waits(eng, self._deps(r, w), pe_self)
        self.nidx[eng] += 1
        idx = self.nidx[eng]
        ent = [waits, fn, None]
        self.entry[eng][idx] = ent
        self.q[eng].append(ent)
        tok = ("E", eng, idx)
        self._record(r, w, tok)
        self.nops += 1
        return tok

    def dma(self, q, fn, r=(), w=(), sem=None):
        r = [self._k(x) for x in r]
        w = [self._k(x) for x in w]
        sem = ("L_" + w[0]) if w else ("S_" + r[0])
        self.semkeys.add(sem)
        waits = self._waits(q, self._deps(r, w))
        self.cnt[sem] = self.cnt.get(sem, 0) + 16
        tok = ("D", sem, self.cnt[sem])
        self.q[q].append([waits, fn, (sem, 16)])
        self._record(r, w, tok)
        self.nops += 1
        return tok

    def _all_toks(self):
        toks = [("E", e, self.nidx[e]) for e in self.CE if self.nidx[e] > 0]
        toks += [("D", k, v) for k, v in self.cnt.items()]
        return toks

    def barrier(self):
        toks = self._all_toks()
        for e in ENGS:
            w = self._waits(e, [t for t in toks if not (t[0] == "E" and t[1] == e)])
            if w:
                self.q[e].append([w, None, None])

    def flush(self):
        nc = self.nc
        if not hasattr(self, "sems"):
            self.sems = {}
        for k in sorted(self.semkeys):
            if k not in self.sems:
                self.sems[k] = self.es.enter_context(nc.semaphore("s_" + k))
        sems = self.sems
        q = self.q
        self.q = {e: [] for e in ENGS}
        self.entry = {e: {} for e in self.CE}

        def run(e, lst):
            for waits, fn, inc in lst:
                for k, v in waits:
                    e.wait_ge(sems[k], v)
                if fn is not None:
                    inst = fn(e)
                    if inc is not None:
                        inst.then_inc(sems[inc[0]], inc[1])

        with nc.Block() as block:
            @block.tensor
            def _(e):
                run(e, q["pe"])

            @block.scalar
            def _(e):
                run(e, q["act"])

            @block.vector
            def _(e):
                run(e, q["dve"])

            @block.gpsimd
            def _(e):
                run(e, q["pool"])

            @block.sync
            def _(e):
                run(e, q["sp"])


def fsz(t):
    n = 1
    for s in list(t.shape)[1:]:
        n *= int(s)
    return n


def V(t, off, dims, npart=128, p0=0):
    f = fsz(t)
    return bass.AP(t, p0 * f + off, [[f, npart]] + [list(d) for d in dims])


class K:
    def __init__(self, nlayers=2):
        self.nlayers = nlayers
        nc = bass.Bass("TRN2", target_bir_lowering=False)
        self.nc = nc
        self.P = Prog(nc)
        self.rr = {"pf": 0, "pb": 0, "ev": 0}
        self.build()

    def din(self, name, shape):
        return self.nc.dram_tensor(name, list(shape), F32, kind="ExternalInput").ap()

    def dout(self, name, shape):
        return self.nc.dram_tensor(name, list(shape), F32, kind="ExternalOutput").ap()

    def bank(self):
        self.rr["pf"] = (self.rr["pf"] + 1) % len(self.pf)
        return self.pf[self.rr["pf"]]

    def bbank(self):
        self.rr["pb"] = (self.rr["pb"] + 1) % len(self.pb)
        return self.pb[self.rr["pb"]]

    def ev(self):
        self.rr["ev"] ^= 1
        return "act" if self.rr["ev"] else "dve"

    def mm(self, ps, out, lhsT, rhs, start, stop, r, sw=None):
        self.P.op("pe", lambda e: e.matmul(out, lhsT=lhsT, rhs=rhs, start=start, stop=stop), r=r, w=[ps],
                  pe_self=(getattr(self, "pe_self", False) if sw is None else sw))

    def tr(self, ps, out, in_, ident, r):
        self.P.op("pe", lambda e: e.transpose(out=out, in_=in_, identity=ident), r=r, w=[ps])

    def tt(self, eng, out, in0, in1, op, r, w):
        self.P.op(eng, lambda e: e.tensor_tensor(out=out, in0=in0, in1=in1, op=op), r=r, w=w)

    def ts(self, eng, out, in0, s1, s2, op0, op1, r, w):
        if op1 is None:
            self.P.op(eng, lambda e: e.tensor_scalar(out=out, in0=in0, scalar1=s1, scalar2=None, op0=op0), r=r, w=w)
        else:
            self.P.op(eng, lambda e: e.tensor_scalar(out=out, in0=in0, scalar1=s1, scalar2=s2, op0=op0, op1=op1), r=r, w=w)

    def stt(self, out, in0, scalar, in1, op0, op1, r, w):
        self.P.op("dve", lambda e: e.scalar_tensor_tensor(out=out, in0=in0, scalar=scalar, in1=in1, op0=op0, op1=op1), r=r, w=w)

    def act(self, out, in_, func, r, w, bias=None, scale=None, accum_out=None):
        kw = {}
        if bias is not None:
            kw["bias"] = bias
        if scale is not None:
            kw["scale"] = scale
        if accum_out is not None:
            kw["accum_out"] = accum_out
        self.P.op("act", lambda e: e.activation(out=out, in_=in_, func=func, **kw), r=r, w=w)

    def cp(self, eng, out, in_, r, w):
        if eng == "act":
            self.P.op("act", lambda e: e.copy(out=out, in_=in_), r=r, w=w)
        else:
            self.P.op(eng, lambda e: e.tensor_copy(out=out, in_=in_), r=r, w=w)

    def ld(self, q, out, in_, w, sem, r=()):
        self.P.dma(q, lambda e: e.dma_start(out=out, in_=in_), r=r, w=w, sem=sem)

    def st(self, q, out, in_, r, sem):
        self.P.dma(q, lambda e: e.dma_start(out=out, in_=in_), r=r, w=(), sem=sem)

    def consts(self):
        P = self.P
        sb = P.sb
        self.ones32 = sb([128, 128], F32, "ones32")
        self.ident32 = sb([128, 128], F32, "ident32")
        self.identb = sb([128, 128], BF16, "identb")
        P.op("pool", lambda e: e.memset(self.ones32[:], 1.0), w=[self.ones32])

        def sel(out, in_, pattern, op, base, cm, r, w, fill=0.0):
            P.op("pool", lambda e: e.affine_select(out=out, in_=in_, pattern=pattern, compare_op=op, fill=fill,
                                                    base=base, channel_multiplier=cm), r=r, w=w)
        o = self.ones32
        sel(self.ident32[:], o[:], [[-1, 128]], ALU.is_equal, 0, 1, [o], [self.ident32])
        self.cp("pool", self.identb[:], self.ident32[:], [self.ident32], [self.identb])
        blk8 = sb([128, 128], F32, "blk8")
        blk32 = sb([128, 128], F32, "blk32")
        for (t, B) in ((blk8, 8), (blk32, 32)):
            v3 = t[:].rearrange("p (s t) -> p s t", t=B)
            o3 = o[:].rearrange("p (s t) -> p s t", t=B)
            sel(v3, o3, [[-B, 128 // B], [0, B]], ALU.is_ge, 0, 1, [o], [t])
            sel(v3, v3, [[B, 128 // B], [0, B]], ALU.is_ge, B - 1, -1, [t], [t])
        low = sb([128, 128], F32, "low")
        up = sb([128, 128], F32, "up")
        sel(low[:], o[:], [[-1, 128]], ALU.is_ge, 0, 1, [o], [low])
        sel(up[:], o[:], [[1, 128]], ALU.is_ge, 0, -1, [o], [up])
        self.lowT = {}
        self.same = {}
        self.addm = {}
        for X in ("p", "s"):
            lowT = sb([128, 128], F32, "lowT" + X)
            lowX = sb([128, 128], F32, "lowX" + X)
            am = sb([128, 4, 128], F32, "addm" + X)
            if X == "p":
                self.cp("pool", lowT[:], up[:], [up], [lowT])
                self.cp("pool", lowX[:], low[:], [low], [lowX])
                self.same[X] = self.ones32
            else:
                self.tt("pool", lowT[:], up[:], blk8[:], ALU.mult, [up, blk8], [lowT])
                self.tt("pool", lowX[:], low[:], blk8[:], ALU.mult, [low, blk8], [lowX])
                self.same[X] = blk8
            for h in range(4):
                self.ts("pool", am[:, h, :], lowX[:], -BIG, BIG, ALU.mult, ALU.add, [lowX], [am])
            self.lowT[X] = lowT
            self.addm[X] = am
        slow = sb([128, 128], F32, "slow")
        sup = sb([128, 128], F32, "sup")
        sel(slow[:], o[:], [[-1, 128]], ALU.is_gt, 0, 1, [o], [slow])
        sel(sup[:], o[:], [[1, 128]], ALU.is_gt, 0, -1, [o], [sup])
        self.m1 = sb([128, 128], BF16, "m1")
        self.m1T = sb([128, 128], BF16, "m1T")
        self.m2 = sb([128, 128], BF16, "m2")
        self.tt("pool", self.m1[:], blk32[:], slow[:], ALU.mult, [blk32, slow], [self.m1])
        self.tt("pool", self.m1T[:], blk32[:], sup[:], ALU.mult, [blk32, sup], [self.m1T])
        self.ts("pool", self.m2[:], blk32[:], -1.0, 1.0, ALU.mult, ALU.add, [blk32], [self.m2])
        self.sm = sb([128, 16], F32, "sm")
        sel(self.sm[:], o[:, 0:16], [[-8, 16]], ALU.is_ge, 0, 1, [o], [self.sm])
        sel(self.sm[:], self.sm[:], [[8, 16]], ALU.is_ge, 7, -1, [self.sm], [self.sm])
        self.cm = sb([128, 16, 128], BF16, "cm")
        P.op("pool", lambda e: e.memset(self.cm[:], 1.0), w=[self.cm])
        sel(self.cm[:], self.cm[:], [[-8, 16], [1, 128]], ALU.is_ge, 0, 0, [self.cm], [self.cm])
        sel(self.cm[:], self.cm[:], [[8, 16], [-1, 128]], ALU.is_ge, 7, 0, [self.cm], [self.cm])

    def bc4(self, m):
        return V(m, 0, [[0, 4], [1, 128]])

    def inverse(self, Mlow, Tt, tag):
        for _ in self.inverse_gen(self.invw, Mlow, Tt):
            pass

    def inverse_gen(self, W, Mlow, Tt, extra=None):
        Ib = self.bc4(self.identb)
        v4 = lambda ps: ps[:, 0:512].rearrange("p (h f) -> p h f", h=4)
        MdT, Md, No = W["MdT"], W["Md"], W["No"]
        self.tt("pool", MdT[:], Mlow[:], self.bc4(self.m1), ALU.mult, [Mlow, self.m1], [MdT])
        pb = self.bbank()
        for hh in range(4):
            self.tr(pb, pb[:, hh * 128:(hh + 1) * 128], Mlow[:, hh, :], self.identb[:], [Mlow, self.identb])
        self.tt("dve", Md[:], v4(pb), self.bc4(self.m1T), ALU.mult, [self.m1T], [pb, Md])
        self.tt("dve", No[:], v4(pb), self.bc4(self.m2), ALU.mult, [self.m2], [pb, No])
        X = W["X0"]
        self.tt("pool", X[:], Md[:], Ib, ALU.add, [Md, self.identb], [X])
        if extra is not None:
            extra()
        yield
        Pk, PTk = Md, MdT
        IpPT_prev = None
        for k in range(1, 5):
            last = (k == 4)
            if IpPT_prev is not None:
                psX = self.bank()
                for hh in range(4):
                    self.mm(psX, psX[:, hh * 128:(hh + 1) * 128], IpPT_prev[:, hh, :], X[:, hh, :], True, True, [IpPT_prev, X])
            psPT = self.bank()
            for hh in range(4):
                self.mm(psPT, psPT[:, hh * 128:(hh + 1) * 128], Pk[:, hh, :], PTk[:, hh, :], True, True, [Pk, PTk])
            if not last:
                psP = self.bank()
                for hh in range(4):
                    self.mm(psP, psP[:, hh * 128:(hh + 1) * 128], PTk[:, hh, :], Pk[:, hh, :], True, True, [Pk, PTk])
            if IpPT_prev is not None:
                Xn = W["X%d" % ((k - 1) % 2)]
                self.cp("act", Xn[:], v4(psX), [], [psX, Xn])
                X = Xn
            IpPT = W["IpPT%d" % (k % 2)]
            if not last:
                PTn = W["PT%d" % (k % 2)]
                Pn = W["P%d" % (k % 2)]
                self.cp("act", PTn[:], v4(psPT), [], [psPT, PTn])
                self.cp("dve", Pn[:], v4(psP), [], [psP, Pn])
                self.tt("pool", IpPT[:], PTn[:], Ib, ALU.add, [PTn, self.identb], [IpPT])
                Pk, PTk = Pn, PTn
            else:
                self.tt("dve", IpPT[:], v4(psPT), Ib, ALU.add, [self.identb], [psPT, IpPT])
            IpPT_prev = IpPT
            yield
        psX = self.bank()
        for hh in range(4):
            self.mm(psX, psX[:, hh * 128:(hh + 1) * 128], IpPT_prev[:, hh, :], X[:, hh, :], True, True, [IpPT_prev, X])
        Xn = W["X0"]
        self.cp(self.ev(), Xn[:], v4(psX), [], [psX, Xn])
        X = Xn
        yield
        pb = self.bbank()
        for hh in range(4):
            self.tr(pb, pb[:, hh * 128:(hh + 1) * 128], X[:, hh, :], self.identb[:], [X, self.identb])
        XT = W["XT"]
        self.cp(self.ev(), XT[:], v4(pb), [], [pb, XT])
        yield
        psV = self.bank()
        psVT = self.bank()
        for hh in range(4):
            self.mm(psV, psV[:, hh * 128:(hh + 1) * 128], XT[:, hh, :], No[:, hh, :], True, True, [XT, No])
        for hh in range(4):
            self.mm(psVT, psVT[:, hh * 128:(hh + 1) * 128], No[:, hh, :], XT[:, hh, :], True, True, [XT, No])
        Vm, VT, IpVT = W["P0"], W["P1"], W["PT0"]
        self.cp("act", Vm[:], v4(psV), [], [psV, Vm])
        self.cp("dve", VT[:], v4(psVT), [], [psVT, VT])
        self.tt("pool", IpVT[:], VT[:], Ib, ALU.add, [VT, self.identb], [IpVT])
        yield
        ps2 = self.bank()
        for hh in range(4):
            self.mm(ps2, ps2[:, hh * 128:(hh + 1) * 128], Vm[:, hh, :], VT[:, hh, :], True, True, [Vm, VT])
        IpV2T = W["PT1"]
        self.tt("dve", IpV2T[:], v4(ps2), Ib, ALU.add, [self.identb], [ps2, IpV2T])
        yield
        psY = self.bank()
        for hh in range(4):
            self.mm(psY, psY[:, hh * 128:(hh + 1) * 128], IpV2T[:, hh, :], X[:, hh, :], True, True, [IpV2T, X])
        Y = W["MdT"]
        self.cp(self.ev(), Y[:], v4(psY), [], [psY, Y])
        yield
        psT = self.bank()
        for hh in range(4):
            self.mm(psT, psT[:, hh * 128:(hh + 1) * 128], IpVT[:, hh, :], Y[:, hh, :], True, True, [IpVT, Y])
        self.cp(self.ev(), Tt[:], v4(psT), [], [psT, Tt])
        yield

    @staticmethod
    def run_rr(gens):
        gens = list(gens)
        while gens:
            for g in list(gens):
                try:
                    next(g)
                except StopIteration:
                    gens.remove(g)

    def inv_bufs(self, alloc, tag):
        return {nm: alloc([128, 4, 128], BF16, "%s_%s" % (tag, nm))
                for nm in ("MdT", "Md", "No", "X0", "X1", "P0", "P1", "PT0", "PT1", "IpPT0", "IpPT1", "XT")}

    def norm_T(self, xt, nwbc, xnT, xn_keep=None, bank=None):
        P = self.P
        ss, xn = self.nb["ss"], (xn_keep if xn_keep is not None else self.nb["xn"])
        self.act(xn[:], xt[:], AF.Square, [xt], [xn, ss], accum_out=ss[:])
        self.ts("dve", ss[:], ss[:], 1.0 / 1024, 1e-6, ALU.mult, ALU.add, [ss], [ss])
        self.act(ss[:], ss[:], AF.Ln, [ss], [ss])
        self.act(ss[:], ss[:], AF.Exp, [ss], [ss], scale=-0.5)
        self.stt(xn[:], xt[:], ss[:, 0:1], nwbc, ALU.mult, ALU.mult, [xt, ss, "normw"], [xn])
        for half in range(2):
            ps = bank() if bank is not None else self.bank()
            for c in range(4):
                cc = half * 4 + c
                self.tr(ps, ps[:, c * 128:(c + 1) * 128], xn[:, cc * 128:(cc + 1) * 128], self.ident32[:], [xn, self.ident32])
            self.cp(self.ev(), xnT[:, half * 4:half * 4 + 4, :], ps[:].rearrange("p (c f) -> p c f", c=4), [], [ps, xnT])

    def build(self):
        P = self.P
        nc = self.nc
        sb = P.sb
        din, dout = self.din, self.dout
        xin = din("xin", [NT, 128, 1024])
        sgdn = din("sgdn", [16, 16, 128, 128])
        sconv = din("sconv", [48, 4096])
        srwkv = din("srwkv", [16, 16, 64, 64])
        sshift = din("sshift", [16, 1024])
        self.d_srwkv, self.d_sshift = srwkv, sshift
        norm_w = din("norm_w", [2, 1024])
        fnorm_w = din("fnorm_w", [1, 1024])
        self.d_norm_w, self.d_fnorm_w = norm_w, fnorm_w
        w_in = din("w_in", [1024, 6176])
        conv_w = din("conv_w", [4096, 4])
        a_log = din("a_log", [1, 16])
        dt_bias = din("dt_bias", [1, 16])
        gnorm_w = din("gnorm_w", [1, 128])
        w_out = din("w_out", [2048, 1024])
        y = dout("y", [NT, 128, 1024])
        p_gdn = dout("p_gdn", [16, 128, 128])
        p_conv = dout("p_conv", [3, 4096])
        s_gdn = dout("s_gdn", [16, 16, 128, 128])
        s_conv = dout("s_conv", [48, 4096])
        oscr = nc.dram_tensor("oscr", [NT, 128, 2048], F32, kind="Internal").ap()
        x1scr = nc.dram_tensor("x1scr", [NT, 128, 1024], F32, kind="Internal").ap()

        self.pf = [P.ps([128, 512], F32, "pf%d" % i) for i in range(5)]
        self.pfx = P.ps([128, 512], F32, "pfx")
        self.pb = [P.ps([128, 1024], BF16, "pb%d" % i) for i in range(2)]
        self.consts()

        def pbc(ap_row, n):
            return bass.AP(ap_row.tensor, ap_row.offset, [[0, 128], [1, n]])

        nw0 = sb([128, 1024], F32, "normw")
        self.nw0 = nw0
        self.ld("sp", nw0[:], pbc(norm_w[0:1, :], 1024), [nw0], "ld_c")
        dtb = sb([128, 16], F32, "dtb")
        nea = sb([128, 16], F32, "nea")
        self.ld("sp", dtb[:], pbc(dt_bias, 16), [dtb], "ld_c")
        self.ld("sp", nea[:], pbc(a_log, 16), [nea], "ld_c")
        self.act(nea[:], nea[:], AF.Exp, [nea], [nea])
        self.ts("dve", nea[:], nea[:], -1.0, None, ALU.mult, None, [nea], [nea])
        gnw = sb([128, 128], F32, "gnw")
        self.ld("sp", gnw[:], pbc(gnorm_w, 128), [gnw], "ld_c")
        cw = sb([128, 32, 4], F32, "cw")
        self.ld("sp", cw[:], conv_w.rearrange("(j p) k -> p j k", p=128), [cw], "ld_c")

        self.nb = {"ss": sb([128, 1], F32, "ss"), "xn": sb([128, 1024], F32, "xn")}

        recA = nc.dram_tensor("recA", [NT, 128, 40 * 128], BF16, kind="Internal").ap()
        srecA = nc.dram_tensor("srecA", [NT, 128, 336], F32, kind="Internal").ap()
        es_a = ExitStack()

        def sba(shape, dt=F32, name=None):
            P.ntens += 1
            return es_a.enter_context(nc.sbuf_tensor(name or f"a{P.ntens}", list(shape), dt))

        NQ = 4128
        wA = sba([128, 8, NQ], BF16, "wA")
        for c in range(8):
            src = w_in[c * 128:(c + 1) * 128, :]
            P.dma("pool", lambda e, c=c, src=src: e.dma_start(out=wA[:, c, 4096:4128], in_=src[:, 6144:6176]), w=["wA_ba"], sem="x")
        for cb in range(8):
            for c in range(8):
                src = w_in[c * 128:(c + 1) * 128, :]
                P.dma("pool", lambda e, c=c, cb=cb, src=src: e.dma_start(out=wA[:, c, cb * 512:(cb + 1) * 512],
                                                                         in_=src[:, cb * 512:(cb + 1) * 512]),
                      w=["wA_%d" % cb], sem="x")
        xt = [sba([128, 1024], F32, "xt%d" % i) for i in range(2)]
        xnTs = [sba([128, 8, 128], BF16, "xnT%d" % i) for i in range(2)]
        halo = sba([128, 32, 3], F32, "halo")
        P.op("pool", lambda e: e.memset(halo[:], 0.0), w=["halo%d" % j for j in range(32)])
        cst_in = sba([48, 512], F32, "cst_in")
        cstT = sba([128, 32, 48], F32, "cstT")
        for jg in range(8):
            self.ld("sp", cst_in[:], sconv[:, jg * 512:(jg + 1) * 512], [cst_in], "ld_c")
            ps = self.bank()
            for jj in range(4):
                self.tr(ps, ps[:, jj * 48:(jj + 1) * 48], cst_in[:, jj * 128:(jj + 1) * 128], self.ident32[0:48, 0:48],
                        [cst_in, self.ident32])
            self.cp(self.ev(), cstT[:, jg * 4:jg * 4 + 4, :], ps[:, 0:192].rearrange("p (j f) -> p j f", j=4), [], [ps, cstT])
        cstN = cstT
        U = [sba([128, 176], BF16, "U%d" % i) for i in range(4)]
        wdiag = sba([128, 128, 128], BF16, "wdiag")
        for q4 in range(4):
            self.tt("pool" if q4 % 2 == 0 else "dve", wdiag[:, q4 * 32:(q4 + 1) * 32, :], V(self.ident32, 0, [[0, 32], [1, 128]]),
                    V(cw, q4 * 32, [[1, 32], [0, 128]]), ALU.mult, [self.ident32, cw], [wdiag])

        c4 = [sba([128, 4, 128], F32, "c4_%d" % i) for i in range(2)]
        sq4 = sba([128, 4, 128], F32, "sq4")
        rn4 = sba([128, 4, 128], F32, "rn4")
        recs = [sba([128, 40, 128], BF16, "rec%d" % i) for i in range(2)]
        srecs = [sba([128, 336], F32, "srec%d" % i) for i in range(2)]
        scs = [{nm: sba([128, 16], F32, "sc%d_%s" % (i, nm)) for nm in
                ("beta", "g", "gc", "glt", "egc", "negbege", "ekd", "negb", "tmp")} for i in range(2)]
        gms = [sba([128, 16, 16], F32, "gm%d" % i) for i in range(2)]
        egls = [sba([128, 16, 16], F32, "egl%d" % i) for i in range(2)]

        def bcs(t, g):
            return V(t, 4 * g, [[1, 4], [0, 128]])

        def prologue(t):
            X = "p" if t < 17 else "s"
            nseq, T = (1, 128) if X == "p" else (16, 8)
            x = xt[t % 2]
            srec = srecs[t % 2]
            xnT, sc, gm, egl = xnTs[t % 2], scs[t % 2], gms[t % 2], egls[t % 2]
            pbank = lambda: self.pfx
            self.ld("sp", x[:], xin[t], [x], "ld_x%d" % (t % 2))
            self.norm_T(x, nw0[:], xnT, bank=pbank)
            ps = pbank()
            for c in range(8):
                self.mm(ps, ps[:, 0:32], xnT[:, c, :], wA[:, c, 4096:4128], c == 0, c == 7, [xnT, "wA_ba"])
            self.act(sc["beta"][:], ps[:, 0:16], AF.Sigmoid, [], [ps, sc["beta"]])
            self.tt("dve", sc["tmp"][:], ps[:, 16:32], dtb[:], ALU.add, [dtb], [ps, sc["tmp"]])
            self.act(sc["tmp"][:], sc["tmp"][:], AF.Exp, [sc["tmp"]], [sc["tmp"]])
            self.act(sc["tmp"][:], sc["tmp"][:], AF.Ln, [sc["tmp"]], [sc["tmp"]], bias=1.0)
            self.tt("dve", sc["g"][:], sc["tmp"][:], nea[:], ALU.mult, [sc["tmp"], nea], [sc["g"]])
            ps = pbank()
            self.mm(ps, ps[:, 0:16], self.lowT[X][:], sc["g"][:], True, True, [self.lowT[X], sc["g"]])
            self.mm(ps, ps[:, 16:32], self.same[X][:], sc["g"][:], True, True, [self.same[X], sc["g"]])
            self.cp("dve", sc["gc"][:], ps[:, 0:16], [], [ps, sc["gc"]])
            self.cp("dve", sc["glt"][:], ps[:, 16:32], [], [ps, sc["glt"]])
            self.act(sc["egc"][:], sc["gc"][:], AF.Exp, [sc["gc"]], [sc["egc"]])
            self.stt(sc["negbege"][:], sc["egc"][:], -1.0, sc["beta"][:], ALU.mult, ALU.mult, [sc["egc"], sc["beta"]], [sc["negbege"]])
            self.tt("dve", sc["tmp"][:], sc["glt"][:], sc["gc"][:], ALU.subtract, [sc["glt"], sc["gc"]], [sc["tmp"]])
            self.act(sc["ekd"][:], sc["tmp"][:], AF.Exp, [sc["tmp"]], [sc["ekd"]])
            self.ts("dve", sc["negb"][:], sc["beta"][:], -1.0, None, ALU.mult, None, [sc["beta"]], [sc["negb"]])
            gmv = gm[:, 0:nseq, :]
            self.tt("pool", gmv, V(sc["g"], 0, [[0, nseq], [1, 16]]), V(self.sm, 0, [[1, nseq], [0, 16]]) if X == "s"
                    else V(self.ones32, 0, [[0, 1], [1, 16]]), ALU.mult, [sc["g"], self.sm, self.ones32], [gm])
            ps = pbank()
            self.mm(ps, ps[:, 0:nseq * 16], self.ones32[:], gm[:, 0:nseq, :].rearrange("p s h -> p (s h)"), True, True,
                    [self.ones32, gm])
            self.act(egl[:, 0:nseq, :].rearrange("p s h -> p (s h)"), ps[:, 0:nseq * 16], AF.Exp, [], [ps, egl])

            for i_, nm in enumerate(("gc", "egc", "negbege", "ekd", "negb")):
                self.cp("pool", srec[:, 16 * i_:16 * i_ + 16], sc[nm][:], [sc[nm]], [srec])
            self.cp("pool", srec[:, 80:336], egl[:].rearrange("p s h -> p (s h)"), [egl], [srec])

        for t in range(NT):
            X = "p" if t < 17 else "s"
            nseq, T = (1, 128) if X == "p" else (16, 8)
            if t == 0:
                prologue(0)
            xnT, sc = xnTs[t % 2], scs[t % 2]
            rec = recs[t % 2]
            srec = srecs[t % 2]

            class _RV:
                def __init__(s_, o, name):
                    s_.o, s_.name = o, name

                def __getitem__(s_, idx):
                    a, b, c_ = idx
                    if isinstance(b, slice):
                        b = slice((b.start or 0) + s_.o, (b.stop if b.stop is not None else 0) + s_.o)
                    else:
                        b = b + s_.o
                    return rec[a, b, c_]
            qT, kT, Ktok, vb = _RV(0, rec.name), _RV(8, rec.name), _RV(16, rec.name), _RV(24, rec.name)
            psUs = {}

            def proj(jg):
                psU = self.bank()
                psUs[jg] = psU
                for jj in range(4):
                    j = jg * 4 + jj
                    for c in range(8):
                        self.mm(psU, psU[:, jj * 128:(jj + 1) * 128], wA[:, c, j * 128:(j + 1) * 128], xnT[:, c, :], c == 0, c == 7,
                                ["wA_%d" % jg, xnT])
            psCs = {}

            def views(j):
                jg, jj = j // 4, j % 4
                u, cc = U[j % 4], c4[jg % 2]
                if X == "p":
                    return u, cc, (lambda k: u[:, k:k + 128])
                u3 = u[:].rearrange("p (s t) -> p s t", t=11)
                return u, cc, (lambda k: u3[:, :, k:k + 8])

            def stA(j):
                jg, jj = j // 4, j % 4
                psU = psUs[jg]
                u, cc, uv = views(j)
                pu = psU[:, jj * 128:(jj + 1) * 128]
                uh, ub, hj = "uh%d" % (j % 4), "ub%d" % (j % 4), "halo%d" % j
                if X == "p":
                    self.cp("pool", u[:, 0:3], halo[:, j, :], [hj], [uh])
                    self.cp("dve", u[:, 3:131], pu, [], [psU, ub])
                    self.cp("dve", halo[:, j, :], psU[:, jj * 128 + 125:jj * 128 + 128], [], [psU, hj])
                else:
                    u3 = u[:].rearrange("p (s t) -> p s t", t=11)
                    pu3 = pu.rearrange("p (s t) -> p s t", t=8)
                    self.cp("pool", u3[:, :, 0:3], cstT[:, j, :].rearrange("p (s k) -> p s k", k=3), [hj, cstT], [uh])
                    self.cp("dve", u3[:, :, 3:11], pu3, [], [psU, ub])
                    self.cp("dve", cstN[:, j, :].rearrange("p (s k) -> p s k", k=3), pu3[:, :, 5:8], [], [psU, hj])

            def stB(j):
                jg, jj = j // 4, j % 4
                if jj == 0:
                    psCs[jg] = self.bank()
                psC = psCs[jg]
                u, cc, uv = views(j)
                for k in range(4):
                    self.mm(psC, psC[:, jj * 128:(jj + 1) * 128], wdiag[:, j * 4 + k, :], uv(k), k == 0, k == 3,
                            [wdiag, "uh%d" % (j % 4), "ub%d" % (j % 4)])

            def stC(j):
                jg, jj = j // 4, j % 4
                u, cc, uv = views(j)
                psC = psCs[jg]
                self.act(cc[:, jj, :], psC[:, jj * 128:(jj + 1) * 128], AF.Silu, [], [psC, cc])
                if jj != 3:
                    return
                if jg < 4:
                    self.tt("pool", sq4[:], cc[:], cc[:], ALU.mult, [cc], [sq4])
                    psN = self.bank()
                    for j2 in range(4):
                        self.mm(psN, psN[:, j2 * 128:(j2 + 1) * 128], self.ones32[:], sq4[:, j2, :], True, True, [self.ones32, sq4])
                    self.act(rn4[:], psN[:].rearrange("p (h f) -> p h f", h=4), AF.Ln, [], [psN, rn4], bias=1e-6)
                    lnsc = float(np.log(128.0 ** -0.5)) if jg < 2 else 0.0
                    self.act(rn4[:], rn4[:], AF.Exp, [rn4], [rn4], scale=-0.5, bias=lnsc)
                    dst = qT if jg < 2 else kT
                    o0 = (jg % 2) * 4
                    self.tt("dve", dst[:, o0:o0 + 4, :], cc[:], rn4[:], ALU.mult, [cc, rn4], [dst])
                else:
                    gv = jg - 4
                    psT = self.bank()
                    for j2 in range(4):
                        self.tr(psT, psT[:, j2 * 128:(j2 + 1) * 128], cc[:, j2, :], self.ident32[:], [cc, self.ident32])
                    self.tt("dve", vb[:, gv * 4:gv * 4 + 4, :], psT[:].rearrange("p (h f) -> p h f", h=4), bcs(sc["beta"], gv),
                            ALU.mult, [sc["beta"]], [psT, vb])

            proj(0)
            for step in range(36):
                if step < 32:
                    if step % 4 == 1 and step // 4 + 1 < 8:
                        proj(step // 4 + 1)
                    stA(step)
                if step == 14 and t + 1 < NT:
                    prologue(t + 1)
                if 0 <= step - 2 < 32:
                    stB(step - 2)
                if 0 <= step - 4 < 32:
                    stC(step - 4)
            pb = self.bbank()
            for kh in range(8):
                self.tr(pb, pb[:, kh * 128:(kh + 1) * 128], kT[:, kh, :], self.identb[:], [kT, self.identb])
            self.cp("act", rec[:, 16:24, :], pb[:].rearrange("p (h f) -> p h f", h=8), [], [pb, rec])
            if t == 16 or t == 17:
                src, ncol, dst = (halo, 3, p_conv) if t == 16 else (cstN, 48, s_conv)
                stg = cst_in
                for jg in range(8):
                    ps = self.bank()
                    for jj in range(4):
                        j = jg * 4 + jj
                        self.tr(ps, ps[0:ncol, jj * 128:(jj + 1) * 128], src[:, j, :], self.ident32[:], [src, self.ident32, "halo%d" % j])
                    self.cp(self.ev(), stg[0:ncol, :], ps[0:ncol, :], [], [ps, stg])
                    self.st("sp", dst[:, jg * 512:(jg + 1) * 512], stg[0:ncol, :], [stg], "st_c")

            self.st("sp", recA[t], rec[:].rearrange("p a f -> p (a f)"), [rec], "st_rec%d" % (t % 2))
            self.st("sp", srecA[t], srec[:], [srec], "st_srec%d" % (t % 2))
        P.barrier()
        P.flush()
        es_a.close()

        es_a = ExitStack()
        rec2 = [sba([128, 40, 128], BF16, "r2ec%d" % i) for i in range(2)]
        srec2 = [sba([128, 336], F32, "s2rec%d" % i) for i in range(2)]
        gcd = [sba([128, 16, 128], F32, "gcd%d" % i) for i in range(2)]
        Sf = sba([128, 16, 128], F32, "Sf")
        Sb = sba([128, 16, 128], BF16, "Sb")
        P.op("pool", lambda e: e.memset(Sf[:], 0.0), w=["Sf%d" % i for i in range(4)])
        P.op("pool", lambda e: e.memset(Sb[:], 0.0), w=["Sb%d" % i for i in range(4)])
        sets = []
        for gi in range(4):
            B = {nm: sba([128, 4, 128], BF16, "g%d_%s" % (gi, nm)) for nm in
                 ("E1", "E1nb", "Mlow", "Tt", "r4", "vn", "vnd", "attn", "attnT")}
            B["o4"] = sba([128, 4, 128], F32, "g%d_o4" % gi)
            B["W"] = self.inv_bufs(sba, "g%d" % gi)
            sets.append(B)
        S0b = sba([128, 16, 128], BF16, "S0b")
        S0f32 = sba([128, 16, 128], F32, "S0f32")
        mk = sba([128, 16, 128], BF16, "mk")
        S0f = [sba([128, 4, 128], F32, "S0f%d" % i) for i in range(2)]
        Sn = [sba([128, 4, 128], F32, "Sn%d" % i) for i in range(2)]
        vnds = [sba([128, 128], BF16, "vnds%d" % i) for i in range(2)]
        v4 = lambda ps: ps[:, 0:512].rearrange("p (h f) -> p h f", h=4)

        def gdn_group(t, g, B, rec, srec, gcd_t):
            X = "p" if t < 17 else "s"
            hs = [4 * g + hh for hh in range(4)]
            khs = [h // 2 for h in hs]
            qT = lambda kh: rec[:, kh, :]
            kT = lambda kh: rec[:, 8 + kh, :]
            Kt = lambda kh: rec[:, 16 + kh, :]
            sv_ = lambda off: V(srec, off + 4 * g, [[1, 4], [0, 128]])
            E1, E1nb, Mlow, Tt, r4, vn, vnd, attn, attnT, o4 = (B[k_] for k_ in
                                                               ("E1", "E1nb", "Mlow", "Tt", "r4", "vn", "vnd", "attn", "attnT", "o4"))
            psR = self.bank()
            self.mm(psR, psR[:], self.ones32[:], gcd_t[:, 4 * g:4 * g + 4, :].rearrange("p h f -> p (h f)"), True, False,
                    [self.ones32, gcd_t])
            self.mm(psR, psR[:], self.ident32[:], self.addm[X][:].rearrange("p h f -> p (h f)"), False, True,
                    [self.ident32, self.addm[X]])
            self.tt("dve", v4(psR), v4(psR), sv_(0), ALU.subtract, [srec], [psR])
            self.act(E1[:], v4(psR), AF.Exp, [], [psR, E1], scale=-1.0)
            self.tt("pool", E1nb[:], E1[:], sv_(64), ALU.mult, [E1, srec], [E1nb])
            yield
            psG = self.bank()
            for hh in range(4):
                self.mm(psG, psG[:, hh * 128:(hh + 1) * 128], kT(khs[hh]), kT(khs[hh]), True, True, [rec])
            psQ = self.bank()
            for hh in range(4):
                self.mm(psQ, psQ[:, hh * 128:(hh + 1) * 128], qT(khs[hh]), kT(khs[hh]), True, True, [rec])
            self.tt("dve", Mlow[:], v4(psG), E1nb[:], ALU.mult, [E1nb], [psG, Mlow])
            self.tt("dve", attn[:], v4(psQ), E1[:], ALU.mult, [E1], [psQ, attn])
            yield

            def extra():
                pb = self.bbank()
                for hh in range(4):
                    self.tr(pb, pb[:, hh * 128:(hh + 1) * 128], attn[:, hh, :], self.identb[:], [attn, self.identb])
                self.cp("act", attnT[:], v4(pb), [], [pb, attnT])
            yield from self.inverse_gen(B["W"], Mlow, Tt, extra)
            psK = self.bank()
            if X == "p":
                for hh in range(4):
                    self.mm(psK, psK[:, hh * 128:(hh + 1) * 128], kT(khs[hh]), Sb[:, hs[hh], :], True, True, [rec, "Sb%d" % g])
            else:
                psA = self.bank()
                for hh in range(4):
                    h = hs[hh]
                    self.ld("sp", S0f32[:], sgdn[:, h].rearrange("s p v -> p s v"), [S0f32], "ld_s0b")
                    self.cp("act", S0b[:], S0f32[:], [S0f32], [S0b])
                    for (srcf, psd) in ((kT, psK), (qT, psA)):
                        a_ = srcf(khs[hh])
                        self.tt("pool", mk[:], bass.AP(a_.tensor, a_.offset, [list(a_.ap[0]), [0, 16], [1, 128]]), self.cm[:], ALU.mult,
                                [rec, self.cm], [mk])
                        for s_ in range(16):
                            self.mm(psd, psd[:, hh * 128:(hh + 1) * 128], mk[:, s_, :], S0b[:, s_, :], s_ == 0, s_ == 15, [mk, S0b])
                self.tt("dve", o4[:], v4(psA), sv_(16), ALU.mult, [srec], [psA, o4])
            self.tt("dve", v4(psK), v4(psK), sv_(32), ALU.mult, [srec], [psK])
            self.tt("dve", r4[:], v4(psK), rec[:, 24 + 4 * g:24 + 4 * g + 4, :], ALU.add, [rec], [psK, r4])
            yield
            psV = self.bank()
            for hh in range(4):
                self.mm(psV, psV[:, hh * 128:(hh + 1) * 128], Tt[:, hh, :], r4[:, hh, :], True, True, [Tt, r4])
            self.cp("act", vn[:], v4(psV), [], [psV, vn])
            self.tt("dve", vnd[:], v4(psV), sv_(48), ALU.mult, [srec], [psV, vnd])
            yield
            psB = self.bank()
            for hh in range(4):
                self.mm(psB, psB[:, hh * 128:(hh + 1) * 128], attnT[:, hh, :], vn[:, hh, :], True, True, [attnT, vn])
            if X == "p":
                psA = self.bank()
                for hh in range(4):
                    self.mm(psA, psA[:, hh * 128:(hh + 1) * 128], qT(khs[hh]), Sb[:, hs[hh], :], True, True, [rec, "Sb%d" % g])
                self.tt("dve", o4[:], v4(psA), sv_(16), ALU.mult, [srec], [psA, o4])
            self.tt("dve", o4[:], v4(psB), o4[:], ALU.add, [o4], [psB, o4])
            self.st("sp", oscr[t, :, g * 512:(g + 1) * 512], o4[:].rearrange("p h f -> p (h f)"), [o4], "st_o%d" % g)
            if X == "p":
                psS = self.bank()
                for hh in range(4):
                    self.mm(psS, psS[:, hh * 128:(hh + 1) * 128], Kt(khs[hh]), vnd[:, hh, :], True, True, [rec, vnd])
                sv = Sf[:, 4 * g:4 * g + 4, :]
                SfN, SbN = "Sf%d" % g, "Sb%d" % g
                self.tt("pool", sv, sv, V(srec, 80 + 4 * g, [[1, 4], [0, 128]]), ALU.mult, [srec, SfN], [SfN])
                self.tt("dve", sv, v4(psS), sv, ALU.add, [SfN], [psS, SfN])
                self.cp("act", Sb[:, 4 * g:4 * g + 4, :], sv, [SfN], [SbN])
                if t == 16:
                    self.st("sp", p_gdn[4 * g:4 * g + 4].rearrange("h p v -> p h v"), sv, [SfN], "st_pg")
            else:
                for hh in range(4):
                    h = hs[hh]
                    for sg in range(4):
                        i2 = (hh * 4 + sg) % 2
                        s0f = S0f[i2]
                        self.ld("sp", s0f[:], sgdn[sg * 4:sg * 4 + 4, h].rearrange("s p v -> p s v"), [s0f], "ld_s0f%d" % i2)
                        psS = self.bank()
                        for si in range(4):
                            s_ = sg * 4 + si
                            vs = vnds[s_ % 2]
                            self.act(vs[:], vnd[:, hh, :], AF.Copy, [vnd, self.sm], [vs], scale=self.sm[:, s_:s_ + 1])
                            self.mm(psS, psS[:, si * 128:(si + 1) * 128], Kt(khs[hh]), vs[:], True, True, [rec, vs])
                        sn = Sn[i2]
                        self.tt("pool", sn[:], s0f[:], V(srec, 80 + sg * 4 * 16 + h, [[16, 4], [0, 128]]), ALU.mult, [s0f, srec], [sn])
                        self.tt("dve", sn[:], v4(psS), sn[:], ALU.add, [sn], [psS, sn])
                        self.st("sp", s_gdn[sg * 4:sg * 4 + 4, h].rearrange("s p v -> p s v"), sn[:], [sn], "st_sn%d" % i2)
            yield

        for t in range(NT):
            rec, srec, gcd_t = rec2[t % 2], srec2[t % 2], gcd[t % 2]
            self.ld("sp", rec[:].rearrange("p a f -> p (a f)"), recA[t], [rec], "ld_rec%d" % (t % 2))
            self.ld("sp", srec[:], srecA[t], [srec], "ld_srec%d" % (t % 2))
            self.tt("pool", gcd_t[:], V(self.ident32, 0, [[0, 16], [1, 128]]), V(srec, 0, [[1, 16], [0, 128]]), ALU.mult,
                    [self.ident32, srec], [gcd_t])
            self.run_rr([gdn_group(t, g, sets[g], rec, srec, gcd_t) for g in range(4)])
        P.barrier()
        P.flush()
        es_a.close()

        self.pm = {(nm, X_): P.sb([128, 128], BF16, nm + X_) for nm in ("mup", "msup", "mneg") for X_ in ("p", "s")}
        self.es_w1 = ExitStack()
        P.ntens += 1
        self.wR = self.es_w1.enter_context(nc.sbuf_tensor("wR", [128, 8, 4096], BF16))
        self.w1a1 = self.es_w1.enter_context(nc.sbuf_tensor("w1a1", [128, 8, 128], BF16))
        self.w2a2 = self.es_w1.enter_context(nc.sbuf_tensor("w2a2", [64, 2, 1024], BF16))
        self.d_rkvz = din("rkvz", [4, 1024, 1024])
        self.d_w1 = din("w1", [1024, 64]); self.d_w2 = din("w2", [64, 1024])
        self.d_a1 = din("a1", [1024, 64]); self.d_a2 = din("a2", [64, 1024])

        es_b = ExitStack()

        def sbb(shape, dt=F32, name=None):
            P.ntens += 1
            return es_b.enter_context(nc.sbuf_tensor(name or f"b{P.ntens}", list(shape), dt))

        wZ = sbb([128, 8, 2048], BF16, "wZ")
        wO = sbb([128, 16, 1024], BF16, "wO")
        for g4 in range(4):
            for c in range(8):
                P.dma("pool", lambda e, c=c, g4=g4: e.dma_start(out=wZ[:, c, g4 * 512:(g4 + 1) * 512],
                                                               in_=w_in[c * 128:(c + 1) * 128, 4096 + g4 * 512:4096 + (g4 + 1) * 512]),
                      w=["wZ_%d" % g4], sem="x")
        for c in range(16):
            P.dma("pool", lambda e, c=c: e.dma_start(out=wO[:, c, :], in_=w_out[c * 128:(c + 1) * 128, :]), w=[wO], sem="ld_w")
        wR_, w1a1_, w2a2_ = self.wR, self.w1a1, self.w2a2
        for i in range(4):
            for c in range(8):
                P.dma("pool", lambda e, i=i, c=c: e.dma_start(out=wR_[:, c, i * 1024:(i + 1) * 1024],
                                                              in_=self.d_rkvz[i, c * 128:(c + 1) * 128, :]), w=[wR_], sem="x")
        for c in range(8):
            P.dma("pool", lambda e, c=c: e.dma_start(out=w1a1_[:, c, 0:64], in_=self.d_w1[c * 128:(c + 1) * 128, :]), w=[w1a1_], sem="x")
            P.dma("pool", lambda e, c=c: e.dma_start(out=w1a1_[:, c, 64:128], in_=self.d_a1[c * 128:(c + 1) * 128, :]), w=[w1a1_], sem="x")
        P.dma("pool", lambda e: e.dma_start(out=w2a2_[:, 0, :], in_=self.d_w2), w=[w2a2_], sem="x")
        P.dma("pool", lambda e: e.dma_start(out=w2a2_[:, 1, :], in_=self.d_a2), w=[w2a2_], sem="x")
        xtb = [sbb([128, 1024], F32, "xtb%d" % i) for i in range(2)]
        xnTb = sbb([128, 8, 128], BF16, "xnTb")
        ot = [sbb([128, 16, 128], F32, "ot%d" % i) for i in range(2)]
        osq = None
        orn = sbb([128, 16], F32, "orn")
        zs = sbb([128, 4, 128], F32, "zs")
        og = sbb([128, 16, 128], F32, "og")
        osq = og
        ogT = sbb([128, 16, 128], BF16, "ogT")
        x1 = [sbb([128, 1024], F32, "x1_%d" % i) for i in range(2)]
        for t in range(NT):
            x = xtb[t % 2]
            o = ot[t % 2]
            self.ld("sp", x[:], xin[t], [x], "ldb_x%d" % (t % 2))
            self.ld("sp", o[:].rearrange("p h f -> p (h f)"), oscr[t], [o], "ldb_o%d" % (t % 2))
            self.norm_T(x, nw0[:], xnTb)
            self.act(osq[:], o[:], AF.Square, [o], [osq])
            P.op("dve", lambda e: e.tensor_reduce(out=orn[:], in_=osq[:], axis=AX.X, op=ALU.add), r=[osq], w=[orn])
            self.ts("dve", orn[:], orn[:], 1.0 / 128, 1e-6, ALU.mult, ALU.add, [orn], [orn])
            self.act(orn[:], orn[:], AF.Ln, [orn], [orn])
            self.act(orn[:], orn[:], AF.Exp, [orn], [orn], scale=-0.5)
            self.tt("dve", og[:], o[:], V(orn, 0, [[1, 16], [0, 128]]), ALU.mult, [o, orn], [og])
            self.tt("pool", og[:], og[:], V(gnw, 0, [[0, 16], [1, 128]]), ALU.mult, [og, gnw], [og])
            for g in range(4):
                psZ = self.bank()
                for c in range(8):
                    self.mm(psZ, psZ[:], xnTb[:, c, :], wZ[:, c, g * 512:(g + 1) * 512], c == 0, c == 7, [xnTb, "wZ_%d" % g])
                self.act(zs[:], psZ[:].rearrange("p (h f) -> p h f", h=4), AF.Silu, [], [psZ, zs])
                self.tt("dve", og[:, 4 * g:4 * g + 4, :], og[:, 4 * g:4 * g + 4, :], zs[:], ALU.mult, [zs, og], [og])
            for g in range(4):
                ps = self.bank()
                for hh in range(4):
                    self.tr(ps, ps[:, hh * 128:(hh + 1) * 128], og[:, 4 * g + hh, :], self.ident32[:], [og, self.ident32])
                self.cp(self.ev(), ogT[:, 4 * g:4 * g + 4, :], ps[:].rearrange("p (h f) -> p h f", h=4), [], [ps, ogT])
            xo = x1[t % 2]
            for half in range(2):
                ps = self.bank()
                for c in range(16):
                    self.mm(ps, ps[:], ogT[:, c, :], wO[:, c, half * 512:(half + 1) * 512], c == 0, c == 15, [ogT, wO])
                self.tt("dve", xo[:, half * 512:(half + 1) * 512], ps[:], x[:, half * 512:(half + 1) * 512], ALU.add, [x], [ps, xo])
            self.st("sp", x1scr[t], xo[:], [xo], "stb_x%d" % (t % 2))
        P.barrier()
        P.flush()
        es_b.close()

        self.layer1(x1scr, y, pbc)

    def layer1(self, x1scr, y, pbc):
        P = self.P
        nc = self.nc
        din, dout = self.din, self.dout
        srwkv = self.d_srwkv
        sshift = self.d_sshift
        mu_d = din("mu", [6, 1024])
        w0_d = din("w0", [1, 1024])
        a0_d = din("a0", [1, 1024])
        kk_d = din("k_k", [1, 1024]); ka_d = din("k_a", [1, 1024]); rk_d = din("r_k", [1, 1024])
        lw_d = din("lnx_w", [1, 1024]); lb_d = din("lnx_b", [1, 1024]); wo_d = din("w_o", [1024, 1024])
        p_rwkv = dout("p_rwkv", [16, 64, 64]); p_shift = dout("p_shift", [1, 1024])
        s_rwkv = dout("s_rwkv", [16, 16, 64, 64]); s_shift = dout("s_shift", [16, 1024])
        pm = self.pm
        es = ExitStack()
        cur = [es]

        def sb(shape, dt=F32, name=None):
            P.ntens += 1
            return cur[0].enter_context(nc.sbuf_tensor(name or f"c{P.ntens}", list(shape), dt))
        o32 = self.ones32
        wR, w1a1, w2a2 = self.wR, self.w1a1, self.w2a2
        xx = sb([128, 1024], F32, "r_xx")
        pst = xx
        self.ld("sp", pst[0:6, :], mu_d, [pst], "ld_c")
        for i, d in enumerate((w0_d, a0_d, kk_d, ka_d, rk_d)):
            self.ld("sp", pst[6 + i:7 + i, :], d, [pst], "ld_c")
        par = sb([128, 8, 16], F32, "par")
        ps = self.bank()
        for c in range(8):
            self.tr(ps, ps[:, c * 16:c * 16 + 11], pst[0:11, c * 128:(c + 1) * 128], self.ident32[0:11, 0:11], [pst, self.ident32])
        self.cp("dve", par[:, :, 0:11], ps[:, 0:128].rearrange("p (c i) -> p c i", i=16)[:, :, 0:11], [], [ps, par])
        self.ts("dve", par[:, :, 11:12], par[:, :, 6:7], -1.0, None, ALU.mult, None, [par], [par])
        self.ts("dve", par[:, :, 12:13], par[:, :, 9:10], -1.0, 1.0, ALU.mult, ALU.add, [par], [par])
        nw1 = self.nw0
        self.ld("sp", nw1[:], pbc(self.d_norm_w[1:2, :], 1024), [nw1], "ld_c")
        def sel(out, in_, pattern, op, base, cm, r, w, fill=0.0):
            P.op("pool", lambda e: e.affine_select(out=out, in_=in_, pattern=pattern, compare_op=op, fill=fill,
                                                    base=base, channel_multiplier=cm), r=r, w=w)
        shp = sb([128, 128], F32, "shp")
        sel(shp[:], o32[:], [[1, 128]], ALU.is_equal, -1, -1, [o32], [shp])
        nb0 = sb([128, 128], F32, "nb0")
        sel(nb0[:].rearrange("p (s t) -> p s t", t=8), o32[:].rearrange("p (s t) -> p s t", t=8), [[0, 16], [1, 8]], ALU.is_gt, 0, 0,
            [o32], [nb0])
        shs = sb([128, 128], F32, "shs")
        self.tt("pool", shs[:], shp[:], nb0[:], ALU.mult, [shp, nb0], [shs])
        elast = sb([128, 128], F32, "elast")
        sel(elast[:], o32[:], [[-1, 128]], ALU.is_equal, -127, 1, [o32], [elast])
        selS = sb([16, 128], F32, "selS")
        sel(selS[:], o32[0:16, :], [[1, 128]], ALU.is_equal, 0, -8, [o32], [selS])
        b64 = sb([128, 128], F32, "b64")
        v3 = b64[:].rearrange("p (s t) -> p s t", t=64)
        sel(v3, o32[:].rearrange("p (s t) -> p s t", t=64), [[-64, 2], [0, 64]], ALU.is_ge, 0, 1, [o32], [b64])
        sel(v3, v3, [[64, 2], [0, 64]], ALU.is_ge, 63, -1, [b64], [b64])
        mneg, msup, mup = {}, {}, {}
        t_tmpm = sb([128, 128], F32, "tmpm")
        for X in ("p", "s"):
            lt = self.lowT[X]
            mup[X] = pm[("mup", X)]
            self.cp("pool", mup[X][:], lt[:], [lt], [mup[X]])
            msup[X] = pm[("msup", X)]
            sel(msup[X][:], lt[:], [[1, 128]], ALU.is_gt, 0, -1, [lt], [msup[X]])
            mneg[X] = pm[("mneg", X)]
            tmpm = t_tmpm
            self.ts("pool", tmpm[:], self.addm[X][:, 0, :], -1.0 / BIG, 1.0, ALU.mult, ALU.add, [self.addm[X]], [tmpm])
            sel(tmpm[:], tmpm[:], [[-1, 128]], ALU.is_gt, 0, 1, [tmpm], [tmpm])
            self.ts("pool", mneg[X][:], tmpm[:], -1.0, None, ALU.mult, None, [tmpm], [mneg[X]])
        hsel = sb([128, 2], F32, "hsel")
        self.cp("pool", hsel[:, 0:1], b64[:, 0:1], [b64], [hsel])
        self.cp("pool", hsel[:, 1:2], b64[:, 127:128], [b64], [hsel])
        self.invw = {}
        for nm in ("MdT", "Md", "No", "X0", "X1", "P0", "P1", "PT0", "PT1", "IpPT0", "IpPT1", "XT"):
            self.invw[nm] = sb([128, 4, 128], BF16, "jw_" + nm)
        for a_, b_ in (("V", "P0"), ("VT", "P1"), ("IpVT", "PT0"), ("IpV2T", "PT1"), ("Y", "MdT")):
            self.invw[a_] = self.invw[b_]
        xn = [sb([128, 1024], F32, "r_xn%d" % i) for i in range(2)]
        P.op("pool", lambda e: e.memset(xn[1][:], 0.0), w=[xn[1]])
        x1t = [sb([128, 1024], F32, "r_x1")] * 2
        xnT = sb([128, 8, 128], BF16, "r_xnT")
        xxT = sb([128, 8, 128], BF16, "r_xxT")
        xs_all = sb([128, 4, 8, 128], BF16, "r_xs")

        class _XS:
            def __init__(s_, i):
                s_.i = i
                s_.name = "r_xs"

            def __getitem__(s_, idx):
                return xs_all[idx[0], s_.i, idx[1], idx[2]]
        xs = [_XS(i) for i in range(4)]
        class _AL:
            def __init__(s_, apf, name):
                s_.apf = apf
                s_.name = name

            def __getitem__(s_, idx):
                return s_.apf()[idx]

        hT = sb([64, 2, 128], BF16, "r_hT")
        fm = {nm: sb([128, 8, 128], BF16, "r_" + nm) for nm in ("kap", "rho", "kt", "bt")}
        kdj = sb([128, 128], BF16, "r_kdj")
        bdj = sb([128, 128], BF16, "r_bdj")
        vT = sb([128, 8, 128], BF16, "r_vT")
        Vb = sb([128, 1024], BF16, "r_Vb")
        kdT = sb([128, 8, 128], BF16, "r_kdT")
        bdT = sb([128, 8, 128], BF16, "r_bdT")
        Pc = sb([128, 8, 16], F32, "r_Pc")
        t_all = []
        for i_ in range(2):
            t_ = {nm: sb([128, 128], F32, "r_t%d_%s" % (i_, nm)) for nm in ("e", "ew", "a", "kk", "sq", "rn", "k2", "b", "cs", "x", "r", "k")}
            t_["dd"] = t_["sq"]
            t_["rk"] = t_["rn"]
            t_all.append(t_)
        kdjs = [kdj, sb([128, 128], BF16, "r_kdj2")]
        bdjs = [bdj, sb([128, 128], BF16, "r_bdj2")]
        Z = sb([128, 8, 64], F32, "r_Z")
        Zb = sb([128, 8, 64], BF16, "r_Zb")
        P.op("pool", lambda e: e.memset(Z[:], 0.0), w=[Z])
        P.op("pool", lambda e: e.memset(Zb[:], 0.0), w=[Zb])
        Mlow = sb([128, 4, 128], BF16, "r_Mlow")
        Tt = sb([128, 4, 128], BF16, "r_Tt")
        MkT = sb([128, 4, 128], BF16, "r_MkT")
        AkT = sb([128, 4, 128], BF16, "r_AkT")
        AbT = sb([128, 4, 128], BF16, "r_AbT")
        rhsS = sb([128, 4, 64], BF16, "r_rhsS")
        SA = sb([128, 4, 64], BF16, "r_SA")
        ytok = sb([128, 16, 64], F32, "r_ytok")
        ysq = xx
        st = {nm: sb([128, 16], F32, "r_st_" + nm) for nm in ("sum", "ssq", "mean", "var", "rstd", "rkb")}
        zs = self.nb["xn"]
        rec1 = nc.dram_tensor("rwrec1", [NT, 128, 56 * 128], BF16, kind="Internal").ap()
        frec1 = nc.dram_tensor("rwfrec1", [NT, 128, 1168], F32, kind="Internal").ap()

        x2 = zs
        ss2 = sb([128, 1], F32, "r_ss2")
        ygT = _AL(lambda: xs_all[:, 3], "r_xs")
        s0in = sb([64, 2, 64], F32, "r_s0in")
        Z0b2 = [_AL(lambda q=q: xs_all[:, 2 + q].rearrange("p c (a f) -> p (c a) f", a=2), "r_xs") for q in range(2)]
        mk = _AL(lambda: xs_all[:, 0:2].rearrange("p a c f -> p (a c) f"), "r_xs")
        Vm = sb([128, 128], BF16, "r_Vm")
        yz0 = sb([128, 4, 64], F32, "r_yz0")
        mk2 = _AL(lambda: xs_all[:, 0:2].rearrange("p a c f -> p (a c) f"), "r_xs")
        SAm = sb([128, 128], BF16, "r_SAm")
        Zn = sb([128, 64], F32, "r_Zn")
        Sout = sb([64, 128], F32, "r_Sout")

        def fmh(tn, h):
            b0 = (h % 2) * 64
            return tn[b0:b0 + 64, h // 2, :]

        for t in range(NT):
            X = "p" if t < 17 else "s"
            nseq, T = (1, 128) if X == "p" else (16, 8)
            x1 = x1t[t % 2]
            xc, xp = xn[t % 2], xn[(t + 1) % 2]
            self.ld("sp", x1[:], x1scr[t], [x1], "ld1_x")
            self.act(xc[:], x1[:], AF.Square, [x1], [xc, ss2], accum_out=ss2[:])
            self.ts("dve", ss2[:], ss2[:], 1.0 / 1024, 1e-6, ALU.mult, ALU.add, [ss2], [ss2])
            self.act(ss2[:], ss2[:], AF.Ln, [ss2], [ss2])
            self.act(ss2[:], ss2[:], AF.Exp, [ss2], [ss2], scale=-0.5)
            self.stt(xc[:], x1[:], ss2[:, 0:1], nw1[:], ALU.mult, ALU.mult, [x1, ss2, nw1], [xc])
            if t == 16:
                self.st("sp", p_shift, xc[127:128, :], [xc], "st_c")
            if t == 17:
                f = fsz(xc)
                self.st("sp", s_shift, bass.AP(xc, 7 * f, [[8 * f, 16], [1, 1024]]), [xc], "st_c")
            for half in range(2):
                ps = self.bank()
                cs_ = slice(half * 512, (half + 1) * 512)
                if X == "p":
                    self.mm(ps, ps[:], shp[:], xc[:, cs_], True, False, [shp, xc])
                    self.mm(ps, ps[:], elast[:], xp[:, cs_], False, True, [elast, xp])
                else:
                    if half == 0:
                        self.ld("sp", xp[0:16, :], sshift, [xp], "ld_c")
                    self.mm(ps, ps[:], shs[:], xc[:, cs_], True, False, [shs, xc])
                    self.mm(ps, ps[:], selS[:], xp[0:16, cs_], False, True, [selS, xp])
                self.tt("dve", xx[:, cs_], ps[:], xc[:, cs_], ALU.subtract, [xc], [ps, xx])
            for (src, dst) in ((xc, xnT), (xx, xxT)):
                for half in range(2):
                    ps = self.bank()
                    for c in range(4):
                        cc = half * 4 + c
                        self.tr(ps, ps[:, c * 128:(c + 1) * 128], src[:, cc * 128:(cc + 1) * 128], self.ident32[:], [src, self.ident32])
                    self.cp(self.ev(), dst[:, half * 4:half * 4 + 4, :], ps[:].rearrange("p (c f) -> p c f", c=4), [], [ps, dst])
            def mkxs(i, dst):
                for c in range(8):
                    self.stt(dst[:, c, :], xxT[:, c, :], par[:, c, i:i + 1], xnT[:, c, :], ALU.mult, ALU.add, [xxT, xnT, par], [dst])
            for i in range(3):
                mkxs(i, xs[i])
            ps = self.bank()
            mkxs(4, xs[3])
            for c in range(8):
                self.mm(ps, ps[0:64, 0:128], w1a1[:, c, 0:64], xs[3][:, c, :], c == 0, c == 7, [w1a1, xs[3]])
            mkxs(5, xs[3])
            for c in range(8):
                self.mm(ps, ps[0:64, 128:256], w1a1[:, c, 64:128], xs[3][:, c, :], c == 0, c == 7, [w1a1, xs[3]])
            self.act(hT[:, 0, :], ps[0:64, 0:128], AF.Tanh, [], [ps, hT])
            self.cp("dve", hT[:, 1, :], ps[0:64, 128:256], [], [ps, hT])
            mkxs(3, xs[3])
            for half in range(2):
                ps = self.bank()
                for c in range(8):
                    self.mm(ps, ps[:], xs[3][:, c, :], wR[:, c, 3072 + half * 512:3072 + (half + 1) * 512], c == 0, c == 7, [xs[3], wR])
                self.act(zs[:, half * 512:(half + 1) * 512], ps[:], AF.Silu, [], [ps, zs])
            psRK = self.pfx
            pjb = {}

            def proj1(j):
                psA_ = self.bank()
                psB_ = self.bank()
                pjb[j] = (psA_, psB_)
                for i in range(3):
                    for c in range(8):
                        self.mm(psA_, psA_[:, i * 128:(i + 1) * 128], wR[:, c, i * 1024 + j * 128:i * 1024 + (j + 1) * 128], xs[i][:, c, :],
                                c == 0, c == 7, [wR, xs[i]])
                self.mm(psA_, psA_[:, 384:512], w2a2[:, 0, j * 128:(j + 1) * 128], hT[:, 0, :], True, True, [w2a2, hT])
                self.mm(psB_, psB_[:, 0:128], w2a2[:, 1, j * 128:(j + 1) * 128], hT[:, 1, :], True, True, [w2a2, hT])
            def elem(j):
                yield
                psA_, psB_ = pjb[j]
                yield
                t_ = t_all[j % 2]
                yield
                kdj, bdj = kdjs[j % 2], bdjs[j % 2]
                yield
                pj = lambda i: par[:, j, i:i + 1]
                yield
                yield
                self.act(t_["e"][:], psA_[:, 384:512], AF.Exp, [par], [psA_, t_["e"]], scale=-1.0, bias=pj(11))
                yield
                self.act(t_["e"][:], t_["e"][:], AF.Ln, [t_["e"]], [t_["e"]], bias=1.0)
                yield
                self.act(t_["ew"][:], t_["e"][:], AF.Exp, [t_["e"]], [t_["ew"]], scale=-1.0, bias=-0.5)
                yield
                self.act(t_["a"][:], psB_[:, 0:128], AF.Sigmoid, [par], [psB_, t_["a"]], bias=pj(7))
                yield
                self.cp("act", t_["r"][:], psA_[:, 0:128], [], [psA_, t_["r"]])
                yield
                self.cp("dve", t_["k"][:], psA_[:, 128:256], [], [psA_, t_["k"]])
                yield
                self.cp("act", vT[:, j, :], psA_[:, 256:384], [], [psA_, vT])
                yield
                self.ts("dve", t_["kk"][:], t_["k"][:], pj(8), None, ALU.mult, None, [t_["k"], par], [t_["kk"]])
                yield
                self.act(t_["sq"][:], t_["kk"][:], AF.Square, [t_["kk"]], [t_["sq"]])
                yield
                psn = self.bank()
                yield
                self.mm(psn, psn[:, 0:128], b64[:], t_["sq"][:], True, True, [b64, t_["sq"]])
                yield
                self.act(t_["rn"][:], psn[:, 0:128], AF.Ln, [], [psn, t_["rn"]], bias=1e-6)
                yield "ps_done"
                self.act(t_["rn"][:], t_["rn"][:], AF.Exp, [t_["rn"]], [t_["rn"]], scale=-0.5)
                yield
                self.tt("dve", t_["kk"][:], t_["kk"][:], t_["rn"][:], ALU.mult, [t_["kk"], t_["rn"]], [t_["kk"]])
                yield
                self.ts("dve", t_["x"][:], t_["a"][:], pj(9), pj(12), ALU.mult, ALU.add, [t_["a"], par], [t_["x"]])
                yield
                self.tt("dve", t_["k2"][:], t_["k"][:], t_["x"][:], ALU.mult, [t_["k"], t_["x"]], [t_["k2"]])
                yield
                self.tt("pool", t_["b"][:], t_["kk"][:], t_["a"][:], ALU.mult, [t_["kk"], t_["a"]], [t_["b"]])
                yield
                yield
                self.stt(t_["rk"][:], t_["r"][:], pj(10), t_["k2"][:], ALU.mult, ALU.mult, [t_["r"], t_["k2"], par], [t_["rk"]])
                yield
                self.mm(psRK, psRK[:, 2 * j:2 * j + 2], t_["rk"][:], hsel[:], True, True, [t_["rk"], hsel])
                yield
                yield
                msk = o32 if X == "p" else nb0
                yield
                P.op("dve", lambda e, msk=msk, t_=t_: e.tensor_tensor_scan(out=t_["cs"][:], data0=msk[:], data1=t_["ew"][:], initial=0.0,
                                                                    op0=ALU.mult, op1=ALU.add), r=[msk, t_["ew"]], w=[t_["cs"]])
                yield
                cs = t_["cs"]
                yield
                yield
                self.act(Pc[:, j, 0:nseq], V(cs, T - 1, [[T, nseq]]), AF.Exp, [cs], [Pc], scale=-1.0)
                yield
                yield
                self.tt("dve", t_["dd"][:].rearrange("p (s t) -> p s t", t=T), cs[:].rearrange("p (s t) -> p s t", t=T),
                        V(cs, T - 1, [[T, nseq], [0, T]]), ALU.subtract, [cs], [t_["dd"]])
                yield
                self.act(t_["dd"][:], t_["dd"][:], AF.Exp, [t_["dd"]], [t_["dd"]])
                yield
                self.tt("dve", kdj[:], t_["dd"][:], t_["k2"][:], ALU.mult, [t_["dd"], t_["k2"]], [kdj])
                yield
                self.tt("pool", bdj[:], t_["dd"][:], t_["b"][:], ALU.mult, [t_["dd"], t_["b"]], [bdj])
                yield
                self.tr(self.pb[0], self.pb[0][:, j * 128:(j + 1) * 128], kdj[:], self.identb[:], [kdj, self.identb])
                yield
                self.tr(self.pb[1], self.pb[1][:, j * 128:(j + 1) * 128], bdj[:], self.identb[:], [bdj, self.identb])
                yield
                yield
                self.act(t_["x"][:], cs[:], AF.Exp, [cs], [t_["x"]])
                yield
                self.tt("dve", fm["kt"][:, j, :], t_["x"][:], t_["k2"][:], ALU.mult, [t_["x"], t_["k2"]], [fm["kt"]])
                yield
                self.tt("pool", fm["bt"][:, j, :], t_["x"][:], t_["b"][:], ALU.mult, [t_["x"], t_["b"]], [fm["bt"]])
                yield
                yield
                self.act(t_["x"][:], cs[:], AF.Exp, [cs], [t_["x"]], scale=-1.0)
                yield
                self.tt("dve", fm["rho"][:, j, :], t_["x"][:], t_["r"][:], ALU.mult, [t_["x"], t_["r"]], [fm["rho"]])
                yield
                self.tt("pool", t_["e"][:], cs[:], t_["ew"][:], ALU.subtract, [cs, t_["ew"]], [t_["e"]])
                yield
                self.act(t_["e"][:], t_["e"][:], AF.Exp, [t_["e"]], [t_["e"]], scale=-1.0)
                yield
                self.tt("dve", fm["kap"][:, j, :], t_["e"][:], t_["kk"][:], ALU.mult, [t_["e"], t_["kk"]], [fm["kap"]])

            proj1(0)
            proj1(1)
            for pr in range(4):
                gens = [elem(2 * pr), elem(2 * pr + 1)]
                ndone = 0
                projected = (pr == 3)
                while gens:
                    for g_ in list(gens):
                        try:
                            r_ = next(g_)
                            if r_ == "ps_done":
                                ndone += 1
                        except StopIteration:
                            gens.remove(g_)
                    if ndone == 2 and not projected:
                        proj1(2 * pr + 2)
                        proj1(2 * pr + 3)
                        projected = True
            self.cp("dve", st["rkb"][:], psRK[:, 0:16], [], [psRK, st["rkb"]])
            self.cp("act", kdT[:], self.pb[0][:].rearrange("p (c f) -> p c f", c=8), [], [self.pb[0], kdT])
            self.cp("dve", bdT[:], self.pb[1][:].rearrange("p (c f) -> p c f", c=8), [], [self.pb[1], bdT])
            pb = self.bbank()
            for c in range(8):
                self.tr(pb, pb[:, c * 128:(c + 1) * 128], vT[:, c, :], self.identb[:], [vT, self.identb])
            self.cp("act", Vb[:], pb[:], [], [pb, Vb])
            for i_, src_ in enumerate((fm["kap"], fm["rho"], fm["kt"], fm["bt"], kdT, bdT)):
                self.st("sp", rec1[t, :, i_ * 1024:(i_ + 1) * 1024], src_[:].rearrange("p c f -> p (c f)"), [src_], "st_r1%d" % i_)
            self.st("sp", rec1[t, :, 6144:7168], Vb[:], [Vb], "st_r16")
            self.st("sp", frec1[t, :, 0:1024], zs[:], [zs], "st_r17")
            self.st("sp", frec1[t, :, 1024:1152], Pc[:].rearrange("p c s -> p (c s)"), [Pc], "st_r18")
            self.st("sp", frec1[t, :, 1152:1168], st["rkb"][:], [st["rkb"]], "st_r19")
        P.barrier()
        P.flush()
        es.close()
        self.es_w1.close()

        es = ExitStack()
        cur[0] = es
        wO = sb([128, 8, 1024], BF16, "wO1")
        for c in range(8):
            P.dma("pool", lambda e, c=c: e.dma_start(out=wO[:, c, :], in_=wo_d[c * 128:(c + 1) * 128, :]), w=[wO], sem="ld_w")
        fnw = sb([128, 1024], F32, "fnw")
        lnw = sb([128, 1024], F32, "lnw")
        lnb = sb([128, 1024], F32, "lnb")
        self.ld("sp", fnw[:], pbc(self.d_fnorm_w, 1024), [fnw], "ld_c")
        self.ld("sp", lnw[:], pbc(lw_d, 1024), [lnw], "ld_c")
        self.ld("sp", lnb[:], pbc(lb_d, 1024), [lnb], "ld_c")
        recs = [sb([128, 56, 128], BF16, "b_rec%d" % i) for i in range(2)]
        frecs = [sb([128, 1168], F32, "b_frec%d" % i) for i in range(2)]
        x1t = [sb([128, 1024], F32, "b_x1")] * 2
        Z = sb([128, 8, 64], F32, "b_Z")
        Zb = sb([128, 8, 64], BF16, "b_Zb")
        P.op("pool", lambda e: e.memset(Z[:], 0.0), w=["Z%d" % i for i in range(8)])
        P.op("pool", lambda e: e.memset(Zb[:], 0.0), w=["Zb%d" % i for i in range(8)])
        sets = []
        for gi in range(4):
            B = {nm: sb([128, 4, 128], BF16, "h%d_%s" % (gi, nm)) for nm in ("Mlow", "Tt", "MkT", "AkT", "AbT")}
            B["rhsS"] = sb([128, 4, 64], BF16, "h%d_rhsS" % gi)
            B["SA"] = sb([128, 4, 64], BF16, "h%d_SA" % gi)
            B["yz0"] = sb([128, 4, 64], F32, "h%d_yz0" % gi)
            B["W"] = self.inv_bufs(sb, "h%d" % gi)
            sets.append(B)
        for gi in range(2):
            sh = {"s0in4": sb([64, 4, 2, 64], F32, "h%d_s0in4" % gi), "Vm4": sb([128, 4, 128], BF16, "h%d_Vm4" % gi),
                  "SAm4": sb([128, 4, 128], BF16, "h%d_SAm4" % gi), "Zn4": sb([128, 4, 64], F32, "h%d_Zn4" % gi),
                  "Sout4": sb([64, 4, 128], F32, "h%d_Sout4" % gi)}
            sets[gi].update(sh)
            sets[gi + 2].update(sh)
        ytoks = [sb([128, 16, 64], F32, "b_ytok%d" % i) for i in range(2)]
        ysq = self.nb["xn"]
        st = {nm: sb([128, 16], F32, "b_st_" + nm) for nm in ("sum", "ssq", "mean", "var", "rstd")}
        ss2 = sb([128, 1], F32, "b_ss2")
        ygT = sb([128, 8, 128], BF16, "b_ygT")
        x2 = ysq
        Z0b2 = [sb([128, 16, 64], BF16, "b_Z0b%d" % i) for i in range(2)]
        mk = sb([128, 16, 128], BF16, "b_mk")
        Sout = sb([64, 128], F32, "b_Sout")
        KAP, RHO, KT, BT, KD, BD, VB = 0, 8, 16, 24, 32, 40, 48
        v4 = lambda ps: ps[:, 0:512].rearrange("p (h f) -> p h f", h=4)

        def rwkv_group(t, g, B, rec, frec, ytok, ytn):
            X = "p" if t < 17 else "s"
            hs = [4 * g + hh for hh in range(4)]
            order = (0, 2, 1, 3)

            def fmh(off, h):
                b0 = (h % 2) * 64
                return rec[b0:b0 + 64, off + h // 2, :]
            Vh = lambda h: rec[:, VB + h // 2, (h % 2) * 64:(h % 2) * 64 + 64]
            Mlow, Tt, MkT, AkT, AbT, rhsS, SA, yz0 = (B[k_] for k_ in ("Mlow", "Tt", "MkT", "AkT", "AbT", "rhsS", "SA", "yz0"))

            def prod4(ps, lo, ro):
                for i_, hh in enumerate(order):
                    h = hs[hh]
                    self.mm(ps, ps[:, hh * 128:(hh + 1) * 128], fmh(lo, h), fmh(ro, h), True, True, [rec], sw=(i_ == 2))
            psM = self.bank()
            prod4(psM, KAP, BT)
            self.tt("dve", Mlow[:], v4(psM), self.bc4(mneg[X]), ALU.mult, [mneg[X]], [psM, Mlow])
            for (lo, ro, dst, msk) in ((KT, KAP, MkT, msup[X]), (KT, RHO, AkT, mup[X]), (BT, RHO, AbT, mup[X])):
                ps = self.bank()
                prod4(ps, lo, ro)
                self.tt("dve", dst[:], v4(ps), self.bc4(msk), ALU.mult, [msk], [ps, dst])
            yield
            yield from self.inverse_gen(B["W"], Mlow, Tt)
            s0in4, Vm4, SAm4, Zn4, Sout4 = (B[k_] for k_ in ("s0in4", "Vm4", "SAm4", "Zn4", "Sout4"))
            if X == "s":
                for q in range(2):
                    m = 2 * g + q
                    for sg in range(4):
                        for h2_ in range(2):
                            self.ld("sp", s0in4[:, :, h2_, :], srwkv[4 * sg:4 * sg + 4, 2 * m + h2_].rearrange("s v k -> v s k"), [s0in4],
                                    "ld_s0in%d" % (g % 2))
                        pz = self.bank()
                        for si in range(4):
                            self.tr(pz, pz[:, si * 64:(si + 1) * 64], s0in4[:, si].rearrange("v h k -> v (h k)"), self.ident32[0:64, 0:64],
                                    [s0in4, self.ident32])
                        self.cp(self.ev(), Z0b2[q][:, 4 * sg:4 * sg + 4, :], pz[:, 0:256].rearrange("p (s f) -> p s f", s=4), [],
                                [pz, Z0b2[q]])
            psR = self.bank()
            if X == "s":
                psY2 = self.bank()
            for hh, h in enumerate(hs):
                b0 = (h % 2) * 64
                m = h // 2
                if X == "s":
                    Z0b = Z0b2[hh // 2]
                    for (off_, psd, lastflag) in ((KAP, psR, False), (RHO, psY2, True)):
                        a_ = fmh(off_, h)
                        self.tt("pool", mk[b0:b0 + 64], bass.AP(a_.tensor, a_.offset, [list(a_.ap[0]), [0, 16], [1, 128]]),
                                self.cm[b0:b0 + 64], ALU.mult, [rec, self.cm], [mk])
                        for s_ in range(16):
                            self.mm(psd, psd[:, hh * 64:(hh + 1) * 64], mk[b0:b0 + 64, s_, :], Z0b[b0:b0 + 64, s_, :], s_ == 0,
                                    lastflag and s_ == 15, [mk, Z0b], sw=(s_ == 0))
                else:
                    self.mm(psR, psR[:, hh * 64:(hh + 1) * 64], fmh(KAP, h), Zb[b0:b0 + 64, m, :], True, False, [rec, "Zb%d" % m], sw=True)
                self.mm(psR, psR[:, hh * 64:(hh + 1) * 64], MkT[:, hh, :], Vh(h), False, True, [MkT, rec])
            if X == "s":
                self.cp("dve", yz0[:].rearrange("p h f -> p (h f)"), psY2[:, 0:256], [], [psY2, yz0])
            self.act(rhsS[:].rearrange("p h f -> p (h f)"), psR[:, 0:256], AF.Copy, [], [psR, rhsS], scale=-1.0)
            yield
            psS = self.bank()
            for hh in range(4):
                self.mm(psS, psS[:, hh * 64:(hh + 1) * 64], Tt[:, hh, :], rhsS[:, hh, :], True, True, [Tt, rhsS])
            self.cp("act", SA[:].rearrange("p h f -> p (h f)"), psS[:, 0:256], [], [psS, SA])
            yield
            psY = self.bank()
            for hh, h in enumerate(hs):
                b0 = (h % 2) * 64
                m = h // 2
                if X == "p":
                    self.mm(psY, psY[:, hh * 64:(hh + 1) * 64], fmh(RHO, h), Zb[b0:b0 + 64, m, :], True, False, [rec, "Zb%d" % m], sw=True)
                self.mm(psY, psY[:, hh * 64:(hh + 1) * 64], AkT[:, hh, :], Vh(h), X == "s", False, [AkT, rec])
                self.mm(psY, psY[:, hh * 64:(hh + 1) * 64], AbT[:, hh, :], SA[:, hh, :], False, True, [AbT, SA])
            yv = ytok[:, 4 * g:4 * g + 4, :].rearrange("p h f -> p (h f)")
            if X == "p":
                self.cp("dve", yv, psY[:, 0:256], [], [psY, ytn + str(g)])
            else:
                self.tt("dve", yv, psY[:, 0:256], yz0[:].rearrange("p h f -> p (h f)"), ALU.add, [yz0], [psY, ytn + str(g)])
            for mm_ in range(2):
                m = 2 * g + mm_
                SApair = SA[:, 2 * mm_:2 * mm_ + 2, :].rearrange("p h f -> p (h f)")
                if X == "p":
                    psZ = self.bank()
                    self.mm(psZ, psZ[:, 0:128], rec[:, KD + m, :], rec[:, VB + m, :], True, False, [rec])
                    self.mm(psZ, psZ[:, 0:128], rec[:, BD + m, :], SApair, False, True, [rec, SA])
                    for h2 in range(2):
                        b0 = h2 * 64
                        self.stt(Z[b0:b0 + 64, m, :], Z[b0:b0 + 64, m, :], V(frec, 1024 + m * 16, [[1, 1]], 64, b0),
                                 psZ[b0:b0 + 64, b0:b0 + 64], ALU.mult, ALU.add, ["Z%d" % m, frec], [psZ, "Z%d" % m])
                    self.cp("act", Zb[:, m, :], Z[:, m, :], ["Z%d" % m], ["Zb%d" % m])
                    if t == 16:
                        pz = self.bank()
                        self.tr(pz, pz[0:64, 0:128], Z[:, m, :], self.ident32[:], ["Z%d" % m, self.ident32])
                        self.cp("act", Sout[:], pz[0:64, 0:128], [], [pz, Sout])
                        self.st("sp", p_rwkv[2 * m:2 * m + 2].rearrange("h v k -> v h k"), Sout[:].rearrange("v (h k) -> v h k", h=2),
                                [Sout], "st_so")
                else:
                    for sg in range(4):
                        for h2_ in range(2):
                            self.ld("sp", s0in4[:, :, h2_, :], srwkv[4 * sg:4 * sg + 4, 2 * m + h2_].rearrange("s v k -> v s k"), [s0in4],
                                    "ld_s0in%d" % (g % 2))
                        pz = self.bank()
                        for si in range(4):
                            self.tr(pz, pz[:, si * 64:(si + 1) * 64], s0in4[:, si].rearrange("v h k -> v (h k)"), self.ident32[0:64, 0:64],
                                    [s0in4, self.ident32])
                        vpair = rec[:, VB + m, :]
                        smb = V(self.sm, 4 * sg, [[1, 4], [0, 128]])
                        self.tt("pool", Vm4[:], bass.AP(vpair.tensor, vpair.offset, [list(vpair.ap[0]), [0, 4], [1, 128]]), smb, ALU.mult,
                                [rec, self.sm], [Vm4])
                        sap = SA[:, 2 * mm_:2 * mm_ + 2, :]
                        self.tt("pool", SAm4[:], bass.AP(sap.tensor, sap.offset, [list(sap.ap[0]), [0, 4], [1, 128]]), smb, ALU.mult,
                                [SA, self.sm], [SAm4])
                        psZ = self.bank()
                        for si in range(4):
                            self.mm(psZ, psZ[:, si * 128:(si + 1) * 128], rec[:, KD + m, :], Vm4[:, si, :], True, False, [rec, Vm4])
                            self.mm(psZ, psZ[:, si * 128:(si + 1) * 128], rec[:, BD + m, :], SAm4[:, si, :], False, True, [rec, SAm4])
                        self.tt("dve", Zn4[:], pz[:, 0:256].rearrange("p (s f) -> p s f", s=4),
                                V(frec, 1024 + m * 16 + 4 * sg, [[1, 4], [0, 64]]), ALU.mult, [frec], [pz, Zn4])
                        for h2 in range(2):
                            b0 = h2 * 64
                            self.tt("dve", Zn4[b0:b0 + 64], Zn4[b0:b0 + 64],
                                    psZ[b0:b0 + 64, 0:512].rearrange("p (s f) -> p s f", s=4)[:, :, b0:b0 + 64], ALU.add, [Zn4], [psZ, Zn4])
                        pz2 = self.bank()
                        for si in range(4):
                            self.tr(pz2, pz2[0:64, si * 128:(si + 1) * 128], Zn4[:, si, :], self.ident32[:], [Zn4, self.ident32])
                        self.cp("act", Sout4[:].rearrange("v s f -> v (s f)"), pz2[0:64, 0:512], [], [pz2, Sout4])
                        for h2_ in range(2):
                            self.st("sp", s_rwkv[4 * sg:4 * sg + 4, 2 * m + h2_].rearrange("s v k -> v s k"),
                                    Sout4[:, :, h2_ * 64:(h2_ + 1) * 64], [Sout4], "st_so%d" % (g % 2))
                        yield
            yield

        def post_gen(t, rec, frec, ytok, ytn):
            x1 = x1t[0]
            ytokr = [ytn + str(g_) for g_ in range(4)]
            self.ld("sp", x1[:], x1scr[t], [x1], "x")
            P.op("dve", lambda e: e.tensor_reduce(out=st["sum"][:], in_=ytok[:], axis=AX.X, op=ALU.add), r=ytokr, w=[st["sum"]])
            ysq3 = ysq[:].rearrange("p (h f) -> p h f", h=16)
            self.act(ysq3, ytok[:], AF.Square, ytokr, [ysq])
            P.op("dve", lambda e: e.tensor_reduce(out=st["ssq"][:], in_=ysq3, axis=AX.X, op=ALU.add), r=[ysq], w=[st["ssq"]])
            yield
            self.ts("dve", st["mean"][:], st["sum"][:], 1.0 / 64, None, ALU.mult, None, [st["sum"]], [st["mean"]])
            self.tt("dve", st["var"][:], st["mean"][:], st["mean"][:], ALU.mult, [st["mean"]], [st["var"]])
            self.stt(st["var"][:], st["ssq"][:], 1.0 / 64, st["var"][:], ALU.mult, ALU.subtract, [st["ssq"], st["var"]], [st["var"]])
            self.ts("dve", st["var"][:], st["var"][:], 64e-5, None, ALU.add, None, [st["var"]], [st["var"]])
            self.act(st["rstd"][:], st["var"][:], AF.Ln, [st["var"]], [st["rstd"]])
            self.act(st["rstd"][:], st["rstd"][:], AF.Exp, [st["rstd"]], [st["rstd"]], scale=-0.5)
            yield
            self.tt("dve", ysq3, ytok[:], V(st["mean"], 0, [[1, 16], [0, 64]]), ALU.subtract, ytokr + [st["mean"]], [ysq])
            self.tt("dve", ysq3, ysq3, V(st["rstd"], 0, [[1, 16], [0, 64]]), ALU.mult, [ysq, st["rstd"]], [ysq])
            yield
            yf = ysq[:]
            self.tt("pool", yf, yf, lnw[:], ALU.mult, [ysq, lnw], [ysq])
            self.tt("pool", yf, yf, lnb[:], ALU.add, [ysq, lnb], [ysq])
            yield
            yt3 = ytok[:]
            self.tt("dve", yt3, rec[:, VB:VB + 8, :].rearrange("p c (a f) -> p (c a) f", a=2), V(frec, 1152, [[1, 16], [0, 64]]), ALU.mult,
                    [rec, frec] + ytokr, ytokr)
            self.tt("dve", yf, yf, ytok[:].rearrange("p h f -> p (h f)"), ALU.add, [ysq] + ytokr, [ysq])
            self.tt("dve", yf, yf, frec[:, 0:1024], ALU.mult, [ysq, frec], [ysq])
            yield
            for half in range(2):
                ps = self.bank()
                for c in range(4):
                    cc = half * 4 + c
                    self.tr(ps, ps[:, c * 128:(c + 1) * 128], yf[:, cc * 128:(cc + 1) * 128], self.ident32[:], [ysq, self.ident32])
                self.cp(self.ev(), ygT[:, half * 4:half * 4 + 4, :], ps[:].rearrange("p (c f) -> p c f", c=4), [], [ps, ygT])
                yield
            for half in range(2):
                ps = self.bank()
                cs_ = slice(half * 512, (half + 1) * 512)
                for c in range(8):
                    self.mm(ps, ps[:], ygT[:, c, :], wO[:, c, cs_], c == 0, c == 7, [ygT, wO])
                self.tt("dve", x2[:, cs_], ps[:], x1[:, cs_], ALU.add, [x1], [ps, x2])
                yield
            yy = ytok
            yyv = ytok[:].rearrange("p h f -> p (h f)")
            self.act(yyv, x2[:], AF.Square, [x2], ytokr + [ss2], accum_out=ss2[:])
            self.ts("dve", ss2[:], ss2[:], 1.0 / 1024, 1e-6, ALU.mult, ALU.add, [ss2], [ss2])
            self.act(ss2[:], ss2[:], AF.Ln, [ss2], [ss2])
            self.act(ss2[:], ss2[:], AF.Exp, [ss2], [ss2], scale=-0.5)
            yield
            self.stt(yyv, x2[:], ss2[:, 0:1], fnw[:], ALU.mult, ALU.mult, [x2, ss2, fnw], ytokr)
            self.st("sp", y[t], yyv, ytokr, "stc_y")

        pending = None
        for t in range(NT):
            rec, frec = recs[t % 2], frecs[t % 2]
            ytok, ytn = ytoks[t % 2], "yt%d_" % (t % 2)
            self.ld("sp", rec[:].rearrange("p a f -> p (a f)"), rec1[t], [rec], "x")
            self.ld("sp", frec[:], frec1[t], [frec], "x")
            gens = [rwkv_group(t, g, sets[g], rec, frec, ytok, ytn) for g in range(4)]
            if pending is not None:
                gens.append(pending)
            self.run_rr(gens)
            pending = post_gen(t, rec, frec, ytok, ytn)
        self.run_rr([pending])
        P.barrier()
        P.flush()
        es.close()
        P.es.close()


_CACHE = {}


def kernel(x_prompt, x_sample, state_gdn, state_gdn_conv, state_rwkv, state_rwkv_shift,
           meta_tokens, norm_w, final_norm_w,
           gdn_w_in, gdn_conv_w, gdn_a_log, gdn_dt_bias, gdn_norm_w, gdn_w_out,
           rwkv_mu, rwkv_w_rkvz, rwkv_w0, rwkv_w1, rwkv_w2, rwkv_a0, rwkv_a1, rwkv_a2,
           rwkv_k_k, rwkv_k_a, rwkv_r_k, rwkv_lnx_w, rwkv_lnx_b, rwkv_w_o):
    f = lambda a: np.ascontiguousarray(np.asarray(a, dtype=np.float32))
    if "nc" not in _CACHE:
        _CACHE["nc"] = K().nc
    nc = _CACHE["nc"]
    x_prompt, x_sample = f(x_prompt), f(x_sample)
    meta = f(meta_tokens)
    in_maps = []
    for c in range(8):
        xin = np.zeros((NT, 128, 1024), np.float32)
        xin[0, 112:128] = meta
        xin[1:17] = x_prompt[c].reshape(16, 128, 1024)
        xin[17] = x_sample[16 * c:16 * c + 16].reshape(128, 1024)
        in_maps.append({
            "xin": xin,
            "sgdn": f(state_gdn[0, 16 * c:16 * c + 16]),
            "sconv": f(state_gdn_conv[0, 16 * c:16 * c + 16]).reshape(48, 4096),
            "srwkv": f(state_rwkv[0, 16 * c:16 * c + 16]),
            "sshift": f(state_rwkv_shift[0, 16 * c:16 * c + 16]),
            "norm_w": f(norm_w), "fnorm_w": f(final_norm_w).reshape(1, 1024),
            "w_in": f(gdn_w_in[0]), "conv_w": f(gdn_conv_w[0]), "a_log": f(gdn_a_log), "dt_bias": f(gdn_dt_bias),
            "gnorm_w": f(gdn_norm_w), "w_out": f(gdn_w_out[0]),
            "mu": f(rwkv_mu[0]), "rkvz": f(rwkv_w_rkvz[0]), "w0": f(rwkv_w0), "w1": f(rwkv_w1[0]), "w2": f(rwkv_w2[0]),
            "a0": f(rwkv_a0), "a1": f(rwkv_a1[0]), "a2": f(rwkv_a2[0]), "k_k": f(rwkv_k_k), "k_a": f(rwkv_k_a),
            "r_k": f(rwkv_r_k).reshape(1, 1024), "lnx_w": f(rwkv_lnx_w), "lnx_b": f(rwkv_lnx_b), "w_o": f(rwkv_w_o[0]),
        })
    res = run_bass_kernel_spmd(nc, in_maps, core_ids=list(range(8)))
    R = res.results
    y_prompt = np.stack([R[c]["y"][1:17].reshape(2048, 1024) for c in range(8)])
    y_sample = np.concatenate([R[c]["y"][17].reshape(16, 8, 1024) for c in range(8)])
    p_gdn = np.stack([R[c]["p_gdn"] for c in range(8)])[None]
    p_conv = np.stack([R[c]["p_conv"] for c in range(8)])[None]
    s_gdn = np.concatenate([R[c]["s_gdn"] for c in range(8)])[None]
    s_conv = np.concatenate([R[c]["s_conv"].reshape(16, 3, 4096) for c in range(8)])[None]
    p_rwkv = np.stack([R[c]["p_rwkv"] for c in range(8)])[None]
    p_shift = np.stack([R[c]["p_shift"].reshape(1024) for c in range(8)])[None]
    s_rwkv = np.concatenate([R[c]["s_rwkv"] for c in range(8)])[None]
    s_shift = np.concatenate([R[c]["s_shift"] for c in range(8)])[None]
    return (y_prompt, y_sample, p_gdn, p_conv, p_rwkv, p_shift, s_gdn, s_conv, s_rwkv, s_shift)
```

```python
from contextlib import ExitStack
import numpy as np
import concourse.bass as bass
import concourse.mybir as mybir
from concourse.bass_utils import run_bass_kernel_spmd

F32 = mybir.dt.float32
BF16 = mybir.dt.bfloat16
AF = mybir.ActivationFunctionType
ALU = mybir.AluOpType
AX = mybir.AxisListType
ENGS = ("pe", "act", "dve", "pool", "sp")
NT = 18
BIG = 30000.0


class Prog:
    CE = ("pe", "act", "dve", "pool")

    def __init__(self, nc):
        self.nc = nc
        self.es = ExitStack()
        self.q = {e: [] for e in ENGS}
        self.cnt = {}
        self.waited = {e: {} for e in ENGS}
        self.lastw = {}
        self.reads = {}
        self.semkeys = set(ENGS)
        self.ntens = 0
        self.nops = 0
        self.nidx = {e: 0 for e in self.CE}
        self.marks = {e: [] for e in self.CE}
        self.entry = {e: {} for e in self.CE}

    def sb(self, shape, dt=F32, name=None):
        self.ntens += 1
        name = name or f"t{self.ntens}"
        return self.es.enter_context(self.nc.sbuf_tensor(name, list(shape), dt))

    def ps(self, shape, dt=F32, name=None):
        self.ntens += 1
        name = name or f"p{self.ntens}"
        return self.es.enter_context(self.nc.psum_tensor(name, list(shape), dt))

    @staticmethod
    def _k(x):
        return x if isinstance(x, str) else x.name

    def _deps(self, r, w):
        toks = []
        for x in r:
            t = self.lastw.get(x)
            if t is not None:
                toks.append(t)
        for x in w:
            t = self.lastw.get(x)
            if t is not None:
                toks.append(t)
            toks.extend(self.reads.get(x, {}).values())
        return toks

    def _record(self, r, w, tok):
        key = tok[1]
        for x in r:
            d = self.reads.setdefault(x, {})
            o = d.get(key)
            if o is None or o[2] < tok[2]:
                d[key] = tok
        for x in w:
            self.lastw[x] = tok
            self.reads[x] = {}

    def _resolve(self, tok, peek=False):
        if tok[0] == "D":
            return tok[1], tok[2]
        _, eng, idx = tok
        m = self.marks[eng]
        if m and m[-1][0] >= idx:
            lo, hi = 0, len(m) - 1
            while lo < hi:
                mid = (lo + hi) // 2
                if m[mid][0] >= idx:
                    hi = mid
                else:
                    lo = mid + 1
            return eng, m[lo][1]
        ent = self.entry[eng][idx]
        cnt = len(m) + 1
        m.append((idx, cnt))
        ent[2] = (eng, 1)
        return eng, cnt

    def _waits(self, eng, toks, pe_self=False):
        need = {}
        for tok in toks:
            if tok[0] == "E" and tok[1] == eng and eng == "pe" and not pe_self:
                continue
            k, v = self._resolve(tok)
            if self.waited[eng].get(k, 0) >= v:
                continue
            need[k] = max(need.get(k, 0), v)
        for k, v in need.items():
            self.waited[eng][k] = v
        return list(need.items())

    def op(self, eng, fn, r=(), w=(), pe_self=False):
        r = [self._k(x) for x in r]
        w = [self._k(x) for x in w]
        waits = self._waits(eng, self._deps(r, w), pe_self)
        self.nidx[eng] += 1
        idx = self.nidx[eng]
        ent = [waits, fn, None]
        self.entry[eng][idx] = ent
        self.q[eng].append(ent)
        tok = ("E", eng, idx)
        self._record(r, w, tok)
        self.nops += 1
        return tok

    def dma(self, q, fn, r=(), w=(), sem=None):
        r = [self._k(x) for x in r]
        w = [self._k(x) for x in w]
        sem = ("L_" + w[0]) if w else ("S_" + r[0])
        self.semkeys.add(sem)
        waits = self._waits(q, self._deps(r, w))
        self.cnt[sem] = self.cnt.get(sem, 0) + 16
        tok = ("D", sem, self.cnt[sem])
        self.q[q].append([waits, fn, (sem, 16)])
        self._record(r, w, tok)
        self.nops += 1
        return tok

    def _all_toks(self):
        toks = [("E", e, self.nidx[e]) for e in self.CE if self.nidx[e] > 0]
        toks += [("D", k, v) for k, v in self.cnt.items()]
        return toks

    def barrier(self):
        toks = self._all_toks()
        for e in ENGS:
            w = self._waits(e, [t for t in toks if not (t[0] == "E" and t[1] == e)])
            if w:
                self.q[e].append([w, None, None])

    def flush(self):
        nc = self.nc
        if not hasattr(self, "sems"):
            self.sems = {}
        for k in sorted(self.semkeys):
            if k not in self.sems:
                self.sems[k] = self.es.enter_context(nc.semaphore("s_" + k))
        sems = self.sems
        q = self.q
        self.q = {e: [] for e in ENGS}
        self.entry = {e: {} for e in self.CE}

        def run(e, lst):
            for waits, fn, inc in lst:
                for k, v in waits:
                    e.wait_ge(sems[k], v)
                if fn is not None:
                    inst = fn(e)
                    if inc is not None:
                        inst.then_inc(sems[inc[0]], inc[1])

        with nc.Block() as block:
            @block.tensor
            def _(e):
                run(e, q["pe"])

            @block.scalar
            def _(e):
                run(e, q["act"])

            @block.vector
            def _(e):
                run(e, q["dve"])

            @block.gpsimd
            def _(e):
                run(e, q["pool"])

            @block.sync
            def _(e):
                run(e, q["sp"])


def fsz(t):
    n = 1
    for s in list(t.shape)[1:]:
        n *= int(s)
    return n


def V(t, off, dims, npart=128, p0=0):
    f = fsz(t)
    return bass.AP(t, p0 * f + off, [[f, npart]] + [list(d) for d in dims])


class K:
    def __init__(self, nlayers=2):
        self.nlayers = nlayers
        nc = bass.Bass("TRN2", target_bir_lowering=False)
        self.nc = nc
        self.P = Prog(nc)
        self.rr = {"pf": 0, "pb": 0, "ev": 0}
        self.build()

    def din(self, name, shape):
        return self.nc.dram_tensor(name, list(shape), F32, kind="ExternalInput").ap()

    def dout(self, name, shape):
        return self.nc.dram_tensor(name, list(shape), F32, kind="ExternalOutput").ap()

    def bank(self):
        self.rr["pf"] = (self.rr["pf"] + 1) % len(self.pf)
        return self.pf[self.rr["pf"]]

    def bbank(self):
        self.rr["pb"] = (self.rr["pb"] + 1) % len(self.pb)
        return self.pb[self.rr["pb"]]

    def ev(self):
        self.rr["ev"] ^= 1
        return "act" if self.rr["ev"] else "dve"

    def mm(self, ps, out, lhsT, rhs, start, stop, r, sw=None):
        self.P.op("pe", lambda e: e.matmul(out, lhsT=lhsT, rhs=rhs, start=start, stop=stop), r=r, w=[ps],
                  pe_self=(getattr(self, "pe_self", False) if sw is None else sw))

    def tr(self, ps, out, in_, ident, r):
        self.P.op("pe", lambda e: e.transpose(out=out, in_=in_, identity=ident), r=r, w=[ps])

    def tt(self, eng, out, in0, in1, op, r, w):
        self.P.op(eng, lambda e: e.tensor_tensor(out=out, in0=in0, in1=in1, op=op), r=r, w=w)

    def ts(self, eng, out, in0, s1, s2, op0, op1, r, w):
        if op1 is None:
            self.P.op(eng, lambda e: e.tensor_scalar(out=out, in0=in0, scalar1=s1, scalar2=None, op0=op0), r=r, w=w)
        else:
            self.P.op(eng, lambda e: e.tensor_scalar(out=out, in0=in0, scalar1=s1, scalar2=s2, op0=op0, op1=op1), r=r, w=w)

    def stt(self, out, in0, scalar, in1, op0, op1, r, w):
        self.P.op("dve", lambda e: e.scalar_tensor_tensor(out=out, in0=in0, scalar=scalar, in1=in1, op0=op0, op1=op1), r=r, w=w)

    def act(self, out, in_, func, r, w, bias=None, scale=None, accum_out=None):
        kw = {}
        if bias is not None:
            kw["bias"] = bias
        if scale is not None:
            kw["scale"] = scale
        if accum_out is not None:
            kw["accum_out"] = accum_out
        self.P.op("act", lambda e: e.activation(out=out, in_=in_, func=func, **kw), r=r, w=w)

    def cp(self, eng, out, in_, r, w):
        if eng == "act":
            self.P.op("act", lambda e: e.copy(out=out, in_=in_), r=r, w=w)
        else:
            self.P.op(eng, lambda e: e.tensor_copy(out=out, in_=in_), r=r, w=w)

    def ld(self, q, out, in_, w, sem, r=()):
        self.P.dma(q, lambda e: e.dma_start(out=out, in_=in_), r=r, w=w, sem=sem)

    def st(self, q, out, in_, r, sem):
        self.P.dma(q, lambda e: e.dma_start(out=out, in_=in_), r=r, w=(), sem=sem)

    def consts(self):
        P = self.P
        sb = P.sb
        self.ones32 = sb([128, 128], F32, "ones32")
        self.ident32 = sb([128, 128], F32, "ident32")
        self.identb = sb([128, 128], BF16, "identb")
        P.op("pool", lambda e: e.memset(self.ones32[:], 1.0), w=[self.ones32])

        def sel(out, in_, pattern, op, base, cm, r, w, fill=0.0):
            P.op("pool", lambda e: e.affine_select(out=out, in_=in_, pattern=pattern, compare_op=op, fill=fill,
                                                    base=base, channel_multiplier=cm), r=r, w=w)
        o = self.ones32
        sel(self.ident32[:], o[:], [[-1, 128]], ALU.is_equal, 0, 1, [o], [self.ident32])
        self.cp("pool", self.identb[:], self.ident32[:], [self.ident32], [self.identb])
        blk8 = sb([128, 128], F32, "blk8")
        blk32 = sb([128, 128], F32, "blk32")
        for (t, B) in ((blk8, 8), (blk32, 32)):
            v3 = t[:].rearrange("p (s t) -> p s t", t=B)
            o3 = o[:].rearrange("p (s t) -> p s t", t=B)
            sel(v3, o3, [[-B, 128 // B], [0, B]], ALU.is_ge, 0, 1, [o], [t])
            sel(v3, v3, [[B, 128 // B], [0, B]], ALU.is_ge, B - 1, -1, [t], [t])
        low = sb([128, 128], F32, "low")
        up = sb([128, 128], F32, "up")
        sel(low[:], o[:], [[-1, 128]], ALU.is_ge, 0, 1, [o], [low])
        sel(up[:], o[:], [[1, 128]], ALU.is_ge, 0, -1, [o], [up])
        self.lowT = {}
        self.same = {}
        self.addm = {}
        for X in ("p", "s"):
            lowT = sb([128, 128], F32, "lowT" + X)
            lowX = sb([128, 128], F32, "lowX" + X)
            am = sb([128, 4, 128], F32, "addm" + X)
            if X == "p":
                self.cp("pool", lowT[:], up[:], [up], [lowT])
                self.cp("pool", lowX[:], low[:], [low], [lowX])
                self.same[X] = self.ones32
            else:
                self.tt("pool", lowT[:], up[:], blk8[:], ALU.mult, [up, blk8], [lowT])
                self.tt("pool", lowX[:], low[:], blk8[:], ALU.mult, [low, blk8], [lowX])
                self.same[X] = blk8
            for h in range(4):
                self.ts("pool", am[:, h, :], lowX[:], -BIG, BIG, ALU.mult, ALU.add, [lowX], [am])
            self.lowT[X] = lowT
            self.addm[X] = am
        slow = sb([128, 128], F32, "slow")
        sup = sb([128, 128], F32, "sup")
        sel(slow[:], o[:], [[-1, 128]], ALU.is_gt, 0, 1, [o], [slow])
        sel(sup[:], o[:], [[1, 128]], ALU.is_gt, 0, -1, [o], [sup])
        self.m1 = sb([128, 128], BF16, "m1")
        self.m1T = sb([128, 128], BF16, "m1T")
        self.m2 = sb([128, 128], BF16, "m2")
        self.tt("pool", self.m1[:], blk32[:], slow[:], ALU.mult, [blk32, slow], [self.m1])
        self.tt("pool", self.m1T[:], blk32[:], sup[:], ALU.mult, [blk32, sup], [self.m1T])
        self.ts("pool", self.m2[:], blk32[:], -1.0, 1.0, ALU.mult, ALU.add, [blk32], [self.m2])
        self.sm = sb([128, 16], F32, "sm")
        sel(self.sm[:], o[:, 0:16], [[-8, 16]], ALU.is_ge, 0, 1, [o], [self.sm])
        sel(self.sm[:], self.sm[:], [[8, 16]], ALU.is_ge, 7, -1, [self.sm], [self.sm])
        self.cm = sb([128, 16, 128], BF16, "cm")
        P.op("pool", lambda e: e.memset(self.cm[:], 1.0), w=[self.cm])
        sel(self.cm[:], self.cm[:], [[-8, 16], [1, 128]], ALU.is_ge, 0, 0, [self.cm], [self.cm])
        sel(self.cm[:], self.cm[:], [[8, 16], [-1, 128]], ALU.is_ge, 7, 0, [self.cm], [self.cm])

    def bc4(self, m):
        return V(m, 0, [[0, 4], [1, 128]])

    def inverse(self, Mlow, Tt, tag):
        for _ in self.inverse_gen(self.invw, Mlow, Tt):
            pass

    def inverse_gen(self, W, Mlow, Tt, extra=None):
        Ib = self.bc4(self.identb)
        v4 = lambda ps: ps[:, 0:512].rearrange("p (h f) -> p h f", h=4)
        MdT, Md, No = W["MdT"], W["Md"], W["No"]
        self.tt("pool", MdT[:], Mlow[:], self.bc4(self.m1), ALU.mult, [Mlow, self.m1], [MdT])
        pb = self.bbank()
        for hh in range(4):
            self.tr(pb, pb[:, hh * 128:(hh + 1) * 128], Mlow[:, hh, :], self.identb[:], [Mlow, self.identb])
        self.tt("dve", Md[:], v4(pb), self.bc4(self.m1T), ALU.mult, [self.m1T], [pb, Md])
        self.tt("dve", No[:], v4(pb), self.bc4(self.m2), ALU.mult, [self.m2], [pb, No])
        X = W["X0"]
        self.tt("pool", X[:], Md[:], Ib, ALU.add, [Md, self.identb], [X])
        if extra is not None:
            extra()
        yield
        Pk, PTk = Md, MdT
        IpPT_prev = None
        for k in range(1, 5):
            last = (k == 4)
            if IpPT_prev is not None:
                psX = self.bank()
                for hh in range(4):
                    self.mm(psX, psX[:, hh * 128:(hh + 1) * 128], IpPT_prev[:, hh, :], X[:, hh, :], True, True, [IpPT_prev, X])
            psPT = self.bank()
            for hh in range(4):
                self.mm(psPT, psPT[:, hh * 128:(hh + 1) * 128], Pk[:, hh, :], PTk[:, hh, :], True, True, [Pk, PTk])
            if not last:
                psP = self.bank()
                for hh in range(4):
                    self.mm(psP, psP[:, hh * 128:(hh + 1) * 128], PTk[:, hh, :], Pk[:, hh, :], True, True, [Pk, PTk])
            if IpPT_prev is not None:
                Xn = W["X%d" % ((k - 1) % 2)]
                self.cp("act", Xn[:], v4(psX), [], [psX, Xn])
                X = Xn
            IpPT = W["IpPT%d" % (k % 2)]
            if not last:
                PTn = W["PT%d" % (k % 2)]
                Pn = W["P%d" % (k % 2)]
                self.cp("act", PTn[:], v4(psPT), [], [psPT, PTn])
                self.cp("dve", Pn[:], v4(psP), [], [psP, Pn])
                self.tt("pool", IpPT[:], PTn[:], Ib, ALU.add, [PTn, self.identb], [IpPT])
                Pk, PTk = Pn, PTn
            else:
                self.tt("dve", IpPT[:], v4(psPT), Ib, ALU.add, [self.identb], [psPT, IpPT])
            IpPT_prev = IpPT
            yield
        psX = self.bank()
        for hh in range(4):
            self.mm(psX, psX[:, hh * 128:(hh + 1) * 128], IpPT_prev[:, hh, :], X[:, hh, :], True, True, [IpPT_prev, X])
        Xn = W["X0"]
        self.cp(self.ev(), Xn[:], v4(psX), [], [psX, Xn])
        X = Xn
        yield
        pb = self.bbank()
        for hh in range(4):
            self.tr(pb, pb[:, hh * 128:(hh + 1) * 128], X[:, hh, :], self.identb[:], [X, self.identb])
        XT = W["XT"]
        self.cp(self.ev(), XT[:], v4(pb), [], [pb, XT])
        yield
        psV = self.bank()
        psVT = self.bank()
        for hh in range(4):
            self.mm(psV, psV[:, hh * 128:(hh + 1) * 128], XT[:, hh, :], No[:, hh, :], True, True, [XT, No])
        for hh in range(4):
            self.mm(psVT, psVT[:, hh * 128:(hh + 1) * 128], No[:, hh, :], XT[:, hh, :], True, True, [XT, No])
        Vm, VT, IpVT = W["P0"], W["P1"], W["PT0"]
        self.cp("act", Vm[:], v4(psV), [], [psV, Vm])
        self.cp("dve", VT[:], v4(psVT), [], [psVT, VT])
        self.tt("pool", IpVT[:], VT[:], Ib, ALU.add, [VT, self.identb], [IpVT])
        yield
        ps2 = self.bank()
        for hh in range(4):
            self.mm(ps2, ps2[:, hh * 128:(hh + 1) * 128], Vm[:, hh, :], VT[:, hh, :], True, True, [Vm, VT])
        IpV2T = W["PT1"]
        self.tt("dve", IpV2T[:], v4(ps2), Ib, ALU.add, [self.identb], [ps2, IpV2T])
        yield
        psY = self.bank()
        for hh in range(4):
            self.mm(psY, psY[:, hh * 128:(hh + 1) * 128], IpV2T[:, hh, :], X[:, hh, :], True, True, [IpV2T, X])
        Y = W["MdT"]
        self.cp(self.ev(), Y[:], v4(psY), [], [psY, Y])
        yield
        psT = self.bank()
        for hh in range(4):
            self.mm(psT, psT[:, hh * 128:(hh + 1) * 128], IpVT[:, hh, :], Y[:, hh, :], True, True, [IpVT, Y])
        self.cp(self.ev(), Tt[:], v4(psT), [], [psT, Tt])
        yield

    @staticmethod
    def run_rr(gens):
        gens = list(gens)
        while gens:
            for g in list(gens):
                try:
                    next(g)
                except StopIteration:
                    gens.remove(g)

    def inv_bufs(self, alloc, tag):
        return {nm: alloc([128, 4, 128], BF16, "%s_%s" % (tag, nm))
                for nm in ("MdT", "Md", "No", "X0", "X1", "P0", "P1", "PT0", "PT1", "IpPT0", "IpPT1", "XT")}

    def norm_T(self, xt, nwbc, xnT, xn_keep=None, bank=None):
        P = self.P
        ss, xn = self.nb["ss"], (xn_keep if xn_keep is not None else self.nb["xn"])
        self.act(xn[:], xt[:], AF.Square, [xt], [xn, ss], accum_out=ss[:])
        self.ts("dve", ss[:], ss[:], 1.0 / 1024, 1e-6, ALU.mult, ALU.add, [ss], [ss])
        self.act(ss[:], ss[:], AF.Ln, [ss], [ss])
        self.act(ss[:], ss[:], AF.Exp, [ss], [ss], scale=-0.5)
        self.stt(xn[:], xt[:], ss[:, 0:1], nwbc, ALU.mult, ALU.mult, [xt, ss, "normw"], [xn])
        for half in range(2):
            ps = bank() if bank is not None else self.bank()
            for c in range(4):
                cc = half * 4 + c
                self.tr(ps, ps[:, c * 128:(c + 1) * 128], xn[:, cc * 128:(cc + 1) * 128], self.ident32[:], [xn, self.ident32])
            self.cp(self.ev(), xnT[:, half * 4:half * 4 + 4, :], ps[:].rearrange("p (c f) -> p c f", c=4), [], [ps, xnT])

    def build(self):
        P = self.P
        nc = self.nc
        sb = P.sb
        din, dout = self.din, self.dout
        xin = din("xin", [NT, 128, 1024])
        sgdn = din("sgdn", [16, 16, 128, 128])
        sconv = din("sconv", [48, 4096])
        srwkv = din("srwkv", [16, 16, 64, 64])
        sshift = din("sshift", [16, 1024])
        self.d_srwkv, self.d_sshift = srwkv, sshift
        norm_w = din("norm_w", [2, 1024])
        fnorm_w = din("fnorm_w", [1, 1024])
        self.d_norm_w, self.d_fnorm_w = norm_w, fnorm_w
        w_in = din("w_in", [1024, 6176])
        conv_w = din("conv_w", [4096, 4])
        a_log = din("a_log", [1, 16])
        dt_bias = din("dt_bias", [1, 16])
        gnorm_w = din("gnorm_w", [1, 128])
        w_out = din("w_out", [2048, 1024])
        y = dout("y", [NT, 128, 1024])
        p_gdn = dout("p_gdn", [16, 128, 128])
        p_conv = dout("p_conv", [3, 4096])
        s_gdn = dout("s_gdn", [16, 16, 128, 128])
        s_conv = dout("s_conv", [48, 4096])
        oscr = nc.dram_tensor("oscr", [NT, 128, 2048], F32, kind="Internal").ap()
        x1scr = nc.dram_tensor("x1scr", [NT, 128, 1024], F32, kind="Internal").ap()

        self.pf = [P.ps([128, 512], F32, "pf%d" % i) for i in range(5)]
        self.pfx = P.ps([128, 512], F32, "pfx")
        self.pb = [P.ps([128, 1024], BF16, "pb%d" % i) for i in range(2)]
        self.consts()

        def pbc(ap_row, n):
            return bass.AP(ap_row.tensor, ap_row.offset, [[0, 128], [1, n]])

        nw0 = sb([128, 1024], F32, "normw")
        self.nw0 = nw0
        self.ld("sp", nw0[:], pbc(norm_w[0:1, :], 1024), [nw0], "ld_c")
        dtb = sb([128, 16], F32, "dtb")
        nea = sb([128, 16], F32, "nea")
        self.ld("sp", dtb[:], pbc(dt_bias, 16), [dtb], "ld_c")
        self.ld("sp", nea[:], pbc(a_log, 16), [nea], "ld_c")
        self.act(nea[:], nea[:], AF.Exp, [nea], [nea])
        self.ts("dve", nea[:], nea[:], -1.0, None, ALU.mult, None, [nea], [nea])
        gnw = sb([128, 128], F32, "gnw")
        self.ld("sp", gnw[:], pbc(gnorm_w, 128), [gnw], "ld_c")
        cw = sb([128, 32, 4], F32, "cw")
        self.ld("sp", cw[:], conv_w.rearrange("(j p) k -> p j k", p=128), [cw], "ld_c")

        self.nb = {"ss": sb([128, 1], F32, "ss"), "xn": sb([128, 1024], F32, "xn")}

        recA = nc.dram_tensor("recA", [NT, 128, 40 * 128], BF16, kind="Internal").ap()
        srecA = nc.dram_tensor("srecA", [NT, 128, 336], F32, kind="Internal").ap()
        es_a = ExitStack()

        def sba(shape, dt=F32, name=None):
            P.ntens += 1
            return es_a.enter_context(nc.sbuf_tensor(name or f"a{P.ntens}", list(shape), dt))

        NQ = 4128
        wA = sba([128, 8, NQ], BF16, "wA")
        for c in range(8):
            src = w_in[c * 128:(c + 1) * 128, :]
            P.dma("pool", lambda e, c=c, src=src: e.dma_start(out=wA[:, c, 0:4096], in_=src[:, 0:4096]), w=[wA], sem="ld_w")
            P.dma("pool", lambda e, c=c, src=src: e.dma_start(out=wA[:, c, 4096:4128], in_=src[:, 6144:6176]), w=[wA], sem="ld_w")
        xt = [sba([128, 1024], F32, "xt%d" % i) for i in range(2)]
        xnTs = [sba([128, 8, 128], BF16, "xnT%d" % i) for i in range(2)]
        halo = sba([128, 32, 3], F32, "halo")
        P.op("pool", lambda e: e.memset(halo[:], 0.0), w=["halo%d" % j for j in range(32)])
        cst_in = sba([48, 512], F32, "cst_in")
        cstT = sba([128, 32, 48], F32, "cstT")
        for jg in range(8):
            self.ld("sp", cst_in[:], sconv[:, jg * 512:(jg + 1) * 512], [cst_in], "ld_c")
            ps = self.bank()
            for jj in range(4):
                self.tr(ps, ps[:, jj * 48:(jj + 1) * 48], cst_in[:, jj * 128:(jj + 1) * 128], self.ident32[0:48, 0:48],
                        [cst_in, self.ident32])
            self.cp(self.ev(), cstT[:, jg * 4:jg * 4 + 4, :], ps[:, 0:192].rearrange("p (j f) -> p j f", j=4), [], [ps, cstT])
        cstN = cstT
        U = [sba([128, 176], BF16, "U%d" % i) for i in range(4)]
        wdiag = sba([128, 128, 128], BF16, "wdiag")
        for q4 in range(4):
            self.tt("pool" if q4 % 2 == 0 else "dve", wdiag[:, q4 * 32:(q4 + 1) * 32, :], V(self.ident32, 0, [[0, 32], [1, 128]]),
                    V(cw, q4 * 32, [[1, 32], [0, 128]]), ALU.mult, [self.ident32, cw], [wdiag])

        c4 = [sba([128, 4, 128], F32, "c4_%d" % i) for i in range(2)]
        sq4 = sba([128, 4, 128], F32, "sq4")
        rn4 = sba([128, 4, 128], F32, "rn4")
        recs = [sba([128, 40, 128], BF16, "rec%d" % i) for i in range(2)]
        srecs = [sba([128, 336], F32, "srec%d" % i) for i in range(2)]
        scs = [{nm: sba([128, 16], F32, "sc%d_%s" % (i, nm)) for nm in
                ("beta", "g", "gc", "glt", "egc", "negbege", "ekd", "negb", "tmp")} for i in range(2)]
        gms = [sba([128, 16, 16], F32, "gm%d" % i) for i in range(2)]
        egls = [sba([128, 16, 16], F32, "egl%d" % i) for i in range(2)]

        def bcs(t, g):
            return V(t, 4 * g, [[1, 4], [0, 128]])

        def prologue(t):
            X = "p" if t < 17 else "s"
            nseq, T = (1, 128) if X == "p" else (16, 8)
            x = xt[t % 2]
            srec = srecs[t % 2]
            xnT, sc, gm, egl = xnTs[t % 2], scs[t % 2], gms[t % 2], egls[t % 2]
            pbank = lambda: self.pfx
            self.ld("sp", x[:], xin[t], [x], "ld_x%d" % (t % 2))
            self.norm_T(x, nw0[:], xnT, bank=pbank)
            ps = pbank()
            for c in range(8):
                self.mm(ps, ps[:, 0:32], xnT[:, c, :], wA[:, c, 4096:4128], c == 0, c == 7, [xnT, wA])
            self.act(sc["beta"][:], ps[:, 0:16], AF.Sigmoid, [], [ps, sc["beta"]])
            self.tt("dve", sc["tmp"][:], ps[:, 16:32], dtb[:], ALU.add, [dtb], [ps, sc["tmp"]])
            self.act(sc["tmp"][:], sc["tmp"][:], AF.Exp, [sc["tmp"]], [sc["tmp"]])
            self.act(sc["tmp"][:], sc["tmp"][:], AF.Ln, [sc["tmp"]], [sc["tmp"]], bias=1.0)
            self.tt("dve", sc["g"][:], sc["tmp"][:], nea[:], ALU.mult, [sc["tmp"], nea], [sc["g"]])
            ps = pbank()
            self.mm(ps, ps[:, 0:16], self.lowT[X][:], sc["g"][:], True, True, [self.lowT[X], sc["g"]])
            self.mm(ps, ps[:, 16:32], self.same[X][:], sc["g"][:], True, True, [self.same[X], sc["g"]])
            self.cp("dve", sc["gc"][:], ps[:, 0:16], [], [ps, sc["gc"]])
            self.cp("dve", sc["glt"][:], ps[:, 16:32], [], [ps, sc["glt"]])
            self.act(sc["egc"][:], sc["gc"][:], AF.Exp, [sc["gc"]], [sc["egc"]])
            self.stt(sc["negbege"][:], sc["egc"][:], -1.0, sc["beta"][:], ALU.mult, ALU.mult, [sc["egc"], sc["beta"]], [sc["negbege"]])
            self.tt("dve", sc["tmp"][:], sc["glt"][:], sc["gc"][:], ALU.subtract, [sc["glt"], sc["gc"]], [sc["tmp"]])
            self.act(sc["ekd"][:], sc["tmp"][:], AF.Exp, [sc["tmp"]], [sc["ekd"]])
            self.ts("dve", sc["negb"][:], sc["beta"][:], -1.0, None, ALU.mult, None, [sc["beta"]], [sc["negb"]])
            gmv = gm[:, 0:nseq, :]
            self.tt("pool", gmv, V(sc["g"], 0, [[0, nseq], [1, 16]]), V(self.sm, 0, [[1, nseq], [0, 16]]) if X == "s"
                    else V(self.ones32, 0, [[0, 1], [1, 16]]), ALU.mult, [sc["g"], self.sm, self.ones32], [gm])
            ps = pbank()
            self.mm(ps, ps[:, 0:nseq * 16], self.ones32[:], gm[:, 0:nseq, :].rearrange("p s h -> p (s h)"), True, True,
                    [self.ones32, gm])
            self.act(egl[:, 0:nseq, :].rearrange("p s h -> p (s h)"), ps[:, 0:nseq * 16], AF.Exp, [], [ps, egl])

            for i_, nm in enumerate(("gc", "egc", "negbege", "ekd", "negb")):
                self.cp("pool", srec[:, 16 * i_:16 * i_ + 16], sc[nm][:], [sc[nm]], [srec])
            self.cp("pool", srec[:, 80:336], egl[:].rearrange("p s h -> p (s h)"), [egl], [srec])

        for t in range(NT):
            X = "p" if t < 17 else "s"
            nseq, T = (1, 128) if X == "p" else (16, 8)
            if t == 0:
                prologue(0)
            xnT, sc = xnTs[t % 2], scs[t % 2]
            rec = recs[t % 2]
            srec = srecs[t % 2]

            class _RV:
                def __init__(s_, o, name):
                    s_.o, s_.name = o, name

                def __getitem__(s_, idx):
                    a, b, c_ = idx
                    if isinstance(b, slice):
                        b = slice((b.start or 0) + s_.o, (b.stop if b.stop is not None else 0) + s_.o)
                    else:
                        b = b + s_.o
                    return rec[a, b, c_]
            qT, kT, Ktok, vb = _RV(0, rec.name), _RV(8, rec.name), _RV(16, rec.name), _RV(24, rec.name)
            psUs = {}

            def proj(jg):
                psU = self.bank()
                psUs[jg] = psU
                for jj in range(4):
                    j = jg * 4 + jj
                    for c in range(8):
                        self.mm(psU, psU[:, jj * 128:(jj + 1) * 128], wA[:, c, j * 128:(j + 1) * 128], xnT[:, c, :], c == 0, c == 7,
                                [wA, xnT])
            psCs = {}

            def views(j):
                jg, jj = j // 4, j % 4
                u, cc = U[j % 4], c4[jg % 2]
                if X == "p":
                    return u, cc, (lambda k: u[:, k:k + 128])
                u3 = u[:].rearrange("p (s t) -> p s t", t=11)
                return u, cc, (lambda k: u3[:, :, k:k + 8])

            def stA(j):
                jg, jj = j // 4, j % 4
                psU = psUs[jg]
                u, cc, uv = views(j)
                pu = psU[:, jj * 128:(jj + 1) * 128]
                uh, ub, hj = "uh%d" % (j % 4), "ub%d" % (j % 4), "halo%d" % j
                if X == "p":
                    self.cp("pool", u[:, 0:3], halo[:, j, :], [hj], [uh])
                    self.cp("dve", u[:, 3:131], pu, [], [psU, ub])
                    self.cp("dve", halo[:, j, :], psU[:, jj * 128 + 125:jj * 128 + 128], [], [psU, hj])
                else:
                    u3 = u[:].rearrange("p (s t) -> p s t", t=11)
                    pu3 = pu.rearrange("p (s t) -> p s t", t=8)
                    self.cp("pool", u3[:, :, 0:3], cstT[:, j, :].rearrange("p (s k) -> p s k", k=3), [hj, cstT], [uh])
                    self.cp("dve", u3[:, :, 3:11], pu3, [], [psU, ub])
                    self.cp("dve", cstN[:, j, :].rearrange("p (s k) -> p s k", k=3), pu3[:, :, 5:8], [], [psU, hj])

            def stB(j):
                jg, jj = j // 4, j % 4
                if jj == 0:
                    psCs[jg] = self.bank()
                psC = psCs[jg]
                u, cc, uv = views(j)
                for k in range(4):
                    self.mm(psC, psC[:, jj * 128:(jj + 1) * 128], wdiag[:, j * 4 + k, :], uv(k), k == 0, k == 3,
                            [wdiag, "uh%d" % (j % 4), "ub%d" % (j % 4)])

            def stC(j):
                jg, jj = j // 4, j % 4
                u, cc, uv = views(j)
                psC = psCs[jg]
                self.act(cc[:, jj, :], psC[:, jj * 128:(jj + 1) * 128], AF.Silu, [], [psC, cc])
                if jj != 3:
                    return
                if jg < 4:
                    self.tt("pool", sq4[:], cc[:], cc[:], ALU.mult, [cc], [sq4])
                    psN = self.bank()
                    for j2 in range(4):
                        self.mm(psN, psN[:, j2 * 128:(j2 + 1) * 128], self.ones32[:], sq4[:, j2, :], True, True, [self.ones32, sq4])
                    self.act(rn4[:], psN[:].rearrange("p (h f) -> p h f", h=4), AF.Ln, [], [psN, rn4], bias=1e-6)
                    lnsc = float(np.log(128.0 ** -0.5)) if jg < 2 else 0.0
                    self.act(rn4[:], rn4[:], AF.Exp, [rn4], [rn4], scale=-0.5, bias=lnsc)
                    dst = qT if jg < 2 else kT
                    o0 = (jg % 2) * 4
                    self.tt("dve", dst[:, o0:o0 + 4, :], cc[:], rn4[:], ALU.mult, [cc, rn4], [dst])
                else:
                    gv = jg - 4
                    psT = self.bank()
                    for j2 in range(4):
                        self.tr(psT, psT[:, j2 * 128:(j2 + 1) * 128], cc[:, j2, :], self.ident32[:], [cc, self.ident32])
                    self.tt("dve", vb[:, gv * 4:gv * 4 + 4, :], psT[:].rearrange("p (h f) -> p h f", h=4), bcs(sc["beta"], gv),
                            ALU.mult, [sc["beta"]], [psT, vb])

            proj(0)
            for step in range(36):
                if step < 32:
                    if step % 4 == 1 and step // 4 + 1 < 8:
                        proj(step // 4 + 1)
                    stA(step)
                if step == 14 and t + 1 < NT:
                    prologue(t + 1)
                if 0 <= step - 2 < 32:
                    stB(step - 2)
                if 0 <= step - 4 < 32:
                    stC(step - 4)
            pb = self.bbank()
            for kh in range(8):
                self.tr(pb, pb[:, kh * 128:(kh + 1) * 128], kT[:, kh, :], self.identb[:], [kT, self.identb])
            self.cp("act", rec[:, 16:24, :], pb[:].rearrange("p (h f) -> p h f", h=8), [], [pb, rec])
            if t == 16 or t == 17:
                src, ncol, dst = (halo, 3, p_conv) if t == 16 else (cstN, 48, s_conv)
                stg = cst_in
                for jg in range(8):
                    ps = self.bank()
                    for jj in range(4):
                        j = jg * 4 + jj
                        self.tr(ps, ps[0:ncol, jj * 128:(jj + 1) * 128], src[:, j, :], self.ident32[:], [src, self.ident32, "halo%d" % j])
                    self.cp(self.ev(), stg[0:ncol, :], ps[0:ncol, :], [], [ps, stg])
                    self.st("sp", dst[:, jg * 512:(jg + 1) * 512], stg[0:ncol, :], [stg], "st_c")

            self.st("sp", recA[t], rec[:].rearrange("p a f -> p (a f)"), [rec], "st_rec%d" % (t % 2))
            self.st("sp", srecA[t], srec[:], [srec], "st_srec%d" % (t % 2))
        P.barrier()
        P.flush()
        es_a.close()

        es_a = ExitStack()
        rec2 = [sba([128, 40, 128], BF16, "r2ec%d" % i) for i in range(2)]
        srec2 = [sba([128, 336], F32, "s2rec%d" % i) for i in range(2)]
        gcd = [sba([128, 16, 128], F32, "gcd%d" % i) for i in range(2)]
        Sf = sba([128, 16, 128], F32, "Sf")
        Sb = sba([128, 16, 128], BF16, "Sb")
        P.op("pool", lambda e: e.memset(Sf[:], 0.0), w=["Sf%d" % i for i in range(4)])
        P.op("pool", lambda e: e.memset(Sb[:], 0.0), w=["Sb%d" % i for i in range(4)])
        sets = []
        for gi in range(4):
            B = {nm: sba([128, 4, 128], BF16, "g%d_%s" % (gi, nm)) for nm in
                 ("E1", "E1nb", "Mlow", "Tt", "r4", "vn", "vnd", "attn", "attnT")}
            B["o4"] = sba([128, 4, 128], F32, "g%d_o4" % gi)
            B["W"] = self.inv_bufs(sba, "g%d" % gi)
            sets.append(B)
        S0bs = [sba([128, 16, 128], BF16, "S0b%d" % i) for i in range(2)]
        S0f32s = [sba([128, 16, 128], F32, "S0f32_%d" % i) for i in range(2)]
        mks = [sba([128, 16, 128], BF16, "mk%d" % i) for i in range(2)]
        S0f = [sba([128, 4, 128], F32, "S0f%d" % i) for i in range(2)]
        Sn = [sba([128, 4, 128], F32, "Sn%d" % i) for i in range(2)]
        vnds = [sba([128, 128], BF16, "vnds%d" % i) for i in range(2)]
        v4 = lambda ps: ps[:, 0:512].rearrange("p (h f) -> p h f", h=4)

        def gdn_group(t, g, B, rec, srec, gcd_t):
            X = "p" if t < 17 else "s"
            hs = [4 * g + hh for hh in range(4)]
            khs = [h // 2 for h in hs]
            qT = lambda kh: rec[:, kh, :]
            kT = lambda kh: rec[:, 8 + kh, :]
            Kt = lambda kh: rec[:, 16 + kh, :]
            sv_ = lambda off: V(srec, off + 4 * g, [[1, 4], [0, 128]])
            E1, E1nb, Mlow, Tt, r4, vn, vnd, attn, attnT, o4 = (B[k_] for k_ in
                                                               ("E1", "E1nb", "Mlow", "Tt", "r4", "vn", "vnd", "attn", "attnT", "o4"))
            psR = self.bank()
            self.mm(psR, psR[:], self.ones32[:], gcd_t[:, 4 * g:4 * g + 4, :].rearrange("p h f -> p (h f)"), True, False,
                    [self.ones32, gcd_t])
            self.mm(psR, psR[:], self.ident32[:], self.addm[X][:].rearrange("p h f -> p (h f)"), False, True,
                    [self.ident32, self.addm[X]])
            self.tt("dve", v4(psR), v4(psR), sv_(0), ALU.subtract, [srec], [psR])
            self.act(E1[:], v4(psR), AF.Exp, [], [psR, E1], scale=-1.0)
            self.tt("pool", E1nb[:], E1[:], sv_(64), ALU.mult, [E1, srec], [E1nb])
            yield
            psG = self.bank()
            for hh in range(4):
                self.mm(psG, psG[:, hh * 128:(hh + 1) * 128], kT(khs[hh]), kT(khs[hh]), True, True, [rec])
            psQ = self.bank()
            for hh in range(4):
                self.mm(psQ, psQ[:, hh * 128:(hh + 1) * 128], qT(khs[hh]), kT(khs[hh]), True, True, [rec])
            self.tt("dve", Mlow[:], v4(psG), E1nb[:], ALU.mult, [E1nb], [psG, Mlow])
            self.tt("dve", attn[:], v4(psQ), E1[:], ALU.mult, [E1], [psQ, attn])
            yield

            def extra():
                pb = self.bbank()
                for hh in range(4):
                    self.tr(pb, pb[:, hh * 128:(hh + 1) * 128], attn[:, hh, :], self.identb[:], [attn, self.identb])
                self.cp("act", attnT[:], v4(pb), [], [pb, attnT])
            yield from self.inverse_gen(B["W"], Mlow, Tt, extra)
            S0b, S0f32, mk = S0bs[g % 2], S0f32s[g % 2], mks[g % 2]
            psK = self.bank()
            if X == "p":
                for hh in range(4):
                    self.mm(psK, psK[:, hh * 128:(hh + 1) * 128], kT(khs[hh]), Sb[:, hs[hh], :], True, True, [rec, "Sb%d" % g])
            else:
                psA = self.bank()
                for hh in range(4):
                    h = hs[hh]
                    self.ld("sp", S0f32[:], sgdn[:, h].rearrange("s p v -> p s v"), [S0f32], "ld_s0b")
                    self.cp("act", S0b[:], S0f32[:], [S0f32], [S0b])
                    for (srcf, psd) in ((kT, psK), (qT, psA)):
                        a_ = srcf(khs[hh])
                        self.tt("pool", mk[:], bass.AP(a_.tensor, a_.offset, [list(a_.ap[0]), [0, 16], [1, 128]]), self.cm[:], ALU.mult,
                                [rec, self.cm], [mk])
                        for s_ in range(16):
                            self.mm(psd, psd[:, hh * 128:(hh + 1) * 128], mk[:, s_, :], S0b[:, s_, :], s_ == 0, s_ == 15, [mk, S0b])
                self.tt("dve", o4[:], v4(psA), sv_(16), ALU.mult, [srec], [psA, o4])
            self.tt("dve", v4(psK), v4(psK), sv_(32), ALU.mult, [srec], [psK])
            self.tt("dve", r4[:], v4(psK), rec[:, 24 + 4 * g:24 + 4 * g + 4, :], ALU.add, [rec], [psK, r4])
            yield
            psV = self.bank()
            for hh in range(4):
                self.mm(psV, psV[:, hh * 128:(hh + 1) * 128], Tt[:, hh, :], r4[:, hh, :], True, True, [Tt, r4])
            self.cp("act", vn[:], v4(psV), [], [psV, vn])
            self.tt("dve", vnd[:], v4(psV), sv_(48), ALU.mult, [srec], [psV, vnd])
            yield
            psB = self.bank()
            for hh in range(4):
                self.mm(psB, psB[:, hh * 128:(hh + 1) * 128], attnT[:, hh, :], vn[:, hh, :], True, True, [attnT, vn])
            if X == "p":
                psA = self.bank()
                for hh in range(4):
                    self.mm(psA, psA[:, hh * 128:(hh + 1) * 128], qT(khs[hh]), Sb[:, hs[hh], :], True, True, [rec, "Sb%d" % g])
                self.tt("dve", o4[:], v4(psA), sv_(16), ALU.mult, [srec], [psA, o4])
            self.tt("dve", o4[:], v4(psB), o4[:], ALU.add, [o4], [psB, o4])
            self.st("sp", oscr[t, :, g * 512:(g + 1) * 512], o4[:].rearrange("p h f -> p (h f)"), [o4], "st_o%d" % g)
            if X == "p":
                psS = self.bank()
                for hh in range(4):
                    self.mm(psS, psS[:, hh * 128:(hh + 1) * 128], Kt(khs[hh]), vnd[:, hh, :], True, True, [rec, vnd])
                sv = Sf[:, 4 * g:4 * g + 4, :]
                SfN, SbN = "Sf%d" % g, "Sb%d" % g
                self.tt("pool", sv, sv, V(srec, 80 + 4 * g, [[1, 4], [0, 128]]), ALU.mult, [srec, SfN], [SfN])
                self.tt("dve", sv, v4(psS), sv, ALU.add, [SfN], [psS, SfN])
                self.cp("act", Sb[:, 4 * g:4 * g + 4, :], sv, [SfN], [SbN])
                if t == 16:
                    self.st("sp", p_gdn[4 * g:4 * g + 4].rearrange("h p v -> p h v"), sv, [SfN], "st_pg")
            else:
                for hh in range(4):
                    h = hs[hh]
                    for sg in range(4):
                        i2 = (hh * 4 + sg) % 2
                        s0f = S0f[i2]
                        self.ld("sp", s0f[:], sgdn[sg * 4:sg * 4 + 4, h].rearrange("s p v -> p s v"), [s0f], "ld_s0f%d" % i2)
                        psS = self.bank()
                        for si in range(4):
                            s_ = sg * 4 + si
                            vs = vnds[s_ % 2]
                            self.act(vs[:], vnd[:, hh, :], AF.Copy, [vnd, self.sm], [vs], scale=self.sm[:, s_:s_ + 1])
                            self.mm(psS, psS[:, si * 128:(si + 1) * 128], Kt(khs[hh]), vs[:], True, True, [rec, vs])
                        sn = Sn[i2]
                        self.tt("pool", sn[:], s0f[:], V(srec, 80 + sg * 4 * 16 + h, [[16, 4], [0, 128]]), ALU.mult, [s0f, srec], [sn])
                        self.tt("dve", sn[:], v4(psS), sn[:], ALU.add, [sn], [psS, sn])
                        self.st("sp", s_gdn[sg * 4:sg * 4 + 4, h].rearrange("s p v -> p s v"), sn[:], [sn], "st_sn%d" % i2)
            yield

        for t in range(NT):
            rec, srec, gcd_t = rec2[t % 2], srec2[t % 2], gcd[t % 2]
            self.ld("sp", rec[:].rearrange("p a f -> p (a f)"), recA[t], [rec], "ld_rec%d" % (t % 2))
            self.ld("sp", srec[:], srecA[t], [srec], "ld_srec%d" % (t % 2))
            self.tt("pool", gcd_t[:], V(self.ident32, 0, [[0, 16], [1, 128]]), V(srec, 0, [[1, 16], [0, 128]]), ALU.mult,
                    [self.ident32, srec], [gcd_t])
            self.run_rr([gdn_group(t, g, sets[g], rec, srec, gcd_t) for g in range(4)])
        P.barrier()
        P.flush()
        es_a.close()

        self.pm = {(nm, X_): P.sb([128, 128], BF16, nm + X_) for nm in ("mup", "msup", "mneg") for X_ in ("p", "s")}
        self.es_w1 = ExitStack()
        P.ntens += 1
        self.wR = self.es_w1.enter_context(nc.sbuf_tensor("wR", [128, 8, 4096], BF16))
        self.w1a1 = self.es_w1.enter_context(nc.sbuf_tensor("w1a1", [128, 8, 128], BF16))
        self.w2a2 = self.es_w1.enter_context(nc.sbuf_tensor("w2a2", [64, 2, 1024], BF16))
        self.d_rkvz = din("rkvz", [4, 1024, 1024])
        self.d_w1 = din("w1", [1024, 64]); self.d_w2 = din("w2", [64, 1024])
        self.d_a1 = din("a1", [1024, 64]); self.d_a2 = din("a2", [64, 1024])

        es_b = ExitStack()

        def sbb(shape, dt=F32, name=None):
            P.ntens += 1
            return es_b.enter_context(nc.sbuf_tensor(name or f"b{P.ntens}", list(shape), dt))

        wZ = sbb([128, 8, 2048], BF16, "wZ")
        wO = sbb([128, 16, 1024], BF16, "wO")
        for c in range(8):
            P.dma("pool", lambda e, c=c: e.dma_start(out=wZ[:, c, :], in_=w_in[c * 128:(c + 1) * 128, 4096:6144]), w=[wZ], sem="ld_w")
        for c in range(16):
            P.dma("pool", lambda e, c=c: e.dma_start(out=wO[:, c, :], in_=w_out[c * 128:(c + 1) * 128, :]), w=[wO], sem="ld_w")
        wR_, w1a1_, w2a2_ = self.wR, self.w1a1, self.w2a2
        for i in range(4):
            for c in range(8):
                P.dma("pool", lambda e, i=i, c=c: e.dma_start(out=wR_[:, c, i * 1024:(i + 1) * 1024],
                                                              in_=self.d_rkvz[i, c * 128:(c + 1) * 128, :]), w=[wR_], sem="x")
        for c in range(8):
            P.dma("pool", lambda e, c=c: e.dma_start(out=w1a1_[:, c, 0:64], in_=self.d_w1[c * 128:(c + 1) * 128, :]), w=[w1a1_], sem="x")
            P.dma("pool", lambda e, c=c: e.dma_start(out=w1a1_[:, c, 64:128], in_=self.d_a1[c * 128:(c + 1) * 128, :]), w=[w1a1_], sem="x")
        P.dma("pool", lambda e: e.dma_start(out=w2a2_[:, 0, :], in_=self.d_w2), w=[w2a2_], sem="x")
        P.dma("pool", lambda e: e.dma_start(out=w2a2_[:, 1, :], in_=self.d_a2), w=[w2a2_], sem="x")
        xtb = [sbb([128, 1024], F32, "xtb%d" % i) for i in range(2)]
        xnTb = sbb([128, 8, 128], BF16, "xnTb")
        ot = [sbb([128, 16, 128], F32, "ot%d" % i) for i in range(2)]
        osq = None
        orn = sbb([128, 16], F32, "orn")
        zs = sbb([128, 4, 128], F32, "zs")
        og = sbb([128, 16, 128], F32, "og")
        osq = og
        ogT = sbb([128, 16, 128], BF16, "ogT")
        x1 = [sbb([128, 1024], F32, "x1_%d" % i) for i in range(2)]
        for t in range(NT):
            x = xtb[t % 2]
            o = ot[t % 2]
            self.ld("sp", x[:], xin[t], [x], "ldb_x%d" % (t % 2))
            self.ld("sp", o[:].rearrange("p h f -> p (h f)"), oscr[t], [o], "ldb_o%d" % (t % 2))
            self.norm_T(x, nw0[:], xnTb)
            self.act(osq[:], o[:], AF.Square, [o], [osq])
            P.op("dve", lambda e: e.tensor_reduce(out=orn[:], in_=osq[:], axis=AX.X, op=ALU.add), r=[osq], w=[orn])
            self.ts("dve", orn[:], orn[:], 1.0 / 128, 1e-6, ALU.mult, ALU.add, [orn], [orn])
            self.act(orn[:], orn[:], AF.Ln, [orn], [orn])
            self.act(orn[:], orn[:], AF.Exp, [orn], [orn], scale=-0.5)
            self.tt("dve", og[:], o[:], V(orn, 0, [[1, 16], [0, 128]]), ALU.mult, [o, orn], [og])
            self.tt("pool", og[:], og[:], V(gnw, 0, [[0, 16], [1, 128]]), ALU.mult, [og, gnw], [og])
            for g in range(4):
                psZ = self.bank()
                for c in range(8):
                    self.mm(psZ, psZ[:], xnTb[:, c, :], wZ[:, c, g * 512:(g + 1) * 512], c == 0, c == 7, [xnTb, wZ])
                self.act(zs[:], psZ[:].rearrange("p (h f) -> p h f", h=4), AF.Silu, [], [psZ, zs])
                self.tt("dve", og[:, 4 * g:4 * g + 4, :], og[:, 4 * g:4 * g + 4, :], zs[:], ALU.mult, [zs, og], [og])
            for g in range(4):
                ps = self.bank()
                for hh in range(4):
                    self.tr(ps, ps[:, hh * 128:(hh + 1) * 128], og[:, 4 * g + hh, :], self.ident32[:], [og, self.ident32])
                self.cp(self.ev(), ogT[:, 4 * g:4 * g + 4, :], ps[:].rearrange("p (h f) -> p h f", h=4), [], [ps, ogT])
            xo = x1[t % 2]
            for half in range(2):
                ps = self.bank()
                for c in range(16):
                    self.mm(ps, ps[:], ogT[:, c, :], wO[:, c, half * 512:(half + 1) * 512], c == 0, c == 15, [ogT, wO])
                self.tt("dve", xo[:, half * 512:(half + 1) * 512], ps[:], x[:, half * 512:(half + 1) * 512], ALU.add, [x], [ps, xo])
            self.st("sp", x1scr[t], xo[:], [xo], "stb_x%d" % (t % 2))
        P.barrier()
        P.flush()
        es_b.close()

        self.layer1(x1scr, y, pbc)

    def layer1(self, x1scr, y, pbc):
        P = self.P
        nc = self.nc
        din, dout = self.din, self.dout
        srwkv = self.d_srwkv
        sshift = self.d_sshift
        mu_d = din("mu", [6, 1024])
        w0_d = din("w0", [1, 1024])
        a0_d = din("a0", [1, 1024])
        kk_d = din("k_k", [1, 1024]); ka_d = din("k_a", [1, 1024]); rk_d = din("r_k", [1, 1024])
        lw_d = din("lnx_w", [1, 1024]); lb_d = din("lnx_b", [1, 1024]); wo_d = din("w_o", [1024, 1024])
        p_rwkv = dout("p_rwkv", [16, 64, 64]); p_shift = dout("p_shift", [1, 1024])
        s_rwkv = dout("s_rwkv", [16, 16, 64, 64]); s_shift = dout("s_shift", [16, 1024])
        pm = self.pm
        es = ExitStack()
        cur = [es]

        def sb(shape, dt=F32, name=None):
            P.ntens += 1
            return cur[0].enter_context(nc.sbuf_tensor(name or f"c{P.ntens}", list(shape), dt))
        o32 = self.ones32
        wR, w1a1, w2a2 = self.wR, self.w1a1, self.w2a2
        xx = sb([128, 1024], F32, "r_xx")
        pst = xx
        self.ld("sp", pst[0:6, :], mu_d, [pst], "ld_c")
        for i, d in enumerate((w0_d, a0_d, kk_d, ka_d, rk_d)):
            self.ld("sp", pst[6 + i:7 + i, :], d, [pst], "ld_c")
        par = sb([128, 8, 16], F32, "par")
        ps = self.bank()
        for c in range(8):
            self.tr(ps, ps[:, c * 16:c * 16 + 11], pst[0:11, c * 128:(c + 1) * 128], self.ident32[0:11, 0:11], [pst, self.ident32])
        self.cp("dve", par[:, :, 0:11], ps[:, 0:128].rearrange("p (c i) -> p c i", i=16)[:, :, 0:11], [], [ps, par])
        self.ts("dve", par[:, :, 11:12], par[:, :, 6:7], -1.0, None, ALU.mult, None, [par], [par])
        self.ts("dve", par[:, :, 12:13], par[:, :, 9:10], -1.0, 1.0, ALU.mult, ALU.add, [par], [par])
        nw1 = self.nw0
        self.ld("sp", nw1[:], pbc(self.d_norm_w[1:2, :], 1024), [nw1], "ld_c")
        def sel(out, in_, pattern, op, base, cm, r, w, fill=0.0):
            P.op("pool", lambda e: e.affine_select(out=out, in_=in_, pattern=pattern, compare_op=op, fill=fill,
                                                    base=base, channel_multiplier=cm), r=r, w=w)
        shp = sb([128, 128], F32, "shp")
        sel(shp[:], o32[:], [[1, 128]], ALU.is_equal, -1, -1, [o32], [shp])
        nb0 = sb([128, 128], F32, "nb0")
        sel(nb0[:].rearrange("p (s t) -> p s t", t=8), o32[:].rearrange("p (s t) -> p s t", t=8), [[0, 16], [1, 8]], ALU.is_gt, 0, 0,
            [o32], [nb0])
        shs = sb([128, 128], F32, "shs")
        self.tt("pool", shs[:], shp[:], nb0[:], ALU.mult, [shp, nb0], [shs])
        elast = sb([128, 128], F32, "elast")
        sel(elast[:], o32[:], [[-1, 128]], ALU.is_equal, -127, 1, [o32], [elast])
        selS = sb([16, 128], F32, "selS")
        sel(selS[:], o32[0:16, :], [[1, 128]], ALU.is_equal, 0, -8, [o32], [selS])
        b64 = sb([128, 128], F32, "b64")
        v3 = b64[:].rearrange("p (s t) -> p s t", t=64)
        sel(v3, o32[:].rearrange("p (s t) -> p s t", t=64), [[-64, 2], [0, 64]], ALU.is_ge, 0, 1, [o32], [b64])
        sel(v3, v3, [[64, 2], [0, 64]], ALU.is_ge, 63, -1, [b64], [b64])
        mneg, msup, mup = {}, {}, {}
        t_tmpm = sb([128, 128], F32, "tmpm")
        for X in ("p", "s"):
            lt = self.lowT[X]
            mup[X] = pm[("mup", X)]
            self.cp("pool", mup[X][:], lt[:], [lt], [mup[X]])
            msup[X] = pm[("msup", X)]
            sel(msup[X][:], lt[:], [[1, 128]], ALU.is_gt, 0, -1, [lt], [msup[X]])
            mneg[X] = pm[("mneg", X)]
            tmpm = t_tmpm
            self.ts("pool", tmpm[:], self.addm[X][:, 0, :], -1.0 / BIG, 1.0, ALU.mult, ALU.add, [self.addm[X]], [tmpm])
            sel(tmpm[:], tmpm[:], [[-1, 128]], ALU.is_gt, 0, 1, [tmpm], [tmpm])
            self.ts("pool", mneg[X][:], tmpm[:], -1.0, None, ALU.mult, None, [tmpm], [mneg[X]])
        hsel = sb([128, 2], F32, "hsel")
        self.cp("pool", hsel[:, 0:1], b64[:, 0:1], [b64], [hsel])
        self.cp("pool", hsel[:, 1:2], b64[:, 127:128], [b64], [hsel])
        self.invw = {}
        for nm in ("MdT", "Md", "No", "X0", "X1", "P0", "P1", "PT0", "PT1", "IpPT0", "IpPT1", "XT"):
            self.invw[nm] = sb([128, 4, 128], BF16, "jw_" + nm)
        for a_, b_ in (("V", "P0"), ("VT", "P1"), ("IpVT", "PT0"), ("IpV2T", "PT1"), ("Y", "MdT")):
            self.invw[a_] = self.invw[b_]
        xn = [sb([128, 1024], F32, "r_xn%d" % i) for i in range(2)]
        P.op("pool", lambda e: e.memset(xn[1][:], 0.0), w=[xn[1]])
        x1t = [sb([128, 1024], F32, "r_x1")] * 2
        xnT = sb([128, 8, 128], BF16, "r_xnT")
        xxT = sb([128, 8, 128], BF16, "r_xxT")
        xs_all = sb([128, 4, 8, 128], BF16, "r_xs")

        class _XS:
            def __init__(s_, i):
                s_.i = i
                s_.name = "r_xs"

            def __getitem__(s_, idx):
                return xs_all[idx[0], s_.i, idx[1], idx[2]]
        xs = [_XS(i) for i in range(4)]
        class _AL:
            def __init__(s_, apf, name):
                s_.apf = apf
                s_.name = name

            def __getitem__(s_, idx):
                return s_.apf()[idx]

        hT = sb([64, 2, 128], BF16, "r_hT")
        fm = {nm: sb([128, 8, 128], BF16, "r_" + nm) for nm in ("kap", "rho", "kt", "bt")}
        kdj = sb([128, 128], BF16, "r_kdj")
        bdj = sb([128, 128], BF16, "r_bdj")
        vT = sb([128, 8, 128], BF16, "r_vT")
        Vb = sb([128, 1024], BF16, "r_Vb")
        kdT = sb([128, 8, 128], BF16, "r_kdT")
        bdT = sb([128, 8, 128], BF16, "r_bdT")
        Pc = sb([128, 8, 16], F32, "r_Pc")
        t_all = []
        for i_ in range(2):
            t_ = {nm: sb([128, 128], F32, "r_t%d_%s" % (i_, nm)) for nm in ("e", "ew", "a", "kk", "sq", "rn", "k2", "b", "cs", "x", "r", "k")}
            t_["dd"] = t_["sq"]
            t_["rk"] = t_["rn"]
            t_all.append(t_)
        kdjs = [kdj, sb([128, 128], BF16, "r_kdj2")]
        bdjs = [bdj, sb([128, 128], BF16, "r_bdj2")]
        Z = sb([128, 8, 64], F32, "r_Z")
        Zb = sb([128, 8, 64], BF16, "r_Zb")
        P.op("pool", lambda e: e.memset(Z[:], 0.0), w=[Z])
        P.op("pool", lambda e: e.memset(Zb[:], 0.0), w=[Zb])
        Mlow = sb([128, 4, 128], BF16, "r_Mlow")
        Tt = sb([128, 4, 128], BF16, "r_Tt")
        MkT = sb([128, 4, 128], BF16, "r_MkT")
        AkT = sb([128, 4, 128], BF16, "r_AkT")
        AbT = sb([128, 4, 128], BF16, "r_AbT")
        rhsS = sb([128, 4, 64], BF16, "r_rhsS")
        SA = sb([128, 4, 64], BF16, "r_SA")
        ytok = sb([128, 16, 64], F32, "r_ytok")
        ysq = xx
        st = {nm: sb([128, 16], F32, "r_st_" + nm) for nm in ("sum", "ssq", "mean", "var", "rstd", "rkb")}
        zs = self.nb["xn"]
        rec1 = nc.dram_tensor("rwrec1", [NT, 128, 56 * 128], BF16, kind="Internal").ap()
        frec1 = nc.dram_tensor("rwfrec1", [NT, 128, 1168], F32, kind="Internal").ap()

        x2 = zs
        ss2 = sb([128, 1], F32, "r_ss2")
        ygT = _AL(lambda: xs_all[:, 3], "r_xs")
        s0in = sb([64, 2, 64], F32, "r_s0in")
        Z0b2 = [_AL(lambda q=q: xs_all[:, 2 + q].rearrange("p c (a f) -> p (c a) f", a=2), "r_xs") for q in range(2)]
        mk = _AL(lambda: xs_all[:, 0:2].rearrange("p a c f -> p (a c) f"), "r_xs")
        Vm = sb([128, 128], BF16, "r_Vm")
        yz0 = sb([128, 4, 64], F32, "r_yz0")
        mk2 = _AL(lambda: xs_all[:, 0:2].rearrange("p a c f -> p (a c) f"), "r_xs")
        SAm = sb([128, 128], BF16, "r_SAm")
        Zn = sb([128, 64], F32, "r_Zn")
        Sout = sb([64, 128], F32, "r_Sout")

        def fmh(tn, h):
            b0 = (h % 2) * 64
            return tn[b0:b0 + 64, h // 2, :]

        for t in range(NT):
            X = "p" if t < 17 else "s"
            nseq, T = (1, 128) if X == "p" else (16, 8)
            x1 = x1t[t % 2]
            xc, xp = xn[t % 2], xn[(t + 1) % 2]
            self.ld("sp", x1[:], x1scr[t], [x1], "ld1_x")
            self.act(xc[:], x1[:], AF.Square, [x1], [xc, ss2], accum_out=ss2[:])
            self.ts("dve", ss2[:], ss2[:], 1.0 / 1024, 1e-6, ALU.mult, ALU.add, [ss2], [ss2])
            self.act(ss2[:], ss2[:], AF.Ln, [ss2], [ss2])
            self.act(ss2[:], ss2[:], AF.Exp, [ss2], [ss2], scale=-0.5)
            self.stt(xc[:], x1[:], ss2[:, 0:1], nw1[:], ALU.mult, ALU.mult, [x1, ss2, nw1], [xc])
            if t == 16:
                self.st("sp", p_shift, xc[127:128, :], [xc], "st_c")
            if t == 17:
                f = fsz(xc)
                self.st("sp", s_shift, bass.AP(xc, 7 * f, [[8 * f, 16], [1, 1024]]), [xc], "st_c")
            for half in range(2):
                ps = self.bank()
                cs_ = slice(half * 512, (half + 1) * 512)
                if X == "p":
                    self.mm(ps, ps[:], shp[:], xc[:, cs_], True, False, [shp, xc])
                    self.mm(ps, ps[:], elast[:], xp[:, cs_], False, True, [elast, xp])
                else:
                    if half == 0:
                        self.ld("sp", xp[0:16, :], sshift, [xp], "ld_c")
                    self.mm(ps, ps[:], shs[:], xc[:, cs_], True, False, [shs, xc])
                    self.mm(ps, ps[:], selS[:], xp[0:16, cs_], False, True, [selS, xp])
                self.tt("dve", xx[:, cs_], ps[:], xc[:, cs_], ALU.subtract, [xc], [ps, xx])
            for (src, dst) in ((xc, xnT), (xx, xxT)):
                for half in range(2):
                    ps = self.bank()
                    for c in range(4):
                        cc = half * 4 + c
                        self.tr(ps, ps[:, c * 128:(c + 1) * 128], src[:, cc * 128:(cc + 1) * 128], self.ident32[:], [src, self.ident32])
                    self.cp(self.ev(), dst[:, half * 4:half * 4 + 4, :], ps[:].rearrange("p (c f) -> p c f", c=4), [], [ps, dst])
            def mkxs(i, dst):
                for c in range(8):
                    self.stt(dst[:, c, :], xxT[:, c, :], par[:, c, i:i + 1], xnT[:, c, :], ALU.mult, ALU.add, [xxT, xnT, par], [dst])
            for i in range(3):
                mkxs(i, xs[i])
            ps = self.bank()
            mkxs(4, xs[3])
            for c in range(8):
                self.mm(ps, ps[0:64, 0:128], w1a1[:, c, 0:64], xs[3][:, c, :], c == 0, c == 7, [w1a1, xs[3]])
            mkxs(5, xs[3])
            for c in range(8):
                self.mm(ps, ps[0:64, 128:256], w1a1[:, c, 64:128], xs[3][:, c, :], c == 0, c == 7, [w1a1, xs[3]])
            self.act(hT[:, 0, :], ps[0:64, 0:128], AF.Tanh, [], [ps, hT])
            self.cp("dve", hT[:, 1, :], ps[0:64, 128:256], [], [ps, hT])
            mkxs(3, xs[3])
            for half in range(2):
                ps = self.bank()
                for c in range(8):
                    self.mm(ps, ps[:], xs[3][:, c, :], wR[:, c, 3072 + half * 512:3072 + (half + 1) * 512], c == 0, c == 7, [xs[3], wR])
                self.act(zs[:, half * 512:(half + 1) * 512], ps[:], AF.Silu, [], [ps, zs])
            psRK = self.pfx
            pjb = {}

            def proj1(j):
                psA_ = self.bank()
                psB_ = self.bank()
                pjb[j] = (psA_, psB_)
                for i in range(3):
                    for c in range(8):
                        self.mm(psA_, psA_[:, i * 128:(i + 1) * 128], wR[:, c, i * 1024 + j * 128:i * 1024 + (j + 1) * 128], xs[i][:, c, :],
                                c == 0, c == 7, [wR, xs[i]])
                self.mm(psA_, psA_[:, 384:512], w2a2[:, 0, j * 128:(j + 1) * 128], hT[:, 0, :], True, True, [w2a2, hT])
                self.mm(psB_, psB_[:, 0:128], w2a2[:, 1, j * 128:(j + 1) * 128], hT[:, 1, :], True, True, [w2a2, hT])
            def elem(j):
                yield
                psA_, psB_ = pjb[j]
                yield
                t_ = t_all[j % 2]
                yield
                kdj, bdj = kdjs[j % 2], bdjs[j % 2]
                yield
                pj = lambda i: par[:, j, i:i + 1]
                yield
                yield
                self.act(t_["e"][:], psA_[:, 384:512], AF.Exp, [par], [psA_, t_["e"]], scale=-1.0, bias=pj(11))
                yield
                self.act(t_["e"][:], t_["e"][:], AF.Ln, [t_["e"]], [t_["e"]], bias=1.0)
                yield
                self.act(t_["ew"][:], t_["e"][:], AF.Exp, [t_["e"]], [t_["ew"]], scale=-1.0, bias=-0.5)
                yield
                self.act(t_["a"][:], psB_[:, 0:128], AF.Sigmoid, [par], [psB_, t_["a"]], bias=pj(7))
                yield
                self.cp("act", t_["r"][:], psA_[:, 0:128], [], [psA_, t_["r"]])
                yield
                self.cp("dve", t_["k"][:], psA_[:, 128:256], [], [psA_, t_["k"]])
                yield
                self.cp("act", vT[:, j, :], psA_[:, 256:384], [], [psA_, vT])
                yield
                self.ts("dve", t_["kk"][:], t_["k"][:], pj(8), None, ALU.mult, None, [t_["k"], par], [t_["kk"]])
                yield
                self.act(t_["sq"][:], t_["kk"][:], AF.Square, [t_["kk"]], [t_["sq"]])
                yield
                psn = self.bank()
                yield
                self.mm(psn, psn[:, 0:128], b64[:], t_["sq"][:], True, True, [b64, t_["sq"]])
                yield
                self.act(t_["rn"][:], psn[:, 0:128], AF.Ln, [], [psn, t_["rn"]], bias=1e-6)
                yield "ps_done"
                self.act(t_["rn"][:], t_["rn"][:], AF.Exp, [t_["rn"]], [t_["rn"]], scale=-0.5)
                yield
                self.tt("dve", t_["kk"][:], t_["kk"][:], t_["rn"][:], ALU.mult, [t_["kk"], t_["rn"]], [t_["kk"]])
                yield
                self.ts("dve", t_["x"][:], t_["a"][:], pj(9), pj(12), ALU.mult, ALU.add, [t_["a"], par], [t_["x"]])
                yield
                self.tt("dve", t_["k2"][:], t_["k"][:], t_["x"][:], ALU.mult, [t_["k"], t_["x"]], [t_["k2"]])
                yield
                self.tt("pool", t_["b"][:], t_["kk"][:], t_["a"][:], ALU.mult, [t_["kk"], t_["a"]], [t_["b"]])
                yield
                yield
                self.stt(t_["rk"][:], t_["r"][:], pj(10), t_["k2"][:], ALU.mult, ALU.mult, [t_["r"], t_["k2"], par], [t_["rk"]])
                yield
                self.mm(psRK, psRK[:, 2 * j:2 * j + 2], t_["rk"][:], hsel[:], True, True, [t_["rk"], hsel])
                yield
                yield
                msk = o32 if X == "p" else nb0
                yield
                P.op("dve", lambda e, msk=msk, t_=t_: e.tensor_tensor_scan(out=t_["cs"][:], data0=msk[:], data1=t_["ew"][:], initial=0.0,
                                                                    op0=ALU.mult, op1=ALU.add), r=[msk, t_["ew"]], w=[t_["cs"]])
                yield
                cs = t_["cs"]
                yield
                yield
                self.act(Pc[:, j, 0:nseq], V(cs, T - 1, [[T, nseq]]), AF.Exp, [cs], [Pc], scale=-1.0)
                yield
                yield
                self.tt("dve", t_["dd"][:].rearrange("p (s t) -> p s t", t=T), cs[:].rearrange("p (s t) -> p s t", t=T),
                        V(cs, T - 1, [[T, nseq], [0, T]]), ALU.subtract, [cs], [t_["dd"]])
                yield
                self.act(t_["dd"][:], t_["dd"][:], AF.Exp, [t_["dd"]], [t_["dd"]])
                yield
                self.tt("dve", kdj[:], t_["dd"][:], t_["k2"][:], ALU.mult, [t_["dd"], t_["k2"]], [kdj])
                yield
                self.tt("pool", bdj[:], t_["dd"][:], t_["b"][:], ALU.mult, [t_["dd"], t_["b"]], [bdj])
                yield
                self.tr(self.pb[0], self.pb[0][:, j * 128:(j + 1) * 128], kdj[:], self.identb[:], [kdj, self.identb])
                yield
                self.tr(self.pb[1], self.pb[1][:, j * 128:(j + 1) * 128], bdj[:], self.identb[:], [bdj, self.identb])
                yield
                yield
                self.act(t_["x"][:], cs[:], AF.Exp, [cs], [t_["x"]])
                yield
                self.tt("dve", fm["kt"][:, j, :], t_["x"][:], t_["k2"][:], ALU.mult, [t_["x"], t_["k2"]], [fm["kt"]])
                yield
                self.tt("pool", fm["bt"][:, j, :], t_["x"][:], t_["b"][:], ALU.mult, [t_["x"], t_["b"]], [fm["bt"]])
                yield
                yield
                self.act(t_["x"][:], cs[:], AF.Exp, [cs], [t_["x"]], scale=-1.0)
                yield
                self.tt("dve", fm["rho"][:, j, :], t_["x"][:], t_["r"][:], ALU.mult, [t_["x"], t_["r"]], [fm["rho"]])
                yield
                self.tt("pool", t_["e"][:], cs[:], t_["ew"][:], ALU.subtract, [cs, t_["ew"]], [t_["e"]])
                yield
                self.act(t_["e"][:], t_["e"][:], AF.Exp, [t_["e"]], [t_["e"]], scale=-1.0)
                yield
                self.tt("dve", fm["kap"][:, j, :], t_["e"][:], t_["kk"][:], ALU.mult, [t_["e"], t_["kk"]], [fm["kap"]])

            proj1(0)
            proj1(1)
            for pr in range(4):
                gens = [elem(2 * pr), elem(2 * pr + 1)]
                ndone = 0
                projected = (pr == 3)
                while gens:
                    for g_ in list(gens):
                        try:
                            r_ = next(g_)
                            if r_ == "ps_done":
                                ndone += 1
                        except StopIteration:
                            gens.remove(g_)
                    if ndone == 2 and not projected:
                        proj1(2 * pr + 2)
                        proj1(2 * pr + 3)
                        projected = True
            self.cp("dve", st["rkb"][:], psRK[:, 0:16], [], [psRK, st["rkb"]])
            self.cp("act", kdT[:], self.pb[0][:].rearrange("p (c f) -> p c f", c=8), [], [self.pb[0], kdT])
            self.cp("dve", bdT[:], self.pb[1][:].rearrange("p (c f) -> p c f", c=8), [], [self.pb[1], bdT])
            pb = self.bbank()
            for c in range(8):
                self.tr(pb, pb[:, c * 128:(c + 1) * 128], vT[:, c, :], self.identb[:], [vT, self.identb])
            self.cp("act", Vb[:], pb[:], [], [pb, Vb])
            for i_, src_ in enumerate((fm["kap"], fm["rho"], fm["kt"], fm["bt"], kdT, bdT)):
                self.st("sp", rec1[t, :, i_ * 1024:(i_ + 1) * 1024], src_[:].rearrange("p c f -> p (c f)"), [src_], "st_r1%d" % i_)
            self.st("sp", rec1[t, :, 6144:7168], Vb[:], [Vb], "st_r16")
            self.st("sp", frec1[t, :, 0:1024], zs[:], [zs], "st_r17")
            self.st("sp", frec1[t, :, 1024:1152], Pc[:].rearrange("p c s -> p (c s)"), [Pc], "st_r18")
            self.st("sp", frec1[t, :, 1152:1168], st["rkb"][:], [st["rkb"]], "st_r19")
        P.barrier()
        P.flush()
        es.close()
        self.es_w1.close()

        es = ExitStack()
        cur[0] = es
        wO = sb([128, 8, 1024], BF16, "wO1")
        for c in range(8):
            P.dma("pool", lambda e, c=c: e.dma_start(out=wO[:, c, :], in_=wo_d[c * 128:(c + 1) * 128, :]), w=[wO], sem="ld_w")
        fnw = sb([128, 1024], F32, "fnw")
        lnw = sb([128, 1024], F32, "lnw")
        lnb = sb([128, 1024], F32, "lnb")
        self.ld("sp", fnw[:], pbc(self.d_fnorm_w, 1024), [fnw], "ld_c")
        self.ld("sp", lnw[:], pbc(lw_d, 1024), [lnw], "ld_c")
        self.ld("sp", lnb[:], pbc(lb_d, 1024), [lnb], "ld_c")
        recs = [sb([128, 56, 128], BF16, "b_rec%d" % i) for i in range(2)]
        frecs = [sb([128, 1168], F32, "b_frec%d" % i) for i in range(2)]
        x1t = [sb([128, 1024], F32, "b_x1")] * 2
        Z = sb([128, 8, 64], F32, "b_Z")
        Zb = sb([128, 8, 64], BF16, "b_Zb")
        P.op("pool", lambda e: e.memset(Z[:], 0.0), w=["Z%d" % i for i in range(8)])
        P.op("pool", lambda e: e.memset(Zb[:], 0.0), w=["Zb%d" % i for i in range(8)])
        sets = []
        for gi in range(4):
            B = {nm: sb([128, 4, 128], BF16, "h%d_%s" % (gi, nm)) for nm in ("Mlow", "Tt", "MkT", "AkT", "AbT")}
            B["rhsS"] = sb([128, 4, 64], BF16, "h%d_rhsS" % gi)
            B["SA"] = sb([128, 4, 64], BF16, "h%d_SA" % gi)
            B["yz0"] = sb([128, 4, 64], F32, "h%d_yz0" % gi)
            B["W"] = self.inv_bufs(sb, "h%d" % gi)
            sets.append(B)
        for gi in range(2):
            sh = {"s0in4": sb([64, 4, 2, 64], F32, "h%d_s0in4" % gi), "Vm4": sb([128, 4, 128], BF16, "h%d_Vm4" % gi),
                  "SAm4": sb([128, 4, 128], BF16, "h%d_SAm4" % gi), "Zn4": sb([128, 4, 64], F32, "h%d_Zn4" % gi),
                  "Sout4": sb([64, 4, 128], F32, "h%d_Sout4" % gi)}
            sets[gi].update(sh)
            sets[gi + 2].update(sh)
        ytoks = [sb([128, 16, 64], F32, "b_ytok%d" % i) for i in range(2)]
        ysq = self.nb["xn"]
        st = {nm: sb([128, 16], F32, "b_st_" + nm) for nm in ("sum", "ssq", "mean", "var", "rstd")}
        ss2 = sb([128, 1], F32, "b_ss2")
        ygT = sb([128, 8, 128], BF16, "b_ygT")
        x2 = ysq
        Z0b2 = [sb([128, 16, 64], BF16, "b_Z0b%d" % i) for i in range(2)]
        mk = sb([128, 16, 128], BF16, "b_mk")
        Sout = sb([64, 128], F32, "b_Sout")
        KAP, RHO, KT, BT, KD, BD, VB = 0, 8, 16, 24, 32, 40, 48
        v4 = lambda ps: ps[:, 0:512].rearrange("p (h f) -> p h f", h=4)

        def rwkv_group(t, g, B, rec, frec, ytok, ytn):
            X = "p" if t < 17 else "s"
            hs = [4 * g + hh for hh in range(4)]
            order = (0, 2, 1, 3)

            def fmh(off, h):
                b0 = (h % 2) * 64
                return rec[b0:b0 + 64, off + h // 2, :]
            Vh = lambda h: rec[:, VB + h // 2, (h % 2) * 64:(h % 2) * 64 + 64]
            Mlow, Tt, MkT, AkT, AbT, rhsS, SA, yz0 = (B[k_] for k_ in ("Mlow", "Tt", "MkT", "AkT", "AbT", "rhsS", "SA", "yz0"))

            def prod4(ps, lo, ro):
                for i_, hh in enumerate(order):
                    h = hs[hh]
                    self.mm(ps, ps[:, hh * 128:(hh + 1) * 128], fmh(lo, h), fmh(ro, h), True, True, [rec], sw=(i_ == 2))
            psM = self.bank()
            prod4(psM, KAP, BT)
            self.tt("dve", Mlow[:], v4(psM), self.bc4(mneg[X]), ALU.mult, [mneg[X]], [psM, Mlow])
            for (lo, ro, dst, msk) in ((KT, KAP, MkT, msup[X]), (KT, RHO, AkT, mup[X]), (BT, RHO, AbT, mup[X])):
                ps = self.bank()
                prod4(ps, lo, ro)
                self.tt("dve", dst[:], v4(ps), self.bc4(msk), ALU.mult, [msk], [ps, dst])
            yield
            yield from self.inverse_gen(B["W"], Mlow, Tt)
            s0in4, Vm4, SAm4, Zn4, Sout4 = (B[k_] for k_ in ("s0in4", "Vm4", "SAm4", "Zn4", "Sout4"))
            if X == "s":
                for q in range(2):
                    m = 2 * g + q
                    for sg in range(4):
                        for h2_ in range(2):
                            self.ld("sp", s0in4[:, :, h2_, :], srwkv[4 * sg:4 * sg + 4, 2 * m + h2_].rearrange("s v k -> v s k"), [s0in4],
                                    "ld_s0in%d" % (g % 2))
                        pz = self.bank()
                        for si in range(4):
                            self.tr(pz, pz[:, si * 64:(si + 1) * 64], s0in4[:, si].rearrange("v h k -> v (h k)"), self.ident32[0:64, 0:64],
                                    [s0in4, self.ident32])
                        self.cp(self.ev(), Z0b2[q][:, 4 * sg:4 * sg + 4, :], pz[:, 0:256].rearrange("p (s f) -> p s f", s=4), [],
                                [pz, Z0b2[q]])
            psR = self.bank()
            if X == "s":
                psY2 = self.bank()
            for hh, h in enumerate(hs):
                b0 = (h % 2) * 64
                m = h // 2
                if X == "s":
                    Z0b = Z0b2[hh // 2]
                    for (off_, psd, lastflag) in ((KAP, psR, False), (RHO, psY2, True)):
                        a_ = fmh(off_, h)
                        self.tt("pool", mk[b0:b0 + 64], bass.AP(a_.tensor, a_.offset, [list(a_.ap[0]), [0, 16], [1, 128]]),
                                self.cm[b0:b0 + 64], ALU.mult, [rec, self.cm], [mk])
                        for s_ in range(16):
                            self.mm(psd, psd[:, hh * 64:(hh + 1) * 64], mk[b0:b0 + 64, s_, :], Z0b[b0:b0 + 64, s_, :], s_ == 0,
                                    lastflag and s_ == 15, [mk, Z0b], sw=(s_ == 0))
                else:
                    self.mm(psR, psR[:, hh * 64:(hh + 1) * 64], fmh(KAP, h), Zb[b0:b0 + 64, m, :], True, False, [rec, "Zb%d" % m], sw=True)
                self.mm(psR, psR[:, hh * 64:(hh + 1) * 64], MkT[:, hh, :], Vh(h), False, True, [MkT, rec])
            if X == "s":
                self.cp("dve", yz0[:].rearrange("p h f -> p (h f)"), psY2[:, 0:256], [], [psY2, yz0])
            self.act(rhsS[:].rearrange("p h f -> p (h f)"), psR[:, 0:256], AF.Copy, [], [psR, rhsS], scale=-1.0)
            yield
            psS = self.bank()
            for hh in range(4):
                self.mm(psS, psS[:, hh * 64:(hh + 1) * 64], Tt[:, hh, :], rhsS[:, hh, :], True, True, [Tt, rhsS])
            self.cp("act", SA[:].rearrange("p h f -> p (h f)"), psS[:, 0:256], [], [psS, SA])
            yield
            psY = self.bank()
            for hh, h in enumerate(hs):
                b0 = (h % 2) * 64
                m = h // 2
                if X == "p":
                    self.mm(psY, psY[:, hh * 64:(hh + 1) * 64], fmh(RHO, h), Zb[b0:b0 + 64, m, :], True, False, [rec, "Zb%d" % m], sw=True)
                self.mm(psY, psY[:, hh * 64:(hh + 1) * 64], AkT[:, hh, :], Vh(h), X == "s", False, [AkT, rec])
                self.mm(psY, psY[:, hh * 64:(hh + 1) * 64], AbT[:, hh, :], SA[:, hh, :], False, True, [AbT, SA])
            yv = ytok[:, 4 * g:4 * g + 4, :].rearrange("p h f -> p (h f)")
            if X == "p":
                self.cp("dve", yv, psY[:, 0:256], [], [psY, ytn + str(g)])
            else:
                self.tt("dve", yv, psY[:, 0:256], yz0[:].rearrange("p h f -> p (h f)"), ALU.add, [yz0], [psY, ytn + str(g)])
            for mm_ in range(2):
                m = 2 * g + mm_
                SApair = SA[:, 2 * mm_:2 * mm_ + 2, :].rearrange("p h f -> p (h f)")
                if X == "p":
                    psZ = self.bank()
                    self.mm(psZ, psZ[:, 0:128], rec[:, KD + m, :], rec[:, VB + m, :], True, False, [rec])
                    self.mm(psZ, psZ[:, 0:128], rec[:, BD + m, :], SApair, False, True, [rec, SA])
                    for h2 in range(2):
                        b0 = h2 * 64
                        self.stt(Z[b0:b0 + 64, m, :], Z[b0:b0 + 64, m, :], V(frec, 1024 + m * 16, [[1, 1]], 64, b0),
                                 psZ[b0:b0 + 64, b0:b0 + 64], ALU.mult, ALU.add, ["Z%d" % m, frec], [psZ, "Z%d" % m])
                    self.cp("act", Zb[:, m, :], Z[:, m, :], ["Z%d" % m], ["Zb%d" % m])
                    if t == 16:
                        pz = self.bank()
                        self.tr(pz, pz[0:64, 0:128], Z[:, m, :], self.ident32[:], ["Z%d" % m, self.ident32])
                        self.cp("act", Sout[:], pz[0:64, 0:128], [], [pz, Sout])
                        self.st("sp", p_rwkv[2 * m:2 * m + 2].rearrange("h v k -> v h k"), Sout[:].rearrange("v (h k) -> v h k", h=2),
                                [Sout], "st_so")
                else:
                    for sg in range(4):
                        for h2_ in range(2):
                            self.ld("sp", s0in4[:, :, h2_, :], srwkv[4 * sg:4 * sg + 4, 2 * m + h2_].rearrange("s v k -> v s k"), [s0in4],
                                    "ld_s0in%d" % (g % 2))
                        pz = self.bank()
                        for si in range(4):
                            self.tr(pz, pz[:, si * 64:(si + 1) * 64], s0in4[:, si].rearrange("v h k -> v (h k)"), self.ident32[0:64, 0:64],
                                    [s0in4, self.ident32])
                        vpair = rec[:, VB + m, :]
                        smb = V(self.sm, 4 * sg, [[1, 4], [0, 128]])
                        self.tt("pool", Vm4[:], bass.AP(vpair.tensor, vpair.offset, [list(vpair.ap[0]), [0, 4], [1, 128]]), smb, ALU.mult,
                                [rec, self.sm], [Vm4])
                        sap = SA[:, 2 * mm_:2 * mm_ + 2, :]
                        self.tt("pool", SAm4[:], bass.AP(sap.tensor, sap.offset, [list(sap.ap[0]), [0, 4], [1, 128]]), smb, ALU.mult,
                                [SA, self.sm], [SAm4])
                        psZ = self.bank()
                        for si in range(4):
                            self.mm(psZ, psZ[:, si * 128:(si + 1) * 128], rec[:, KD + m, :], Vm4[:, si, :], True, False, [rec, Vm4])
                            self.mm(psZ, psZ[:, si * 128:(si + 1) * 128], rec[:, BD + m, :], SAm4[:, si, :], False, True, [rec, SAm4])
                        self.tt("dve", Zn4[:], pz[:, 0:256].rearrange("p (s f) -> p s f", s=4),
                                V(frec, 1024 + m * 16 + 4 * sg, [[1, 4], [0, 64]]), ALU.mult, [frec], [pz, Zn4])
                        for h2 in range(2):
                            b0 = h2 * 64
                            self.tt("dve", Zn4[b0:b0 + 64], Zn4[b0:b0 + 64],
                                    psZ[b0:b0 + 64, 0:512].rearrange("p (s f) -> p s f", s=4)[:, :, b0:b0 + 64], ALU.add, [Zn4], [psZ, Zn4])
                        pz2 = self.bank()
                        for si in range(4):
                            self.tr(pz2, pz2[0:64, si * 128:(si + 1) * 128], Zn4[:, si, :], self.ident32[:], [Zn4, self.ident32])
                        self.cp("act", Sout4[:].rearrange("v s f -> v (s f)"), pz2[0:64, 0:512], [], [pz2, Sout4])
                        for h2_ in range(2):
                            self.st("sp", s_rwkv[4 * sg:4 * sg + 4, 2 * m + h2_].rearrange("s v k -> v s k"),
                                    Sout4[:, :, h2_ * 64:(h2_ + 1) * 64], [Sout4], "st_so%d" % (g % 2))
                        yield
            yield

        def post_gen(t, rec, frec, ytok, ytn):
            x1 = x1t[0]
            ytokr = [ytn + str(g_) for g_ in range(4)]
            self.ld("sp", x1[:], x1scr[t], [x1], "x")
            P.op("dve", lambda e: e.tensor_reduce(out=st["sum"][:], in_=ytok[:], axis=AX.X, op=ALU.add), r=ytokr, w=[st["sum"]])
            ysq3 = ysq[:].rearrange("p (h f) -> p h f", h=16)
            self.act(ysq3, ytok[:], AF.Square, ytokr, [ysq])
            P.op("dve", lambda e: e.tensor_reduce(out=st["ssq"][:], in_=ysq3, axis=AX.X, op=ALU.add), r=[ysq], w=[st["ssq"]])
            yield
            self.ts("dve", st["mean"][:], st["sum"][:], 1.0 / 64, None, ALU.mult, None, [st["sum"]], [st["mean"]])
            self.tt("dve", st["var"][:], st["mean"][:], st["mean"][:], ALU.mult, [st["mean"]], [st["var"]])
            self.stt(st["var"][:], st["ssq"][:], 1.0 / 64, st["var"][:], ALU.mult, ALU.subtract, [st["ssq"], st["var"]], [st["var"]])
            self.ts("dve", st["var"][:], st["var"][:], 64e-5, None, ALU.add, None, [st["var"]], [st["var"]])
            self.act(st["rstd"][:], st["var"][:], AF.Ln, [st["var"]], [st["rstd"]])
            self.act(st["rstd"][:], st["rstd"][:], AF.Exp, [st["rstd"]], [st["rstd"]], scale=-0.5)
            yield
            self.tt("dve", ysq3, ytok[:], V(st["mean"], 0, [[1, 16], [0, 64]]), ALU.subtract, ytokr + [st["mean"]], [ysq])
            self.tt("dve", ysq3, ysq3, V(st["rstd"], 0, [[1, 16], [0, 64]]), ALU.mult, [ysq, st["rstd"]], [ysq])
            yield
            yf = ysq[:]
            self.tt("pool", yf, yf, lnw[:], ALU.mult, [ysq, lnw], [ysq])
            self.tt("pool", yf, yf, lnb[:], ALU.add, [ysq, lnb], [ysq])
            yield
            yt3 = ytok[:]
            self.tt("dve", yt3, rec[:, VB:VB + 8, :].rearrange("p c (a f) -> p (c a) f", a=2), V(frec, 1152, [[1, 16], [0, 64]]), ALU.mult,
                    [rec, frec] + ytokr, ytokr)
            self.tt("dve", yf, yf, ytok[:].rearrange("p h f -> p (h f)"), ALU.add, [ysq] + ytokr, [ysq])
            self.tt("dve", yf, yf, frec[:, 0:1024], ALU.mult, [ysq, frec], [ysq])
            yield
            for half in range(2):
                ps = self.bank()
                for c in range(4):
                    cc = half * 4 + c
                    self.tr(ps, ps[:, c * 128:(c + 1) * 128], yf[:, cc * 128:(cc + 1) * 128], self.ident32[:], [ysq, self.ident32])
                self.cp(self.ev(), ygT[:, half * 4:half * 4 + 4, :], ps[:].rearrange("p (c f) -> p c f", c=4), [], [ps, ygT])
                yield
            for half in range(2):
                ps = self.bank()
                cs_ = slice(half * 512, (half + 1) * 512)
                for c in range(8):
                    self.mm(ps, ps[:], ygT[:, c, :], wO[:, c, cs_], c == 0, c == 7, [ygT, wO])
                self.tt("dve", x2[:, cs_], ps[:], x1[:, cs_], ALU.add, [x1], [ps, x2])
                yield
            yy = ytok
            yyv = ytok[:].rearrange("p h f -> p (h f)")
            self.act(yyv, x2[:], AF.Square, [x2], ytokr + [ss2], accum_out=ss2[:])
            self.ts("dve", ss2[:], ss2[:], 1.0 / 1024, 1e-6, ALU.mult, ALU.add, [ss2], [ss2])
            self.act(ss2[:], ss2[:], AF.Ln, [ss2], [ss2])
            self.act(ss2[:], ss2[:], AF.Exp, [ss2], [ss2], scale=-0.5)
            yield
            self.stt(yyv, x2[:], ss2[:, 0:1], fnw[:], ALU.mult, ALU.mult, [x2, ss2, fnw], ytokr)
            self.st("sp", y[t], yyv, ytokr, "stc_y")

        pending = None
        for t in range(NT):
            rec, frec = recs[t % 2], frecs[t % 2]
            ytok, ytn = ytoks[t % 2], "yt%d_" % (t % 2)
            self.ld("sp", rec[:].rearrange("p a f -> p (a f)"), rec1[t], [rec], "x")
            self.ld("sp", frec[:], frec1[t], [frec], "x")
            gens = [rwkv_group(t, g, sets[g], rec, frec, ytok, ytn) for g in range(4)]
            if pending is not None:
                gens.append(pending)
            self.run_rr(gens)
            pending = post_gen(t, rec, frec, ytok, ytn)
        self.run_rr([pending])
        P.barrier()
        P.flush()
        es.close()
        P.es.close()


_CACHE = {}


def kernel(x_prompt, x_sample, state_gdn, state_gdn_conv, state_rwkv, state_rwkv_shift,
           meta_tokens, norm_w, final_norm_w,
           gdn_w_in, gdn_conv_w, gdn_a_log, gdn_dt_bias, gdn_norm_w, gdn_w_out,
           rwkv_mu, rwkv_w_rkvz, rwkv_w0, rwkv_w1, rwkv_w2, rwkv_a0, rwkv_a1, rwkv_a2,
           rwkv_k_k, rwkv_k_a, rwkv_r_k, rwkv_lnx_w, rwkv_lnx_b, rwkv_w_o):
    f = lambda a: np.ascontiguousarray(np.asarray(a, dtype=np.float32))
    if "nc" not in _CACHE:
        _CACHE["nc"] = K().nc
    nc = _CACHE["nc"]
    x_prompt, x_sample = f(x_prompt), f(x_sample)
    meta = f(meta_tokens)
    in_maps = []
    for c in range(8):
        xin = np.zeros((NT, 128, 1024), np.float32)
        xin[0, 112:128] = meta
        xin[1:17] = x_prompt[c].reshape(16, 128, 1024)
        xin[17] = x_sample[16 * c:16 * c + 16].reshape(128, 1024)
        in_maps.append({
            "xin": xin,
            "sgdn": f(state_gdn[0, 16 * c:16 * c + 16]),
            "sconv": f(state_gdn_conv[0, 16 * c:16 * c + 16]).reshape(48, 4096),
            "srwkv": f(state_rwkv[0, 16 * c:16 * c + 16]),
            "sshift": f(state_rwkv_shift[0, 16 * c:16 * c + 16]),
            "norm_w": f(norm_w), "fnorm_w": f(final_norm_w).reshape(1, 1024),
            "w_in": f(gdn_w_in[0]), "conv_w": f(gdn_conv_w[0]), "a_log": f(gdn_a_log), "dt_bias": f(gdn_dt_bias),
            "gnorm_w": f(gdn_norm_w), "w_out": f(gdn_w_out[0]),
            "mu": f(rwkv_mu[0]), "rkvz": f(rwkv_w_rkvz[0]), "w0": f(rwkv_w0), "w1": f(rwkv_w1[0]), "w2": f(rwkv_w2[0]),
            "a0": f(rwkv_a0), "a1": f(rwkv_a1[0]), "a2": f(rwkv_a2[0]), "k_k": f(rwkv_k_k), "k_a": f(rwkv_k_a),
            "r_k": f(rwkv_r_k).reshape(1, 1024), "lnx_w": f(rwkv_lnx_w), "lnx_b": f(rwkv_lnx_b), "w_o": f(rwkv_w_o[0]),
        })
    res = run_bass_kernel_spmd(nc, in_maps, core_ids=list(range(8)))
    R = res.results
    y_prompt = np.stack([R[c]["y"][1:17].reshape(2048, 1024) for c in range(8)])
    y_sample = np.concatenate([R[c]["y"][17].reshape(16, 8, 1024) for c in range(8)])
    p_gdn = np.stack([R[c]["p_gdn"] for c in range(8)])[None]
    p_conv = np.stack([R[c]["p_conv"] for c in range(8)])[None]
    s_gdn = np.concatenate([R[c]["s_gdn"] for c in range(8)])[None]
    s_conv = np.concatenate([R[c]["s_conv"].reshape(16, 3, 4096) for c in range(8)])[None]
    p_rwkv = np.stack([R[c]["p_rwkv"] for c in range(8)])[None]
    p_shift = np.stack([R[c]["p_shift"].reshape(1024) for c in range(8)])[None]
    s_rwkv = np.concatenate([R[c]["s_rwkv"] for c in range(8)])[None]
    s_shift = np.concatenate([R[c]["s_shift"] for c in range(8)])[None]
    return (y_prompt, y_sample, p_gdn, p_conv, p_rwkv, p_shift, s_gdn, s_conv, s_rwkv, s_shift)
```

```python
from contextlib import ExitStack
import numpy as np
import concourse.bass as bass
import concourse.mybir as mybir
from concourse.bass_utils import run_bass_kernel_spmd

F32 = mybir.dt.float32
BF16 = mybir.dt.bfloat16
AF = mybir.ActivationFunctionType
ALU = mybir.AluOpType
AX = mybir.AxisListType
ENGS = ("pe", "act", "dve", "pool", "sp")
NT = 18
BIG = 30000.0


class Prog:
    CE = ("pe", "act", "dve", "pool")

    def __init__(self, nc):
        self.nc = nc
        self.es = ExitStack()
        self.q = {e: [] for e in ENGS}
        self.cnt = {}
        self.waited = {e: {} for e in ENGS}
        self.lastw = {}
        self.reads = {}
        self.semkeys = set(ENGS)
        self.ntens = 0
        self.nops = 0
        self.nidx = {e: 0 for e in self.CE}
        self.marks = {e: [] for e in self.CE}
        self.entry = {e: {} for e in self.CE}

    def sb(self, shape, dt=F32, name=None):
        self.ntens += 1
        name = name or f"t{self.ntens}"
        return self.es.enter_context(self.nc.sbuf_tensor(name, list(shape), dt))

    def ps(self, shape, dt=F32, name=None):
        self.ntens += 1
        name = name or f"p{self.ntens}"
        return self.es.enter_context(self.nc.psum_tensor(name, list(shape), dt))

    @staticmethod
    def _k(x):
        return x if isinstance(x, str) else x.name

    def _deps(self, r, w):
        toks = []
        for x in r:
            t = self.lastw.get(x)
            if t is not None:
                toks.append(t)
        for x in w:
            t = self.lastw.get(x)
            if t is not None:
                toks.append(t)
            toks.extend(self.reads.get(x, {}).values())
        return toks

    def _record(self, r, w, tok):
        key = tok[1]
        for x in r:
            d = self.reads.setdefault(x, {})
            o = d.get(key)
            if o is None or o[2] < tok[2]:
                d[key] = tok
        for x in w:
            self.lastw[x] = tok
            self.reads[x] = {}

    def _resolve(self, tok, peek=False):
        if tok[0] == "D":
            return tok[1], tok[2]
        _, eng, idx = tok
        m = self.marks[eng]
        if m and m[-1][0] >= idx:
            lo, hi = 0, len(m) - 1
            while lo < hi:
                mid = (lo + hi) // 2
                if m[mid][0] >= idx:
                    hi = mid
                else:
                    lo = mid + 1
            return eng, m[lo][1]
        ent = self.entry[eng][idx]
        cnt = len(m) + 1
        m.append((idx, cnt))
        ent[2] = (eng, 1)
        return eng, cnt

    def _waits(self, eng, toks, pe_self=False):
        need = {}
        for tok in toks:
            if tok[0] == "E" and tok[1] == eng and eng == "pe" and not pe_self:
                continue
            k, v = self._resolve(tok)
            if self.waited[eng].get(k, 0) >= v:
                continue
            need[k] = max(need.get(k, 0), v)
        for k, v in need.items():
            self.waited[eng][k] = v
        return list(need.items())

    def op(self, eng, fn, r=(), w=(), pe_self=False):
        r = [self._k(x) for x in r]
        w = [self._k(x) for x in w]
        waits = self._waits(eng, self._deps(r, w), pe_self)
        self.nidx[eng] += 1
        idx = self.nidx[eng]
        ent = [waits, fn, None]
        self.entry[eng][idx] = ent
        self.q[eng].append(ent)
        tok = ("E", eng, idx)
        self._record(r, w, tok)
        self.nops += 1
        return tok

    def dma(self, q, fn, r=(), w=(), sem=None):
        r = [self._k(x) for x in r]
        w = [self._k(x) for x in w]
        sem = ("L_" + w[0]) if w else ("S_" + r[0])
        self.semkeys.add(sem)
        waits = self._waits(q, self._deps(r, w))
        self.cnt[sem] = self.cnt.get(sem, 0) + 16
        tok = ("D", sem, self.cnt[sem])
        self.q[q].append([waits, fn, (sem, 16)])
        self._record(r, w, tok)
        self.nops += 1
        return tok

    def _all_toks(self):
        toks = [("E", e, self.nidx[e]) for e in self.CE if self.nidx[e] > 0]
        toks += [("D", k, v) for k, v in self.cnt.items()]
        return toks

    def barrier(self):
        toks = self._all_toks()
        for e in ENGS:
            w = self._waits(e, [t for t in toks if not (t[0] == "E" and t[1] == e)])
            if w:
                self.q[e].append([w, None, None])

    def flush(self):
        nc = self.nc
        if not hasattr(self, "sems"):
            self.sems = {}
        for k in sorted(self.semkeys):
            if k not in self.sems:
                self.sems[k] = self.es.enter_context(nc.semaphore("s_" + k))
        sems = self.sems
        q = self.q
        self.q = {e: [] for e in ENGS}
        self.entry = {e: {} for e in self.CE}

        def run(e, lst):
            for waits, fn, inc in lst:
                for k, v in waits:
                    e.wait_ge(sems[k], v)
                if fn is not None:
                    inst = fn(e)
                    if inc is not None:
                        inst.then_inc(sems[inc[0]], inc[1])

        with nc.Block() as block:
            @block.tensor
            def _(e):
                run(e, q["pe"])

            @block.scalar
            def _(e):
                run(e, q["act"])

            @block.vector
            def _(e):
                run(e, q["dve"])

            @block.gpsimd
            def _(e):
                run(e, q["pool"])

            @block.sync
            def _(e):
                run(e, q["sp"])


def fsz(t):
    n = 1
    for s in list(t.shape)[1:]:
        n *= int(s)
    return n


def V(t, off, dims, npart=128, p0=0):
    f = fsz(t)
    return bass.AP(t, p0 * f + off, [[f, npart]] + [list(d) for d in dims])


class K:
    def __init__(self, nlayers=2):
        self.nlayers = nlayers
        nc = bass.Bass("TRN2", target_bir_lowering=False)
        self.nc = nc
        self.P = Prog(nc)
        self.rr = {"pf": 0, "pb": 0, "ev": 0}
        self.build()

    def din(self, name, shape):
        return self.nc.dram_tensor(name, list(shape), F32, kind="ExternalInput").ap()

    def dout(self, name, shape):
        return self.nc.dram_tensor(name, list(shape), F32, kind="ExternalOutput").ap()

    def bank(self):
        self.rr["pf"] = (self.rr["pf"] + 1) % len(self.pf)
        return self.pf[self.rr["pf"]]

    def bbank(self):
        self.rr["pb"] = (self.rr["pb"] + 1) % len(self.pb)
        return self.pb[self.rr["pb"]]

    def ev(self):
        self.rr["ev"] ^= 1
        return "act" if self.rr["ev"] else "dve"

    def mm(self, ps, out, lhsT, rhs, start, stop, r, sw=None):
        self.P.op("pe", lambda e: e.matmul(out, lhsT=lhsT, rhs=rhs, start=start, stop=stop), r=r, w=[ps],
                  pe_self=(getattr(self, "pe_self", False) if sw is None else sw))

    def tr(self, ps, out, in_, ident, r):
        self.P.op("pe", lambda e: e.transpose(out=out, in_=in_, identity=ident), r=r, w=[ps])

    def tt(self, eng, out, in0, in1, op, r, w):
        self.P.op(eng, lambda e: e.tensor_tensor(out=out, in0=in0, in1=in1, op=op), r=r, w=w)

    def ts(self, eng, out, in0, s1, s2, op0, op1, r, w):
        if op1 is None:
            self.P.op(eng, lambda e: e.tensor_scalar(out=out, in0=in0, scalar1=s1, scalar2=None, op0=op0), r=r, w=w)
        else:
            self.P.op(eng, lambda e: e.tensor_scalar(out=out, in0=in0, scalar1=s1, scalar2=s2, op0=op0, op1=op1), r=r, w=w)

    def stt(self, out, in0, scalar, in1, op0, op1, r, w):
        self.P.op("dve", lambda e: e.scalar_tensor_tensor(out=out, in0=in0, scalar=scalar, in1=in1, op0=op0, op1=op1), r=r, w=w)

    def act(self, out, in_, func, r, w, bias=None, scale=None, accum_out=None):
        kw = {}
        if bias is not None:
            kw["bias"] = bias
        if scale is not None:
            kw["scale"] = scale
        if accum_out is not None:
            kw["accum_out"] = accum_out
        self.P.op("act", lambda e: e.activation(out=out, in_=in_, func=func, **kw), r=r, w=w)

    def cp(self, eng, out, in_, r, w):
        if eng == "act":
            self.P.op("act", lambda e: e.copy(out=out, in_=in_), r=r, w=w)
        else:
            self.P.op(eng, lambda e: e.tensor_copy(out=out, in_=in_), r=r, w=w)

    def ld(self, q, out, in_, w, sem, r=()):
        self.P.dma(q, lambda e: e.dma_start(out=out, in_=in_), r=r, w=w, sem=sem)

    def st(self, q, out, in_, r, sem):
        self.P.dma(q, lambda e: e.dma_start(out=out, in_=in_), r=r, w=(), sem=sem)

    def consts(self):
        P = self.P
        sb = P.sb
        self.ones32 = sb([128, 128], F32, "ones32")
        self.ident32 = sb([128, 128], F32, "ident32")
        self.identb = sb([128, 128], BF16, "identb")
        P.op("pool", lambda e: e.memset(self.ones32[:], 1.0), w=[self.ones32])

        def sel(out, in_, pattern, op, base, cm, r, w, fill=0.0):
            P.op("pool", lambda e: e.affine_select(out=out, in_=in_, pattern=pattern, compare_op=op, fill=fill,
                                                    base=base, channel_multiplier=cm), r=r, w=w)
        o = self.ones32
        sel(self.ident32[:], o[:], [[-1, 128]], ALU.is_equal, 0, 1, [o], [self.ident32])
        self.cp("pool", self.identb[:], self.ident32[:], [self.ident32], [self.identb])
        blk8 = sb([128, 128], F32, "blk8")
        blk32 = sb([128, 128], F32, "blk32")
        for (t, B) in ((blk8, 8), (blk32, 32)):
            v3 = t[:].rearrange("p (s t) -> p s t", t=B)
            o3 = o[:].rearrange("p (s t) -> p s t", t=B)
            sel(v3, o3, [[-B, 128 // B], [0, B]], ALU.is_ge, 0, 1, [o], [t])
            sel(v3, v3, [[B, 128 // B], [0, B]], ALU.is_ge, B - 1, -1, [t], [t])
        low = sb([128, 128], F32, "low")
        up = sb([128, 128], F32, "up")
        sel(low[:], o[:], [[-1, 128]], ALU.is_ge, 0, 1, [o], [low])
        sel(up[:], o[:], [[1, 128]], ALU.is_ge, 0, -1, [o], [up])
        self.lowT = {}
        self.same = {}
        self.addm = {}
        for X in ("p", "s"):
            lowT = sb([128, 128], F32, "lowT" + X)
            lowX = sb([128, 128], F32, "lowX" + X)
            am = sb([128, 4, 128], F32, "addm" + X)
            if X == "p":
                self.cp("pool", lowT[:], up[:], [up], [lowT])
                self.cp("pool", lowX[:], low[:], [low], [lowX])
                self.same[X] = self.ones32
            else:
                self.tt("pool", lowT[:], up[:], blk8[:], ALU.mult, [up, blk8], [lowT])
                self.tt("pool", lowX[:], low[:], blk8[:], ALU.mult, [low, blk8], [lowX])
                self.same[X] = blk8
            for h in range(4):
                self.ts("pool", am[:, h, :], lowX[:], -BIG, BIG, ALU.mult, ALU.add, [lowX], [am])
            self.lowT[X] = lowT
            self.addm[X] = am
        slow = sb([128, 128], F32, "slow")
        sup = sb([128, 128], F32, "sup")
        sel(slow[:], o[:], [[-1, 128]], ALU.is_gt, 0, 1, [o], [slow])
        sel(sup[:], o[:], [[1, 128]], ALU.is_gt, 0, -1, [o], [sup])
        self.m1 = sb([128, 128], BF16, "m1")
        self.m1T = sb([128, 128], BF16, "m1T")
        self.m2 = sb([128, 128], BF16, "m2")
        self.tt("pool", self.m1[:], blk32[:], slow[:], ALU.mult, [blk32, slow], [self.m1])
        self.tt("pool", self.m1T[:], blk32[:], sup[:], ALU.mult, [blk32, sup], [self.m1T])
        self.ts("pool", self.m2[:], blk32[:], -1.0, 1.0, ALU.mult, ALU.add, [blk32], [self.m2])
        self.sm = sb([128, 16], F32, "sm")
        sel(self.sm[:], o[:, 0:16], [[-8, 16]], ALU.is_ge, 0, 1, [o], [self.sm])
        sel(self.sm[:], self.sm[:], [[8, 16]], ALU.is_ge, 7, -1, [self.sm], [self.sm])
        self.cm = sb([128, 16, 128], BF16, "cm")
        P.op("pool", lambda e: e.memset(self.cm[:], 1.0), w=[self.cm])
        sel(self.cm[:], self.cm[:], [[-8, 16], [1, 128]], ALU.is_ge, 0, 0, [self.cm], [self.cm])
        sel(self.cm[:], self.cm[:], [[8, 16], [-1, 128]], ALU.is_ge, 7, 0, [self.cm], [self.cm])

    def bc4(self, m):
        return V(m, 0, [[0, 4], [1, 128]])

    def inverse(self, Mlow, Tt, tag):
        for _ in self.inverse_gen(self.invw, Mlow, Tt):
            pass

    def inverse_gen(self, W, Mlow, Tt, extra=None):
        Ib = self.bc4(self.identb)
        v4 = lambda ps: ps[:, 0:512].rearrange("p (h f) -> p h f", h=4)
        MdT, Md, No = W["MdT"], W["Md"], W["No"]
        self.tt("pool", MdT[:], Mlow[:], self.bc4(self.m1), ALU.mult, [Mlow, self.m1], [MdT])
        pb = self.bbank()
        for hh in range(4):
            self.tr(pb, pb[:, hh * 128:(hh + 1) * 128], Mlow[:, hh, :], self.identb[:], [Mlow, self.identb])
        self.tt("dve", Md[:], v4(pb), self.bc4(self.m1T), ALU.mult, [self.m1T], [pb, Md])
        self.tt("dve", No[:], v4(pb), self.bc4(self.m2), ALU.mult, [self.m2], [pb, No])
        X = W["X0"]
        self.tt("pool", X[:], Md[:], Ib, ALU.add, [Md, self.identb], [X])
        if extra is not None:
            extra()
        yield
        Pk, PTk = Md, MdT
        IpPT_prev = None
        for k in range(1, 5):
            last = (k == 4)
            if IpPT_prev is not None:
                psX = self.bank()
                for hh in range(4):
                    self.mm(psX, psX[:, hh * 128:(hh + 1) * 128], IpPT_prev[:, hh, :], X[:, hh, :], True, True, [IpPT_prev, X])
            psPT = self.bank()
            for hh in range(4):
                self.mm(psPT, psPT[:, hh * 128:(hh + 1) * 128], Pk[:, hh, :], PTk[:, hh, :], True, True, [Pk, PTk])
            if not last:
                psP = self.bank()
                for hh in range(4):
                    self.mm(psP, psP[:, hh * 128:(hh + 1) * 128], PTk[:, hh, :], Pk[:, hh, :], True, True, [Pk, PTk])
            if IpPT_prev is not None:
                Xn = W["X%d" % ((k - 1) % 2)]
                self.cp("act", Xn[:], v4(psX), [], [psX, Xn])
                X = Xn
            IpPT = W["IpPT%d" % (k % 2)]
            if not last:
                PTn = W["PT%d" % (k % 2)]
                Pn = W["P%d" % (k % 2)]
                self.cp("act", PTn[:], v4(psPT), [], [psPT, PTn])
                self.cp("dve", Pn[:], v4(psP), [], [psP, Pn])
                self.tt("pool", IpPT[:], PTn[:], Ib, ALU.add, [PTn, self.identb], [IpPT])
                Pk, PTk = Pn, PTn
            else:
                self.tt("dve", IpPT[:], v4(psPT), Ib, ALU.add, [self.identb], [psPT, IpPT])
            IpPT_prev = IpPT
            yield
        psX = self.bank()
        for hh in range(4):
            self.mm(psX, psX[:, hh * 128:(hh + 1) * 128], IpPT_prev[:, hh, :], X[:, hh, :], True, True, [IpPT_prev, X])
        Xn = W["X0"]
        self.cp(self.ev(), Xn[:], v4(psX), [], [psX, Xn])
        X = Xn
        yield
        pb = self.bbank()
        for hh in range(4):
            self.tr(pb, pb[:, hh * 128:(hh + 1) * 128], X[:, hh, :], self.identb[:], [X, self.identb])
        XT = W["XT"]
        self.cp(self.ev(), XT[:], v4(pb), [], [pb, XT])
        yield
        psV = self.bank()
        psVT = self.bank()
        for hh in range(4):
            self.mm(psV, psV[:, hh * 128:(hh + 1) * 128], XT[:, hh, :], No[:, hh, :], True, True, [XT, No])
        for hh in range(4):
            self.mm(psVT, psVT[:, hh * 128:(hh + 1) * 128], No[:, hh, :], XT[:, hh, :], True, True, [XT, No])
        Vm, VT, IpVT = W["P0"], W["P1"], W["PT0"]
        self.cp("act", Vm[:], v4(psV), [], [psV, Vm])
        self.cp("dve", VT[:], v4(psVT), [], [psVT, VT])
        self.tt("pool", IpVT[:], VT[:], Ib, ALU.add, [VT, self.identb], [IpVT])
        yield
        ps2 = self.bank()
        for hh in range(4):
            self.mm(ps2, ps2[:, hh * 128:(hh + 1) * 128], Vm[:, hh, :], VT[:, hh, :], True, True, [Vm, VT])
        IpV2T = W["PT1"]
        self.tt("dve", IpV2T[:], v4(ps2), Ib, ALU.add, [self.identb], [ps2, IpV2T])
        yield
        psY = self.bank()
        for hh in range(4):
            self.mm(psY, psY[:, hh * 128:(hh + 1) * 128], IpV2T[:, hh, :], X[:, hh, :], True, True, [IpV2T, X])
        Y = W["MdT"]
        self.cp(self.ev(), Y[:], v4(psY), [], [psY, Y])
        yield
        psT = self.bank()
        for hh in range(4):
            self.mm(psT, psT[:, hh * 128:(hh + 1) * 128], IpVT[:, hh, :], Y[:, hh, :], True, True, [IpVT, Y])
        self.cp(self.ev(), Tt[:], v4(psT), [], [psT, Tt])
        yield

    @staticmethod
    def run_rr(gens):
        gens = list(gens)
        while gens:
            for g in list(gens):
                try:
                    next(g)
                except StopIteration:
                    gens.remove(g)

    def inv_bufs(self, alloc, tag):
        return {nm: alloc([128, 4, 128], BF16, "%s_%s" % (tag, nm))
                for nm in ("MdT", "Md", "No", "X0", "X1", "P0", "P1", "PT0", "PT1", "IpPT0", "IpPT1", "XT")}

    def norm_T(self, xt, nwbc, xnT, xn_keep=None, bank=None):
        P = self.P
        ss, xn = self.nb["ss"], (xn_keep if xn_keep is not None else self.nb["xn"])
        self.act(xn[:], xt[:], AF.Square, [xt], [xn, ss], accum_out=ss[:])
        self.ts("dve", ss[:], ss[:], 1.0 / 1024, 1e-6, ALU.mult, ALU.add, [ss], [ss])
        self.act(ss[:], ss[:], AF.Ln, [ss], [ss])
        self.act(ss[:], ss[:], AF.Exp, [ss], [ss], scale=-0.5)
        self.stt(xn[:], xt[:], ss[:, 0:1], nwbc, ALU.mult, ALU.mult, [xt, ss, "normw"], [xn])
        for half in range(2):
            ps = bank() if bank is not None else self.bank()
            for c in range(4):
                cc = half * 4 + c
                self.tr(ps, ps[:, c * 128:(c + 1) * 128], xn[:, cc * 128:(cc + 1) * 128], self.ident32[:], [xn, self.ident32])
            self.cp(self.ev(), xnT[:, half * 4:half * 4 + 4, :], ps[:].rearrange("p (c f) -> p c f", c=4), [], [ps, xnT])

    def build(self):
        P = self.P
        nc = self.nc
        sb = P.sb
        din, dout = self.din, self.dout
        xin = din("xin", [NT, 128, 1024])
        sgdn = din("sgdn", [16, 16, 128, 128])
        sconv = din("sconv", [48, 4096])
        srwkv = din("srwkv", [16, 16, 64, 64])
        sshift = din("sshift", [16, 1024])
        self.d_srwkv, self.d_sshift = srwkv, sshift
        norm_w = din("norm_w", [2, 1024])
        fnorm_w = din("fnorm_w", [1, 1024])
        self.d_norm_w, self.d_fnorm_w = norm_w, fnorm_w
        w_in = din("w_in", [1024, 6176])
        conv_w = din("conv_w", [4096, 4])
        a_log = din("a_log", [1, 16])
        dt_bias = din("dt_bias", [1, 16])
        gnorm_w = din("gnorm_w", [1, 128])
        w_out = din("w_out", [2048, 1024])
        y = dout("y", [NT, 128, 1024])
        p_gdn = dout("p_gdn", [16, 128, 128])
        p_conv = dout("p_conv", [3, 4096])
        s_gdn = dout("s_gdn", [16, 16, 128, 128])
        s_conv = dout("s_conv", [48, 4096])
        oscr = nc.dram_tensor("oscr", [NT, 128, 2048], F32, kind="Internal").ap()
        x1scr = nc.dram_tensor("x1scr", [NT, 128, 1024], F32, kind="Internal").ap()

        self.pf = [P.ps([128, 512], F32, "pf%d" % i) for i in range(5)]
        self.pfx = P.ps([128, 512], F32, "pfx")
        self.pb = [P.ps([128, 1024], BF16, "pb%d" % i) for i in range(2)]
        self.consts()

        def pbc(ap_row, n):
            return bass.AP(ap_row.tensor, ap_row.offset, [[0, 128], [1, n]])

        nw0 = sb([128, 1024], F32, "normw")
        self.nw0 = nw0
        self.ld("sp", nw0[:], pbc(norm_w[0:1, :], 1024), [nw0], "ld_c")
        dtb = sb([128, 16], F32, "dtb")
        nea = sb([128, 16], F32, "nea")
        self.ld("sp", dtb[:], pbc(dt_bias, 16), [dtb], "ld_c")
        self.ld("sp", nea[:], pbc(a_log, 16), [nea], "ld_c")
        self.act(nea[:], nea[:], AF.Exp, [nea], [nea])
        self.ts("dve", nea[:], nea[:], -1.0, None, ALU.mult, None, [nea], [nea])
        gnw = sb([128, 128], F32, "gnw")
        self.ld("sp", gnw[:], pbc(gnorm_w, 128), [gnw], "ld_c")
        cw = sb([128, 32, 4], F32, "cw")
        self.ld("sp", cw[:], conv_w.rearrange("(j p) k -> p j k", p=128), [cw], "ld_c")

        self.nb = {"ss": sb([128, 1], F32, "ss"), "xn": sb([128, 1024], F32, "xn")}

        recA = nc.dram_tensor("recA", [NT, 128, 40 * 128], BF16, kind="Internal").ap()
        srecA = nc.dram_tensor("srecA", [NT, 128, 336], F32, kind="Internal").ap()
        es_a = ExitStack()

        def sba(shape, dt=F32, name=None):
            P.ntens += 1
            return es_a.enter_context(nc.sbuf_tensor(name or f"a{P.ntens}", list(shape), dt))

        NQ = 4128
        wA = sba([128, 8, NQ], BF16, "wA")
        for c in range(8):
            src = w_in[c * 128:(c + 1) * 128, :]
            P.dma("pool", lambda e, c=c, src=src: e.dma_start(out=wA[:, c, 0:4096], in_=src[:, 0:4096]), w=[wA], sem="ld_w")
            P.dma("pool", lambda e, c=c, src=src: e.dma_start(out=wA[:, c, 4096:4128], in_=src[:, 6144:6176]), w=[wA], sem="ld_w")
        xt = [sba([128, 1024], F32, "xt%d" % i) for i in range(2)]
        xnTs = [sba([128, 8, 128], BF16, "xnT%d" % i) for i in range(2)]
        halo = sba([128, 32, 3], F32, "halo")
        P.op("pool", lambda e: e.memset(halo[:], 0.0), w=["halo%d" % j for j in range(32)])
        cst_in = sba([48, 512], F32, "cst_in")
        cstT = sba([128, 32, 48], F32, "cstT")
        for jg in range(8):
            self.ld("sp", cst_in[:], sconv[:, jg * 512:(jg + 1) * 512], [cst_in], "ld_c")
            ps = self.bank()
            for jj in range(4):
                self.tr(ps, ps[:, jj * 48:(jj + 1) * 48], cst_in[:, jj * 128:(jj + 1) * 128], self.ident32[0:48, 0:48],
                        [cst_in, self.ident32])
            self.cp(self.ev(), cstT[:, jg * 4:jg * 4 + 4, :], ps[:, 0:192].rearrange("p (j f) -> p j f", j=4), [], [ps, cstT])
        cstN = cstT
        U = [sba([128, 176], BF16, "U%d" % i) for i in range(4)]
        wdiag = sba([128, 128, 128], BF16, "wdiag")
        for q4 in range(4):
            self.tt("pool" if q4 % 2 == 0 else "dve", wdiag[:, q4 * 32:(q4 + 1) * 32, :], V(self.ident32, 0, [[0, 32], [1, 128]]),
                    V(cw, q4 * 32, [[1, 32], [0, 128]]), ALU.mult, [self.ident32, cw], [wdiag])

        c4 = [sba([128, 4, 128], F32, "c4_%d" % i) for i in range(2)]
        sq4 = sba([128, 4, 128], BF16, "sq4")
        self.onesb = sba([128, 128], BF16, "onesb")
        P.op("pool", lambda e: e.memset(self.onesb[:], 1.0), w=[self.onesb])
        rn4 = sba([128, 4, 128], F32, "rn4")
        recs = [sba([128, 40, 128], BF16, "rec%d" % i) for i in range(2)]
        srecs = [sba([128, 336], F32, "srec%d" % i) for i in range(2)]
        scs = [{nm: sba([128, 16], F32, "sc%d_%s" % (i, nm)) for nm in
                ("beta", "g", "gc", "glt", "egc", "negbege", "ekd", "negb", "tmp")} for i in range(2)]
        gms = [sba([128, 16, 16], F32, "gm%d" % i) for i in range(2)]
        egls = [sba([128, 16, 16], F32, "egl%d" % i) for i in range(2)]

        def bcs(t, g):
            return V(t, 4 * g, [[1, 4], [0, 128]])

        def prologue(t):
            X = "p" if t < 17 else "s"
            nseq, T = (1, 128) if X == "p" else (16, 8)
            x = xt[t % 2]
            srec = srecs[t % 2]
            xnT, sc, gm, egl = xnTs[t % 2], scs[t % 2], gms[t % 2], egls[t % 2]
            pbank = lambda: self.pfx
            self.ld("sp", x[:], xin[t], [x], "ld_x%d" % (t % 2))
            self.norm_T(x, nw0[:], xnT, bank=pbank)
            ps = pbank()
            for c in range(8):
                self.mm(ps, ps[:, 0:32], xnT[:, c, :], wA[:, c, 4096:4128], c == 0, c == 7, [xnT, wA])
            self.act(sc["beta"][:], ps[:, 0:16], AF.Sigmoid, [], [ps, sc["beta"]])
            self.tt("dve", sc["tmp"][:], ps[:, 16:32], dtb[:], ALU.add, [dtb], [ps, sc["tmp"]])
            self.act(sc["tmp"][:], sc["tmp"][:], AF.Exp, [sc["tmp"]], [sc["tmp"]])
            self.act(sc["tmp"][:], sc["tmp"][:], AF.Ln, [sc["tmp"]], [sc["tmp"]], bias=1.0)
            self.tt("dve", sc["g"][:], sc["tmp"][:], nea[:], ALU.mult, [sc["tmp"], nea], [sc["g"]])
            ps = pbank()
            self.mm(ps, ps[:, 0:16], self.lowT[X][:], sc["g"][:], True, True, [self.lowT[X], sc["g"]])
            self.mm(ps, ps[:, 16:32], self.same[X][:], sc["g"][:], True, True, [self.same[X], sc["g"]])
            self.cp("dve", sc["gc"][:], ps[:, 0:16], [], [ps, sc["gc"]])
            self.cp("dve", sc["glt"][:], ps[:, 16:32], [], [ps, sc["glt"]])
            self.act(sc["egc"][:], sc["gc"][:], AF.Exp, [sc["gc"]], [sc["egc"]])
            self.stt(sc["negbege"][:], sc["egc"][:], -1.0, sc["beta"][:], ALU.mult, ALU.mult, [sc["egc"], sc["beta"]], [sc["negbege"]])
            self.tt("dve", sc["tmp"][:], sc["glt"][:], sc["gc"][:], ALU.subtract, [sc["glt"], sc["gc"]], [sc["tmp"]])
            self.act(sc["ekd"][:], sc["tmp"][:], AF.Exp, [sc["tmp"]], [sc["ekd"]])
            self.ts("dve", sc["negb"][:], sc["beta"][:], -1.0, None, ALU.mult, None, [sc["beta"]], [sc["negb"]])
            gmv = gm[:, 0:nseq, :]
            self.tt("pool", gmv, V(sc["g"], 0, [[0, nseq], [1, 16]]), V(self.sm, 0, [[1, nseq], [0, 16]]) if X == "s"
                    else V(self.ones32, 0, [[0, 1], [1, 16]]), ALU.mult, [sc["g"], self.sm, self.ones32], [gm])
            ps = pbank()
            self.mm(ps, ps[:, 0:nseq * 16], self.ones32[:], gm[:, 0:nseq, :].rearrange("p s h -> p (s h)"), True, True,
                    [self.ones32, gm])
            self.act(egl[:, 0:nseq, :].rearrange("p s h -> p (s h)"), ps[:, 0:nseq * 16], AF.Exp, [], [ps, egl])

            for i_, nm in enumerate(("gc", "egc", "negbege", "ekd", "negb")):
                self.cp("pool", srec[:, 16 * i_:16 * i_ + 16], sc[nm][:], [sc[nm]], [srec])
            self.cp("pool", srec[:, 80:336], egl[:].rearrange("p s h -> p (s h)"), [egl], [srec])

        for t in range(NT):
            X = "p" if t < 17 else "s"
            nseq, T = (1, 128) if X == "p" else (16, 8)
            if t == 0:
                prologue(0)
            xnT, sc = xnTs[t % 2], scs[t % 2]
            rec = recs[t % 2]
            srec = srecs[t % 2]

            class _RV:
                def __init__(s_, o, name):
                    s_.o, s_.name = o, name

                def __getitem__(s_, idx):
                    a, b, c_ = idx
                    if isinstance(b, slice):
                        b = slice((b.start or 0) + s_.o, (b.stop if b.stop is not None else 0) + s_.o)
                    else:
                        b = b + s_.o
                    return rec[a, b, c_]
            qT, kT, Ktok, vb = _RV(0, rec.name), _RV(8, rec.name), _RV(16, rec.name), _RV(24, rec.name)
            psUs = {}

            def proj(jg):
                psU = self.bank()
                psUs[jg] = psU
                for jj in range(4):
                    j = jg * 4 + jj
                    for c in range(8):
                        self.mm(psU, psU[:, jj * 128:(jj + 1) * 128], wA[:, c, j * 128:(j + 1) * 128], xnT[:, c, :], c == 0, c == 7,
                                [wA, xnT])
            psCs = {}

            def views(j):
                jg, jj = j // 4, j % 4
                u, cc = U[j % 4], c4[jg % 2]
                if X == "p":
                    return u, cc, (lambda k: u[:, k:k + 128])
                u3 = u[:].rearrange("p (s t) -> p s t", t=11)
                return u, cc, (lambda k: u3[:, :, k:k + 8])

            def stA(j):
                jg, jj = j // 4, j % 4
                psU = psUs[jg]
                u, cc, uv = views(j)
                pu = psU[:, jj * 128:(jj + 1) * 128]
                uh, ub, hj = "uh%d" % (j % 4), "ub%d" % (j % 4), "halo%d" % j
                if X == "p":
                    self.cp("pool", u[:, 0:3], halo[:, j, :], [hj], [uh])
                    self.cp("dve", u[:, 3:131], pu, [], [psU, ub])
                    self.cp("dve", halo[:, j, :], psU[:, jj * 128 + 125:jj * 128 + 128], [], [psU, hj])
                else:
                    u3 = u[:].rearrange("p (s t) -> p s t", t=11)
                    pu3 = pu.rearrange("p (s t) -> p s t", t=8)
                    self.cp("pool", u3[:, :, 0:3], cstT[:, j, :].rearrange("p (s k) -> p s k", k=3), [hj, cstT], [uh])
                    self.cp("dve", u3[:, :, 3:11], pu3, [], [psU, ub])
                    self.cp("dve", cstN[:, j, :].rearrange("p (s k) -> p s k", k=3), pu3[:, :, 5:8], [], [psU, hj])

            def stB(j):
                jg, jj = j // 4, j % 4
                if jj == 0:
                    psCs[jg] = self.bank()
                psC = psCs[jg]
                u, cc, uv = views(j)
                for k in range(4):
                    self.mm(psC, psC[:, jj * 128:(jj + 1) * 128], wdiag[:, j * 4 + k, :], uv(k), k == 0, k == 3,
                            [wdiag, "uh%d" % (j % 4), "ub%d" % (j % 4)])

            def stC(j):
                jg, jj = j // 4, j % 4
                u, cc, uv = views(j)
                psC = psCs[jg]
                self.act(cc[:, jj, :], psC[:, jj * 128:(jj + 1) * 128], AF.Silu, [], [psC, cc])
                if jj != 3:
                    return
                if jg < 4:
                    self.tt("pool", sq4[:], cc[:], cc[:], ALU.mult, [cc], [sq4])
                    psN = self.bank()
                    for j2 in range(4):
                        self.mm(psN, psN[:, j2 * 128:(j2 + 1) * 128], self.onesb[:], sq4[:, j2, :], True, True, [self.onesb, sq4])
                    self.act(rn4[:], psN[:].rearrange("p (h f) -> p h f", h=4), AF.Ln, [], [psN, rn4], bias=1e-6)
                    lnsc = float(np.log(128.0 ** -0.5)) if jg < 2 else 0.0
                    self.act(rn4[:], rn4[:], AF.Exp, [rn4], [rn4], scale=-0.5, bias=lnsc)
                    dst = qT if jg < 2 else kT
                    o0 = (jg % 2) * 4
                    self.tt("dve", dst[:, o0:o0 + 4, :], cc[:], rn4[:], ALU.mult, [cc, rn4], [dst])
                else:
                    gv = jg - 4
                    psT = self.bank()
                    for j2 in range(4):
                        self.tr(psT, psT[:, j2 * 128:(j2 + 1) * 128], cc[:, j2, :], self.ident32[:], [cc, self.ident32])
                    self.tt("dve", vb[:, gv * 4:gv * 4 + 4, :], psT[:].rearrange("p (h f) -> p h f", h=4), bcs(sc["beta"], gv),
                            ALU.mult, [sc["beta"]], [psT, vb])

            proj(0)
            for step in range(36):
                if step < 32:
                    if step % 4 == 1 and step // 4 + 1 < 8:
                        proj(step // 4 + 1)
                    stA(step)
                if step == 14 and t + 1 < NT:
                    prologue(t + 1)
                if 0 <= step - 2 < 32:
                    stB(step - 2)
                if 0 <= step - 4 < 32:
                    stC(step - 4)
            pb = self.bbank()
            for kh in range(8):
                self.tr(pb, pb[:, kh * 128:(kh + 1) * 128], kT[:, kh, :], self.identb[:], [kT, self.identb])
            self.cp("act", rec[:, 16:24, :], pb[:].rearrange("p (h f) -> p h f", h=8), [], [pb, rec])
            if t == 16 or t == 17:
                src, ncol, dst = (halo, 3, p_conv) if t == 16 else (cstN, 48, s_conv)
                stg = cst_in
                for jg in range(8):
                    ps = self.bank()
                    for jj in range(4):
                        j = jg * 4 + jj
                        self.tr(ps, ps[0:ncol, jj * 128:(jj + 1) * 128], src[:, j, :], self.ident32[:], [src, self.ident32, "halo%d" % j])
                    self.cp(self.ev(), stg[0:ncol, :], ps[0:ncol, :], [], [ps, stg])
                    self.st("sp", dst[:, jg * 512:(jg + 1) * 512], stg[0:ncol, :], [stg], "st_c")

            self.st("sp", recA[t], rec[:].rearrange("p a f -> p (a f)"), [rec], "st_rec%d" % (t % 2))
            self.st("sp", srecA[t], srec[:], [srec], "st_srec%d" % (t % 2))
        P.barrier()
        P.flush()
        es_a.close()

        es_a = ExitStack()
        rec2 = [sba([128, 40, 128], BF16, "r2ec%d" % i) for i in range(2)]
        srec2 = [sba([128, 336], F32, "s2rec%d" % i) for i in range(2)]
        gcd = [sba([128, 16, 128], F32, "gcd%d" % i) for i in range(2)]
        Sf = sba([128, 16, 128], F32, "Sf")
        Sb = sba([128, 16, 128], BF16, "Sb")
        P.op("pool", lambda e: e.memset(Sf[:], 0.0), w=["Sf%d" % i for i in range(4)])
        P.op("pool", lambda e: e.memset(Sb[:], 0.0), w=["Sb%d" % i for i in range(4)])
        sets = []
        for gi in range(4):
            B = {nm: sba([128, 4, 128], BF16, "g%d_%s" % (gi, nm)) for nm in
                 ("E1", "E1nb", "Mlow", "Tt", "r4", "vn", "vnd", "attn", "attnT")}
            B["o4"] = sba([128, 4, 128], F32, "g%d_o4" % gi)
            B["W"] = self.inv_bufs(sba, "g%d" % gi)
            sets.append(B)
        S0bs = [sba([128, 16, 128], BF16, "S0b%d" % i) for i in range(2)]
        S0f32s = [sba([128, 16, 128], F32, "S0f32_%d" % i) for i in range(2)]
        mks = [sba([128, 16, 128], BF16, "mk%d" % i) for i in range(2)]
        S0f = [sba([128, 4, 128], F32, "S0f%d" % i) for i in range(2)]
        Sn = [sba([128, 4, 128], F32, "Sn%d" % i) for i in range(2)]
        vnds = [sba([128, 128], BF16, "vnds%d" % i) for i in range(2)]
        v4 = lambda ps: ps[:, 0:512].rearrange("p (h f) -> p h f", h=4)

        def gdn_group(t, g, B, rec, srec, gcd_t):
            X = "p" if t < 17 else "s"
            hs = [4 * g + hh for hh in range(4)]
            khs = [h // 2 for h in hs]
            qT = lambda kh: rec[:, kh, :]
            kT = lambda kh: rec[:, 8 + kh, :]
            Kt = lambda kh: rec[:, 16 + kh, :]
            sv_ = lambda off: V(srec, off + 4 * g, [[1, 4], [0, 128]])
            E1, E1nb, Mlow, Tt, r4, vn, vnd, attn, attnT, o4 = (B[k_] for k_ in
                                                               ("E1", "E1nb", "Mlow", "Tt", "r4", "vn", "vnd", "attn", "attnT", "o4"))
            psR = self.bank()
            self.mm(psR, psR[:], self.ones32[:], gcd_t[:, 4 * g:4 * g + 4, :].rearrange("p h f -> p (h f)"), True, False,
                    [self.ones32, gcd_t])
            self.mm(psR, psR[:], self.ident32[:], self.addm[X][:].rearrange("p h f -> p (h f)"), False, True,
                    [self.ident32, self.addm[X]])
            self.tt("dve", v4(psR), v4(psR), sv_(0), ALU.subtract, [srec], [psR])
            self.act(E1[:], v4(psR), AF.Exp, [], [psR, E1], scale=-1.0)
            self.tt("pool", E1nb[:], E1[:], sv_(64), ALU.mult, [E1, srec], [E1nb])
            yield
            psG = self.bank()
            for hh in range(4):
                self.mm(psG, psG[:, hh * 128:(hh + 1) * 128], kT(khs[hh]), kT(khs[hh]), True, True, [rec])
            psQ = self.bank()
            for hh in range(4):
                self.mm(psQ, psQ[:, hh * 128:(hh + 1) * 128], qT(khs[hh]), kT(khs[hh]), True, True, [rec])
            self.tt("dve", Mlow[:], v4(psG), E1nb[:], ALU.mult, [E1nb], [psG, Mlow])
            self.tt("dve", attn[:], v4(psQ), E1[:], ALU.mult, [E1], [psQ, attn])
            yield

            def extra():
                pb = self.bbank()
                for hh in range(4):
                    self.tr(pb, pb[:, hh * 128:(hh + 1) * 128], attn[:, hh, :], self.identb[:], [attn, self.identb])
                self.cp("act", attnT[:], v4(pb), [], [pb, attnT])
            yield from self.inverse_gen(B["W"], Mlow, Tt, extra)
            S0b, S0f32, mk = S0bs[g % 2], S0f32s[g % 2], mks[g % 2]
            psK = self.bank()
            if X == "p":
                for hh in range(4):
                    self.mm(psK, psK[:, hh * 128:(hh + 1) * 128], kT(khs[hh]), Sb[:, hs[hh], :], True, True, [rec, "Sb%d" % g])
            else:
                psA = self.bank()
                for hh in range(4):
                    h = hs[hh]
                    self.ld("sp", S0f32[:], sgdn[:, h].rearrange("s p v -> p s v"), [S0f32], "ld_s0b")
                    self.cp("act", S0b[:], S0f32[:], [S0f32], [S0b])
                    for (srcf, psd) in ((kT, psK), (qT, psA)):
                        a_ = srcf(khs[hh])
                        self.tt("pool", mk[:], bass.AP(a_.tensor, a_.offset, [list(a_.ap[0]), [0, 16], [1, 128]]), self.cm[:], ALU.mult,
                                [rec, self.cm], [mk])
                        for s_ in range(16):
                            self.mm(psd, psd[:, hh * 128:(hh + 1) * 128], mk[:, s_, :], S0b[:, s_, :], s_ == 0, s_ == 15, [mk, S0b])
                self.tt("dve", o4[:], v4(psA), sv_(16), ALU.mult, [srec], [psA, o4])
            self.tt("dve", v4(psK), v4(psK), sv_(32), ALU.mult, [srec], [psK])
            self.tt("dve", r4[:], v4(psK), rec[:, 24 + 4 * g:24 + 4 * g + 4, :], ALU.add, [rec], [psK, r4])
            yield
            psV = self.bank()
            for hh in range(4):
                self.mm(psV, psV[:, hh * 128:(hh + 1) * 128], Tt[:, hh, :], r4[:, hh, :], True, True, [Tt, r4])
            self.cp("act", vn[:], v4(psV), [], [psV, vn])
            self.tt("dve", vnd[:], v4(psV), sv_(48), ALU.mult, [srec], [psV, vnd])
            yield
            psB = self.bank()
            for hh in range(4):
                self.mm(psB, psB[:, hh * 128:(hh + 1) * 128], attnT[:, hh, :], vn[:, hh, :], True, True, [attnT, vn])
            if X == "p":
                psA = self.bank()
                for hh in range(4):
                    self.mm(psA, psA[:, hh * 128:(hh + 1) * 128], qT(khs[hh]), Sb[:, hs[hh], :], True, True, [rec, "Sb%d" % g])
                self.tt("dve", o4[:], v4(psA), sv_(16), ALU.mult, [srec], [psA, o4])
            self.tt("dve", o4[:], v4(psB), o4[:], ALU.add, [o4], [psB, o4])
            self.st("sp", oscr[t, :, g * 512:(g + 1) * 512], o4[:].rearrange("p h f -> p (h f)"), [o4], "st_o%d" % g)
            if X == "p":
                psS = self.bank()
                for hh in range(4):
                    self.mm(psS, psS[:, hh * 128:(hh + 1) * 128], Kt(khs[hh]), vnd[:, hh, :], True, True, [rec, vnd])
                sv = Sf[:, 4 * g:4 * g + 4, :]
                SfN, SbN = "Sf%d" % g, "Sb%d" % g
                self.tt("pool", sv, sv, V(srec, 80 + 4 * g, [[1, 4], [0, 128]]), ALU.mult, [srec, SfN], [SfN])
                self.tt("dve", sv, v4(psS), sv, ALU.add, [SfN], [psS, SfN])
                self.cp("act", Sb[:, 4 * g:4 * g + 4, :], sv, [SfN], [SbN])
                if t == 16:
                    self.st("sp", p_gdn[4 * g:4 * g + 4].rearrange("h p v -> p h v"), sv, [SfN], "st_pg")
            else:
                for hh in range(4):
                    h = hs[hh]
                    for sg in range(4):
                        i2 = (hh * 4 + sg) % 2
                        s0f = S0f[i2]
                        self.ld("sp", s0f[:], sgdn[sg * 4:sg * 4 + 4, h].rearrange("s p v -> p s v"), [s0f], "ld_s0f%d" % i2)
                        psS = self.bank()
                        for si in range(4):
                            s_ = sg * 4 + si
                            vs = vnds[s_ % 2]
                            self.act(vs[:], vnd[:, hh, :], AF.Copy, [vnd, self.sm], [vs], scale=self.sm[:, s_:s_ + 1])
                            self.mm(psS, psS[:, si * 128:(si + 1) * 128], Kt(khs[hh]), vs[:], True, True, [rec, vs])
                        sn = Sn[i2]
                        self.tt("pool", sn[:], s0f[:], V(srec, 80 + sg * 4 * 16 + h, [[16, 4], [0, 128]]), ALU.mult, [s0f, srec], [sn])
                        self.tt("dve", sn[:], v4(psS), sn[:], ALU.add, [sn], [psS, sn])
                        self.st("sp", s_gdn[sg * 4:sg * 4 + 4, h].rearrange("s p v -> p s v"), sn[:], [sn], "st_sn%d" % i2)
            yield

        for t in range(NT):
            rec, srec, gcd_t = rec2[t % 2], srec2[t % 2], gcd[t % 2]
            self.ld("sp", rec[:].rearrange("p a f -> p (a f)"), recA[t], [rec], "ld_rec%d" % (t % 2))
            self.ld("sp", srec[:], srecA[t], [srec], "ld_srec%d" % (t % 2))
            self.tt("pool", gcd_t[:], V(self.ident32, 0, [[0, 16], [1, 128]]), V(srec, 0, [[1, 16], [0, 128]]), ALU.mult,
                    [self.ident32, srec], [gcd_t])
            self.run_rr([gdn_group(t, g, sets[g], rec, srec, gcd_t) for g in range(4)])
        P.barrier()
        P.flush()
        es_a.close()

        self.pm = {(nm, X_): P.sb([128, 128], BF16, nm + X_) for nm in ("mup", "msup", "mneg") for X_ in ("p", "s")}
        self.es_w1 = ExitStack()
        P.ntens += 1
        self.wR = self.es_w1.enter_context(nc.sbuf_tensor("wR", [128, 8, 4096], BF16))
        self.w1a1 = self.es_w1.enter_context(nc.sbuf_tensor("w1a1", [128, 8, 128], BF16))
        self.w2a2 = self.es_w1.enter_context(nc.sbuf_tensor("w2a2", [64, 2, 1024], BF16))
        self.d_rkvz = din("rkvz", [4, 1024, 1024])
        self.d_w1 = din("w1", [1024, 64]); self.d_w2 = din("w2", [64, 1024])
        self.d_a1 = din("a1", [1024, 64]); self.d_a2 = din("a2", [64, 1024])

        es_b = ExitStack()

        def sbb(shape, dt=F32, name=None):
            P.ntens += 1
            return es_b.enter_context(nc.sbuf_tensor(name or f"b{P.ntens}", list(shape), dt))

        wZ = sbb([128, 8, 2048], BF16, "wZ")
        wO = sbb([128, 16, 1024], BF16, "wO")
        for c in range(8):
            P.dma("pool", lambda e, c=c: e.dma_start(out=wZ[:, c, :], in_=w_in[c * 128:(c + 1) * 128, 4096:6144]), w=[wZ], sem="ld_w")
        for c in range(16):
            P.dma("pool", lambda e, c=c: e.dma_start(out=wO[:, c, :], in_=w_out[c * 128:(c + 1) * 128, :]), w=[wO], sem="ld_w")
        wR_, w1a1_, w2a2_ = self.wR, self.w1a1, self.w2a2
        for i in range(4):
            for c in range(8):
                P.dma("pool", lambda e, i=i, c=c: e.dma_start(out=wR_[:, c, i * 1024:(i + 1) * 1024],
                                                              in_=self.d_rkvz[i, c * 128:(c + 1) * 128, :]), w=[wR_], sem="x")
        for c in range(8):
            P.dma("pool", lambda e, c=c: e.dma_start(out=w1a1_[:, c, 0:64], in_=self.d_w1[c * 128:(c + 1) * 128, :]), w=[w1a1_], sem="x")
            P.dma("pool", lambda e, c=c: e.dma_start(out=w1a1_[:, c, 64:128], in_=self.d_a1[c * 128:(c + 1) * 128, :]), w=[w1a1_], sem="x")
        P.dma("pool", lambda e: e.dma_start(out=w2a2_[:, 0, :], in_=self.d_w2), w=[w2a2_], sem="x")
        P.dma("pool", lambda e: e.dma_start(out=w2a2_[:, 1, :], in_=self.d_a2), w=[w2a2_], sem="x")
        xtb = [sbb([128, 1024], F32, "xtb%d" % i) for i in range(2)]
        xnTb = sbb([128, 8, 128], BF16, "xnTb")
        ot = [sbb([128, 16, 128], F32, "ot%d" % i) for i in range(2)]
        osq = None
        orn = sbb([128, 16], F32, "orn")
        zs = sbb([128, 4, 128], F32, "zs")
        og = sbb([128, 16, 128], F32, "og")
        osq = og
        ogT = sbb([128, 16, 128], BF16, "ogT")
        x1 = [sbb([128, 1024], F32, "x1_%d" % i) for i in range(2)]
        for t in range(NT):
            x = xtb[t % 2]
            o = ot[t % 2]
            self.ld("sp", x[:], xin[t], [x], "ldb_x%d" % (t % 2))
            self.ld("sp", o[:].rearrange("p h f -> p (h f)"), oscr[t], [o], "ldb_o%d" % (t % 2))
            self.norm_T(x, nw0[:], xnTb)
            self.act(osq[:], o[:], AF.Square, [o], [osq])
            P.op("dve", lambda e: e.tensor_reduce(out=orn[:], in_=osq[:], axis=AX.X, op=ALU.add), r=[osq], w=[orn])
            self.ts("dve", orn[:], orn[:], 1.0 / 128, 1e-6, ALU.mult, ALU.add, [orn], [orn])
            self.act(orn[:], orn[:], AF.Ln, [orn], [orn])
            self.act(orn[:], orn[:], AF.Exp, [orn], [orn], scale=-0.5)
            self.tt("dve", og[:], o[:], V(orn, 0, [[1, 16], [0, 128]]), ALU.mult, [o, orn], [og])
            self.tt("pool", og[:], og[:], V(gnw, 0, [[0, 16], [1, 128]]), ALU.mult, [og, gnw], [og])
            for g in range(4):
                psZ = self.bank()
                for c in range(8):
                    self.mm(psZ, psZ[:], xnTb[:, c, :], wZ[:, c, g * 512:(g + 1) * 512], c == 0, c == 7, [xnTb, wZ])
                self.act(zs[:], psZ[:].rearrange("p (h f) -> p h f", h=4), AF.Silu, [], [psZ, zs])
                self.tt("dve", og[:, 4 * g:4 * g + 4, :], og[:, 4 * g:4 * g + 4, :], zs[:], ALU.mult, [zs, og], [og])
            for g in range(4):
                ps = self.bank()
                for hh in range(4):
                    self.tr(ps, ps[:, hh * 128:(hh + 1) * 128], og[:, 4 * g + hh, :], self.ident32[:], [og, self.ident32])
                self.cp(self.ev(), ogT[:, 4 * g:4 * g + 4, :], ps[:].rearrange("p (h f) -> p h f", h=4), [], [ps, ogT])
            xo = x1[t % 2]
            for half in range(2):
                ps = self.bank()
                for c in range(16):
                    self.mm(ps, ps[:], ogT[:, c, :], wO[:, c, half * 512:(half + 1) * 512], c == 0, c == 15, [ogT, wO])
                self.tt("dve", xo[:, half * 512:(half + 1) * 512], ps[:], x[:, half * 512:(half + 1) * 512], ALU.add, [x], [ps, xo])
            self.st("sp", x1scr[t], xo[:], [xo], "stb_x%d" % (t % 2))
        P.barrier()
        P.flush()
        es_b.close()

        self.layer1(x1scr, y, pbc)

    def layer1(self, x1scr, y, pbc):
        P = self.P
        nc = self.nc
        din, dout = self.din, self.dout
        srwkv = self.d_srwkv
        sshift = self.d_sshift
        mu_d = din("mu", [6, 1024])
        w0_d = din("w0", [1, 1024])
        a0_d = din("a0", [1, 1024])
        kk_d = din("k_k", [1, 1024]); ka_d = din("k_a", [1, 1024]); rk_d = din("r_k", [1, 1024])
        lw_d = din("lnx_w", [1, 1024]); lb_d = din("lnx_b", [1, 1024]); wo_d = din("w_o", [1024, 1024])
        p_rwkv = dout("p_rwkv", [16, 64, 64]); p_shift = dout("p_shift", [1, 1024])
        s_rwkv = dout("s_rwkv", [16, 16, 64, 64]); s_shift = dout("s_shift", [16, 1024])
        pm = self.pm
        es = ExitStack()
        cur = [es]

        def sb(shape, dt=F32, name=None):
            P.ntens += 1
            return cur[0].enter_context(nc.sbuf_tensor(name or f"c{P.ntens}", list(shape), dt))
        o32 = self.ones32
        wR, w1a1, w2a2 = self.wR, self.w1a1, self.w2a2
        xx = sb([128, 1024], F32, "r_xx")
        pst = xx
        self.ld("sp", pst[0:6, :], mu_d, [pst], "ld_c")
        for i, d in enumerate((w0_d, a0_d, kk_d, ka_d, rk_d)):
            self.ld("sp", pst[6 + i:7 + i, :], d, [pst], "ld_c")
        par = sb([128, 8, 16], F32, "par")
        ps = self.bank()
        for c in range(8):
            self.tr(ps, ps[:, c * 16:c * 16 + 11], pst[0:11, c * 128:(c + 1) * 128], self.ident32[0:11, 0:11], [pst, self.ident32])
        self.cp("dve", par[:, :, 0:11], ps[:, 0:128].rearrange("p (c i) -> p c i", i=16)[:, :, 0:11], [], [ps, par])
        self.ts("dve", par[:, :, 11:12], par[:, :, 6:7], -1.0, None, ALU.mult, None, [par], [par])
        self.ts("dve", par[:, :, 12:13], par[:, :, 9:10], -1.0, 1.0, ALU.mult, ALU.add, [par], [par])
        self.ts("dve", par[:, :, 13:14], par[:, :, 7:8], 0.5, None, ALU.mult, None, [par], [par])
        nw1 = self.nw0
        self.ld("sp", nw1[:], pbc(self.d_norm_w[1:2, :], 1024), [nw1], "ld_c")
        def sel(out, in_, pattern, op, base, cm, r, w, fill=0.0):
            P.op("pool", lambda e: e.affine_select(out=out, in_=in_, pattern=pattern, compare_op=op, fill=fill,
                                                    base=base, channel_multiplier=cm), r=r, w=w)
        shp = sb([128, 128], F32, "shp")
        sel(shp[:], o32[:], [[1, 128]], ALU.is_equal, -1, -1, [o32], [shp])
        nb0 = sb([128, 128], F32, "nb0")
        sel(nb0[:].rearrange("p (s t) -> p s t", t=8), o32[:].rearrange("p (s t) -> p s t", t=8), [[0, 16], [1, 8]], ALU.is_gt, 0, 0,
            [o32], [nb0])
        shs = sb([128, 128], F32, "shs")
        self.tt("pool", shs[:], shp[:], nb0[:], ALU.mult, [shp, nb0], [shs])
        elast = sb([128, 128], F32, "elast")
        sel(elast[:], o32[:], [[-1, 128]], ALU.is_equal, -127, 1, [o32], [elast])
        selS = sb([16, 128], F32, "selS")
        sel(selS[:], o32[0:16, :], [[1, 128]], ALU.is_equal, 0, -8, [o32], [selS])
        b64 = sb([128, 128], F32, "b64")
        v3 = b64[:].rearrange("p (s t) -> p s t", t=64)
        sel(v3, o32[:].rearrange("p (s t) -> p s t", t=64), [[-64, 2], [0, 64]], ALU.is_ge, 0, 1, [o32], [b64])
        sel(v3, v3, [[64, 2], [0, 64]], ALU.is_ge, 63, -1, [b64], [b64])
        mneg, msup, mup = {}, {}, {}
        t_tmpm = sb([128, 128], F32, "tmpm")
        for X in ("p", "s"):
            lt = self.lowT[X]
            mup[X] = pm[("mup", X)]
            self.cp("pool", mup[X][:], lt[:], [lt], [mup[X]])
            msup[X] = pm[("msup", X)]
            sel(msup[X][:], lt[:], [[1, 128]], ALU.is_gt, 0, -1, [lt], [msup[X]])
            mneg[X] = pm[("mneg", X)]
            tmpm = t_tmpm
            self.ts("pool", tmpm[:], self.addm[X][:, 0, :], -1.0 / BIG, 1.0, ALU.mult, ALU.add, [self.addm[X]], [tmpm])
            sel(tmpm[:], tmpm[:], [[-1, 128]], ALU.is_gt, 0, 1, [tmpm], [tmpm])
            self.ts("pool", mneg[X][:], tmpm[:], -1.0, None, ALU.mult, None, [tmpm], [mneg[X]])
        hsel = sb([128, 2], F32, "hsel")
        self.cp("pool", hsel[:, 0:1], b64[:, 0:1], [b64], [hsel])
        self.cp("pool", hsel[:, 1:2], b64[:, 127:128], [b64], [hsel])
        self.invw = {}
        for nm in ("MdT", "Md", "No", "X0", "X1", "P0", "P1", "PT0", "PT1", "IpPT0", "IpPT1", "XT"):
            self.invw[nm] = sb([128, 4, 128], BF16, "jw_" + nm)
        for a_, b_ in (("V", "P0"), ("VT", "P1"), ("IpVT", "PT0"), ("IpV2T", "PT1"), ("Y", "MdT")):
            self.invw[a_] = self.invw[b_]
        xn = [sb([128, 1024], F32, "r_xn%d" % i) for i in range(2)]
        P.op("pool", lambda e: e.memset(xn[1][:], 0.0), w=[xn[1]])
        x1t = [sb([128, 1024], F32, "r_x1")] * 2
        xnT = sb([128, 8, 128], BF16, "r_xnT")
        xxT = sb([128, 8, 128], BF16, "r_xxT")
        xs_all = sb([128, 4, 8, 128], BF16, "r_xs")

        class _XS:
            def __init__(s_, i):
                s_.i = i
                s_.name = "r_xs"

            def __getitem__(s_, idx):
                return xs_all[idx[0], s_.i, idx[1], idx[2]]
        xs = [_XS(i) for i in range(4)]
        class _AL:
            def __init__(s_, apf, name):
                s_.apf = apf
                s_.name = name

            def __getitem__(s_, idx):
                return s_.apf()[idx]

        hT = sb([64, 2, 128], BF16, "r_hT")
        fm = {nm: sb([128, 8, 128], BF16, "r_" + nm) for nm in ("kap", "rho", "kt", "bt")}
        kdj = sb([128, 128], BF16, "r_kdj")
        bdj = sb([128, 128], BF16, "r_bdj")
        vT = sb([128, 8, 128], BF16, "r_vT")
        Vb = sb([128, 1024], BF16, "r_Vb")
        kdT = sb([128, 8, 128], BF16, "r_kdT")
        bdT = sb([128, 8, 128], BF16, "r_bdT")
        Pc = sb([128, 8, 16], F32, "r_Pc")
        t_all = []
        for i_ in range(2):
            t_ = {nm: sb([128, 128], F32, "r_t%d_%s" % (i_, nm)) for nm in ("e", "ew", "a", "kk", "sq", "rn", "k2", "b", "cs", "x", "r", "k")}
            t_["dd"] = t_["sq"]
            t_["rk"] = t_["rn"]
            t_all.append(t_)
        kdjs = [kdj, sb([128, 128], BF16, "r_kdj2")]
        bdjs = [bdj, sb([128, 128], BF16, "r_bdj2")]
        Z = sb([128, 8, 64], F32, "r_Z")
        Zb = sb([128, 8, 64], BF16, "r_Zb")
        P.op("pool", lambda e: e.memset(Z[:], 0.0), w=[Z])
        P.op("pool", lambda e: e.memset(Zb[:], 0.0), w=[Zb])
        Mlow = sb([128, 4, 128], BF16, "r_Mlow")
        Tt = sb([128, 4, 128], BF16, "r_Tt")
        MkT = sb([128, 4, 128], BF16, "r_MkT")
        AkT = sb([128, 4, 128], BF16, "r_AkT")
        AbT = sb([128, 4, 128], BF16, "r_AbT")
        rhsS = sb([128, 4, 64], BF16, "r_rhsS")
        SA = sb([128, 4, 64], BF16, "r_SA")
        ytok = sb([128, 16, 64], F32, "r_ytok")
        ysq = xx
        st = {nm: sb([128, 16], F32, "r_st_" + nm) for nm in ("sum", "ssq", "mean", "var", "rstd", "rkb")}
        zs = self.nb["xn"]
        rec1 = nc.dram_tensor("rwrec1", [NT, 128, 56 * 128], BF16, kind="Internal").ap()
        frec1 = nc.dram_tensor("rwfrec1", [NT, 128, 1168], F32, kind="Internal").ap()

        x2 = zs
        ss2 = sb([128, 1], F32, "r_ss2")
        ygT = _AL(lambda: xs_all[:, 3], "r_xs")
        s0in = sb([64, 2, 64], F32, "r_s0in")
        Z0b2 = [_AL(lambda q=q: xs_all[:, 2 + q].rearrange("p c (a f) -> p (c a) f", a=2), "r_xs") for q in range(2)]
        mk = _AL(lambda: xs_all[:, 0:2].rearrange("p a c f -> p (a c) f"), "r_xs")
        Vm = sb([128, 128], BF16, "r_Vm")
        yz0 = sb([128, 4, 64], F32, "r_yz0")
        mk2 = _AL(lambda: xs_all[:, 0:2].rearrange("p a c f -> p (a c) f"), "r_xs")
        SAm = sb([128, 128], BF16, "r_SAm")
        Zn = sb([128, 64], F32, "r_Zn")
        Sout = sb([64, 128], F32, "r_Sout")

        def fmh(tn, h):
            b0 = (h % 2) * 64
            return tn[b0:b0 + 64, h // 2, :]

        for t in range(NT):
            X = "p" if t < 17 else "s"
            nseq, T = (1, 128) if X == "p" else (16, 8)
            x1 = x1t[t % 2]
            xc, xp = xn[t % 2], xn[(t + 1) % 2]
            self.ld("sp", x1[:], x1scr[t], [x1], "ld1_x")
            self.act(xc[:], x1[:], AF.Square, [x1], [xc, ss2], accum_out=ss2[:])
            self.ts("dve", ss2[:], ss2[:], 1.0 / 1024, 1e-6, ALU.mult, ALU.add, [ss2], [ss2])
            self.act(ss2[:], ss2[:], AF.Ln, [ss2], [ss2])
            self.act(ss2[:], ss2[:], AF.Exp, [ss2], [ss2], scale=-0.5)
            self.stt(xc[:], x1[:], ss2[:, 0:1], nw1[:], ALU.mult, ALU.mult, [x1, ss2, nw1], [xc])
            if t == 16:
                self.st("sp", p_shift, xc[127:128, :], [xc], "st_c")
            if t == 17:
                f = fsz(xc)
                self.st("sp", s_shift, bass.AP(xc, 7 * f, [[8 * f, 16], [1, 1024]]), [xc], "st_c")
            for half in range(2):
                ps = self.bank()
                cs_ = slice(half * 512, (half + 1) * 512)
                if X == "p":
                    self.mm(ps, ps[:], shp[:], xc[:, cs_], True, False, [shp, xc])
                    self.mm(ps, ps[:], elast[:], xp[:, cs_], False, True, [elast, xp])
                else:
                    if half == 0:
                        self.ld("sp", xp[0:16, :], sshift, [xp], "ld_c")
                    self.mm(ps, ps[:], shs[:], xc[:, cs_], True, False, [shs, xc])
                    self.mm(ps, ps[:], selS[:], xp[0:16, cs_], False, True, [selS, xp])
                self.tt("dve", xx[:, cs_], ps[:], xc[:, cs_], ALU.subtract, [xc], [ps, xx])
            for (src, dst) in ((xc, xnT), (xx, xxT)):
                for half in range(2):
                    ps = self.bank()
                    for c in range(4):
                        cc = half * 4 + c
                        self.tr(ps, ps[:, c * 128:(c + 1) * 128], src[:, cc * 128:(cc + 1) * 128], self.ident32[:], [src, self.ident32])
                    self.cp(self.ev(), dst[:, half * 4:half * 4 + 4, :], ps[:].rearrange("p (c f) -> p c f", c=4), [], [ps, dst])
            def mkxs(i, dst):
                for c in range(8):
                    self.stt(dst[:, c, :], xxT[:, c, :], par[:, c, i:i + 1], xnT[:, c, :], ALU.mult, ALU.add, [xxT, xnT, par], [dst])
            for i in range(3):
                mkxs(i, xs[i])
            ps = self.bank()
            mkxs(4, xs[3])
            for c in range(8):
                self.mm(ps, ps[0:64, 0:128], w1a1[:, c, 0:64], xs[3][:, c, :], c == 0, c == 7, [w1a1, xs[3]])
            mkxs(5, xs[3])
            for c in range(8):
                self.mm(ps, ps[0:64, 128:256], w1a1[:, c, 64:128], xs[3][:, c, :], c == 0, c == 7, [w1a1, xs[3]])
            self.act(hT[:, 0, :], ps[0:64, 0:128], AF.Tanh, [], [ps, hT])
            self.cp("dve", hT[:, 1, :], ps[0:64, 128:256], [], [ps, hT])
            mkxs(3, xs[3])
            for half in range(2):
                ps = self.bank()
                for c in range(8):
                    self.mm(ps, ps[:], xs[3][:, c, :], wR[:, c, 3072 + half * 512:3072 + (half + 1) * 512], c == 0, c == 7, [xs[3], wR])
                self.act(zs[:, half * 512:(half + 1) * 512], ps[:], AF.Silu, [], [ps, zs])
            psRK = self.pfx
            pjb = {}

            def proj1(j):
                psA_ = self.bank()
                psB_ = self.bank()
                pjb[j] = (psA_, psB_)
                for i in range(3):
                    for c in range(8):
                        self.mm(psA_, psA_[:, i * 128:(i + 1) * 128], wR[:, c, i * 1024 + j * 128:i * 1024 + (j + 1) * 128], xs[i][:, c, :],
                                c == 0, c == 7, [wR, xs[i]])
                self.mm(psA_, psA_[:, 384:512], w2a2[:, 0, j * 128:(j + 1) * 128], hT[:, 0, :], True, True, [w2a2, hT])
                self.mm(psB_, psB_[:, 0:128], w2a2[:, 1, j * 128:(j + 1) * 128], hT[:, 1, :], True, True, [w2a2, hT])
            def elem(j):
                yield
                psA_, psB_ = pjb[j]
                yield
                t_ = t_all[j % 2]
                yield
                kdj, bdj = kdjs[j % 2], bdjs[j % 2]
                yield
                pj = lambda i: par[:, j, i:i + 1]
                yield
                yield
                self.act(t_["e"][:], psA_[:, 384:512], AF.Exp, [par], [psA_, t_["e"]], scale=-1.0, bias=pj(11))
                yield
                self.act(t_["e"][:], t_["e"][:], AF.Ln, [t_["e"]], [t_["e"]], bias=1.0)
                yield
                self.act(t_["ew"][:], t_["e"][:], AF.Exp, [t_["e"]], [t_["ew"]], scale=-1.0, bias=-0.5)
                yield
                self.act(t_["a"][:], psB_[:, 0:128], AF.Tanh, [par], [psB_, t_["a"]], bias=pj(13), scale=0.5)
                yield
                self.ts("dve", t_["a"][:], t_["a"][:], 0.5, 0.5, ALU.mult, ALU.add, [t_["a"]], [t_["a"]])
                yield
                self.cp("act", t_["r"][:], psA_[:, 0:128], [], [psA_, t_["r"]])
                yield
                self.cp("dve", t_["k"][:], psA_[:, 128:256], [], [psA_, t_["k"]])
                yield
                self.cp("act", vT[:, j, :], psA_[:, 256:384], [], [psA_, vT])
                yield
                self.ts("dve", t_["kk"][:], t_["k"][:], pj(8), None, ALU.mult, None, [t_["k"], par], [t_["kk"]])
                yield
                self.act(t_["sq"][:], t_["kk"][:], AF.Square, [t_["kk"]], [t_["sq"]])
                yield
                psn = self.bank()
                yield
                self.mm(psn, psn[:, 0:128], b64[:], t_["sq"][:], True, True, [b64, t_["sq"]])
                yield
                self.act(t_["rn"][:], psn[:, 0:128], AF.Ln, [], [psn, t_["rn"]], bias=1e-6)
                yield "ps_done"
                self.act(t_["rn"][:], t_["rn"][:], AF.Exp, [t_["rn"]], [t_["rn"]], scale=-0.5)
                yield
                self.tt("dve", t_["kk"][:], t_["kk"][:], t_["rn"][:], ALU.mult, [t_["kk"], t_["rn"]], [t_["kk"]])
                yield
                self.ts("dve", t_["x"][:], t_["a"][:], pj(9), pj(12), ALU.mult, ALU.add, [t_["a"], par], [t_["x"]])
                yield
                self.tt("dve", t_["k2"][:], t_["k"][:], t_["x"][:], ALU.mult, [t_["k"], t_["x"]], [t_["k2"]])
                yield
                self.tt("pool", t_["b"][:], t_["kk"][:], t_["a"][:], ALU.mult, [t_["kk"], t_["a"]], [t_["b"]])
                yield
                yield
                self.stt(t_["rk"][:], t_["r"][:], pj(10), t_["k2"][:], ALU.mult, ALU.mult, [t_["r"], t_["k2"], par], [t_["rk"]])
                yield
                self.mm(psRK, psRK[:, 2 * j:2 * j + 2], t_["rk"][:], hsel[:], True, True, [t_["rk"], hsel])
                yield
                yield
                msk = o32 if X == "p" else nb0
                yield
                P.op("dve", lambda e, msk=msk, t_=t_: e.tensor_tensor_scan(out=t_["cs"][:], data0=msk[:], data1=t_["ew"][:], initial=0.0,
                                                                    op0=ALU.mult, op1=ALU.add), r=[msk, t_["ew"]], w=[t_["cs"]])
                yield
                cs = t_["cs"]
                yield
                yield
                self.act(Pc[:, j, 0:nseq], V(cs, T - 1, [[T, nseq]]), AF.Exp, [cs], [Pc], scale=-1.0)
                yield
                yield
                self.tt("dve", t_["dd"][:].rearrange("p (s t) -> p s t", t=T), cs[:].rearrange("p (s t) -> p s t", t=T),
                        V(cs, T - 1, [[T, nseq], [0, T]]), ALU.subtract, [cs], [t_["dd"]])
                yield
                self.act(t_["dd"][:], t_["dd"][:], AF.Exp, [t_["dd"]], [t_["dd"]])
                yield
                self.tt("dve", kdj[:], t_["dd"][:], t_["k2"][:], ALU.mult, [t_["dd"], t_["k2"]], [kdj])
                yield
                self.tt("pool", bdj[:], t_["dd"][:], t_["b"][:], ALU.mult, [t_["dd"], t_["b"]], [bdj])
                yield
                self.tr(self.pb[0], self.pb[0][:, j * 128:(j + 1) * 128], kdj[:], self.identb[:], [kdj, self.identb])
                yield
                self.tr(self.pb[1], self.pb[1][:, j * 128:(j + 1) * 128], bdj[:], self.identb[:], [bdj, self.identb])
                yield
                yield
                self.act(t_["x"][:], cs[:], AF.Exp, [cs], [t_["x"]])
                yield
                self.tt("dve", fm["kt"][:, j, :], t_["x"][:], t_["k2"][:], ALU.mult, [t_["x"], t_["k2"]], [fm["kt"]])
                yield
                self.tt("pool", fm["bt"][:, j, :], t_["x"][:], t_["b"][:], ALU.mult, [t_["x"], t_["b"]], [fm["bt"]])
                yield
                yield
                self.act(t_["x"][:], cs[:], AF.Exp, [cs], [t_["x"]], scale=-1.0)
                yield
                self.tt("dve", fm["rho"][:, j, :], t_["x"][:], t_["r"][:], ALU.mult, [t_["x"], t_["r"]], [fm["rho"]])
                yield
                self.tt("pool", t_["e"][:], cs[:], t_["ew"][:], ALU.subtract, [cs, t_["ew"]], [t_["e"]])
                yield
                self.act(t_["e"][:], t_["e"][:], AF.Exp, [t_["e"]], [t_["e"]], scale=-1.0)
                yield
                self.tt("dve", fm["kap"][:, j, :], t_["e"][:], t_["kk"][:], ALU.mult, [t_["e"], t_["kk"]], [fm["kap"]])

            proj1(0)
            proj1(1)
            for pr in range(4):
                gens = [elem(2 * pr), elem(2 * pr + 1)]
                ndone = 0
                projected = (pr == 3)
                while gens:
                    for g_ in list(gens):
                        try:
                            r_ = next(g_)
                            if r_ == "ps_done":
                                ndone += 1
                        except StopIteration:
                            gens.remove(g_)
                    if ndone == 2 and not projected:
                        proj1(2 * pr + 2)
                        proj1(2 * pr + 3)
                        projected = True
            self.cp("dve", st["rkb"][:], psRK[:, 0:16], [], [psRK, st["rkb"]])
            self.cp("act", kdT[:], self.pb[0][:].rearrange("p (c f) -> p c f", c=8), [], [self.pb[0], kdT])
            self.cp("dve", bdT[:], self.pb[1][:].rearrange("p (c f) -> p c f", c=8), [], [self.pb[1], bdT])
            pb = self.bbank()
            for c in range(8):
                self.tr(pb, pb[:, c * 128:(c + 1) * 128], vT[:, c, :], self.identb[:], [vT, self.identb])
            self.cp("act", Vb[:], pb[:], [], [pb, Vb])
            for i_, src_ in enumerate((fm["kap"], fm["rho"], fm["kt"], fm["bt"], kdT, bdT)):
                self.st("sp", rec1[t, :, i_ * 1024:(i_ + 1) * 1024], src_[:].rearrange("p c f -> p (c f)"), [src_], "st_r1%d" % i_)
            self.st("sp", rec1[t, :, 6144:7168], Vb[:], [Vb], "st_r16")
            self.st("sp", frec1[t, :, 0:1024], zs[:], [zs], "st_r17")
            self.st("sp", frec1[t, :, 1024:1152], Pc[:].rearrange("p c s -> p (c s)"), [Pc], "st_r18")
            self.st("sp", frec1[t, :, 1152:1168], st["rkb"][:], [st["rkb"]], "st_r19")
        P.barrier()
        P.flush()
        es.close()
        self.es_w1.close()

        es = ExitStack()
        cur[0] = es
        wO = sb([128, 8, 1024], BF16, "wO1")
        for c in range(8):
            P.dma("pool", lambda e, c=c: e.dma_start(out=wO[:, c, :], in_=wo_d[c * 128:(c + 1) * 128, :]), w=[wO], sem="ld_w")
        fnw = sb([128, 1024], F32, "fnw")
        lnw = sb([128, 1024], F32, "lnw")
        lnb = sb([128, 1024], F32, "lnb")
        self.ld("sp", fnw[:], pbc(self.d_fnorm_w, 1024), [fnw], "ld_c")
        self.ld("sp", lnw[:], pbc(lw_d, 1024), [lnw], "ld_c")
        self.ld("sp", lnb[:], pbc(lb_d, 1024), [lnb], "ld_c")
        recs = [sb([128, 56, 128], BF16, "b_rec%d" % i) for i in range(2)]
        frecs = [sb([128, 1168], F32, "b_frec%d" % i) for i in range(2)]
        x1t = [sb([128, 1024], F32, "b_x1")] * 2
        Z = sb([128, 8, 64], F32, "b_Z")
        Zb = sb([128, 8, 64], BF16, "b_Zb")
        P.op("pool", lambda e: e.memset(Z[:], 0.0), w=["Z%d" % i for i in range(8)])
        P.op("pool", lambda e: e.memset(Zb[:], 0.0), w=["Zb%d" % i for i in range(8)])
        sets = []
        for gi in range(4):
            B = {nm: sb([128, 4, 128], BF16, "h%d_%s" % (gi, nm)) for nm in ("Mlow", "Tt", "MkT", "AkT", "AbT")}
            B["rhsS"] = sb([128, 4, 64], BF16, "h%d_rhsS" % gi)
            B["SA"] = sb([128, 4, 64], BF16, "h%d_SA" % gi)
            B["yz0"] = sb([128, 4, 64], F32, "h%d_yz0" % gi)
            B["W"] = self.inv_bufs(sb, "h%d" % gi)
            sets.append(B)
        for gi in range(2):
            sh = {"s0in4": sb([64, 4, 2, 64], F32, "h%d_s0in4" % gi), "Vm4": sb([128, 4, 128], BF16, "h%d_Vm4" % gi),
                  "SAm4": sb([128, 4, 128], BF16, "h%d_SAm4" % gi), "Zn4": sb([128, 4, 64], F32, "h%d_Zn4" % gi),
                  "Sout4": sb([64, 4, 128], F32, "h%d_Sout4" % gi)}
            sets[gi].update(sh)
            sets[gi + 2].update(sh)
        ytoks = [sb([128, 16, 64], F32, "b_ytok%d" % i) for i in range(2)]
        ysq = self.nb["xn"]
        st = {nm: sb([128, 16], F32, "b_st_" + nm) for nm in ("sum", "ssq", "mean", "var", "rstd")}
        ss2 = sb([128, 1], F32, "b_ss2")
        ygT = sb([128, 8, 128], BF16, "b_ygT")
        x2 = ysq
        Z0b2 = [sb([128, 16, 64], BF16, "b_Z0b%d" % i) for i in range(2)]
        mk = sb([128, 16, 128], BF16, "b_mk")
        Sout = sb([64, 128], F32, "b_Sout")
        KAP, RHO, KT, BT, KD, BD, VB = 0, 8, 16, 24, 32, 40, 48
        v4 = lambda ps: ps[:, 0:512].rearrange("p (h f) -> p h f", h=4)

        def rwkv_group(t, g, B, rec, frec, ytok, ytn):
            X = "p" if t < 17 else "s"
            hs = [4 * g + hh for hh in range(4)]
            order = (0, 2, 1, 3)

            def fmh(off, h):
                b0 = (h % 2) * 64
                return rec[b0:b0 + 64, off + h // 2, :]
            Vh = lambda h: rec[:, VB + h // 2, (h % 2) * 64:(h % 2) * 64 + 64]
            Mlow, Tt, MkT, AkT, AbT, rhsS, SA, yz0 = (B[k_] for k_ in ("Mlow", "Tt", "MkT", "AkT", "AbT", "rhsS", "SA", "yz0"))

            def prod4(ps, lo, ro):
                for i_, hh in enumerate(order):
                    h = hs[hh]
                    self.mm(ps, ps[:, hh * 128:(hh + 1) * 128], fmh(lo, h), fmh(ro, h), True, True, [rec], sw=(i_ == 2))
            psM = self.bank()
            prod4(psM, KAP, BT)
            self.tt("dve", Mlow[:], v4(psM), self.bc4(mneg[X]), ALU.mult, [mneg[X]], [psM, Mlow])
            for (lo, ro, dst, msk) in ((KT, KAP, MkT, msup[X]), (KT, RHO, AkT, mup[X]), (BT, RHO, AbT, mup[X])):
                ps = self.bank()
                prod4(ps, lo, ro)
                self.tt("dve", dst[:], v4(ps), self.bc4(msk), ALU.mult, [msk], [ps, dst])
            yield
            yield from self.inverse_gen(B["W"], Mlow, Tt)
            s0in4, Vm4, SAm4, Zn4, Sout4 = (B[k_] for k_ in ("s0in4", "Vm4", "SAm4", "Zn4", "Sout4"))
            if X == "s":
                for q in range(2):
                    m = 2 * g + q
                    for sg in range(4):
                        for h2_ in range(2):
                            self.ld("sp", s0in4[:, :, h2_, :], srwkv[4 * sg:4 * sg + 4, 2 * m + h2_].rearrange("s v k -> v s k"), [s0in4],
                                    "ld_s0in%d" % (g % 2))
                        pz = self.bank()
                        for si in range(4):
                            self.tr(pz, pz[:, si * 64:(si + 1) * 64], s0in4[:, si].rearrange("v h k -> v (h k)"), self.ident32[0:64, 0:64],
                                    [s0in4, self.ident32])
                        self.cp(self.ev(), Z0b2[q][:, 4 * sg:4 * sg + 4, :], pz[:, 0:256].rearrange("p (s f) -> p s f", s=4), [],
                                [pz, Z0b2[q]])
            psR = self.bank()
            if X == "s":
                psY2 = self.bank()
            for hh, h in enumerate(hs):
                b0 = (h % 2) * 64
                m = h // 2
                if X == "s":
                    Z0b = Z0b2[hh // 2]
                    for (off_, psd, lastflag) in ((KAP, psR, False), (RHO, psY2, True)):
                        a_ = fmh(off_, h)
                        self.tt("pool", mk[b0:b0 + 64], bass.AP(a_.tensor, a_.offset, [list(a_.ap[0]), [0, 16], [1, 128]]),
                                self.cm[b0:b0 + 64], ALU.mult, [rec, self.cm], [mk])
                        for s_ in range(16):
                            self.mm(psd, psd[:, hh * 64:(hh + 1) * 64], mk[b0:b0 + 64, s_, :], Z0b[b0:b0 + 64, s_, :], s_ == 0,
                                    lastflag and s_ == 15, [mk, Z0b], sw=(s_ == 0))
                else:
                    self.mm(psR, psR[:, hh * 64:(hh + 1) * 64], fmh(KAP, h), Zb[b0:b0 + 64, m, :], True, False, [rec, "Zb%d" % m], sw=True)
                self.mm(psR, psR[:, hh * 64:(hh + 1) * 64], MkT[:, hh, :], Vh(h), False, True, [MkT, rec])
            if X == "s":
                self.cp("dve", yz0[:].rearrange("p h f -> p (h f)"), psY2[:, 0:256], [], [psY2, yz0])
            self.act(rhsS[:].rearrange("p h f -> p (h f)"), psR[:, 0:256], AF.Copy, [], [psR, rhsS], scale=-1.0)
            yield
            psS = self.bank()
            for hh in range(4):
                self.mm(psS, psS[:, hh * 64:(hh + 1) * 64], Tt[:, hh, :], rhsS[:, hh, :], True, True, [Tt, rhsS])
            self.cp("act", SA[:].rearrange("p h f -> p (h f)"), psS[:, 0:256], [], [psS, SA])
            yield
            psY = self.bank()
            for hh, h in enumerate(hs):
                b0 = (h % 2) * 64
                m = h // 2
                if X == "p":
                    self.mm(psY, psY[:, hh * 64:(hh + 1) * 64], fmh(RHO, h), Zb[b0:b0 + 64, m, :], True, False, [rec, "Zb%d" % m], sw=True)
                self.mm(psY, psY[:, hh * 64:(hh + 1) * 64], AkT[:, hh, :], Vh(h), X == "s", False, [AkT, rec])
                self.mm(psY, psY[:, hh * 64:(hh + 1) * 64], AbT[:, hh, :], SA[:, hh, :], False, True, [AbT, SA])
            yv = ytok[:, 4 * g:4 * g + 4, :].rearrange("p h f -> p (h f)")
            if X == "p":
                self.cp("dve", yv, psY[:, 0:256], [], [psY, ytn + str(g)])
            else:
                self.tt("dve", yv, psY[:, 0:256], yz0[:].rearrange("p h f -> p (h f)"), ALU.add, [yz0], [psY, ytn + str(g)])
            for mm_ in range(2):
                m = 2 * g + mm_
                SApair = SA[:, 2 * mm_:2 * mm_ + 2, :].rearrange("p h f -> p (h f)")
                if X == "p":
                    psZ = self.bank()
                    self.mm(psZ, psZ[:, 0:128], rec[:, KD + m, :], rec[:, VB + m, :], True, False, [rec])
                    self.mm(psZ, psZ[:, 0:128], rec[:, BD + m, :], SApair, False, True, [rec, SA])
                    for h2 in range(2):
                        b0 = h2 * 64
                        self.stt(Z[b0:b0 + 64, m, :], Z[b0:b0 + 64, m, :], V(frec, 1024 + m * 16, [[1, 1]], 64, b0),
                                 psZ[b0:b0 + 64, b0:b0 + 64], ALU.mult, ALU.add, ["Z%d" % m, frec], [psZ, "Z%d" % m])
                    self.cp("act", Zb[:, m, :], Z[:, m, :], ["Z%d" % m], ["Zb%d" % m])
                    if t == 16:
                        pz = self.bank()
                        self.tr(pz, pz[0:64, 0:128], Z[:, m, :], self.ident32[:], ["Z%d" % m, self.ident32])
                        self.cp("act", Sout[:], pz[0:64, 0:128], [], [pz, Sout])
                        self.st("sp", p_rwkv[2 * m:2 * m + 2].rearrange("h v k -> v h k"), Sout[:].rearrange("v (h k) -> v h k", h=2),
                                [Sout], "st_so")
                else:
                    for sg in range(4):
                        for h2_ in range(2):
                            self.ld("sp", s0in4[:, :, h2_, :], srwkv[4 * sg:4 * sg + 4, 2 * m + h2_].rearrange("s v k -> v s k"), [s0in4],
                                    "ld_s0in%d" % (g % 2))
                        pz = self.bank()
                        for si in range(4):
                            self.tr(pz, pz[:, si * 64:(si + 1) * 64], s0in4[:, si].rearrange("v h k -> v (h k)"), self.ident32[0:64, 0:64],
                                    [s0in4, self.ident32])
                        vpair = rec[:, VB + m, :]
                        smb = V(self.sm, 4 * sg, [[1, 4], [0, 128]])
                        self.tt("pool", Vm4[:], bass.AP(vpair.tensor, vpair.offset, [list(vpair.ap[0]), [0, 4], [1, 128]]), smb, ALU.mult,
                                [rec, self.sm], [Vm4])
                        sap = SA[:, 2 * mm_:2 * mm_ + 2, :]
                        self.tt("pool", SAm4[:], bass.AP(sap.tensor, sap.offset, [list(sap.ap[0]), [0, 4], [1, 128]]), smb, ALU.mult,
                                [SA, self.sm], [SAm4])
                        psZ = self.bank()
                        for si in range(4):
                            self.mm(psZ, psZ[:, si * 128:(si + 1) * 128], rec[:, KD + m, :], Vm4[:, si, :], True, False, [rec, Vm4])
                            self.mm(psZ, psZ[:, si * 128:(si + 1) * 128], rec[:, BD + m, :], SAm4[:, si, :], False, True, [rec, SAm4])
                        self.tt("dve", Zn4[:], pz[:, 0:256].rearrange("p (s f) -> p s f", s=4),
                                V(frec, 1024 + m * 16 + 4 * sg, [[1, 4], [0, 64]]), ALU.mult, [frec], [pz, Zn4])
                        for h2 in range(2):
                            b0 = h2 * 64
                            self.tt("dve", Zn4[b0:b0 + 64], Zn4[b0:b0 + 64],
                                    psZ[b0:b0 + 64, 0:512].rearrange("p (s f) -> p s f", s=4)[:, :, b0:b0 + 64], ALU.add, [Zn4], [psZ, Zn4])
                        pz2 = self.bank()
                        for si in range(4):
                            self.tr(pz2, pz2[0:64, si * 128:(si + 1) * 128], Zn4[:, si, :], self.ident32[:], [Zn4, self.ident32])
                        self.cp("act", Sout4[:].rearrange("v s f -> v (s f)"), pz2[0:64, 0:512], [], [pz2, Sout4])
                        for h2_ in range(2):
                            self.st("sp", s_rwkv[4 * sg:4 * sg + 4, 2 * m + h2_].rearrange("s v k -> v s k"),
                                    Sout4[:, :, h2_ * 64:(h2_ + 1) * 64], [Sout4], "st_so%d" % (g % 2))
                        yield
            yield

        def post_gen(t, rec, frec, ytok, ytn):
            x1 = x1t[0]
            ytokr = [ytn + str(g_) for g_ in range(4)]
            self.ld("sp", x1[:], x1scr[t], [x1], "x")
            P.op("dve", lambda e: e.tensor_reduce(out=st["sum"][:], in_=ytok[:], axis=AX.X, op=ALU.add), r=ytokr, w=[st["sum"]])
            ysq3 = ysq[:].rearrange("p (h f) -> p h f", h=16)
            self.act(ysq3, ytok[:], AF.Square, ytokr, [ysq])
            P.op("dve", lambda e: e.tensor_reduce(out=st["ssq"][:], in_=ysq3, axis=AX.X, op=ALU.add), r=[ysq], w=[st["ssq"]])
            yield
            self.ts("dve", st["mean"][:], st["sum"][:], 1.0 / 64, None, ALU.mult, None, [st["sum"]], [st["mean"]])
            self.tt("dve", st["var"][:], st["mean"][:], st["mean"][:], ALU.mult, [st["mean"]], [st["var"]])
            self.stt(st["var"][:], st["ssq"][:], 1.0 / 64, st["var"][:], ALU.mult, ALU.subtract, [st["ssq"], st["var"]], [st["var"]])
            self.ts("dve", st["var"][:], st["var"][:], 64e-5, None, ALU.add, None, [st["var"]], [st["var"]])
            self.act(st["rstd"][:], st["var"][:], AF.Ln, [st["var"]], [st["rstd"]])
            self.act(st["rstd"][:], st["rstd"][:], AF.Exp, [st["rstd"]], [st["rstd"]], scale=-0.5)
            yield
            self.tt("dve", ysq3, ytok[:], V(st["mean"], 0, [[1, 16], [0, 64]]), ALU.subtract, ytokr + [st["mean"]], [ysq])
            self.tt("dve", ysq3, ysq3, V(st["rstd"], 0, [[1, 16], [0, 64]]), ALU.mult, [ysq, st["rstd"]], [ysq])
            yield
            yf = ysq[:]
            self.tt("pool", yf, yf, lnw[:], ALU.mult, [ysq, lnw], [ysq])
            self.tt("pool", yf, yf, lnb[:], ALU.add, [ysq, lnb], [ysq])
            yield
            yt3 = ytok[:]
            self.tt("dve", yt3, rec[:, VB:VB + 8, :].rearrange("p c (a f) -> p (c a) f", a=2), V(frec, 1152, [[1, 16], [0, 64]]), ALU.mult,
                    [rec, frec] + ytokr, ytokr)
            self.tt("dve", yf, yf, ytok[:].rearrange("p h f -> p (h f)"), ALU.add, [ysq] + ytokr, [ysq])
            self.tt("dve", yf, yf, frec[:, 0:1024], ALU.mult, [ysq, frec], [ysq])
            yield
            for half in range(2):
                ps = self.bank()
                for c in range(4):
                    cc = half * 4 + c
                    self.tr(ps, ps[:, c * 128:(c + 1) * 128], yf[:, cc * 128:(cc + 1) * 128], self.ident32[:], [ysq, self.ident32])
                self.cp(self.ev(), ygT[:, half * 4:half * 4 + 4, :], ps[:].rearrange("p (c f) -> p c f", c=4), [], [ps, ygT])
                yield
            for half in range(2):
                ps = self.bank()
                cs_ = slice(half * 512, (half + 1) * 512)
                for c in range(8):
                    self.mm(ps, ps[:], ygT[:, c, :], wO[:, c, cs_], c == 0, c == 7, [ygT, wO])
                self.tt("dve", x2[:, cs_], ps[:], x1[:, cs_], ALU.add, [x1], [ps, x2])
                yield
            yy = ytok
            yyv = ytok[:].rearrange("p h f -> p (h f)")
            self.act(yyv, x2[:], AF.Square, [x2], ytokr + [ss2], accum_out=ss2[:])
            self.ts("dve", ss2[:], ss2[:], 1.0 / 1024, 1e-6, ALU.mult, ALU.add, [ss2], [ss2])
            self.act(ss2[:], ss2[:], AF.Ln, [ss2], [ss2])
            self.act(ss2[:], ss2[:], AF.Exp, [ss2], [ss2], scale=-0.5)
            yield
            self.stt(yyv, x2[:], ss2[:, 0:1], fnw[:], ALU.mult, ALU.mult, [x2, ss2, fnw], ytokr)
            self.st("sp", y[t], yyv, ytokr, "stc_y")

        pending = None
        for t in range(NT):
            rec, frec = recs[t % 2], frecs[t % 2]
            ytok, ytn = ytoks[t % 2], "yt%d_" % (t % 2)
            self.ld("sp", rec[:].rearrange("p a f -> p (a f)"), rec1[t], [rec], "x")
            self.ld("sp", frec[:], frec1[t], [frec], "x")
            gens = [rwkv_group(t, g, sets[g], rec, frec, ytok, ytn) for g in range(4)]
            if pending is not None:
                gens.append(pending)
            self.run_rr(gens)
            pending = post_gen(t, rec, frec, ytok, ytn)
        self.run_rr([pending])
        P.barrier()
        P.flush()
        es.close()
        P.es.close()


_CACHE = {}


def kernel(x_prompt, x_sample, state_gdn, state_gdn_conv, state_rwkv, state_rwkv_shift,
           meta_tokens, norm_w, final_norm_w,
           gdn_w_in, gdn_conv_w, gdn_a_log, gdn_dt_bias, gdn_norm_w, gdn_w_out,
           rwkv_mu, rwkv_w_rkvz, rwkv_w0, rwkv_w1, rwkv_w2, rwkv_a0, rwkv_a1, rwkv_a2,
           rwkv_k_k, rwkv_k_a, rwkv_r_k, rwkv_lnx_w, rwkv_lnx_b, rwkv_w_o):
    f = lambda a: np.ascontiguousarray(np.asarray(a, dtype=np.float32))
    if "nc" not in _CACHE:
        _CACHE["nc"] = K().nc
    nc = _CACHE["nc"]
    x_prompt, x_sample = f(x_prompt), f(x_sample)
    meta = f(meta_tokens)
    in_maps = []
    for c in range(8):
        xin = np.zeros((NT, 128, 1024), np.float32)
        xin[0, 112:128] = meta
        xin[1:17] = x_prompt[c].reshape(16, 128, 1024)
        xin[17] = x_sample[16 * c:16 * c + 16].reshape(128, 1024)
        in_maps.append({
            "xin": xin,
            "sgdn": f(state_gdn[0, 16 * c:16 * c + 16]),
            "sconv": f(state_gdn_conv[0, 16 * c:16 * c + 16]).reshape(48, 4096),
            "srwkv": f(state_rwkv[0, 16 * c:16 * c + 16]),
            "sshift": f(state_rwkv_shift[0, 16 * c:16 * c + 16]),
            "norm_w": f(norm_w), "fnorm_w": f(final_norm_w).reshape(1, 1024),
            "w_in": f(gdn_w_in[0]), "conv_w": f(gdn_conv_w[0]), "a_log": f(gdn_a_log), "dt_bias": f(gdn_dt_bias),
            "gnorm_w": f(gdn_norm_w), "w_out": f(gdn_w_out[0]),
            "mu": f(rwkv_mu[0]), "rkvz": f(rwkv_w_rkvz[0]), "w0": f(rwkv_w0), "w1": f(rwkv_w1[0]), "w2": f(rwkv_w2[0]),
            "a0": f(rwkv_a0), "a1": f(rwkv_a1[0]), "a2": f(rwkv_a2[0]), "k_k": f(rwkv_k_k), "k_a": f(rwkv_k_a),
            "r_k": f(rwkv_r_k).reshape(1, 1024), "lnx_w": f(rwkv_lnx_w), "lnx_b": f(rwkv_lnx_b), "w_o": f(rwkv_w_o[0]),
        })
    res = run_bass_kernel_spmd(nc, in_maps, core_ids=list(range(8)))
    R = res.results
    y_prompt = np.stack([R[c]["y"][1:17].reshape(2048, 1024) for c in range(8)])
    y_sample = np.concatenate([R[c]["y"][17].reshape(16, 8, 1024) for c in range(8)])
    p_gdn = np.stack([R[c]["p_gdn"] for c in range(8)])[None]
    p_conv = np.stack([R[c]["p_conv"] for c in range(8)])[None]
    s_gdn = np.concatenate([R[c]["s_gdn"] for c in range(8)])[None]
    s_conv = np.concatenate([R[c]["s_conv"].reshape(16, 3, 4096) for c in range(8)])[None]
    p_rwkv = np.stack([R[c]["p_rwkv"] for c in range(8)])[None]
    p_shift = np.stack([R[c]["p_shift"].reshape(1024) for c in range(8)])[None]
    s_rwkv = np.concatenate([R[c]["s_rwkv"] for c in range(8)])[None]
    s_shift = np.concatenate([R[c]["s_shift"] for c in range(8)])[None]
    return (y_prompt, y_sample, p_gdn, p_conv, p_rwkv, p_shift, s_gdn, s_conv, s_rwkv, s_shift)
```

```python
from contextlib import ExitStack
import numpy as np
import concourse.bass as bass
import concourse.mybir as mybir
from concourse.bass_utils import run_bass_kernel_spmd

F32 = mybir.dt.float32
BF16 = mybir.dt.bfloat16
AF = mybir.ActivationFunctionType
ALU = mybir.AluOpType
AX = mybir.AxisListType
ENGS = ("pe", "act", "dve", "pool", "sp")
NT = 18
BIG = 30000.0


class Prog:
    CE = ("pe", "act", "dve", "pool")

    def __init__(self, nc):
        self.nc = nc
        self.es = ExitStack()
        self.q = {e: [] for e in ENGS}
        self.cnt = {}
        self.waited = {e: {} for e in ENGS}
        self.lastw = {}
        self.reads = {}
        self.semkeys = set(ENGS)
        self.ntens = 0
        self.nops = 0
        self.nidx = {e: 0 for e in self.CE}
        self.marks = {e: [] for e in self.CE}
        self.entry = {e: {} for e in self.CE}

    def sb(self, shape, dt=F32, name=None):
        self.ntens += 1
        name = name or f"t{self.ntens}"
        return self.es.enter_context(self.nc.sbuf_tensor(name, list(shape), dt))

    def ps(self, shape, dt=F32, name=None):
        self.ntens += 1
        name = name or f"p{self.ntens}"
        return self.es.enter_context(self.nc.psum_tensor(name, list(shape), dt))

    @staticmethod
    def _k(x):
        return x if isinstance(x, str) else x.name

    def _deps(self, r, w):
        toks = []
        for x in r:
            t = self.lastw.get(x)
            if t is not None:
                toks.append(t)
        for x in w:
            t = self.lastw.get(x)
            if t is not None:
                toks.append(t)
            toks.extend(self.reads.get(x, {}).values())
        return toks

    def _record(self, r, w, tok):
        key = tok[1]
        for x in r:
            d = self.reads.setdefault(x, {})
            o = d.get(key)
            if o is None or o[2] < tok[2]:
                d[key] = tok
        for x in w:
            self.lastw[x] = tok
            self.reads[x] = {}

    def _resolve(self, tok, peek=False):
        if tok[0] == "D":
            return tok[1], tok[2]
        _, eng, idx = tok
        m = self.marks[eng]
        if m and m[-1][0] >= idx:
            lo, hi = 0, len(m) - 1
            while lo < hi:
                mid = (lo + hi) // 2
                if m[mid][0] >= idx:
                    hi = mid
                else:
                    lo = mid + 1
            return eng, m[lo][1]
        ent = self.entry[eng][idx]
        cnt = len(m) + 1
        m.append((idx, cnt))
        ent[2] = (eng, 1)
        return eng, cnt

    def _waits(self, eng, toks, pe_self=False):
        need = {}
        for tok in toks:
            if tok[0] == "E" and tok[1] == eng and eng == "pe" and not pe_self:
                continue
            k, v = self._resolve(tok)
            if self.waited[eng].get(k, 0) >= v:
                continue
            need[k] = max(need.get(k, 0), v)
        for k, v in need.items():
            self.waited[eng][k] = v
        return list(need.items())

    def op(self, eng, fn, r=(), w=(), pe_self=False):
        r = [self._k(x) for x in r]
        w = [self._k(x) for x in w]
        waits = self._waits(eng, self._deps(r, w), pe_self)
        self.nidx[eng] += 1
        idx = self.nidx[eng]
        ent = [waits, fn, None]
        self.entry[eng][idx] = ent
        self.q[eng].append(ent)
        tok = ("E", eng, idx)
        self._record(r, w, tok)
        self.nops += 1
        return tok

    def dma(self, q, fn, r=(), w=(), sem=None):
        r = [self._k(x) for x in r]
        w = [self._k(x) for x in w]
        sem = ("L_" + w[0]) if w else ("S_" + r[0])
        self.semkeys.add(sem)
        waits = self._waits(q, self._deps(r, w))
        self.cnt[sem] = self.cnt.get(sem, 0) + 16
        tok = ("D", sem, self.cnt[sem])
        self.q[q].append([waits, fn, (sem, 16)])
        self._record(r, w, tok)
        self.nops += 1
        return tok

    def _all_toks(self):
        toks = [("E", e, self.nidx[e]) for e in self.CE if self.nidx[e] > 0]
        toks += [("D", k, v) for k, v in self.cnt.items()]
        return toks

    def barrier(self):
        toks = self._all_toks()
        for e in ENGS:
            w = self._waits(e, [t for t in toks if not (t[0] == "E" and t[1] == e)])
            if w:
                self.q[e].append([w, None, None])

    def flush(self):
        nc = self.nc
        if not hasattr(self, "sems"):
            self.sems = {}
        for k in sorted(self.semkeys):
            if k not in self.sems:
                self.sems[k] = self.es.enter_context(nc.semaphore("s_" + k))
        sems = self.sems
        q = self.q
        self.q = {e: [] for e in ENGS}
        self.entry = {e: {} for e in self.CE}

        def run(e, lst):
            for waits, fn, inc in lst:
                for k, v in waits:
                    e.wait_ge(sems[k], v)
                if fn is not None:
                    inst = fn(e)
                    if inc is not None:
                        inst.then_inc(sems[inc[0]], inc[1])

        with nc.Block() as block:
            @block.tensor
            def _(e):
                run(e, q["pe"])

            @block.scalar
            def _(e):
                run(e, q["act"])

            @block.vector
            def _(e):
                run(e, q["dve"])

            @block.gpsimd
            def _(e):
                run(e, q["pool"])

            @block.sync
            def _(e):
                run(e, q["sp"])


def fsz(t):
    n = 1
    for s in list(t.shape)[1:]:
        n *= int(s)
    return n


def V(t, off, dims, npart=128, p0=0):
    f = fsz(t)
    return bass.AP(t, p0 * f + off, [[f, npart]] + [list(d) for d in dims])


class K:
    def __init__(self, nlayers=2):
        self.nlayers = nlayers
        nc = bass.Bass("TRN2", target_bir_lowering=False)
        self.nc = nc
        self.P = Prog(nc)
        self.rr = {"pf": 0, "pb": 0, "ev": 0}
        self.build()

    def din(self, name, shape):
        return self.nc.dram_tensor(name, list(shape), F32, kind="ExternalInput").ap()

    def dout(self, name, shape):
        return self.nc.dram_tensor(name, list(shape), F32, kind="ExternalOutput").ap()

    def bank(self):
        self.rr["pf"] = (self.rr["pf"] + 1) % len(self.pf)
        return self.pf[self.rr["pf"]]

    def bbank(self):
        self.rr["pb"] = (self.rr["pb"] + 1) % len(self.pb)
        return self.pb[self.rr["pb"]]

    def ev(self):
        self.rr["ev"] ^= 1
        return "act" if self.rr["ev"] else "dve"

    def mm(self, ps, out, lhsT, rhs, start, stop, r, sw=None):
        self.P.op("pe", lambda e: e.matmul(out, lhsT=lhsT, rhs=rhs, start=start, stop=stop), r=r, w=[ps],
                  pe_self=(getattr(self, "pe_self", False) if sw is None else sw))

    def tr(self, ps, out, in_, ident, r):
        self.P.op("pe", lambda e: e.transpose(out=out, in_=in_, identity=ident), r=r, w=[ps])

    def tt(self, eng, out, in0, in1, op, r, w):
        self.P.op(eng, lambda e: e.tensor_tensor(out=out, in0=in0, in1=in1, op=op), r=r, w=w)

    def ts(self, eng, out, in0, s1, s2, op0, op1, r, w):
        if op1 is None:
            self.P.op(eng, lambda e: e.tensor_scalar(out=out, in0=in0, scalar1=s1, scalar2=None, op0=op0), r=r, w=w)
        else:
            self.P.op(eng, lambda e: e.tensor_scalar(out=out, in0=in0, scalar1=s1, scalar2=s2, op0=op0, op1=op1), r=r, w=w)

    def stt(self, out, in0, scalar, in1, op0, op1, r, w):
        self.P.op("dve", lambda e: e.scalar_tensor_tensor(out=out, in0=in0, scalar=scalar, in1=in1, op0=op0, op1=op1), r=r, w=w)

    def act(self, out, in_, func, r, w, bias=None, scale=None, accum_out=None):
        kw = {}
        if bias is not None:
            kw["bias"] = bias
        if scale is not None:
            kw["scale"] = scale
        if accum_out is not None:
            kw["accum_out"] = accum_out
        self.P.op("act", lambda e: e.activation(out=out, in_=in_, func=func, **kw), r=r, w=w)

    def cp(self, eng, out, in_, r, w):
        if eng == "act":
            self.P.op("act", lambda e: e.copy(out=out, in_=in_), r=r, w=w)
        else:
            self.P.op(eng, lambda e: e.tensor_copy(out=out, in_=in_), r=r, w=w)

    def ld(self, q, out, in_, w, sem, r=()):
        self.P.dma(q, lambda e: e.dma_start(out=out, in_=in_), r=r, w=w, sem=sem)

    def st(self, q, out, in_, r, sem):
        self.P.dma(q, lambda e: e.dma_start(out=out, in_=in_), r=r, w=(), sem=sem)

    def consts(self):
        P = self.P
        sb = P.sb
        self.ones32 = sb([128, 128], F32, "ones32")
        self.ident32 = sb([128, 128], F32, "ident32")
        self.identb = sb([128, 128], BF16, "identb")
        P.op("pool", lambda e: e.memset(self.ones32[:], 1.0), w=[self.ones32])

        def sel(out, in_, pattern, op, base, cm, r, w, fill=0.0):
            P.op("pool", lambda e: e.affine_select(out=out, in_=in_, pattern=pattern, compare_op=op, fill=fill,
                                                    base=base, channel_multiplier=cm), r=r, w=w)
        o = self.ones32
        sel(self.ident32[:], o[:], [[-1, 128]], ALU.is_equal, 0, 1, [o], [self.ident32])
        self.cp("pool", self.identb[:], self.ident32[:], [self.ident32], [self.identb])
        blk8 = sb([128, 128], F32, "blk8")
        blk32 = sb([128, 128], F32, "blk32")
        for (t, B) in ((blk8, 8), (blk32, 32)):
            v3 = t[:].rearrange("p (s t) -> p s t", t=B)
            o3 = o[:].rearrange("p (s t) -> p s t", t=B)
            sel(v3, o3, [[-B, 128 // B], [0, B]], ALU.is_ge, 0, 1, [o], [t])
            sel(v3, v3, [[B, 128 // B], [0, B]], ALU.is_ge, B - 1, -1, [t], [t])
        low = sb([128, 128], F32, "low")
        up = sb([128, 128], F32, "up")
        sel(low[:], o[:], [[-1, 128]], ALU.is_ge, 0, 1, [o], [low])
        sel(up[:], o[:], [[1, 128]], ALU.is_ge, 0, -1, [o], [up])
        self.lowT = {}
        self.same = {}
        self.addm = {}
        for X in ("p", "s"):
            lowT = sb([128, 128], F32, "lowT" + X)
            lowX = sb([128, 128], F32, "lowX" + X)
            am = sb([128, 4, 128], F32, "addm" + X)
            if X == "p":
                self.cp("pool", lowT[:], up[:], [up], [lowT])
                self.cp("pool", lowX[:], low[:], [low], [lowX])
                self.same[X] = self.ones32
            else:
                self.tt("pool", lowT[:], up[:], blk8[:], ALU.mult, [up, blk8], [lowT])
                self.tt("pool", lowX[:], low[:], blk8[:], ALU.mult, [low, blk8], [lowX])
                self.same[X] = blk8
            for h in range(4):
                self.ts("pool", am[:, h, :], lowX[:], -BIG, BIG, ALU.mult, ALU.add, [lowX], [am])
            self.lowT[X] = lowT
            self.addm[X] = am
        slow = sb([128, 128], F32, "slow")
        sup = sb([128, 128], F32, "sup")
        sel(slow[:], o[:], [[-1, 128]], ALU.is_gt, 0, 1, [o], [slow])
        sel(sup[:], o[:], [[1, 128]], ALU.is_gt, 0, -1, [o], [sup])
        self.m1 = sb([128, 128], BF16, "m1")
        self.m1T = sb([128, 128], BF16, "m1T")
        self.m2 = sb([128, 128], BF16, "m2")
        self.tt("pool", self.m1[:], blk32[:], slow[:], ALU.mult, [blk32, slow], [self.m1])
        self.tt("pool", self.m1T[:], blk32[:], sup[:], ALU.mult, [blk32, sup], [self.m1T])
        self.ts("pool", self.m2[:], blk32[:], -1.0, 1.0, ALU.mult, ALU.add, [blk32], [self.m2])
        self.sm = sb([128, 16], F32, "sm")
        sel(self.sm[:], o[:, 0:16], [[-8, 16]], ALU.is_ge, 0, 1, [o], [self.sm])
        sel(self.sm[:], self.sm[:], [[8, 16]], ALU.is_ge, 7, -1, [self.sm], [self.sm])
        self.cm = sb([128, 16, 128], BF16, "cm")
        P.op("pool", lambda e: e.memset(self.cm[:], 1.0), w=[self.cm])
        sel(self.cm[:], self.cm[:], [[-8, 16], [1, 128]], ALU.is_ge, 0, 0, [self.cm], [self.cm])
        sel(self.cm[:], self.cm[:], [[8, 16], [-1, 128]], ALU.is_ge, 7, 0, [self.cm], [self.cm])

    def bc4(self, m):
        return V(m, 0, [[0, 4], [1, 128]])

    def inverse(self, Mlow, Tt, tag):
        for _ in self.inverse_gen(self.invw, Mlow, Tt):
            pass

    def inverse_gen(self, W, Mlow, Tt, extra=None):
        Ib = self.bc4(self.identb)
        v4 = lambda ps: ps[:, 0:512].rearrange("p (h f) -> p h f", h=4)
        MdT, Md, No = W["MdT"], W["Md"], W["No"]
        self.tt("pool", MdT[:], Mlow[:], self.bc4(self.m1), ALU.mult, [Mlow, self.m1], [MdT])
        pb = self.bbank()
        for hh in range(4):
            self.tr(pb, pb[:, hh * 128:(hh + 1) * 128], Mlow[:, hh, :], self.identb[:], [Mlow, self.identb])
        self.tt("dve", Md[:], v4(pb), self.bc4(self.m1T), ALU.mult, [self.m1T], [pb, Md])
        self.tt("dve", No[:], v4(pb), self.bc4(self.m2), ALU.mult, [self.m2], [pb, No])
        X = W["X0"]
        self.tt("pool", X[:], Md[:], Ib, ALU.add, [Md, self.identb], [X])
        if extra is not None:
            extra()
        yield
        Pk, PTk = Md, MdT
        IpPT_prev = None
        for k in range(1, 5):
            last = (k == 4)
            if IpPT_prev is not None:
                psX = self.bank()
                for hh in range(4):
                    self.mm(psX, psX[:, hh * 128:(hh + 1) * 128], IpPT_prev[:, hh, :], X[:, hh, :], True, True, [IpPT_prev, X])
            psPT = self.bank()
            for hh in range(4):
                self.mm(psPT, psPT[:, hh * 128:(hh + 1) * 128], Pk[:, hh, :], PTk[:, hh, :], True, True, [Pk, PTk])
            if not last:
                psP = self.bank()
                for hh in range(4):
                    self.mm(psP, psP[:, hh * 128:(hh + 1) * 128], PTk[:, hh, :], Pk[:, hh, :], True, True, [Pk, PTk])
            if IpPT_prev is not None:
                Xn = W["X%d" % ((k - 1) % 2)]
                self.cp("act", Xn[:], v4(psX), [], [psX, Xn])
                X = Xn
            IpPT = W["IpPT%d" % (k % 2)]
            if not last:
                PTn = W["PT%d" % (k % 2)]
                Pn = W["P%d" % (k % 2)]
                self.cp("act", PTn[:], v4(psPT), [], [psPT, PTn])
                self.cp("act", Pn[:], v4(psP), [], [psP, Pn])
                self.tt("pool", IpPT[:], PTn[:], Ib, ALU.add, [PTn, self.identb], [IpPT])
                Pk, PTk = Pn, PTn
            else:
                self.tt("dve", IpPT[:], v4(psPT), Ib, ALU.add, [self.identb], [psPT, IpPT])
            IpPT_prev = IpPT
            yield
        psX = self.bank()
        for hh in range(4):
            self.mm(psX, psX[:, hh * 128:(hh + 1) * 128], IpPT_prev[:, hh, :], X[:, hh, :], True, True, [IpPT_prev, X])
        Xn = W["X0"]
        self.cp(self.ev(), Xn[:], v4(psX), [], [psX, Xn])
        X = Xn
        yield
        pb = self.bbank()
        for hh in range(4):
            self.tr(pb, pb[:, hh * 128:(hh + 1) * 128], X[:, hh, :], self.identb[:], [X, self.identb])
        XT = W["XT"]
        self.cp(self.ev(), XT[:], v4(pb), [], [pb, XT])
        yield
        psV = self.bank()
        psVT = self.bank()
        for hh in range(4):
            self.mm(psV, psV[:, hh * 128:(hh + 1) * 128], XT[:, hh, :], No[:, hh, :], True, True, [XT, No])
        for hh in range(4):
            self.mm(psVT, psVT[:, hh * 128:(hh + 1) * 128], No[:, hh, :], XT[:, hh, :], True, True, [XT, No])
        Vm, VT, IpVT = W["P0"], W["P1"], W["PT0"]
        self.cp("act", Vm[:], v4(psV), [], [psV, Vm])
        self.cp("act", VT[:], v4(psVT), [], [psVT, VT])
        self.tt("pool", IpVT[:], VT[:], Ib, ALU.add, [VT, self.identb], [IpVT])
        yield
        ps2 = self.bank()
        for hh in range(4):
            self.mm(ps2, ps2[:, hh * 128:(hh + 1) * 128], Vm[:, hh, :], VT[:, hh, :], True, True, [Vm, VT])
        IpV2T = W["PT1"]
        self.tt("dve", IpV2T[:], v4(ps2), Ib, ALU.add, [self.identb], [ps2, IpV2T])
        yield
        psY = self.bank()
        for hh in range(4):
            self.mm(psY, psY[:, hh * 128:(hh + 1) * 128], IpV2T[:, hh, :], X[:, hh, :], True, True, [IpV2T, X])
        Y = W["MdT"]
        self.cp(self.ev(), Y[:], v4(psY), [], [psY, Y])
        yield
        psT = self.bank()
        for hh in range(4):
            self.mm(psT, psT[:, hh * 128:(hh + 1) * 128], IpVT[:, hh, :], Y[:, hh, :], True, True, [IpVT, Y])
        self.cp(self.ev(), Tt[:], v4(psT), [], [psT, Tt])
        yield

    @staticmethod
    def run_rr(gens):
        gens = list(gens)
        while gens:
            for g in list(gens):
                try:
                    next(g)
                except StopIteration:
                    gens.remove(g)

    def inv_bufs(self, alloc, tag):
        return {nm: alloc([128, 4, 128], BF16, "%s_%s" % (tag, nm))
                for nm in ("MdT", "Md", "No", "X0", "X1", "P0", "P1", "PT0", "PT1", "IpPT0", "IpPT1", "XT")}

    def norm_T(self, xt, nwbc, xnT, xn_keep=None, bank=None):
        P = self.P
        ss, xn = self.nb["ss"], (xn_keep if xn_keep is not None else self.nb["xn"])
        self.act(xn[:], xt[:], AF.Square, [xt], [xn, ss], accum_out=ss[:])
        self.ts("dve", ss[:], ss[:], 1.0 / 1024, 1e-6, ALU.mult, ALU.add, [ss], [ss])
        self.act(ss[:], ss[:], AF.Ln, [ss], [ss])
        self.act(ss[:], ss[:], AF.Exp, [ss], [ss], scale=-0.5)
        self.stt(xn[:], xt[:], ss[:, 0:1], nwbc, ALU.mult, ALU.mult, [xt, ss, "normw"], [xn])
        for half in range(2):
            ps = bank() if bank is not None else self.bank()
            for c in range(4):
                cc = half * 4 + c
                self.tr(ps, ps[:, c * 128:(c + 1) * 128], xn[:, cc * 128:(cc + 1) * 128], self.ident32[:], [xn, self.ident32])
            self.cp(self.ev(), xnT[:, half * 4:half * 4 + 4, :], ps[:].rearrange("p (c f) -> p c f", c=4), [], [ps, xnT])

    def build(self):
        P = self.P
        nc = self.nc
        sb = P.sb
        din, dout = self.din, self.dout
        xin = din("xin", [NT, 128, 1024])
        sgdn = din("sgdn", [16, 16, 128, 128])
        sconv = din("sconv", [48, 4096])
        srwkv = din("srwkv", [16, 16, 64, 64])
        sshift = din("sshift", [16, 1024])
        self.d_srwkv, self.d_sshift = srwkv, sshift
        norm_w = din("norm_w", [2, 1024])
        fnorm_w = din("fnorm_w", [1, 1024])
        self.d_norm_w, self.d_fnorm_w = norm_w, fnorm_w
        w_in = din("w_in", [1024, 6176])
        conv_w = din("conv_w", [4096, 4])
        a_log = din("a_log", [1, 16])
        dt_bias = din("dt_bias", [1, 16])
        gnorm_w = din("gnorm_w", [1, 128])
        w_out = din("w_out", [2048, 1024])
        y = dout("y", [NT, 128, 1024])
        p_gdn = dout("p_gdn", [16, 128, 128])
        p_conv = dout("p_conv", [3, 4096])
        s_gdn = dout("s_gdn", [16, 16, 128, 128])
        s_conv = dout("s_conv", [48, 4096])
        oscr = nc.dram_tensor("oscr", [NT, 128, 2048], F32, kind="Internal").ap()
        x1scr = nc.dram_tensor("x1scr", [NT, 128, 1024], F32, kind="Internal").ap()

        self.pf = [P.ps([128, 512], F32, "pf%d" % i) for i in range(5)]
        self.pfx = P.ps([128, 512], F32, "pfx")
        self.pb = [P.ps([128, 1024], BF16, "pb%d" % i) for i in range(2)]
        self.consts()

        def pbc(ap_row, n):
            return bass.AP(ap_row.tensor, ap_row.offset, [[0, 128], [1, n]])

        nw0 = sb([128, 1024], F32, "normw")
        self.nw0 = nw0
        self.ld("sp", nw0[:], pbc(norm_w[0:1, :], 1024), [nw0], "ld_c")
        dtb = sb([128, 16], F32, "dtb")
        nea = sb([128, 16], F32, "nea")
        self.ld("sp", dtb[:], pbc(dt_bias, 16), [dtb], "ld_c")
        self.ld("sp", nea[:], pbc(a_log, 16), [nea], "ld_c")
        self.act(nea[:], nea[:], AF.Exp, [nea], [nea])
        self.ts("dve", nea[:], nea[:], -1.0, None, ALU.mult, None, [nea], [nea])
        gnw = sb([128, 128], F32, "gnw")
        self.ld("sp", gnw[:], pbc(gnorm_w, 128), [gnw], "ld_c")
        cw = sb([128, 32, 4], F32, "cw")
        self.ld("sp", cw[:], conv_w.rearrange("(j p) k -> p j k", p=128), [cw], "ld_c")

        self.nb = {"ss": sb([128, 1], F32, "ss"), "xn": sb([128, 1024], F32, "xn")}

        recA = nc.dram_tensor("recA", [NT, 128, 40 * 128], BF16, kind="Internal").ap()
        srecA = nc.dram_tensor("srecA", [NT, 128, 336], F32, kind="Internal").ap()
        es_a = ExitStack()

        def sba(shape, dt=F32, name=None):
            P.ntens += 1
            return es_a.enter_context(nc.sbuf_tensor(name or f"a{P.ntens}", list(shape), dt))

        NQ = 4128
        wA = sba([128, 8, NQ], BF16, "wA")
        for c in range(8):
            src = w_in[c * 128:(c + 1) * 128, :]
            P.dma("pool", lambda e, c=c, src=src: e.dma_start(out=wA[:, c, 0:4096], in_=src[:, 0:4096]), w=[wA], sem="ld_w")
            P.dma("pool", lambda e, c=c, src=src: e.dma_start(out=wA[:, c, 4096:4128], in_=src[:, 6144:6176]), w=[wA], sem="ld_w")
        xt = [sba([128, 1024], F32, "xt%d" % i) for i in range(2)]
        xnTs = [sba([128, 8, 128], BF16, "xnT%d" % i) for i in range(2)]
        halo = sba([128, 32, 3], F32, "halo")
        P.op("pool", lambda e: e.memset(halo[:], 0.0), w=["halo%d" % j for j in range(32)])
        cst_in = sba([48, 512], F32, "cst_in")
        cstT = sba([128, 32, 48], F32, "cstT")
        for jg in range(8):
            self.ld("sp", cst_in[:], sconv[:, jg * 512:(jg + 1) * 512], [cst_in], "ld_c")
            ps = self.bank()
            for jj in range(4):
                self.tr(ps, ps[:, jj * 48:(jj + 1) * 48], cst_in[:, jj * 128:(jj + 1) * 128], self.ident32[0:48, 0:48],
                        [cst_in, self.ident32])
            self.cp(self.ev(), cstT[:, jg * 4:jg * 4 + 4, :], ps[:, 0:192].rearrange("p (j f) -> p j f", j=4), [], [ps, cstT])
        cstN = cstT
        U = [sba([128, 176], BF16, "U%d" % i) for i in range(4)]
        wdiag = sba([128, 128, 128], BF16, "wdiag")
        for q4 in range(4):
            self.tt("pool" if q4 % 2 == 0 else "dve", wdiag[:, q4 * 32:(q4 + 1) * 32, :], V(self.ident32, 0, [[0, 32], [1, 128]]),
                    V(cw, q4 * 32, [[1, 32], [0, 128]]), ALU.mult, [self.ident32, cw], [wdiag])

        c4 = [sba([128, 4, 128], F32, "c4_%d" % i) for i in range(2)]
        sq4 = sba([128, 4, 128], BF16, "sq4")
        self.onesb = sba([128, 128], BF16, "onesb")
        P.op("pool", lambda e: e.memset(self.onesb[:], 1.0), w=[self.onesb])
        rn4 = sba([128, 4, 128], F32, "rn4")
        recs = [sba([128, 40, 128], BF16, "rec%d" % i) for i in range(2)]
        srecs = [sba([128, 336], F32, "srec%d" % i) for i in range(2)]
        scs = [{nm: sba([128, 16], F32, "sc%d_%s" % (i, nm)) for nm in
                ("beta", "g", "gc", "glt", "egc", "negbege", "ekd", "negb", "tmp")} for i in range(2)]
        gms = [sba([128, 16, 16], F32, "gm%d" % i) for i in range(2)]
        egls = [sba([128, 16, 16], F32, "egl%d" % i) for i in range(2)]

        def bcs(t, g):
            return V(t, 4 * g, [[1, 4], [0, 128]])

        def prologue(t):
            X = "p" if t < 17 else "s"
            nseq, T = (1, 128) if X == "p" else (16, 8)
            x = xt[t % 2]
            srec = srecs[t % 2]
            xnT, sc, gm, egl = xnTs[t % 2], scs[t % 2], gms[t % 2], egls[t % 2]
            pbank = lambda: self.pfx
            self.ld("sp", x[:], xin[t], [x], "ld_x%d" % (t % 2))
            self.norm_T(x, nw0[:], xnT, bank=pbank)
            ps = pbank()
            for c in range(8):
                self.mm(ps, ps[:, 0:32], xnT[:, c, :], wA[:, c, 4096:4128], c == 0, c == 7, [xnT, wA])
            self.act(sc["beta"][:], ps[:, 0:16], AF.Sigmoid, [], [ps, sc["beta"]])
            self.tt("dve", sc["tmp"][:], ps[:, 16:32], dtb[:], ALU.add, [dtb], [ps, sc["tmp"]])
            self.act(sc["tmp"][:], sc["tmp"][:], AF.Exp, [sc["tmp"]], [sc["tmp"]])
            self.act(sc["tmp"][:], sc["tmp"][:], AF.Ln, [sc["tmp"]], [sc["tmp"]], bias=1.0)
            self.tt("dve", sc["g"][:], sc["tmp"][:], nea[:], ALU.mult, [sc["tmp"], nea], [sc["g"]])
            ps = pbank()
            self.mm(ps, ps[:, 0:16], self.lowT[X][:], sc["g"][:], True, True, [self.lowT[X], sc["g"]])
            self.mm(ps, ps[:, 16:32], self.same[X][:], sc["g"][:], True, True, [self.same[X], sc["g"]])
            self.cp("dve", sc["gc"][:], ps[:, 0:16], [], [ps, sc["gc"]])
            self.cp("dve", sc["glt"][:], ps[:, 16:32], [], [ps, sc["glt"]])
            self.act(sc["egc"][:], sc["gc"][:], AF.Exp, [sc["gc"]], [sc["egc"]])
            self.stt(sc["negbege"][:], sc["egc"][:], -1.0, sc["beta"][:], ALU.mult, ALU.mult, [sc["egc"], sc["beta"]], [sc["negbege"]])
            self.tt("dve", sc["tmp"][:], sc["glt"][:], sc["gc"][:], ALU.subtract, [sc["glt"], sc["gc"]], [sc["tmp"]])
            self.act(sc["ekd"][:], sc["tmp"][:], AF.Exp, [sc["tmp"]], [sc["ekd"]])
            self.ts("dve", sc["negb"][:], sc["beta"][:], -1.0, None, ALU.mult, None, [sc["beta"]], [sc["negb"]])
            gmv = gm[:, 0:nseq, :]
            self.tt("pool", gmv, V(sc["g"], 0, [[0, nseq], [1, 16]]), V(self.sm, 0, [[1, nseq], [0, 16]]) if X == "s"
                    else V(self.ones32, 0, [[0, 1], [1, 16]]), ALU.mult, [sc["g"], self.sm, self.ones32], [gm])
            ps = pbank()
            self.mm(ps, ps[:, 0:nseq * 16], self.ones32[:], gm[:, 0:nseq, :].rearrange("p s h -> p (s h)"), True, True,
                    [self.ones32, gm])
            self.act(egl[:, 0:nseq, :].rearrange("p s h -> p (s h)"), ps[:, 0:nseq * 16], AF.Exp, [], [ps, egl])

            for i_, nm in enumerate(("gc", "egc", "negbege", "ekd", "negb")):
                self.cp("pool", srec[:, 16 * i_:16 * i_ + 16], sc[nm][:], [sc[nm]], [srec])
            self.cp("pool", srec[:, 80:336], egl[:].rearrange("p s h -> p (s h)"), [egl], [srec])

        for t in range(NT):
            X = "p" if t < 17 else "s"
            nseq, T = (1, 128) if X == "p" else (16, 8)
            if t == 0:
                prologue(0)
            xnT, sc = xnTs[t % 2], scs[t % 2]
            rec = recs[t % 2]
            srec = srecs[t % 2]

            class _RV:
                def __init__(s_, o, name):
                    s_.o, s_.name = o, name

                def __getitem__(s_, idx):
                    a, b, c_ = idx
                    if isinstance(b, slice):
                        b = slice((b.start or 0) + s_.o, (b.stop if b.stop is not None else 0) + s_.o)
                    else:
                        b = b + s_.o
                    return rec[a, b, c_]
            qT, kT, Ktok, vb = _RV(0, rec.name), _RV(8, rec.name), _RV(16, rec.name), _RV(24, rec.name)
            psUs = {}

            def proj(jg):
                psU = self.bank()
                psUs[jg] = psU
                for jj in range(4):
                    j = jg * 4 + jj
                    for c in range(8):
                        self.mm(psU, psU[:, jj * 128:(jj + 1) * 128], wA[:, c, j * 128:(j + 1) * 128], xnT[:, c, :], c == 0, c == 7,
                                [wA, xnT])
            psCs = {}

            def views(j):
                jg, jj = j // 4, j % 4
                u, cc = U[j % 4], c4[jg % 2]
                if X == "p":
                    return u, cc, (lambda k: u[:, k:k + 128])
                u3 = u[:].rearrange("p (s t) -> p s t", t=11)
                return u, cc, (lambda k: u3[:, :, k:k + 8])

            def stA(j):
                jg, jj = j // 4, j % 4
                psU = psUs[jg]
                u, cc, uv = views(j)
                pu = psU[:, jj * 128:(jj + 1) * 128]
                uh, ub, hj = "uh%d" % (j % 4), "ub%d" % (j % 4), "halo%d" % j
                if X == "p":
                    self.cp("pool", u[:, 0:3], halo[:, j, :], [hj], [uh])
                    self.cp("dve", u[:, 3:131], pu, [], [psU, ub])
                    self.cp("dve", halo[:, j, :], psU[:, jj * 128 + 125:jj * 128 + 128], [], [psU, hj])
                else:
                    u3 = u[:].rearrange("p (s t) -> p s t", t=11)
                    pu3 = pu.rearrange("p (s t) -> p s t", t=8)
                    self.cp("pool", u3[:, :, 0:3], cstT[:, j, :].rearrange("p (s k) -> p s k", k=3), [hj, cstT], [uh])
                    self.cp("dve", u3[:, :, 3:11], pu3, [], [psU, ub])
                    self.cp("dve", cstN[:, j, :].rearrange("p (s k) -> p s k", k=3), pu3[:, :, 5:8], [], [psU, hj])

            def stB(j):
                jg, jj = j // 4, j % 4
                if jj == 0:
                    psCs[jg] = self.bank()
                psC = psCs[jg]
                u, cc, uv = views(j)
                for k in range(4):
                    self.mm(psC, psC[:, jj * 128:(jj + 1) * 128], wdiag[:, j * 4 + k, :], uv(k), k == 0, k == 3,
                            [wdiag, "uh%d" % (j % 4), "ub%d" % (j % 4)])

            def stC(j):
                jg, jj = j // 4, j % 4
                u, cc, uv = views(j)
                psC = psCs[jg]
                self.act(cc[:, jj, :], psC[:, jj * 128:(jj + 1) * 128], AF.Silu, [], [psC, cc])
                if jj != 3:
                    return
                if jg < 4:
                    self.tt("pool", sq4[:], cc[:], cc[:], ALU.mult, [cc], [sq4])
                    psN = self.bank()
                    for j2 in range(4):
                        self.mm(psN, psN[:, j2 * 128:(j2 + 1) * 128], self.onesb[:], sq4[:, j2, :], True, True, [self.onesb, sq4])
                    self.act(rn4[:], psN[:].rearrange("p (h f) -> p h f", h=4), AF.Ln, [], [psN, rn4], bias=1e-6)
                    lnsc = float(np.log(128.0 ** -0.5)) if jg < 2 else 0.0
                    self.act(rn4[:], rn4[:], AF.Exp, [rn4], [rn4], scale=-0.5, bias=lnsc)
                    dst = qT if jg < 2 else kT
                    o0 = (jg % 2) * 4
                    self.tt("dve", dst[:, o0:o0 + 4, :], cc[:], rn4[:], ALU.mult, [cc, rn4], [dst])
                else:
                    gv = jg - 4
                    psT = self.bank()
                    for j2 in range(4):
                        self.tr(psT, psT[:, j2 * 128:(j2 + 1) * 128], cc[:, j2, :], self.ident32[:], [cc, self.ident32])
                    self.tt("dve", vb[:, gv * 4:gv * 4 + 4, :], psT[:].rearrange("p (h f) -> p h f", h=4), bcs(sc["beta"], gv),
                            ALU.mult, [sc["beta"]], [psT, vb])

            proj(0)
            for step in range(36):
                if step < 32:
                    if step % 4 == 1 and step // 4 + 1 < 8:
                        proj(step // 4 + 1)
                    stA(step)
                if step == 14 and t + 1 < NT:
                    prologue(t + 1)
                if 0 <= step - 2 < 32:
                    stB(step - 2)
                if 0 <= step - 4 < 32:
                    stC(step - 4)
            pb = self.bbank()
            for kh in range(8):
                self.tr(pb, pb[:, kh * 128:(kh + 1) * 128], kT[:, kh, :], self.identb[:], [kT, self.identb])
            self.cp("act", rec[:, 16:24, :], pb[:].rearrange("p (h f) -> p h f", h=8), [], [pb, rec])
            if t == 16 or t == 17:
                src, ncol, dst = (halo, 3, p_conv) if t == 16 else (cstN, 48, s_conv)
                stg = cst_in
                for jg in range(8):
                    ps = self.bank()
                    for jj in range(4):
                        j = jg * 4 + jj
                        self.tr(ps, ps[0:ncol, jj * 128:(jj + 1) * 128], src[:, j, :], self.ident32[:], [src, self.ident32, "halo%d" % j])
                    self.cp(self.ev(), stg[0:ncol, :], ps[0:ncol, :], [], [ps, stg])
                    self.st("sp", dst[:, jg * 512:(jg + 1) * 512], stg[0:ncol, :], [stg], "st_c")

            self.st("sp", recA[t], rec[:].rearrange("p a f -> p (a f)"), [rec], "st_rec%d" % (t % 2))
            self.st("sp", srecA[t], srec[:], [srec], "st_srec%d" % (t % 2))
        P.barrier()
        P.flush()
        es_a.close()

        es_a = ExitStack()
        rec2 = [sba([128, 40, 128], BF16, "r2ec%d" % i) for i in range(2)]
        srec2 = [sba([128, 336], F32, "s2rec%d" % i) for i in range(2)]
        gcd = [sba([128, 16, 128], F32, "gcd%d" % i) for i in range(2)]
        Sf = sba([128, 16, 128], F32, "Sf")
        Sb = sba([128, 16, 128], BF16, "Sb")
        P.op("pool", lambda e: e.memset(Sf[:], 0.0), w=["Sf%d" % i for i in range(4)])
        P.op("pool", lambda e: e.memset(Sb[:], 0.0), w=["Sb%d" % i for i in range(4)])
        sets = []
        for gi in range(4):
            B = {nm: sba([128, 4, 128], BF16, "g%d_%s" % (gi, nm)) for nm in
                 ("E1", "E1nb", "Mlow", "Tt", "r4", "vn", "vnd", "attn", "attnT")}
            B["o4"] = sba([128, 4, 128], F32, "g%d_o4" % gi)
            B["W"] = self.inv_bufs(sba, "g%d" % gi)
            sets.append(B)
        S0bs = [sba([128, 16, 128], BF16, "S0b%d" % i) for i in range(2)]
        S0f32s = [sba([128, 16, 128], F32, "S0f32_%d" % i) for i in range(2)]
        mks = [sba([128, 16, 128], BF16, "mk%d" % i) for i in range(2)]
        S0f = [sba([128, 4, 128], F32, "S0f%d" % i) for i in range(2)]
        Sn = [sba([128, 4, 128], F32, "Sn%d" % i) for i in range(2)]
        vnds = [sba([128, 128], BF16, "vnds%d" % i) for i in range(2)]
        v4 = lambda ps: ps[:, 0:512].rearrange("p (h f) -> p h f", h=4)

        def gdn_group(t, g, B, rec, srec, gcd_t):
            X = "p" if t < 17 else "s"
            hs = [4 * g + hh for hh in range(4)]
            khs = [h // 2 for h in hs]
            qT = lambda kh: rec[:, kh, :]
            kT = lambda kh: rec[:, 8 + kh, :]
            Kt = lambda kh: rec[:, 16 + kh, :]
            sv_ = lambda off: V(srec, off + 4 * g, [[1, 4], [0, 128]])
            E1, E1nb, Mlow, Tt, r4, vn, vnd, attn, attnT, o4 = (B[k_] for k_ in
                                                               ("E1", "E1nb", "Mlow", "Tt", "r4", "vn", "vnd", "attn", "attnT", "o4"))
            psR = self.bank()
            self.mm(psR, psR[:], self.ones32[:], gcd_t[:, 4 * g:4 * g + 4, :].rearrange("p h f -> p (h f)"), True, False,
                    [self.ones32, gcd_t])
            self.mm(psR, psR[:], self.ident32[:], self.addm[X][:].rearrange("p h f -> p (h f)"), False, True,
                    [self.ident32, self.addm[X]])
            self.tt("dve", v4(psR), v4(psR), sv_(0), ALU.subtract, [srec], [psR])
            self.act(E1[:], v4(psR), AF.Exp, [], [psR, E1], scale=-1.0)
            self.tt("pool", E1nb[:], E1[:], sv_(64), ALU.mult, [E1, srec], [E1nb])
            yield
            psG = self.bank()
            for hh in range(4):
                self.mm(psG, psG[:, hh * 128:(hh + 1) * 128], kT(khs[hh]), kT(khs[hh]), True, True, [rec])
            psQ = self.bank()
            for hh in range(4):
                self.mm(psQ, psQ[:, hh * 128:(hh + 1) * 128], qT(khs[hh]), kT(khs[hh]), True, True, [rec])
            self.tt("dve", Mlow[:], v4(psG), E1nb[:], ALU.mult, [E1nb], [psG, Mlow])
            self.tt("dve", attn[:], v4(psQ), E1[:], ALU.mult, [E1], [psQ, attn])
            yield

            def extra():
                pb = self.bbank()
                for hh in range(4):
                    self.tr(pb, pb[:, hh * 128:(hh + 1) * 128], attn[:, hh, :], self.identb[:], [attn, self.identb])
                self.cp("act", attnT[:], v4(pb), [], [pb, attnT])
            yield from self.inverse_gen(B["W"], Mlow, Tt, extra)
            S0b, S0f32, mk = S0bs[g % 2], S0f32s[g % 2], mks[g % 2]
            psK = self.bank()
            if X == "p":
                for hh in range(4):
                    self.mm(psK, psK[:, hh * 128:(hh + 1) * 128], kT(khs[hh]), Sb[:, hs[hh], :], True, True, [rec, "Sb%d" % g])
            else:
                psA = self.bank()
                for hh in range(4):
                    h = hs[hh]
                    self.ld("sp", S0f32[:], sgdn[:, h].rearrange("s p v -> p s v"), [S0f32], "ld_s0b")
                    self.cp("act", S0b[:], S0f32[:], [S0f32], [S0b])
                    for (srcf, psd) in ((kT, psK), (qT, psA)):
                        a_ = srcf(khs[hh])
                        self.tt("pool", mk[:], bass.AP(a_.tensor, a_.offset, [list(a_.ap[0]), [0, 16], [1, 128]]), self.cm[:], ALU.mult,
                                [rec, self.cm], [mk])
                        for s_ in range(16):
                            self.mm(psd, psd[:, hh * 128:(hh + 1) * 128], mk[:, s_, :], S0b[:, s_, :], s_ == 0, s_ == 15, [mk, S0b])
                self.tt("dve", o4[:], v4(psA), sv_(16), ALU.mult, [srec], [psA, o4])
            self.tt("dve", v4(psK), v4(psK), sv_(32), ALU.mult, [srec], [psK])
            self.tt("dve", r4[:], v4(psK), rec[:, 24 + 4 * g:24 + 4 * g + 4, :], ALU.add, [rec], [psK, r4])
            yield
            psV = self.bank()
            for hh in range(4):
                self.mm(psV, psV[:, hh * 128:(hh + 1) * 128], Tt[:, hh, :], r4[:, hh, :], True, True, [Tt, r4])
            self.cp("act", vn[:], v4(psV), [], [psV, vn])
            self.tt("dve", vnd[:], v4(psV), sv_(48), ALU.mult, [srec], [psV, vnd])
            yield
            psB = self.bank()
            for hh in range(4):
                self.mm(psB, psB[:, hh * 128:(hh + 1) * 128], attnT[:, hh, :], vn[:, hh, :], True, True, [attnT, vn])
            if X == "p":
                psA = self.bank()
                for hh in range(4):
                    self.mm(psA, psA[:, hh * 128:(hh + 1) * 128], qT(khs[hh]), Sb[:, hs[hh], :], True, True, [rec, "Sb%d" % g])
                self.tt("dve", o4[:], v4(psA), sv_(16), ALU.mult, [srec], [psA, o4])
            self.tt("dve", o4[:], v4(psB), o4[:], ALU.add, [o4], [psB, o4])
            self.st("sp", oscr[t, :, g * 512:(g + 1) * 512], o4[:].rearrange("p h f -> p (h f)"), [o4], "st_o%d" % g)
            if X == "p":
                psS = self.bank()
                for hh in range(4):
                    self.mm(psS, psS[:, hh * 128:(hh + 1) * 128], Kt(khs[hh]), vnd[:, hh, :], True, True, [rec, vnd])
                sv = Sf[:, 4 * g:4 * g + 4, :]
                SfN, SbN = "Sf%d" % g, "Sb%d" % g
                self.tt("pool", sv, sv, V(srec, 80 + 4 * g, [[1, 4], [0, 128]]), ALU.mult, [srec, SfN], [SfN])
                self.tt("dve", sv, v4(psS), sv, ALU.add, [SfN], [psS, SfN])
                self.cp("act", Sb[:, 4 * g:4 * g + 4, :], sv, [SfN], [SbN])
                if t == 16:
                    self.st("sp", p_gdn[4 * g:4 * g + 4].rearrange("h p v -> p h v"), sv, [SfN], "st_pg")
            else:
                for hh in range(4):
                    h = hs[hh]
                    for sg in range(4):
                        i2 = (hh * 4 + sg) % 2
                        s0f = S0f[i2]
                        self.ld("sp", s0f[:], sgdn[sg * 4:sg * 4 + 4, h].rearrange("s p v -> p s v"), [s0f], "ld_s0f%d" % i2)
                        psS = self.bank()
                        for si in range(4):
                            s_ = sg * 4 + si
                            vs = vnds[s_ % 2]
                            self.act(vs[:], vnd[:, hh, :], AF.Copy, [vnd, self.sm], [vs], scale=self.sm[:, s_:s_ + 1])
                            self.mm(psS, psS[:, si * 128:(si + 1) * 128], Kt(khs[hh]), vs[:], True, True, [rec, vs])
                        sn = Sn[i2]
                        self.tt("pool", sn[:], s0f[:], V(srec, 80 + sg * 4 * 16 + h, [[16, 4], [0, 128]]), ALU.mult, [s0f, srec], [sn])
                        self.tt("dve", sn[:], v4(psS), sn[:], ALU.add, [sn], [psS, sn])
                        self.st("sp", s_gdn[sg * 4:sg * 4 + 4, h].rearrange("s p v -> p s v"), sn[:], [sn], "st_sn%d" % i2)
            yield

        for t in range(NT):
            rec, srec, gcd_t = rec2[t % 2], srec2[t % 2], gcd[t % 2]
            self.ld("sp", rec[:].rearrange("p a f -> p (a f)"), recA[t], [rec], "ld_rec%d" % (t % 2))
            self.ld("sp", srec[:], srecA[t], [srec], "ld_srec%d" % (t % 2))
            self.tt("pool", gcd_t[:], V(self.ident32, 0, [[0, 16], [1, 128]]), V(srec, 0, [[1, 16], [0, 128]]), ALU.mult,
                    [self.ident32, srec], [gcd_t])
            self.run_rr([gdn_group(t, g, sets[g], rec, srec, gcd_t) for g in range(4)])
        P.barrier()
        P.flush()
        es_a.close()

        self.pm = {(nm, X_): P.sb([128, 128], BF16, nm + X_) for nm in ("mup", "msup", "mneg") for X_ in ("p", "s")}
        self.es_w1 = ExitStack()
        P.ntens += 1
        self.wR = self.es_w1.enter_context(nc.sbuf_tensor("wR", [128, 8, 4096], BF16))
        self.w1a1 = self.es_w1.enter_context(nc.sbuf_tensor("w1a1", [128, 8, 128], BF16))
        self.w2a2 = self.es_w1.enter_context(nc.sbuf_tensor("w2a2", [64, 2, 1024], BF16))
        self.d_rkvz = din("rkvz", [4, 1024, 1024])
        self.d_w1 = din("w1", [1024, 64]); self.d_w2 = din("w2", [64, 1024])
        self.d_a1 = din("a1", [1024, 64]); self.d_a2 = din("a2", [64, 1024])

        es_b = ExitStack()

        def sbb(shape, dt=F32, name=None):
            P.ntens += 1
            return es_b.enter_context(nc.sbuf_tensor(name or f"b{P.ntens}", list(shape), dt))

        wZ = sbb([128, 8, 2048], BF16, "wZ")
        wO = sbb([128, 16, 1024], BF16, "wO")
        for c in range(8):
            P.dma("pool", lambda e, c=c: e.dma_start(out=wZ[:, c, :], in_=w_in[c * 128:(c + 1) * 128, 4096:6144]), w=[wZ], sem="ld_w")
        for c in range(16):
            P.dma("pool", lambda e, c=c: e.dma_start(out=wO[:, c, :], in_=w_out[c * 128:(c + 1) * 128, :]), w=[wO], sem="ld_w")
        wR_, w1a1_, w2a2_ = self.wR, self.w1a1, self.w2a2
        for i in range(4):
            for c in range(8):
                P.dma("pool", lambda e, i=i, c=c: e.dma_start(out=wR_[:, c, i * 1024:(i + 1) * 1024],
                                                              in_=self.d_rkvz[i, c * 128:(c + 1) * 128, :]), w=[wR_], sem="x")
        for c in range(8):
            P.dma("pool", lambda e, c=c: e.dma_start(out=w1a1_[:, c, 0:64], in_=self.d_w1[c * 128:(c + 1) * 128, :]), w=[w1a1_], sem="x")
            P.dma("pool", lambda e, c=c: e.dma_start(out=w1a1_[:, c, 64:128], in_=self.d_a1[c * 128:(c + 1) * 128, :]), w=[w1a1_], sem="x")
        P.dma("pool", lambda e: e.dma_start(out=w2a2_[:, 0, :], in_=self.d_w2), w=[w2a2_], sem="x")
        P.dma("pool", lambda e: e.dma_start(out=w2a2_[:, 1, :], in_=self.d_a2), w=[w2a2_], sem="x")
        xtb = [sbb([128, 1024], F32, "xtb%d" % i) for i in range(2)]
        xnTb = sbb([128, 8, 128], BF16, "xnTb")
        ot = [sbb([128, 16, 128], F32, "ot%d" % i) for i in range(2)]
        osq = None
        orn = sbb([128, 16], F32, "orn")
        zs = sbb([128, 4, 128], F32, "zs")
        og = sbb([128, 16, 128], F32, "og")
        osq = og
        ogT = sbb([128, 16, 128], BF16, "ogT")
        x1 = [sbb([128, 1024], F32, "x1_%d" % i) for i in range(2)]
        for t in range(NT):
            x = xtb[t % 2]
            o = ot[t % 2]
            self.ld("sp", x[:], xin[t], [x], "ldb_x%d" % (t % 2))
            self.ld("sp", o[:].rearrange("p h f -> p (h f)"), oscr[t], [o], "ldb_o%d" % (t % 2))
            self.norm_T(x, nw0[:], xnTb)
            self.act(osq[:], o[:], AF.Square, [o], [osq])
            P.op("dve", lambda e: e.tensor_reduce(out=orn[:], in_=osq[:], axis=AX.X, op=ALU.add), r=[osq], w=[orn])
            self.ts("dve", orn[:], orn[:], 1.0 / 128, 1e-6, ALU.mult, ALU.add, [orn], [orn])
            self.act(orn[:], orn[:], AF.Ln, [orn], [orn])
            self.act(orn[:], orn[:], AF.Exp, [orn], [orn], scale=-0.5)
            self.tt("dve", og[:], o[:], V(orn, 0, [[1, 16], [0, 128]]), ALU.mult, [o, orn], [og])
            self.tt("pool", og[:], og[:], V(gnw, 0, [[0, 16], [1, 128]]), ALU.mult, [og, gnw], [og])
            for g in range(4):
                psZ = self.bank()
                for c in range(8):
                    self.mm(psZ, psZ[:], xnTb[:, c, :], wZ[:, c, g * 512:(g + 1) * 512], c == 0, c == 7, [xnTb, wZ])
                self.act(zs[:], psZ[:].rearrange("p (h f) -> p h f", h=4), AF.Silu, [], [psZ, zs])
                self.tt("dve", og[:, 4 * g:4 * g + 4, :], og[:, 4 * g:4 * g + 4, :], zs[:], ALU.mult, [zs, og], [og])
            for g in range(4):
                ps = self.bank()
                for hh in range(4):
                    self.tr(ps, ps[:, hh * 128:(hh + 1) * 128], og[:, 4 * g + hh, :], self.ident32[:], [og, self.ident32])
                self.cp(self.ev(), ogT[:, 4 * g:4 * g + 4, :], ps[:].rearrange("p (h f) -> p h f", h=4), [], [ps, ogT])
            xo = x1[t % 2]
            for half in range(2):
                ps = self.bank()
                for c in range(16):
                    self.mm(ps, ps[:], ogT[:, c, :], wO[:, c, half * 512:(half + 1) * 512], c == 0, c == 15, [ogT, wO])
                self.tt("dve", xo[:, half * 512:(half + 1) * 512], ps[:], x[:, half * 512:(half + 1) * 512], ALU.add, [x], [ps, xo])
            self.st("sp", x1scr[t], xo[:], [xo], "stb_x%d" % (t % 2))
        P.barrier()
        P.flush()
        es_b.close()

        self.layer1(x1scr, y, pbc)

    def layer1(self, x1scr, y, pbc):
        P = self.P
        nc = self.nc
        din, dout = self.din, self.dout
        srwkv = self.d_srwkv
        sshift = self.d_sshift
        mu_d = din("mu", [6, 1024])
        w0_d = din("w0", [1, 1024])
        a0_d = din("a0", [1, 1024])
        kk_d = din("k_k", [1, 1024]); ka_d = din("k_a", [1, 1024]); rk_d = din("r_k", [1, 1024])
        lw_d = din("lnx_w", [1, 1024]); lb_d = din("lnx_b", [1, 1024]); wo_d = din("w_o", [1024, 1024])
        p_rwkv = dout("p_rwkv", [16, 64, 64]); p_shift = dout("p_shift", [1, 1024])
        s_rwkv = dout("s_rwkv", [16, 16, 64, 64]); s_shift = dout("s_shift", [16, 1024])
        pm = self.pm
        es = ExitStack()
        cur = [es]

        def sb(shape, dt=F32, name=None):
            P.ntens += 1
            return cur[0].enter_context(nc.sbuf_tensor(name or f"c{P.ntens}", list(shape), dt))
        o32 = self.ones32
        wR, w1a1, w2a2 = self.wR, self.w1a1, self.w2a2
        xx = sb([128, 1024], F32, "r_xx")
        pst = xx
        self.ld("sp", pst[0:6, :], mu_d, [pst], "ld_c")
        for i, d in enumerate((w0_d, a0_d, kk_d, ka_d, rk_d)):
            self.ld("sp", pst[6 + i:7 + i, :], d, [pst], "ld_c")
        par = sb([128, 8, 16], F32, "par")
        ps = self.bank()
        for c in range(8):
            self.tr(ps, ps[:, c * 16:c * 16 + 11], pst[0:11, c * 128:(c + 1) * 128], self.ident32[0:11, 0:11], [pst, self.ident32])
        self.cp("dve", par[:, :, 0:11], ps[:, 0:128].rearrange("p (c i) -> p c i", i=16)[:, :, 0:11], [], [ps, par])
        self.ts("dve", par[:, :, 11:12], par[:, :, 6:7], -1.0, None, ALU.mult, None, [par], [par])
        self.ts("dve", par[:, :, 12:13], par[:, :, 9:10], -1.0, 1.0, ALU.mult, ALU.add, [par], [par])
        self.ts("dve", par[:, :, 13:14], par[:, :, 7:8], 0.5, None, ALU.mult, None, [par], [par])
        nw1 = self.nw0
        self.ld("sp", nw1[:], pbc(self.d_norm_w[1:2, :], 1024), [nw1], "ld_c")
        def sel(out, in_, pattern, op, base, cm, r, w, fill=0.0):
            P.op("pool", lambda e: e.affine_select(out=out, in_=in_, pattern=pattern, compare_op=op, fill=fill,
                                                    base=base, channel_multiplier=cm), r=r, w=w)
        shp = sb([128, 128], F32, "shp")
        sel(shp[:], o32[:], [[1, 128]], ALU.is_equal, -1, -1, [o32], [shp])
        nb0 = sb([128, 128], F32, "nb0")
        sel(nb0[:].rearrange("p (s t) -> p s t", t=8), o32[:].rearrange("p (s t) -> p s t", t=8), [[0, 16], [1, 8]], ALU.is_gt, 0, 0,
            [o32], [nb0])
        shs = sb([128, 128], F32, "shs")
        self.tt("pool", shs[:], shp[:], nb0[:], ALU.mult, [shp, nb0], [shs])
        elast = sb([128, 128], F32, "elast")
        sel(elast[:], o32[:], [[-1, 128]], ALU.is_equal, -127, 1, [o32], [elast])
        selS = sb([16, 128], F32, "selS")
        sel(selS[:], o32[0:16, :], [[1, 128]], ALU.is_equal, 0, -8, [o32], [selS])
        b64 = sb([128, 128], F32, "b64")
        v3 = b64[:].rearrange("p (s t) -> p s t", t=64)
        sel(v3, o32[:].rearrange("p (s t) -> p s t", t=64), [[-64, 2], [0, 64]], ALU.is_ge, 0, 1, [o32], [b64])
        sel(v3, v3, [[64, 2], [0, 64]], ALU.is_ge, 63, -1, [b64], [b64])
        mneg, msup, mup = {}, {}, {}
        t_tmpm = sb([128, 128], F32, "tmpm")
        for X in ("p", "s"):
            lt = self.lowT[X]
            mup[X] = pm[("mup", X)]
            self.cp("pool", mup[X][:], lt[:], [lt], [mup[X]])
            msup[X] = pm[("msup", X)]
            sel(msup[X][:], lt[:], [[1, 128]], ALU.is_gt, 0, -1, [lt], [msup[X]])
            mneg[X] = pm[("mneg", X)]
            tmpm = t_tmpm
            self.ts("pool", tmpm[:], self.addm[X][:, 0, :], -1.0 / BIG, 1.0, ALU.mult, ALU.add, [self.addm[X]], [tmpm])
            sel(tmpm[:], tmpm[:], [[-1, 128]], ALU.is_gt, 0, 1, [tmpm], [tmpm])
            self.ts("pool", mneg[X][:], tmpm[:], -1.0, None, ALU.mult, None, [tmpm], [mneg[X]])
        hsel = sb([128, 2], F32, "hsel")
        self.cp("pool", hsel[:, 0:1], b64[:, 0:1], [b64], [hsel])
        self.cp("pool", hsel[:, 1:2], b64[:, 127:128], [b64], [hsel])
        self.invw = {}
        for nm in ("MdT", "Md", "No", "X0", "X1", "P0", "P1", "PT0", "PT1", "IpPT0", "IpPT1", "XT"):
            self.invw[nm] = sb([128, 4, 128], BF16, "jw_" + nm)
        for a_, b_ in (("V", "P0"), ("VT", "P1"), ("IpVT", "PT0"), ("IpV2T", "PT1"), ("Y", "MdT")):
            self.invw[a_] = self.invw[b_]
        xn = [sb([128, 1024], F32, "r_xn%d" % i) for i in range(2)]
        P.op("pool", lambda e: e.memset(xn[1][:], 0.0), w=[xn[1]])
        x1t = [sb([128, 1024], F32, "r_x1")] * 2
        xnT = sb([128, 8, 128], BF16, "r_xnT")
        xxT = sb([128, 8, 128], BF16, "r_xxT")
        xs_all = sb([128, 4, 8, 128], BF16, "r_xs")

        class _XS:
            def __init__(s_, i):
                s_.i = i
                s_.name = "r_xs"

            def __getitem__(s_, idx):
                return xs_all[idx[0], s_.i, idx[1], idx[2]]
        xs = [_XS(i) for i in range(4)]
        class _AL:
            def __init__(s_, apf, name):
                s_.apf = apf
                s_.name = name

            def __getitem__(s_, idx):
                return s_.apf()[idx]

        hT = sb([64, 2, 128], BF16, "r_hT")
        fm = {nm: sb([128, 8, 128], BF16, "r_" + nm) for nm in ("kap", "rho", "kt", "bt")}
        kdj = sb([128, 128], BF16, "r_kdj")
        bdj = sb([128, 128], BF16, "r_bdj")
        vT = sb([128, 8, 128], BF16, "r_vT")
        Vb = sb([128, 1024], BF16, "r_Vb")
        kdT = sb([128, 8, 128], BF16, "r_kdT")
        bdT = sb([128, 8, 128], BF16, "r_bdT")
        Pc = sb([128, 8, 16], F32, "r_Pc")
        t_all = []
        for i_ in range(2):
            t_ = {nm: sb([128, 128], F32, "r_t%d_%s" % (i_, nm)) for nm in ("e", "ew", "a", "kk", "sq", "rn", "k2", "b", "cs", "x", "r", "k")}
            t_["dd"] = t_["sq"]
            t_["rk"] = t_["rn"]
            t_all.append(t_)
        kdjs = [kdj, sb([128, 128], BF16, "r_kdj2")]
        bdjs = [bdj, sb([128, 128], BF16, "r_bdj2")]
        Z = sb([128, 8, 64], F32, "r_Z")
        Zb = sb([128, 8, 64], BF16, "r_Zb")
        P.op("pool", lambda e: e.memset(Z[:], 0.0), w=[Z])
        P.op("pool", lambda e: e.memset(Zb[:], 0.0), w=[Zb])
        Mlow = sb([128, 4, 128], BF16, "r_Mlow")
        Tt = sb([128, 4, 128], BF16, "r_Tt")
        MkT = sb([128, 4, 128], BF16, "r_MkT")
        AkT = sb([128, 4, 128], BF16, "r_AkT")
        AbT = sb([128, 4, 128], BF16, "r_AbT")
        rhsS = sb([128, 4, 64], BF16, "r_rhsS")
        SA = sb([128, 4, 64], BF16, "r_SA")
        ytok = sb([128, 16, 64], F32, "r_ytok")
        ysq = xx
        st = {nm: sb([128, 16], F32, "r_st_" + nm) for nm in ("sum", "ssq", "mean", "var", "rstd", "rkb")}
        zs = self.nb["xn"]
        rec1 = nc.dram_tensor("rwrec1", [NT, 128, 56 * 128], BF16, kind="Internal").ap()
        frec1 = nc.dram_tensor("rwfrec1", [NT, 128, 1168], F32, kind="Internal").ap()

        x2 = zs
        ss2 = sb([128, 1], F32, "r_ss2")
        ygT = _AL(lambda: xs_all[:, 3], "r_xs")
        s0in = sb([64, 2, 64], F32, "r_s0in")
        Z0b2 = [_AL(lambda q=q: xs_all[:, 2 + q].rearrange("p c (a f) -> p (c a) f", a=2), "r_xs") for q in range(2)]
        mk = _AL(lambda: xs_all[:, 0:2].rearrange("p a c f -> p (a c) f"), "r_xs")
        Vm = sb([128, 128], BF16, "r_Vm")
        yz0 = sb([128, 4, 64], F32, "r_yz0")
        mk2 = _AL(lambda: xs_all[:, 0:2].rearrange("p a c f -> p (a c) f"), "r_xs")
        SAm = sb([128, 128], BF16, "r_SAm")
        Zn = sb([128, 64], F32, "r_Zn")
        Sout = sb([64, 128], F32, "r_Sout")

        def fmh(tn, h):
            b0 = (h % 2) * 64
            return tn[b0:b0 + 64, h // 2, :]

        for t in range(NT):
            X = "p" if t < 17 else "s"
            nseq, T = (1, 128) if X == "p" else (16, 8)
            x1 = x1t[t % 2]
            xc, xp = xn[t % 2], xn[(t + 1) % 2]
            self.ld("sp", x1[:], x1scr[t], [x1], "ld1_x")
            self.act(xc[:], x1[:], AF.Square, [x1], [xc, ss2], accum_out=ss2[:])
            self.ts("dve", ss2[:], ss2[:], 1.0 / 1024, 1e-6, ALU.mult, ALU.add, [ss2], [ss2])
            self.act(ss2[:], ss2[:], AF.Ln, [ss2], [ss2])
            self.act(ss2[:], ss2[:], AF.Exp, [ss2], [ss2], scale=-0.5)
            self.stt(xc[:], x1[:], ss2[:, 0:1], nw1[:], ALU.mult, ALU.mult, [x1, ss2, nw1], [xc])
            if t == 16:
                self.st("sp", p_shift, xc[127:128, :], [xc], "st_c")
            if t == 17:
                f = fsz(xc)
                self.st("sp", s_shift, bass.AP(xc, 7 * f, [[8 * f, 16], [1, 1024]]), [xc], "st_c")
            for half in range(2):
                ps = self.bank()
                cs_ = slice(half * 512, (half + 1) * 512)
                if X == "p":
                    self.mm(ps, ps[:], shp[:], xc[:, cs_], True, False, [shp, xc])
                    self.mm(ps, ps[:], elast[:], xp[:, cs_], False, True, [elast, xp])
                else:
                    if half == 0:
                        self.ld("sp", xp[0:16, :], sshift, [xp], "ld_c")
                    self.mm(ps, ps[:], shs[:], xc[:, cs_], True, False, [shs, xc])
                    self.mm(ps, ps[:], selS[:], xp[0:16, cs_], False, True, [selS, xp])
                self.tt("dve", xx[:, cs_], ps[:], xc[:, cs_], ALU.subtract, [xc], [ps, xx])
            for (src, dst) in ((xc, xnT), (xx, xxT)):
                for half in range(2):
                    ps = self.bank()
                    for c in range(4):
                        cc = half * 4 + c
                        self.tr(ps, ps[:, c * 128:(c + 1) * 128], src[:, cc * 128:(cc + 1) * 128], self.ident32[:], [src, self.ident32])
                    self.cp(self.ev(), dst[:, half * 4:half * 4 + 4, :], ps[:].rearrange("p (c f) -> p c f", c=4), [], [ps, dst])
            def mkxs(i, dst):
                for c in range(8):
                    self.stt(dst[:, c, :], xxT[:, c, :], par[:, c, i:i + 1], xnT[:, c, :], ALU.mult, ALU.add, [xxT, xnT, par], [dst])
            for i in range(3):
                mkxs(i, xs[i])
            ps = self.bank()
            mkxs(4, xs[3])
            for c in range(8):
                self.mm(ps, ps[0:64, 0:128], w1a1[:, c, 0:64], xs[3][:, c, :], c == 0, c == 7, [w1a1, xs[3]])
            mkxs(5, xs[3])
            for c in range(8):
                self.mm(ps, ps[0:64, 128:256], w1a1[:, c, 64:128], xs[3][:, c, :], c == 0, c == 7, [w1a1, xs[3]])
            self.act(hT[:, 0, :], ps[0:64, 0:128], AF.Tanh, [], [ps, hT])
            self.cp("dve", hT[:, 1, :], ps[0:64, 128:256], [], [ps, hT])
            mkxs(3, xs[3])
            for half in range(2):
                ps = self.bank()
                for c in range(8):
                    self.mm(ps, ps[:], xs[3][:, c, :], wR[:, c, 3072 + half * 512:3072 + (half + 1) * 512], c == 0, c == 7, [xs[3], wR])
                self.act(zs[:, half * 512:(half + 1) * 512], ps[:], AF.Silu, [], [ps, zs])
            psRK = self.pfx
            pjb = {}

            def proj1(j):
                psA_ = self.bank()
                psB_ = self.bank()
                pjb[j] = (psA_, psB_)
                for i in range(3):
                    for c in range(8):
                        self.mm(psA_, psA_[:, i * 128:(i + 1) * 128], wR[:, c, i * 1024 + j * 128:i * 1024 + (j + 1) * 128], xs[i][:, c, :],
                                c == 0, c == 7, [wR, xs[i]])
                self.mm(psA_, psA_[:, 384:512], w2a2[:, 0, j * 128:(j + 1) * 128], hT[:, 0, :], True, True, [w2a2, hT])
                self.mm(psB_, psB_[:, 0:128], w2a2[:, 1, j * 128:(j + 1) * 128], hT[:, 1, :], True, True, [w2a2, hT])
            def elem(j):
                yield
                psA_, psB_ = pjb[j]
                yield
                t_ = t_all[j % 2]
                yield
                kdj, bdj = kdjs[j % 2], bdjs[j % 2]
                yield
                pj = lambda i: par[:, j, i:i + 1]
                yield
                yield
                self.act(t_["e"][:], psA_[:, 384:512], AF.Exp, [par], [psA_, t_["e"]], scale=-1.0, bias=pj(11))
                yield
                self.act(t_["e"][:], t_["e"][:], AF.Ln, [t_["e"]], [t_["e"]], bias=1.0)
                yield
                self.act(t_["ew"][:], t_["e"][:], AF.Exp, [t_["e"]], [t_["ew"]], scale=-1.0, bias=-0.5)
                yield
                self.act(t_["a"][:], psB_[:, 0:128], AF.Tanh, [par], [psB_, t_["a"]], bias=pj(13), scale=0.5)
                yield
                self.ts("dve", t_["a"][:], t_["a"][:], 0.5, 0.5, ALU.mult, ALU.add, [t_["a"]], [t_["a"]])
                yield
                self.cp("act", t_["r"][:], psA_[:, 0:128], [], [psA_, t_["r"]])
                yield
                self.cp("dve", t_["k"][:], psA_[:, 128:256], [], [psA_, t_["k"]])
                yield
                self.cp("act", vT[:, j, :], psA_[:, 256:384], [], [psA_, vT])
                yield
                self.ts("dve", t_["kk"][:], t_["k"][:], pj(8), None, ALU.mult, None, [t_["k"], par], [t_["kk"]])
                yield
                self.act(t_["sq"][:], t_["kk"][:], AF.Square, [t_["kk"]], [t_["sq"]])
                yield
                psn = self.bank()
                yield
                self.mm(psn, psn[:, 0:128], b64[:], t_["sq"][:], True, True, [b64, t_["sq"]])
                yield
                self.act(t_["rn"][:], psn[:, 0:128], AF.Ln, [], [psn, t_["rn"]], bias=1e-6)
                yield "ps_done"
                self.act(t_["rn"][:], t_["rn"][:], AF.Exp, [t_["rn"]], [t_["rn"]], scale=-0.5)
                yield
                self.tt("dve", t_["kk"][:], t_["kk"][:], t_["rn"][:], ALU.mult, [t_["kk"], t_["rn"]], [t_["kk"]])
                yield
                self.ts("dve", t_["x"][:], t_["a"][:], pj(9), pj(12), ALU.mult, ALU.add, [t_["a"], par], [t_["x"]])
                yield
                self.tt("dve", t_["k2"][:], t_["k"][:], t_["x"][:], ALU.mult, [t_["k"], t_["x"]], [t_["k2"]])
                yield
                self.tt("pool", t_["b"][:], t_["kk"][:], t_["a"][:], ALU.mult, [t_["kk"], t_["a"]], [t_["b"]])
                yield
                yield
                self.stt(t_["rk"][:], t_["r"][:], pj(10), t_["k2"][:], ALU.mult, ALU.mult, [t_["r"], t_["k2"], par], [t_["rk"]])
                yield
                self.mm(psRK, psRK[:, 2 * j:2 * j + 2], t_["rk"][:], hsel[:], True, True, [t_["rk"], hsel])
                yield
                yield
                msk = o32 if X == "p" else nb0
                yield
                P.op("dve", lambda e, msk=msk, t_=t_: e.tensor_tensor_scan(out=t_["cs"][:], data0=msk[:], data1=t_["ew"][:], initial=0.0,
                                                                    op0=ALU.mult, op1=ALU.add), r=[msk, t_["ew"]], w=[t_["cs"]])
                yield
                cs = t_["cs"]
                yield
                yield
                self.act(Pc[:, j, 0:nseq], V(cs, T - 1, [[T, nseq]]), AF.Exp, [cs], [Pc], scale=-1.0)
                yield
                yield
                self.tt("dve", t_["dd"][:].rearrange("p (s t) -> p s t", t=T), cs[:].rearrange("p (s t) -> p s t", t=T),
                        V(cs, T - 1, [[T, nseq], [0, T]]), ALU.subtract, [cs], [t_["dd"]])
                yield
                self.act(t_["dd"][:], t_["dd"][:], AF.Exp, [t_["dd"]], [t_["dd"]])
                yield
                self.tt("dve", kdj[:], t_["dd"][:], t_["k2"][:], ALU.mult, [t_["dd"], t_["k2"]], [kdj])
                yield
                self.tt("pool", bdj[:], t_["dd"][:], t_["b"][:], ALU.mult, [t_["dd"], t_["b"]], [bdj])
                yield
                self.tr(self.pb[0], self.pb[0][:, j * 128:(j + 1) * 128], kdj[:], self.identb[:], [kdj, self.identb])
                yield
                self.tr(self.pb[1], self.pb[1][:, j * 128:(j + 1) * 128], bdj[:], self.identb[:], [bdj, self.identb])
                yield
                yield
                self.act(t_["x"][:], cs[:], AF.Exp, [cs], [t_["x"]])
                yield
                self.tt("dve", fm["kt"][:, j, :], t_["x"][:], t_["k2"][:], ALU.mult, [t_["x"], t_["k2"]], [fm["kt"]])
                yield
                self.tt("pool", fm["bt"][:, j, :], t_["x"][:], t_["b"][:], ALU.mult, [t_["x"], t_["b"]], [fm["bt"]])
                yield
                yield
                self.act(t_["x"][:], cs[:], AF.Exp, [cs], [t_["x"]], scale=-1.0)
                yield
                self.tt("dve", fm["rho"][:, j, :], t_["x"][:], t_["r"][:], ALU.mult, [t_["x"], t_["r"]], [fm["rho"]])
                yield
                self.tt("pool", t_["e"][:], cs[:], t_["ew"][:], ALU.subtract, [cs, t_["ew"]], [t_["e"]])
                yield
                self.act(t_["e"][:], t_["e"][:], AF.Exp, [t_["e"]], [t_["e"]], scale=-1.0)
                yield
                self.tt("dve", fm["kap"][:, j, :], t_["e"][:], t_["kk"][:], ALU.mult, [t_["e"], t_["kk"]], [fm["kap"]])

            proj1(0)
            proj1(1)
            for pr in range(4):
                gens = [elem(2 * pr), elem(2 * pr + 1)]
                ndone = 0
                projected = (pr == 3)
                while gens:
                    for g_ in list(gens):
                        try:
                            r_ = next(g_)
                            if r_ == "ps_done":
                                ndone += 1
                        except StopIteration:
                            gens.remove(g_)
                    if ndone == 2 and not projected:
                        proj1(2 * pr + 2)
                        proj1(2 * pr + 3)
                        projected = True
            self.cp("dve", st["rkb"][:], psRK[:, 0:16], [], [psRK, st["rkb"]])
            self.cp("act", kdT[:], self.pb[0][:].rearrange("p (c f) -> p c f", c=8), [], [self.pb[0], kdT])
            self.cp("dve", bdT[:], self.pb[1][:].rearrange("p (c f) -> p c f", c=8), [], [self.pb[1], bdT])
            pb = self.bbank()
            for c in range(8):
                self.tr(pb, pb[:, c * 128:(c + 1) * 128], vT[:, c, :], self.identb[:], [vT, self.identb])
            self.cp("act", Vb[:], pb[:], [], [pb, Vb])
            for i_, src_ in enumerate((fm["kap"], fm["rho"], fm["kt"], fm["bt"], kdT, bdT)):
                self.st("sp", rec1[t, :, i_ * 1024:(i_ + 1) * 1024], src_[:].rearrange("p c f -> p (c f)"), [src_], "st_r1%d" % i_)
            self.st("sp", rec1[t, :, 6144:7168], Vb[:], [Vb], "st_r16")
            self.st("sp", frec1[t, :, 0:1024], zs[:], [zs], "st_r17")
            self.st("sp", frec1[t, :, 1024:1152], Pc[:].rearrange("p c s -> p (c s)"), [Pc], "st_r18")
            self.st("sp", frec1[t, :, 1152:1168], st["rkb"][:], [st["rkb"]], "st_r19")
        P.barrier()
        P.flush()
        es.close()
        self.es_w1.close()

        es = ExitStack()
        cur[0] = es
        wO = sb([128, 8, 1024], BF16, "wO1")
        for c in range(8):
            P.dma("pool", lambda e, c=c: e.dma_start(out=wO[:, c, :], in_=wo_d[c * 128:(c + 1) * 128, :]), w=[wO], sem="ld_w")
        fnw = sb([128, 1024], F32, "fnw")
        lnw = sb([128, 1024], F32, "lnw")
        lnb = sb([128, 1024], F32, "lnb")
        self.ld("sp", fnw[:], pbc(self.d_fnorm_w, 1024), [fnw], "ld_c")
        self.ld("sp", lnw[:], pbc(lw_d, 1024), [lnw], "ld_c")
        self.ld("sp", lnb[:], pbc(lb_d, 1024), [lnb], "ld_c")
        recs = [sb([128, 56, 128], BF16, "b_rec%d" % i) for i in range(2)]
        frecs = [sb([128, 1168], F32, "b_frec%d" % i) for i in range(2)]
        x1t = [sb([128, 1024], F32, "b_x1")] * 2
        Z = sb([128, 8, 64], F32, "b_Z")
        Zb = sb([128, 8, 64], BF16, "b_Zb")
        P.op("pool", lambda e: e.memset(Z[:], 0.0), w=["Z%d" % i for i in range(8)])
        P.op("pool", lambda e: e.memset(Zb[:], 0.0), w=["Zb%d" % i for i in range(8)])
        sets = []
        for gi in range(4):
            B = {nm: sb([128, 4, 128], BF16, "h%d_%s" % (gi, nm)) for nm in ("Mlow", "Tt", "MkT", "AkT", "AbT")}
            B["rhsS"] = sb([128, 4, 64], BF16, "h%d_rhsS" % gi)
            B["SA"] = sb([128, 4, 64], BF16, "h%d_SA" % gi)
            B["yz0"] = sb([128, 4, 64], F32, "h%d_yz0" % gi)
            B["W"] = self.inv_bufs(sb, "h%d" % gi)
            sets.append(B)
        for gi in range(2):
            sh = {"s0in4": sb([64, 4, 2, 64], F32, "h%d_s0in4" % gi), "Vm4": sb([128, 4, 128], BF16, "h%d_Vm4" % gi),
                  "SAm4": sb([128, 4, 128], BF16, "h%d_SAm4" % gi), "Zn4": sb([128, 4, 64], F32, "h%d_Zn4" % gi),
                  "Sout4": sb([64, 4, 128], F32, "h%d_Sout4" % gi)}
            sets[gi].update(sh)
            sets[gi + 2].update(sh)
        ytoks = [sb([128, 16, 64], F32, "b_ytok%d" % i) for i in range(2)]
        ysq = self.nb["xn"]
        st = {nm: sb([128, 16], F32, "b_st_" + nm) for nm in ("sum", "ssq", "mean", "var", "rstd")}
        ss2 = sb([128, 1], F32, "b_ss2")
        ygT = sb([128, 8, 128], BF16, "b_ygT")
        x2 = ysq
        Z0b2 = [sb([128, 16, 64], BF16, "b_Z0b%d" % i) for i in range(2)]
        mk = sb([128, 16, 128], BF16, "b_mk")
        Sout = sb([64, 128], F32, "b_Sout")
        KAP, RHO, KT, BT, KD, BD, VB = 0, 8, 16, 24, 32, 40, 48
        v4 = lambda ps: ps[:, 0:512].rearrange("p (h f) -> p h f", h=4)

        def rwkv_group(t, g, B, rec, frec, ytok, ytn):
            X = "p" if t < 17 else "s"
            hs = [4 * g + hh for hh in range(4)]
            order = (0, 2, 1, 3)

            def fmh(off, h):
                b0 = (h % 2) * 64
                return rec[b0:b0 + 64, off + h // 2, :]
            Vh = lambda h: rec[:, VB + h // 2, (h % 2) * 64:(h % 2) * 64 + 64]
            Mlow, Tt, MkT, AkT, AbT, rhsS, SA, yz0 = (B[k_] for k_ in ("Mlow", "Tt", "MkT", "AkT", "AbT", "rhsS", "SA", "yz0"))

            def prod4(ps, lo, ro):
                for i_, hh in enumerate(order):
                    h = hs[hh]
                    self.mm(ps, ps[:, hh * 128:(hh + 1) * 128], fmh(lo, h), fmh(ro, h), True, True, [rec], sw=(i_ == 2))
            psM = self.bank()
            prod4(psM, KAP, BT)
            self.tt("dve", Mlow[:], v4(psM), self.bc4(mneg[X]), ALU.mult, [mneg[X]], [psM, Mlow])
            for (lo, ro, dst, msk) in ((KT, KAP, MkT, msup[X]), (KT, RHO, AkT, mup[X]), (BT, RHO, AbT, mup[X])):
                ps = self.bank()
                prod4(ps, lo, ro)
                self.tt("dve", dst[:], v4(ps), self.bc4(msk), ALU.mult, [msk], [ps, dst])
            yield
            yield from self.inverse_gen(B["W"], Mlow, Tt)
            s0in4, Vm4, SAm4, Zn4, Sout4 = (B[k_] for k_ in ("s0in4", "Vm4", "SAm4", "Zn4", "Sout4"))
            if X == "s":
                for q in range(2):
                    m = 2 * g + q
                    for sg in range(4):
                        for h2_ in range(2):
                            self.ld("sp", s0in4[:, :, h2_, :], srwkv[4 * sg:4 * sg + 4, 2 * m + h2_].rearrange("s v k -> v s k"), [s0in4],
                                    "ld_s0in%d" % (g % 2))
                        pz = self.bank()
                        for si in range(4):
                            self.tr(pz, pz[:, si * 64:(si + 1) * 64], s0in4[:, si].rearrange("v h k -> v (h k)"), self.ident32[0:64, 0:64],
                                    [s0in4, self.ident32])
                        self.cp(self.ev(), Z0b2[q][:, 4 * sg:4 * sg + 4, :], pz[:, 0:256].rearrange("p (s f) -> p s f", s=4), [],
                                [pz, Z0b2[q]])
            psR = self.bank()
            if X == "s":
                psY2 = self.bank()
            for hh, h in enumerate(hs):
                b0 = (h % 2) * 64
                m = h // 2
                if X == "s":
                    Z0b = Z0b2[hh // 2]
                    for (off_, psd, lastflag) in ((KAP, psR, False), (RHO, psY2, True)):
                        a_ = fmh(off_, h)
                        self.tt("pool", mk[b0:b0 + 64], bass.AP(a_.tensor, a_.offset, [list(a_.ap[0]), [0, 16], [1, 128]]),
                                self.cm[b0:b0 + 64], ALU.mult, [rec, self.cm], [mk])
                        for s_ in range(16):
                            self.mm(psd, psd[:, hh * 64:(hh + 1) * 64], mk[b0:b0 + 64, s_, :], Z0b[b0:b0 + 64, s_, :], s_ == 0,
                                    lastflag and s_ == 15, [mk, Z0b], sw=(s_ == 0))
                else:
                    self.mm(psR, psR[:, hh * 64:(hh + 1) * 64], fmh(KAP, h), Zb[b0:b0 + 64, m, :], True, False, [rec, "Zb%d" % m], sw=True)
                self.mm(psR, psR[:, hh * 64:(hh + 1) * 64], MkT[:, hh, :], Vh(h), False, True, [MkT, rec])
            if X == "s":
                self.cp("dve", yz0[:].rearrange("p h f -> p (h f)"), psY2[:, 0:256], [], [psY2, yz0])
            self.act(rhsS[:].rearrange("p h f -> p (h f)"), psR[:, 0:256], AF.Copy, [], [psR, rhsS], scale=-1.0)
            yield
            psS = self.bank()
            for hh in range(4):
                self.mm(psS, psS[:, hh * 64:(hh + 1) * 64], Tt[:, hh, :], rhsS[:, hh, :], True, True, [Tt, rhsS])
            self.cp("act", SA[:].rearrange("p h f -> p (h f)"), psS[:, 0:256], [], [psS, SA])
            yield
            psY = self.bank()
            for hh, h in enumerate(hs):
                b0 = (h % 2) * 64
                m = h // 2
                if X == "p":
                    self.mm(psY, psY[:, hh * 64:(hh + 1) * 64], fmh(RHO, h), Zb[b0:b0 + 64, m, :], True, False, [rec, "Zb%d" % m], sw=True)
                self.mm(psY, psY[:, hh * 64:(hh + 1) * 64], AkT[:, hh, :], Vh(h), X == "s", False, [AkT, rec])
                self.mm(psY, psY[:, hh * 64:(hh + 1) * 64], AbT[:, hh, :], SA[:, hh, :], False, True, [AbT, SA])
            yv = ytok[:, 4 * g:4 * g + 4, :].rearrange("p h f -> p (h f)")
            if X == "p":
                self.cp("dve", yv, psY[:, 0:256], [], [psY, ytn + str(g)])
            else:
                self.tt("dve", yv, psY[:, 0:256], yz0[:].rearrange("p h f -> p (h f)"), ALU.add, [yz0], [psY, ytn + str(g)])
            for mm_ in range(2):
                m = 2 * g + mm_
                SApair = SA[:, 2 * mm_:2 * mm_ + 2, :].rearrange("p h f -> p (h f)")
                if X == "p":
                    psZ = self.bank()
                    self.mm(psZ, psZ[:, 0:128], rec[:, KD + m, :], rec[:, VB + m, :], True, False, [rec])
                    self.mm(psZ, psZ[:, 0:128], rec[:, BD + m, :], SApair, False, True, [rec, SA])
                    for h2 in range(2):
                        b0 = h2 * 64
                        self.stt(Z[b0:b0 + 64, m, :], Z[b0:b0 + 64, m, :], V(frec, 1024 + m * 16, [[1, 1]], 64, b0),
                                 psZ[b0:b0 + 64, b0:b0 + 64], ALU.mult, ALU.add, ["Z%d" % m, frec], [psZ, "Z%d" % m])
                    self.cp("act", Zb[:, m, :], Z[:, m, :], ["Z%d" % m], ["Zb%d" % m])
                    if t == 16:
                        pz = self.bank()
                        self.tr(pz, pz[0:64, 0:128], Z[:, m, :], self.ident32[:], ["Z%d" % m, self.ident32])
                        self.cp("act", Sout[:], pz[0:64, 0:128], [], [pz, Sout])
                        self.st("sp", p_rwkv[2 * m:2 * m + 2].rearrange("h v k -> v h k"), Sout[:].rearrange("v (h k) -> v h k", h=2),
                                [Sout], "st_so")
                else:
                    for sg in range(4):
                        for h2_ in range(2):
                            self.ld("sp", s0in4[:, :, h2_, :], srwkv[4 * sg:4 * sg + 4, 2 * m + h2_].rearrange("s v k -> v s k"), [s0in4],
                                    "ld_s0in%d" % (g % 2))
                        pz = self.bank()
                        for si in range(4):
                            self.tr(pz, pz[:, si * 64:(si + 1) * 64], s0in4[:, si].rearrange("v h k -> v (h k)"), self.ident32[0:64, 0:64],
                                    [s0in4, self.ident32])
                        vpair = rec[:, VB + m, :]
                        smb = V(self.sm, 4 * sg, [[1, 4], [0, 128]])
                        self.tt("pool", Vm4[:], bass.AP(vpair.tensor, vpair.offset, [list(vpair.ap[0]), [0, 4], [1, 128]]), smb, ALU.mult,
                                [rec, self.sm], [Vm4])
                        sap = SA[:, 2 * mm_:2 * mm_ + 2, :]
                        self.tt("pool", SAm4[:], bass.AP(sap.tensor, sap.offset, [list(sap.ap[0]), [0, 4], [1, 128]]), smb, ALU.mult,
                                [SA, self.sm], [SAm4])
                        psZ = self.bank()
                        for si in range(4):
                            self.mm(psZ, psZ[:, si * 128:(si + 1) * 128], rec[:, KD + m, :], Vm4[:, si, :], True, False, [rec, Vm4])
                            self.mm(psZ, psZ[:, si * 128:(si + 1) * 128], rec[:, BD + m, :], SAm4[:, si, :], False, True, [rec, SAm4])
                        self.tt("dve", Zn4[:], pz[:, 0:256].rearrange("p (s f) -> p s f", s=4),
                                V(frec, 1024 + m * 16 + 4 * sg, [[1, 4], [0, 64]]), ALU.mult, [frec], [pz, Zn4])
                        for h2 in range(2):
                            b0 = h2 * 64
                            self.tt("dve", Zn4[b0:b0 + 64], Zn4[b0:b0 + 64],
                                    psZ[b0:b0 + 64, 0:512].rearrange("p (s f) -> p s f", s=4)[:, :, b0:b0 + 64], ALU.add, [Zn4], [psZ, Zn4])
                        pz2 = self.bank()
                        for si in range(4):
                            self.tr(pz2, pz2[0:64, si * 128:(si + 1) * 128], Zn4[:, si, :], self.ident32[:], [Zn4, self.ident32])
                        self.cp("act", Sout4[:].rearrange("v s f -> v (s f)"), pz2[0:64, 0:512], [], [pz2, Sout4])
                        for h2_ in range(2):
                            self.st("sp", s_rwkv[4 * sg:4 * sg + 4, 2 * m + h2_].rearrange("s v k -> v s k"),
                                    Sout4[:, :, h2_ * 64:(h2_ + 1) * 64], [Sout4], "st_so%d" % (g % 2))
                        yield
            yield

        def post_gen(t, rec, frec, ytok, ytn):
            x1 = x1t[0]
            ytokr = [ytn + str(g_) for g_ in range(4)]
            self.ld("sp", x1[:], x1scr[t], [x1], "x")
            P.op("dve", lambda e: e.tensor_reduce(out=st["sum"][:], in_=ytok[:], axis=AX.X, op=ALU.add), r=ytokr, w=[st["sum"]])
            ysq3 = ysq[:].rearrange("p (h f) -> p h f", h=16)
            self.act(ysq3, ytok[:], AF.Square, ytokr, [ysq])
            P.op("dve", lambda e: e.tensor_reduce(out=st["ssq"][:], in_=ysq3, axis=AX.X, op=ALU.add), r=[ysq], w=[st["ssq"]])
            yield
            self.ts("dve", st["mean"][:], st["sum"][:], 1.0 / 64, None, ALU.mult, None, [st["sum"]], [st["mean"]])
            self.tt("dve", st["var"][:], st["mean"][:], st["mean"][:], ALU.mult, [st["mean"]], [st["var"]])
            self.stt(st["var"][:], st["ssq"][:], 1.0 / 64, st["var"][:], ALU.mult, ALU.subtract, [st["ssq"], st["var"]], [st["var"]])
            self.ts("dve", st["var"][:], st["var"][:], 64e-5, None, ALU.add, None, [st["var"]], [st["var"]])
            self.act(st["rstd"][:], st["var"][:], AF.Ln, [st["var"]], [st["rstd"]])
            self.act(st["rstd"][:], st["rstd"][:], AF.Exp, [st["rstd"]], [st["rstd"]], scale=-0.5)
            yield
            self.tt("dve", ysq3, ytok[:], V(st["mean"], 0, [[1, 16], [0, 64]]), ALU.subtract, ytokr + [st["mean"]], [ysq])
            self.tt("dve", ysq3, ysq3, V(st["rstd"], 0, [[1, 16], [0, 64]]), ALU.mult, [ysq, st["rstd"]], [ysq])
            yield
            yf = ysq[:]
            self.tt("pool", yf, yf, lnw[:], ALU.mult, [ysq, lnw], [ysq])
            self.tt("pool", yf, yf, lnb[:], ALU.add, [ysq, lnb], [ysq])
            yield
            yt3 = ytok[:]
            self.tt("dve", yt3, rec[:, VB:VB + 8, :].rearrange("p c (a f) -> p (c a) f", a=2), V(frec, 1152, [[1, 16], [0, 64]]), ALU.mult,
                    [rec, frec] + ytokr, ytokr)
            self.tt("dve", yf, yf, ytok[:].rearrange("p h f -> p (h f)"), ALU.add, [ysq] + ytokr, [ysq])
            self.tt("dve", yf, yf, frec[:, 0:1024], ALU.mult, [ysq, frec], [ysq])
            yield
            for half in range(2):
                ps = self.bank()
                for c in range(4):
                    cc = half * 4 + c
                    self.tr(ps, ps[:, c * 128:(c + 1) * 128], yf[:, cc * 128:(cc + 1) * 128], self.ident32[:], [ysq, self.ident32])
                self.cp(self.ev(), ygT[:, half * 4:half * 4 + 4, :], ps[:].rearrange("p (c f) -> p c f", c=4), [], [ps, ygT])
                yield
            for half in range(2):
                ps = self.bank()
                cs_ = slice(half * 512, (half + 1) * 512)
                for c in range(8):
                    self.mm(ps, ps[:], ygT[:, c, :], wO[:, c, cs_], c == 0, c == 7, [ygT, wO])
                self.tt("dve", x2[:, cs_], ps[:], x1[:, cs_], ALU.add, [x1], [ps, x2])
                yield
            yy = ytok
            yyv = ytok[:].rearrange("p h f -> p (h f)")
            self.act(yyv, x2[:], AF.Square, [x2], ytokr + [ss2], accum_out=ss2[:])
            self.ts("dve", ss2[:], ss2[:], 1.0 / 1024, 1e-6, ALU.mult, ALU.add, [ss2], [ss2])
            self.act(ss2[:], ss2[:], AF.Ln, [ss2], [ss2])
            self.act(ss2[:], ss2[:], AF.Exp, [ss2], [ss2], scale=-0.5)
            yield
            self.stt(yyv, x2[:], ss2[:, 0:1], fnw[:], ALU.mult, ALU.mult, [x2, ss2, fnw], ytokr)
            self.st("sp", y[t], yyv, ytokr, "stc_y")

        pending = None
        for t in range(NT):
            rec, frec = recs[t % 2], frecs[t % 2]
            ytok, ytn = ytoks[t % 2], "yt%d_" % (t % 2)
            self.ld("sp", rec[:].rearrange("p a f -> p (a f)"), rec1[t], [rec], "x")
            self.ld("sp", frec[:], frec1[t], [frec], "x")
            gens = [rwkv_group(t, g, sets[g], rec, frec, ytok, ytn) for g in range(4)]
            if pending is not None:
                gens.append(pending)
            self.run_rr(gens)
            pending = post_gen(t, rec, frec, ytok, ytn)
        self.run_rr([pending])
        P.barrier()
        P.flush()
        es.close()
        P.es.close()


_CACHE = {}


def kernel(x_prompt, x_sample, state_gdn, state_gdn_conv, state_rwkv, state_rwkv_shift,
           meta_tokens, norm_w, final_norm_w,
           gdn_w_in, gdn_conv_w, gdn_a_log, gdn_dt_bias, gdn_norm_w, gdn_w_out,
           rwkv_mu, rwkv_w_rkvz, rwkv_w0, rwkv_w1, rwkv_w2, rwkv_a0, rwkv_a1, rwkv_a2,
           rwkv_k_k, rwkv_k_a, rwkv_r_k, rwkv_lnx_w, rwkv_lnx_b, rwkv_w_o):
    f = lambda a: np.ascontiguousarray(np.asarray(a, dtype=np.float32))
    if "nc" not in _CACHE:
        _CACHE["nc"] = K().nc
    nc = _CACHE["nc"]
    x_prompt, x_sample = f(x_prompt), f(x_sample)
    meta = f(meta_tokens)
    in_maps = []
    for c in range(8):
        xin = np.zeros((NT, 128, 1024), np.float32)
        xin[0, 112:128] = meta
        xin[1:17] = x_prompt[c].reshape(16, 128, 1024)
        xin[17] = x_sample[16 * c:16 * c + 16].reshape(128, 1024)
        in_maps.append({
            "xin": xin,
            "sgdn": f(state_gdn[0, 16 * c:16 * c + 16]),
            "sconv": f(state_gdn_conv[0, 16 * c:16 * c + 16]).reshape(48, 4096),
            "srwkv": f(state_rwkv[0, 16 * c:16 * c + 16]),
            "sshift": f(state_rwkv_shift[0, 16 * c:16 * c + 16]),
            "norm_w": f(norm_w), "fnorm_w": f(final_norm_w).reshape(1, 1024),
            "w_in": f(gdn_w_in[0]), "conv_w": f(gdn_conv_w[0]), "a_log": f(gdn_a_log), "dt_bias": f(gdn_dt_bias),
            "gnorm_w": f(gdn_norm_w), "w_out": f(gdn_w_out[0]),
            "mu": f(rwkv_mu[0]), "rkvz": f(rwkv_w_rkvz[0]), "w0": f(rwkv_w0), "w1": f(rwkv_w1[0]), "w2": f(rwkv_w2[0]),
            "a0": f(rwkv_a0), "a1": f(rwkv_a1[0]), "a2": f(rwkv_a2[0]), "k_k": f(rwkv_k_k), "k_a": f(rwkv_k_a),
            "r_k": f(rwkv_r_k).reshape(1, 1024), "lnx_w": f(rwkv_lnx_w), "lnx_b": f(rwkv_lnx_b), "w_o": f(rwkv_w_o[0]),
        })
    res = run_bass_kernel_spmd(nc, in_maps, core_ids=list(range(8)))
    R = res.results
    y_prompt = np.stack([R[c]["y"][1:17].reshape(2048, 1024) for c in range(8)])
    y_sample = np.concatenate([R[c]["y"][17].reshape(16, 8, 1024) for c in range(8)])
    p_gdn = np.stack([R[c]["p_gdn"] for c in range(8)])[None]
    p_conv = np.stack([R[c]["p_conv"] for c in range(8)])[None]
    s_gdn = np.concatenate([R[c]["s_gdn"] for c in range(8)])[None]
    s_conv = np.concatenate([R[c]["s_conv"].reshape(16, 3, 4096) for c in range(8)])[None]
    p_rwkv = np.stack([R[c]["p_rwkv"] for c in range(8)])[None]
    p_shift = np.stack([R[c]["p_shift"].reshape(1024) for c in range(8)])[None]
    s_rwkv = np.concatenate([R[c]["s_rwkv"] for c in range(8)])[None]
    s_shift = np.concatenate([R[c]["s_shift"] for c in range(8)])[None]
    return (y_prompt, y_sample, p_gdn, p_conv, p_rwkv, p_shift, s_gdn, s_conv, s_rwkv, s_shift)
```

```python
from contextlib import ExitStack
import numpy as np
import concourse.bass as bass
import concourse.mybir as mybir
from concourse.bass_utils import run_bass_kernel_spmd

F32 = mybir.dt.float32
BF16 = mybir.dt.bfloat16
AF = mybir.ActivationFunctionType
ALU = mybir.AluOpType
AX = mybir.AxisListType
ENGS = ("pe", "act", "dve", "pool", "sp")
NT = 18
BIG = 30000.0


class Prog:
    CE = ("pe", "act", "dve", "pool")

    def __init__(self, nc):
        self.nc = nc
        self.es = ExitStack()
        self.q = {e: [] for e in ENGS}
        self.cnt = {}
        self.waited = {e: {} for e in ENGS}
        self.lastw = {}
        self.reads = {}
        self.semkeys = set(ENGS)
        self.ntens = 0
        self.nops = 0
        self.nidx = {e: 0 for e in self.CE}
        self.marks = {e: [] for e in self.CE}
        self.entry = {e: {} for e in self.CE}

    def sb(self, shape, dt=F32, name=None):
        self.ntens += 1
        name = name or f"t{self.ntens}"
        return self.es.enter_context(self.nc.sbuf_tensor(name, list(shape), dt))

    def ps(self, shape, dt=F32, name=None):
        self.ntens += 1
        name = name or f"p{self.ntens}"
        return self.es.enter_context(self.nc.psum_tensor(name, list(shape), dt))

    @staticmethod
    def _k(x):
        return x if isinstance(x, str) else x.name

    def _deps(self, r, w):
        toks = []
        for x in r:
            t = self.lastw.get(x)
            if t is not None:
                toks.append(t)
        for x in w:
            t = self.lastw.get(x)
            if t is not None:
                toks.append(t)
            toks.extend(self.reads.get(x, {}).values())
        return toks

    def _record(self, r, w, tok):
        key = tok[1]
        for x in r:
            d = self.reads.setdefault(x, {})
            o = d.get(key)
            if o is None or o[2] < tok[2]:
                d[key] = tok
        for x in w:
            self.lastw[x] = tok
            self.reads[x] = {}

    def _resolve(self, tok, peek=False):
        if tok[0] == "D":
            return tok[1], tok[2]
        _, eng, idx = tok
        m = self.marks[eng]
        if m and m[-1][0] >= idx:
            lo, hi = 0, len(m) - 1
            while lo < hi:
                mid = (lo + hi) // 2
                if m[mid][0] >= idx:
                    hi = mid
                else:
                    lo = mid + 1
            return eng, m[lo][1]
        ent = self.entry[eng][idx]
        cnt = len(m) + 1
        m.append((idx, cnt))
        ent[2] = (eng, 1)
        return eng, cnt

    def _waits(self, eng, toks, pe_self=False):
        need = {}
        for tok in toks:
            if tok[0] == "E" and tok[1] == eng and eng == "pe" and not pe_self:
                continue
            k, v = self._resolve(tok)
            if self.waited[eng].get(k, 0) >= v:
                continue
            need[k] = max(need.get(k, 0), v)
        for k, v in need.items():
            self.waited[eng][k] = v
        return list(need.items())

    def op(self, eng, fn, r=(), w=(), pe_self=False):
        r = [self._k(x) for x in r]
        w = [self._k(x) for x in w]
        waits = self._waits(eng, self._deps(r, w), pe_self)
        self.nidx[eng] += 1
        idx = self.nidx[eng]
        ent = [waits, fn, None]
        self.entry[eng][idx] = ent
        self.q[eng].append(ent)
        tok = ("E", eng, idx)
        self._record(r, w, tok)
        self.nops += 1
        return tok

    def dma(self, q, fn, r=(), w=(), sem=None):
        r = [self._k(x) for x in r]
        w = [self._k(x) for x in w]
        sem = ("L_" + w[0]) if w else ("S_" + r[0])
        self.semkeys.add(sem)
        waits = self._waits(q, self._deps(r, w))
        self.cnt[sem] = self.cnt.get(sem, 0) + 16
        tok = ("D", sem, self.cnt[sem])
        self.q[q].append([waits, fn, (sem, 16)])
        self._record(r, w, tok)
        self.nops += 1
        return tok

    def _all_toks(self):
        toks = [("E", e, self.nidx[e]) for e in self.CE if self.nidx[e] > 0]
        toks += [("D", k, v) for k, v in self.cnt.items()]
        return toks

    def barrier(self):
        toks = self._all_toks()
        for e in ENGS:
            w = self._waits(e, [t for t in toks if not (t[0] == "E" and t[1] == e)])
            if w:
                self.q[e].append([w, None, None])

    def flush(self):
        nc = self.nc
        if not hasattr(self, "sems"):
            self.sems = {}
        for k in sorted(self.semkeys):
            if k not in self.sems:
                self.sems[k] = self.es.enter_context(nc.semaphore("s_" + k))
        sems = self.sems
        q = self.q
        self.q = {e: [] for e in ENGS}
        self.entry = {e: {} for e in self.CE}

        def run(e, lst):
            for waits, fn, inc in lst:
                for k, v in waits:
                    e.wait_ge(sems[k], v)
                if fn is not None:
                    inst = fn(e)
                    if inc is not None:
                        inst.then_inc(sems[inc[0]], inc[1])

        with nc.Block() as block:
            @block.tensor
            def _(e):
                run(e, q["pe"])

            @block.scalar
            def _(e):
                run(e, q["act"])

            @block.vector
            def _(e):
                run(e, q["dve"])

            @block.gpsimd
            def _(e):
                run(e, q["pool"])

            @block.sync
            def _(e):
                run(e, q["sp"])


def fsz(t):
    n = 1
    for s in list(t.shape)[1:]:
        n *= int(s)
    return n


def V(t, off, dims, npart=128, p0=0):
    f = fsz(t)
    return bass.AP(t, p0 * f + off, [[f, npart]] + [list(d) for d in dims])


class K:
    def __init__(self, nlayers=2):
        self.nlayers = nlayers
        nc = bass.Bass("TRN2", target_bir_lowering=False)
        self.nc = nc
        self.P = Prog(nc)
        self.rr = {"pf": 0, "pb": 0, "ev": 0}
        self.build()

    def din(self, name, shape):
        return self.nc.dram_tensor(name, list(shape), F32, kind="ExternalInput").ap()

    def dout(self, name, shape):
        return self.nc.dram_tensor(name, list(shape), F32, kind="ExternalOutput").ap()

    def bank(self):
        self.rr["pf"] = (self.rr["pf"] + 1) % len(self.pf)
        return self.pf[self.rr["pf"]]

    def bbank(self):
        self.rr["pb"] = (self.rr["pb"] + 1) % len(self.pb)
        return self.pb[self.rr["pb"]]

    def ev(self):
        self.rr["ev"] ^= 1
        return "act" if self.rr["ev"] else "dve"

    def mm(self, ps, out, lhsT, rhs, start, stop, r, sw=None):
        self.P.op("pe", lambda e: e.matmul(out, lhsT=lhsT, rhs=rhs, start=start, stop=stop), r=r, w=[ps],
                  pe_self=(getattr(self, "pe_self", False) if sw is None else sw))

    def tr(self, ps, out, in_, ident, r):
        self.P.op("pe", lambda e: e.transpose(out=out, in_=in_, identity=ident), r=r, w=[ps])

    def tt(self, eng, out, in0, in1, op, r, w):
        self.P.op(eng, lambda e: e.tensor_tensor(out=out, in0=in0, in1=in1, op=op), r=r, w=w)

    def ts(self, eng, out, in0, s1, s2, op0, op1, r, w):
        if op1 is None:
            self.P.op(eng, lambda e: e.tensor_scalar(out=out, in0=in0, scalar1=s1, scalar2=None, op0=op0), r=r, w=w)
        else:
            self.P.op(eng, lambda e: e.tensor_scalar(out=out, in0=in0, scalar1=s1, scalar2=s2, op0=op0, op1=op1), r=r, w=w)

    def stt(self, out, in0, scalar, in1, op0, op1, r, w):
        self.P.op("dve", lambda e: e.scalar_tensor_tensor(out=out, in0=in0, scalar=scalar, in1=in1, op0=op0, op1=op1), r=r, w=w)

    def act(self, out, in_, func, r, w, bias=None, scale=None, accum_out=None):
        kw = {}
        if bias is not None:
            kw["bias"] = bias
        if scale is not None:
            kw["scale"] = scale
        if accum_out is not None:
            kw["accum_out"] = accum_out
        self.P.op("act", lambda e: e.activation(out=out, in_=in_, func=func, **kw), r=r, w=w)

    def cp(self, eng, out, in_, r, w):
        if eng == "act":
            self.P.op("act", lambda e: e.copy(out=out, in_=in_), r=r, w=w)
        else:
            self.P.op(eng, lambda e: e.tensor_copy(out=out, in_=in_), r=r, w=w)

    def ld(self, q, out, in_, w, sem, r=()):
        self.P.dma(q, lambda e: e.dma_start(out=out, in_=in_), r=r, w=w, sem=sem)

    def st(self, q, out, in_, r, sem):
        self.P.dma(q, lambda e: e.dma_start(out=out, in_=in_), r=r, w=(), sem=sem)

    def consts(self):
        P = self.P
        sb = P.sb
        self.ones32 = sb([128, 128], F32, "ones32")
        self.ident32 = sb([128, 128], F32, "ident32")
        self.identb = sb([128, 128], BF16, "identb")
        P.op("pool", lambda e: e.memset(self.ones32[:], 1.0), w=[self.ones32])

        def sel(out, in_, pattern, op, base, cm, r, w, fill=0.0):
            P.op("pool", lambda e: e.affine_select(out=out, in_=in_, pattern=pattern, compare_op=op, fill=fill,
                                                    base=base, channel_multiplier=cm), r=r, w=w)
        o = self.ones32
        sel(self.ident32[:], o[:], [[-1, 128]], ALU.is_equal, 0, 1, [o], [self.ident32])
        self.cp("pool", self.identb[:], self.ident32[:], [self.ident32], [self.identb])
        blk8 = sb([128, 128], F32, "blk8")
        blk32 = sb([128, 128], F32, "blk32")
        for (t, B) in ((blk8, 8), (blk32, 32)):
            v3 = t[:].rearrange("p (s t) -> p s t", t=B)
            o3 = o[:].rearrange("p (s t) -> p s t", t=B)
            sel(v3, o3, [[-B, 128 // B], [0, B]], ALU.is_ge, 0, 1, [o], [t])
            sel(v3, v3, [[B, 128 // B], [0, B]], ALU.is_ge, B - 1, -1, [t], [t])
        low = sb([128, 128], F32, "low")
        up = sb([128, 128], F32, "up")
        sel(low[:], o[:], [[-1, 128]], ALU.is_ge, 0, 1, [o], [low])
        sel(up[:], o[:], [[1, 128]], ALU.is_ge, 0, -1, [o], [up])
        self.lowT = {}
        self.same = {}
        self.addm = {}
        for X in ("p", "s"):
            lowT = sb([128, 128], F32, "lowT" + X)
            lowX = sb([128, 128], F32, "lowX" + X)
            am = sb([128, 4, 128], F32, "addm" + X)
            if X == "p":
                self.cp("pool", lowT[:], up[:], [up], [lowT])
                self.cp("pool", lowX[:], low[:], [low], [lowX])
                self.same[X] = self.ones32
            else:
                self.tt("pool", lowT[:], up[:], blk8[:], ALU.mult, [up, blk8], [lowT])
                self.tt("pool", lowX[:], low[:], blk8[:], ALU.mult, [low, blk8], [lowX])
                self.same[X] = blk8
            for h in range(4):
                self.ts("pool", am[:, h, :], lowX[:], -BIG, BIG, ALU.mult, ALU.add, [lowX], [am])
            self.lowT[X] = lowT
            self.addm[X] = am
        slow = sb([128, 128], F32, "slow")
        sup = sb([128, 128], F32, "sup")
        sel(slow[:], o[:], [[-1, 128]], ALU.is_gt, 0, 1, [o], [slow])
        sel(sup[:], o[:], [[1, 128]], ALU.is_gt, 0, -1, [o], [sup])
        self.m1 = sb([128, 128], BF16, "m1")
        self.m1T = sb([128, 128], BF16, "m1T")
        self.m2 = sb([128, 128], BF16, "m2")
        self.tt("pool", self.m1[:], blk32[:], slow[:], ALU.mult, [blk32, slow], [self.m1])
        self.tt("pool", self.m1T[:], blk32[:], sup[:], ALU.mult, [blk32, sup], [self.m1T])
        self.ts("pool", self.m2[:], blk32[:], -1.0, 1.0, ALU.mult, ALU.add, [blk32], [self.m2])
        self.sm = sb([128, 16], F32, "sm")
        sel(self.sm[:], o[:, 0:16], [[-8, 16]], ALU.is_ge, 0, 1, [o], [self.sm])
        sel(self.sm[:], self.sm[:], [[8, 16]], ALU.is_ge, 7, -1, [self.sm], [self.sm])
        self.cm = sb([128, 16, 128], BF16, "cm")
        P.op("pool", lambda e: e.memset(self.cm[:], 1.0), w=[self.cm])
        sel(self.cm[:], self.cm[:], [[-8, 16], [1, 128]], ALU.is_ge, 0, 0, [self.cm], [self.cm])
        sel(self.cm[:], self.cm[:], [[8, 16], [-1, 128]], ALU.is_ge, 7, 0, [self.cm], [self.cm])

    def bc4(self, m):
        return V(m, 0, [[0, 4], [1, 128]])

    def inverse(self, Mlow, Tt, tag):
        for _ in self.inverse_gen(self.invw, Mlow, Tt):
            pass

    def inverse_gen(self, W, Mlow, Tt, extra=None):
        Ib = self.bc4(self.identb)
        v4 = lambda ps: ps[:, 0:512].rearrange("p (h f) -> p h f", h=4)
        MdT, Md, No = W["MdT"], W["Md"], W["No"]
        self.tt("pool", MdT[:], Mlow[:], self.bc4(self.m1), ALU.mult, [Mlow, self.m1], [MdT])
        pb = self.bbank()
        for hh in range(4):
            self.tr(pb, pb[:, hh * 128:(hh + 1) * 128], Mlow[:, hh, :], self.identb[:], [Mlow, self.identb])
        self.tt("dve", Md[:], v4(pb), self.bc4(self.m1T), ALU.mult, [self.m1T], [pb, Md])
        self.tt("dve", No[:], v4(pb), self.bc4(self.m2), ALU.mult, [self.m2], [pb, No])
        X = W["X0"]
        self.tt("pool", X[:], Md[:], Ib, ALU.add, [Md, self.identb], [X])
        if extra is not None:
            extra()
        yield
        Pk, PTk = Md, MdT
        IpPT_prev = None
        for k in range(1, 5):
            last = (k == 4)
            if IpPT_prev is not None:
                psX = self.bank()
                for hh in range(4):
                    self.mm(psX, psX[:, hh * 128:(hh + 1) * 128], IpPT_prev[:, hh, :], X[:, hh, :], True, True, [IpPT_prev, X])
            psPT = self.bank()
            for hh in range(4):
                self.mm(psPT, psPT[:, hh * 128:(hh + 1) * 128], Pk[:, hh, :], PTk[:, hh, :], True, True, [Pk, PTk])
            if not last:
                psP = self.bank()
                for hh in range(4):
                    self.mm(psP, psP[:, hh * 128:(hh + 1) * 128], PTk[:, hh, :], Pk[:, hh, :], True, True, [Pk, PTk])
            if IpPT_prev is not None:
                Xn = W["X%d" % ((k - 1) % 2)]
                self.cp("act", Xn[:], v4(psX), [], [psX, Xn])
                X = Xn
            IpPT = W["IpPT%d" % (k % 2)]
            if not last:
                PTn = W["PT%d" % (k % 2)]
                Pn = W["P%d" % (k % 2)]
                self.cp("act", PTn[:], v4(psPT), [], [psPT, PTn])
                self.cp("dve", Pn[:], v4(psP), [], [psP, Pn])
                self.tt("pool", IpPT[:], PTn[:], Ib, ALU.add, [PTn, self.identb], [IpPT])
                Pk, PTk = Pn, PTn
            else:
                self.tt("dve", IpPT[:], v4(psPT), Ib, ALU.add, [self.identb], [psPT, IpPT])
            IpPT_prev = IpPT
            yield
        psX = self.bank()
        for hh in range(4):
            self.mm(psX, psX[:, hh * 128:(hh + 1) * 128], IpPT_prev[:, hh, :], X[:, hh, :], True, True, [IpPT_prev, X])
        Xn = W["X0"]
        self.cp(self.ev(), Xn[:], v4(psX), [], [psX, Xn])
        X = Xn
        yield
        pb = self.bbank()
        for hh in range(4):
            self.tr(pb, pb[:, hh * 128:(hh + 1) * 128], X[:, hh, :], self.identb[:], [X, self.identb])
        XT = W["XT"]
        self.cp(self.ev(), XT[:], v4(pb), [], [pb, XT])
        yield
        psV = self.bank()
        psVT = self.bank()
        for hh in range(4):
            self.mm(psV, psV[:, hh * 128:(hh + 1) * 128], XT[:, hh, :], No[:, hh, :], True, True, [XT, No])
        for hh in range(4):
            self.mm(psVT, psVT[:, hh * 128:(hh + 1) * 128], No[:, hh, :], XT[:, hh, :], True, True, [XT, No])
        Vm, VT, IpVT = W["P0"], W["P1"], W["PT0"]
        self.cp("act", Vm[:], v4(psV), [], [psV, Vm])
        self.cp("dve", VT[:], v4(psVT), [], [psVT, VT])
        self.tt("pool", IpVT[:], VT[:], Ib, ALU.add, [VT, self.identb], [IpVT])
        yield
        ps2 = self.bank()
        for hh in range(4):
            self.mm(ps2, ps2[:, hh * 128:(hh + 1) * 128], Vm[:, hh, :], VT[:, hh, :], True, True, [Vm, VT])
        IpV2T = W["PT1"]
        self.tt("dve", IpV2T[:], v4(ps2), Ib, ALU.add, [self.identb], [ps2, IpV2T])
        yield
        psY = self.bank()
        for hh in range(4):
            self.mm(psY, psY[:, hh * 128:(hh + 1) * 128], IpV2T[:, hh, :], X[:, hh, :], True, True, [IpV2T, X])
        Y = W["MdT"]
        self.cp(self.ev(), Y[:], v4(psY), [], [psY, Y])
        yield
        psT = self.bank()
        for hh in range(4):
            self.mm(psT, psT[:, hh * 128:(hh + 1) * 128], IpVT[:, hh, :], Y[:, hh, :], True, True, [IpVT, Y])
        self.cp(self.ev(), Tt[:], v4(psT), [], [psT, Tt])
        yield

    @staticmethod
    def run_rr(gens):
        gens = list(gens)
        while gens:
            for g in list(gens):
                try:
                    next(g)
                except StopIteration:
                    gens.remove(g)

    def inv_bufs(self, alloc, tag):
        return {nm: alloc([128, 4, 128], BF16, "%s_%s" % (tag, nm))
                for nm in ("MdT", "Md", "No", "X0", "X1", "P0", "P1", "PT0", "PT1", "IpPT0", "IpPT1", "XT")}

    def norm_T(self, xt, nwbc, xnT, xn_keep=None, bank=None):
        P = self.P
        ss, xn = self.nb["ss"], (xn_keep if xn_keep is not None else self.nb["xn"])
        self.act(xn[:], xt[:], AF.Square, [xt], [xn, ss], accum_out=ss[:])
        self.ts("dve", ss[:], ss[:], 1.0 / 1024, 1e-6, ALU.mult, ALU.add, [ss], [ss])
        self.act(ss[:], ss[:], AF.Ln, [ss], [ss])
        self.act(ss[:], ss[:], AF.Exp, [ss], [ss], scale=-0.5)
        self.stt(xn[:], xt[:], ss[:, 0:1], nwbc, ALU.mult, ALU.mult, [xt, ss, "normw"], [xn])
        for half in range(2):
            ps = bank() if bank is not None else self.bank()
            for c in range(4):
                cc = half * 4 + c
                self.tr(ps, ps[:, c * 128:(c + 1) * 128], xn[:, cc * 128:(cc + 1) * 128], self.ident32[:], [xn, self.ident32])
            self.cp(self.ev(), xnT[:, half * 4:half * 4 + 4, :], ps[:].rearrange("p (c f) -> p c f", c=4), [], [ps, xnT])

    def build(self):
        P = self.P
        nc = self.nc
        sb = P.sb
        din, dout = self.din, self.dout
        xin = din("xin", [NT, 128, 1024])
        sgdn = din("sgdn", [16, 16, 128, 128])
        sconv = din("sconv", [48, 4096])
        srwkv = din("srwkv", [16, 16, 64, 64])
        sshift = din("sshift", [16, 1024])
        self.d_srwkv, self.d_sshift = srwkv, sshift
        norm_w = din("norm_w", [2, 1024])
        fnorm_w = din("fnorm_w", [1, 1024])
        self.d_norm_w, self.d_fnorm_w = norm_w, fnorm_w
        w_in = din("w_in", [1024, 6176])
        conv_w = din("conv_w", [4096, 4])
        a_log = din("a_log", [1, 16])
        dt_bias = din("dt_bias", [1, 16])
        gnorm_w = din("gnorm_w", [1, 128])
        w_out = din("w_out", [2048, 1024])
        y = dout("y", [NT, 128, 1024])
        p_gdn = dout("p_gdn", [16, 128, 128])
        p_conv = dout("p_conv", [3, 4096])
        s_gdn = dout("s_gdn", [16, 16, 128, 128])
        s_conv = dout("s_conv", [48, 4096])
        oscr = nc.dram_tensor("oscr", [NT, 128, 2048], F32, kind="Internal").ap()
        x1scr = nc.dram_tensor("x1scr", [NT, 128, 1024], F32, kind="Internal").ap()

        self.pf = [P.ps([128, 512], F32, "pf%d" % i) for i in range(5)]
        self.pfx = P.ps([128, 512], F32, "pfx")
        self.pb = [P.ps([128, 1024], BF16, "pb%d" % i) for i in range(2)]
        self.consts()

        def pbc(ap_row, n):
            return bass.AP(ap_row.tensor, ap_row.offset, [[0, 128], [1, n]])

        nw0 = sb([128, 1024], F32, "normw")
        self.nw0 = nw0
        self.ld("sp", nw0[:], pbc(norm_w[0:1, :], 1024), [nw0], "ld_c")
        dtb = sb([128, 16], F32, "dtb")
        nea = sb([128, 16], F32, "nea")
        self.ld("sp", dtb[:], pbc(dt_bias, 16), [dtb], "ld_c")
        self.ld("sp", nea[:], pbc(a_log, 16), [nea], "ld_c")
        self.act(nea[:], nea[:], AF.Exp, [nea], [nea])
        self.ts("dve", nea[:], nea[:], -1.0, None, ALU.mult, None, [nea], [nea])
        gnw = sb([128, 128], F32, "gnw")
        self.ld("sp", gnw[:], pbc(gnorm_w, 128), [gnw], "ld_c")
        cw = sb([128, 32, 4], F32, "cw")
        self.ld("sp", cw[:], conv_w.rearrange("(j p) k -> p j k", p=128), [cw], "ld_c")

        self.nb = {"ss": sb([128, 1], F32, "ss"), "xn": sb([128, 1024], F32, "xn")}

        recA = nc.dram_tensor("recA", [NT, 128, 40 * 128], BF16, kind="Internal").ap()
        srecA = nc.dram_tensor("srecA", [NT, 128, 336], F32, kind="Internal").ap()
        es_a = ExitStack()

        def sba(shape, dt=F32, name=None):
            P.ntens += 1
            return es_a.enter_context(nc.sbuf_tensor(name or f"a{P.ntens}", list(shape), dt))

        NQ = 4128
        wA = sba([128, 8, NQ], BF16, "wA")
        for c in range(8):
            src = w_in[c * 128:(c + 1) * 128, :]
            P.dma("pool", lambda e, c=c, src=src: e.dma_start(out=wA[:, c, 0:4096], in_=src[:, 0:4096]), w=[wA], sem="ld_w")
            P.dma("pool", lambda e, c=c, src=src: e.dma_start(out=wA[:, c, 4096:4128], in_=src[:, 6144:6176]), w=[wA], sem="ld_w")
        xt = [sba([128, 1024], F32, "xt%d" % i) for i in range(2)]
        xnTs = [sba([128, 8, 128], BF16, "xnT%d" % i) for i in range(2)]
        halo = sba([128, 32, 3], F32, "halo")
        P.op("pool", lambda e: e.memset(halo[:], 0.0), w=["halo%d" % j for j in range(32)])
        cst_in = sba([48, 512], F32, "cst_in")
        cstT = sba([128, 32, 48], F32, "cstT")
        for jg in range(8):
            self.ld("sp", cst_in[:], sconv[:, jg * 512:(jg + 1) * 512], [cst_in], "ld_c")
            ps = self.bank()
            for jj in range(4):
                self.tr(ps, ps[:, jj * 48:(jj + 1) * 48], cst_in[:, jj * 128:(jj + 1) * 128], self.ident32[0:48, 0:48],
                        [cst_in, self.ident32])
            self.cp(self.ev(), cstT[:, jg * 4:jg * 4 + 4, :], ps[:, 0:192].rearrange("p (j f) -> p j f", j=4), [], [ps, cstT])
        cstN = cstT
        U = [sba([128, 176], BF16, "U%d" % i) for i in range(4)]
        wdiag = sba([128, 128, 128], BF16, "wdiag")
        for q4 in range(4):
            self.tt("pool" if q4 % 2 == 0 else "dve", wdiag[:, q4 * 32:(q4 + 1) * 32, :], V(self.ident32, 0, [[0, 32], [1, 128]]),
                    V(cw, q4 * 32, [[1, 32], [0, 128]]), ALU.mult, [self.ident32, cw], [wdiag])

        c4 = [sba([128, 4, 128], F32, "c4_%d" % i) for i in range(2)]
        sq4 = sba([128, 4, 128], BF16, "sq4")
        self.onesb = sba([128, 128], BF16, "onesb")
        P.op("pool", lambda e: e.memset(self.onesb[:], 1.0), w=[self.onesb])
        rn4 = sba([128, 4, 128], F32, "rn4")
        recs = [sba([128, 40, 128], BF16, "rec%d" % i) for i in range(2)]
        srecs = [sba([128, 336], F32, "srec%d" % i) for i in range(2)]
        scs = [{nm: sba([128, 16], F32, "sc%d_%s" % (i, nm)) for nm in
                ("beta", "g", "gc", "glt", "egc", "negbege", "ekd", "negb", "tmp")} for i in range(2)]
        gms = [sba([128, 16, 16], F32, "gm%d" % i) for i in range(2)]
        egls = [sba([128, 16, 16], F32, "egl%d" % i) for i in range(2)]

        def bcs(t, g):
            return V(t, 4 * g, [[1, 4], [0, 128]])

        def prologue(t):
            X = "p" if t < 17 else "s"
            nseq, T = (1, 128) if X == "p" else (16, 8)
            x = xt[t % 2]
            srec = srecs[t % 2]
            xnT, sc, gm, egl = xnTs[t % 2], scs[t % 2], gms[t % 2], egls[t % 2]
            pbank = lambda: self.pfx
            self.ld("sp", x[:], xin[t], [x], "ld_x%d" % (t % 2))
            self.norm_T(x, nw0[:], xnT, bank=pbank)
            ps = pbank()
            for c in range(8):
                self.mm(ps, ps[:, 0:32], xnT[:, c, :], wA[:, c, 4096:4128], c == 0, c == 7, [xnT, wA])
            self.act(sc["beta"][:], ps[:, 0:16], AF.Sigmoid, [], [ps, sc["beta"]])
            self.tt("dve", sc["tmp"][:], ps[:, 16:32], dtb[:], ALU.add, [dtb], [ps, sc["tmp"]])
            self.act(sc["tmp"][:], sc["tmp"][:], AF.Exp, [sc["tmp"]], [sc["tmp"]])
            self.act(sc["tmp"][:], sc["tmp"][:], AF.Ln, [sc["tmp"]], [sc["tmp"]], bias=1.0)
            self.tt("dve", sc["g"][:], sc["tmp"][:], nea[:], ALU.mult, [sc["tmp"], nea], [sc["g"]])
            ps = pbank()
            self.mm(ps, ps[:, 0:16], self.lowT[X][:], sc["g"][:], True, True, [self.lowT[X], sc["g"]])
            self.mm(ps, ps[:, 16:32], self.same[X][:], sc["g"][:], True, True, [self.same[X], sc["g"]])
            self.cp("dve", sc["gc"][:], ps[:, 0:16], [], [ps, sc["gc"]])
            self.cp("dve", sc["glt"][:], ps[:, 16:32], [], [ps, sc["glt"]])
            self.act(sc["egc"][:], sc["gc"][:], AF.Exp, [sc["gc"]], [sc["egc"]])
            self.stt(sc["negbege"][:], sc["egc"][:], -1.0, sc["beta"][:], ALU.mult, ALU.mult, [sc["egc"], sc["beta"]], [sc["negbege"]])
            self.tt("dve", sc["tmp"][:], sc["glt"][:], sc["gc"][:], ALU.subtract, [sc["glt"], sc["gc"]], [sc["tmp"]])
            self.act(sc["ekd"][:], sc["tmp"][:], AF.Exp, [sc["tmp"]], [sc["ekd"]])
            self.ts("dve", sc["negb"][:], sc["beta"][:], -1.0, None, ALU.mult, None, [sc["beta"]], [sc["negb"]])
            gmv = gm[:, 0:nseq, :]
            self.tt("pool", gmv, V(sc["g"], 0, [[0, nseq], [1, 16]]), V(self.sm, 0, [[1, nseq], [0, 16]]) if X == "s"
                    else V(self.ones32, 0, [[0, 1], [1, 16]]), ALU.mult, [sc["g"], self.sm, self.ones32], [gm])
            ps = pbank()
            self.mm(ps, ps[:, 0:nseq * 16], self.ones32[:], gm[:, 0:nseq, :].rearrange("p s h -> p (s h)"), True, True,
                    [self.ones32, gm])
            self.act(egl[:, 0:nseq, :].rearrange("p s h -> p (s h)"), ps[:, 0:nseq * 16], AF.Exp, [], [ps, egl])

            for i_, nm in enumerate(("gc", "egc", "negbege", "ekd", "negb")):
                self.cp("pool", srec[:, 16 * i_:16 * i_ + 16], sc[nm][:], [sc[nm]], [srec])
            self.cp("pool", srec[:, 80:336], egl[:].rearrange("p s h -> p (s h)"), [egl], [srec])

        for t in range(NT):
            X = "p" if t < 17 else "s"
            nseq, T = (1, 128) if X == "p" else (16, 8)
            if t == 0:
                prologue(0)
            xnT, sc = xnTs[t % 2], scs[t % 2]
            rec = recs[t % 2]
            srec = srecs[t % 2]

            class _RV:
                def __init__(s_, o, name):
                    s_.o, s_.name = o, name

                def __getitem__(s_, idx):
                    a, b, c_ = idx
                    if isinstance(b, slice):
                        b = slice((b.start or 0) + s_.o, (b.stop if b.stop is not None else 0) + s_.o)
                    else:
                        b = b + s_.o
                    return rec[a, b, c_]
            qT, kT, Ktok, vb = _RV(0, rec.name), _RV(8, rec.name), _RV(16, rec.name), _RV(24, rec.name)
            psUs = {}

            def proj(jg):
                psU = self.bank()
                psUs[jg] = psU
                for jj in range(4):
                    j = jg * 4 + jj
                    for c in range(8):
                        self.mm(psU, psU[:, jj * 128:(jj + 1) * 128], wA[:, c, j * 128:(j + 1) * 128], xnT[:, c, :], c == 0, c == 7,
                                [wA, xnT])
            psCs = {}

            def views(j):
                jg, jj = j // 4, j % 4
                u, cc = U[j % 4], c4[jg % 2]
                if X == "p":
                    return u, cc, (lambda k: u[:, k:k + 128])
                u3 = u[:].rearrange("p (s t) -> p s t", t=11)
                return u, cc, (lambda k: u3[:, :, k:k + 8])

            def stA(j):
                jg, jj = j // 4, j % 4
                psU = psUs[jg]
                u, cc, uv = views(j)
                pu = psU[:, jj * 128:(jj + 1) * 128]
                uh, ub, hj = "uh%d" % (j % 4), "ub%d" % (j % 4), "halo%d" % j
                if X == "p":
                    self.cp("pool", u[:, 0:3], halo[:, j, :], [hj], [uh])
                    self.cp("dve", u[:, 3:131], pu, [], [psU, ub])
                    self.cp("dve", halo[:, j, :], psU[:, jj * 128 + 125:jj * 128 + 128], [], [psU, hj])
                else:
                    u3 = u[:].rearrange("p (s t) -> p s t", t=11)
                    pu3 = pu.rearrange("p (s t) -> p s t", t=8)
                    self.cp("pool", u3[:, :, 0:3], cstT[:, j, :].rearrange("p (s k) -> p s k", k=3), [hj, cstT], [uh])
                    self.cp("dve", u3[:, :, 3:11], pu3, [], [psU, ub])
                    self.cp("dve", cstN[:, j, :].rearrange("p (s k) -> p s k", k=3), pu3[:, :, 5:8], [], [psU, hj])

            def stB(j):
                jg, jj = j // 4, j % 4
                if jj == 0:
                    psCs[jg] = self.bank()
                psC = psCs[jg]
                u, cc, uv = views(j)
                for k in range(4):
                    self.mm(psC, psC[:, jj * 128:(jj + 1) * 128], wdiag[:, j * 4 + k, :], uv(k), k == 0, k == 3,
                            [wdiag, "uh%d" % (j % 4), "ub%d" % (j % 4)])

            def stC(j):
                jg, jj = j // 4, j % 4
                u, cc, uv = views(j)
                psC = psCs[jg]
                self.act(cc[:, jj, :], psC[:, jj * 128:(jj + 1) * 128], AF.Silu, [], [psC, cc])
                if jj != 3:
                    return
                if jg < 4:
                    self.tt("pool", sq4[:], cc[:], cc[:], ALU.mult, [cc], [sq4])
                    psN = self.bank()
                    for j2 in range(4):
                        self.mm(psN, psN[:, j2 * 128:(j2 + 1) * 128], self.onesb[:], sq4[:, j2, :], True, True, [self.onesb, sq4])
                    self.act(rn4[:], psN[:].rearrange("p (h f) -> p h f", h=4), AF.Ln, [], [psN, rn4], bias=1e-6)
                    lnsc = float(np.log(128.0 ** -0.5)) if jg < 2 else 0.0
                    self.act(rn4[:], rn4[:], AF.Exp, [rn4], [rn4], scale=-0.5, bias=lnsc)
                    dst = qT if jg < 2 else kT
                    o0 = (jg % 2) * 4
                    self.tt("dve", dst[:, o0:o0 + 4, :], cc[:], rn4[:], ALU.mult, [cc, rn4], [dst])
                else:
                    gv = jg - 4
                    psT = self.bank()
                    for j2 in range(4):
                        self.tr(psT, psT[:, j2 * 128:(j2 + 1) * 128], cc[:, j2, :], self.ident32[:], [cc, self.ident32])
                    self.tt("dve", vb[:, gv * 4:gv * 4 + 4, :], psT[:].rearrange("p (h f) -> p h f", h=4), bcs(sc["beta"], gv),
                            ALU.mult, [sc["beta"]], [psT, vb])

            proj(0)
            for step in range(36):
                if step < 32:
                    if step % 4 == 1 and step // 4 + 1 < 8:
                        proj(step // 4 + 1)
                    stA(step)
                if step == 14 and t + 1 < NT:
                    prologue(t + 1)
                if 0 <= step - 2 < 32:
                    stB(step - 2)
                if 0 <= step - 4 < 32:
                    stC(step - 4)
            pb = self.bbank()
            for kh in range(8):
                self.tr(pb, pb[:, kh * 128:(kh + 1) * 128], kT[:, kh, :], self.identb[:], [kT, self.identb])
            self.cp("act", rec[:, 16:24, :], pb[:].rearrange("p (h f) -> p h f", h=8), [], [pb, rec])
            if t == 16 or t == 17:
                src, ncol, dst = (halo, 3, p_conv) if t == 16 else (cstN, 48, s_conv)
                stg = cst_in
                for jg in range(8):
                    ps = self.bank()
                    for jj in range(4):
                        j = jg * 4 + jj
                        self.tr(ps, ps[0:ncol, jj * 128:(jj + 1) * 128], src[:, j, :], self.ident32[:], [src, self.ident32, "halo%d" % j])
                    self.cp(self.ev(), stg[0:ncol, :], ps[0:ncol, :], [], [ps, stg])
                    self.st("sp", dst[:, jg * 512:(jg + 1) * 512], stg[0:ncol, :], [stg], "st_c")

            self.st("sp", recA[t], rec[:].rearrange("p a f -> p (a f)"), [rec], "st_rec%d" % (t % 2))
            self.st("sp", srecA[t], srec[:], [srec], "st_srec%d" % (t % 2))
        P.barrier()
        P.flush()
        es_a.close()

        es_a = ExitStack()
        rec2 = [sba([128, 40, 128], BF16, "r2ec%d" % i) for i in range(2)]
        srec2 = [sba([128, 336], F32, "s2rec%d" % i) for i in range(2)]
        gcd = [sba([128, 16, 128], F32, "gcd%d" % i) for i in range(2)]
        Sf = sba([128, 16, 128], F32, "Sf")
        Sb = sba([128, 16, 128], BF16, "Sb")
        P.op("pool", lambda e: e.memset(Sf[:], 0.0), w=["Sf%d" % i for i in range(4)])
        P.op("pool", lambda e: e.memset(Sb[:], 0.0), w=["Sb%d" % i for i in range(4)])
        sets = []
        for gi in range(4):
            B = {nm: sba([128, 4, 128], BF16, "g%d_%s" % (gi, nm)) for nm in
                 ("E1", "E1nb", "Mlow", "Tt", "r4", "vn", "vnd", "attn", "attnT")}
            B["o4"] = sba([128, 4, 128], F32, "g%d_o4" % gi)
            B["W"] = self.inv_bufs(sba, "g%d" % gi)
            sets.append(B)
        S0bs = [sba([128, 16, 128], BF16, "S0b%d" % i) for i in range(2)]
        S0f32s = [sba([128, 16, 128], F32, "S0f32_%d" % i) for i in range(2)]
        mks = [sba([128, 16, 128], BF16, "mk%d" % i) for i in range(2)]
        S0f = [sba([128, 4, 128], F32, "S0f%d" % i) for i in range(2)]
        Sn = [sba([128, 4, 128], F32, "Sn%d" % i) for i in range(2)]
        vnds = [sba([128, 128], BF16, "vnds%d" % i) for i in range(2)]
        v4 = lambda ps: ps[:, 0:512].rearrange("p (h f) -> p h f", h=4)

        def gdn_group(t, g, B, rec, srec, gcd_t):
            X = "p" if t < 17 else "s"
            hs = [4 * g + hh for hh in range(4)]
            khs = [h // 2 for h in hs]
            qT = lambda kh: rec[:, kh, :]
            kT = lambda kh: rec[:, 8 + kh, :]
            Kt = lambda kh: rec[:, 16 + kh, :]
            sv_ = lambda off: V(srec, off + 4 * g, [[1, 4], [0, 128]])
            E1, E1nb, Mlow, Tt, r4, vn, vnd, attn, attnT, o4 = (B[k_] for k_ in
                                                               ("E1", "E1nb", "Mlow", "Tt", "r4", "vn", "vnd", "attn", "attnT", "o4"))
            psR = self.bank()
            self.mm(psR, psR[:], self.ones32[:], gcd_t[:, 4 * g:4 * g + 4, :].rearrange("p h f -> p (h f)"), True, False,
                    [self.ones32, gcd_t])
            self.mm(psR, psR[:], self.ident32[:], self.addm[X][:].rearrange("p h f -> p (h f)"), False, True,
                    [self.ident32, self.addm[X]])
            self.tt("dve", v4(psR), v4(psR), sv_(0), ALU.subtract, [srec], [psR])
            self.act(E1[:], v4(psR), AF.Exp, [], [psR, E1], scale=-1.0)
            self.tt("pool", E1nb[:], E1[:], sv_(64), ALU.mult, [E1, srec], [E1nb])
            yield
            psG = self.bank()
            for hh in range(4):
                self.mm(psG, psG[:, hh * 128:(hh + 1) * 128], kT(khs[hh]), kT(khs[hh]), True, True, [rec])
            psQ = self.bank()
            for hh in range(4):
                self.mm(psQ, psQ[:, hh * 128:(hh + 1) * 128], qT(khs[hh]), kT(khs[hh]), True, True, [rec])
            self.tt("dve", Mlow[:], v4(psG), E1nb[:], ALU.mult, [E1nb], [psG, Mlow])
            self.tt("dve", attn[:], v4(psQ), E1[:], ALU.mult, [E1], [psQ, attn])
            yield

            def extra():
                pb = self.bbank()
                for hh in range(4):
                    self.tr(pb, pb[:, hh * 128:(hh + 1) * 128], attn[:, hh, :], self.identb[:], [attn, self.identb])
                self.cp("act", attnT[:], v4(pb), [], [pb, attnT])
            yield from self.inverse_gen(B["W"], Mlow, Tt, extra)
            S0b, S0f32, mk = S0bs[g % 2], S0f32s[g % 2], mks[g % 2]
            psK = self.bank()
            if X == "p":
                for hh in range(4):
                    self.mm(psK, psK[:, hh * 128:(hh + 1) * 128], kT(khs[hh]), Sb[:, hs[hh], :], True, True, [rec, "Sb%d" % g])
            else:
                psA = self.bank()
                for hh in range(4):
                    h = hs[hh]
                    self.ld("sp", S0f32[:], sgdn[:, h].rearrange("s p v -> p s v"), [S0f32], "ld_s0b")
                    self.cp("act", S0b[:], S0f32[:], [S0f32], [S0b])
                    for (srcf, psd) in ((kT, psK), (qT, psA)):
                        a_ = srcf(khs[hh])
                        self.tt("pool", mk[:], bass.AP(a_.tensor, a_.offset, [list(a_.ap[0]), [0, 16], [1, 128]]), self.cm[:], ALU.mult,
                                [rec, self.cm], [mk])
                        for s_ in range(16):
                            self.mm(psd, psd[:, hh * 128:(hh + 1) * 128], mk[:, s_, :], S0b[:, s_, :], s_ == 0, s_ == 15, [mk, S0b])
                self.tt("dve", o4[:], v4(psA), sv_(16), ALU.mult, [srec], [psA, o4])
            self.tt("dve", v4(psK), v4(psK), sv_(32), ALU.mult, [srec], [psK])
            self.tt("dve", r4[:], v4(psK), rec[:, 24 + 4 * g:24 + 4 * g + 4, :], ALU.add, [rec], [psK, r4])
            yield
            psV = self.bank()
            for hh in range(4):
                self.mm(psV, psV[:, hh * 128:(hh + 1) * 128], Tt[:, hh, :], r4[:, hh, :], True, True, [Tt, r4])
            self.cp("act", vn[:], v4(psV), [], [psV, vn])
            self.tt("dve", vnd[:], v4(psV), sv_(48), ALU.mult, [srec], [psV, vnd])
            yield
            psB = self.bank()
            for hh in range(4):
                self.mm(psB, psB[:, hh * 128:(hh + 1) * 128], attnT[:, hh, :], vn[:, hh, :], True, True, [attnT, vn])
            if X == "p":
                psA = self.bank()
                for hh in range(4):
                    self.mm(psA, psA[:, hh * 128:(hh + 1) * 128], qT(khs[hh]), Sb[:, hs[hh], :], True, True, [rec, "Sb%d" % g])
                self.tt("dve", o4[:], v4(psA), sv_(16), ALU.mult, [srec], [psA, o4])
            self.tt("dve", o4[:], v4(psB), o4[:], ALU.add, [o4], [psB, o4])
            self.st("sp", oscr[t, :, g * 512:(g + 1) * 512], o4[:].rearrange("p h f -> p (h f)"), [o4], "st_o%d" % g)
            if X == "p":
                psS = self.bank()
                for hh in range(4):
                    self.mm(psS, psS[:, hh * 128:(hh + 1) * 128], Kt(khs[hh]), vnd[:, hh, :], True, True, [rec, vnd])
                sv = Sf[:, 4 * g:4 * g + 4, :]
                SfN, SbN = "Sf%d" % g, "Sb%d" % g
                self.tt("pool", sv, sv, V(srec, 80 + 4 * g, [[1, 4], [0, 128]]), ALU.mult, [srec, SfN], [SfN])
                self.tt("dve", sv, v4(psS), sv, ALU.add, [SfN], [psS, SfN])
                self.cp("act", Sb[:, 4 * g:4 * g + 4, :], sv, [SfN], [SbN])
                if t == 16:
                    self.st("sp", p_gdn[4 * g:4 * g + 4].rearrange("h p v -> p h v"), sv, [SfN], "st_pg")
            else:
                for hh in range(4):
                    h = hs[hh]
                    for sg in range(4):
                        i2 = (hh * 4 + sg) % 2
                        s0f = S0f[i2]
                        self.ld("sp", s0f[:], sgdn[sg * 4:sg * 4 + 4, h].rearrange("s p v -> p s v"), [s0f], "ld_s0f%d" % i2)
                        psS = self.bank()
                        for si in range(4):
                            s_ = sg * 4 + si
                            vs = vnds[s_ % 2]
                            self.act(vs[:], vnd[:, hh, :], AF.Copy, [vnd, self.sm], [vs], scale=self.sm[:, s_:s_ + 1])
                            self.mm(psS, psS[:, si * 128:(si + 1) * 128], Kt(khs[hh]), vs[:], True, True, [rec, vs])
                        sn = Sn[i2]
                        self.tt("pool", sn[:], s0f[:], V(srec, 80 + sg * 4 * 16 + h, [[16, 4], [0, 128]]), ALU.mult, [s0f, srec], [sn])
                        self.tt("dve", sn[:], v4(psS), sn[:], ALU.add, [sn], [psS, sn])
                        self.st("sp", s_gdn[sg * 4:sg * 4 + 4, h].rearrange("s p v -> p s v"), sn[:], [sn], "st_sn%d" % i2)
            yield

        for t in range(NT):
            rec, srec, gcd_t = rec2[t % 2], srec2[t % 2], gcd[t % 2]
            self.ld("sp", rec[:].rearrange("p a f -> p (a f)"), recA[t], [rec], "ld_rec%d" % (t % 2))
            self.ld("sp", srec[:], srecA[t], [srec], "ld_srec%d" % (t % 2))
            self.tt("pool", gcd_t[:], V(self.ident32, 0, [[0, 16], [1, 128]]), V(srec, 0, [[1, 16], [0, 128]]), ALU.mult,
                    [self.ident32, srec], [gcd_t])
            self.run_rr([gdn_group(t, g, sets[g], rec, srec, gcd_t) for g in range(4)])
        P.barrier()
        P.flush()
        es_a.close()

        self.pm = {(nm, X_): P.sb([128, 128], BF16, nm + X_) for nm in ("mup", "msup", "mneg") for X_ in ("p", "s")}
        self.es_w1 = ExitStack()
        P.ntens += 1
        self.wR = self.es_w1.enter_context(nc.sbuf_tensor("wR", [128, 8, 4096], BF16))
        self.w1a1 = self.es_w1.enter_context(nc.sbuf_tensor("w1a1", [128, 8, 128], BF16))
        self.w2a2 = self.es_w1.enter_context(nc.sbuf_tensor("w2a2", [64, 2, 1024], BF16))
        self.d_rkvz = din("rkvz", [4, 1024, 1024])
        self.d_w1 = din("w1", [1024, 64]); self.d_w2 = din("w2", [64, 1024])
        self.d_a1 = din("a1", [1024, 64]); self.d_a2 = din("a2", [64, 1024])

        es_b = ExitStack()

        def sbb(shape, dt=F32, name=None):
            P.ntens += 1
            return es_b.enter_context(nc.sbuf_tensor(name or f"b{P.ntens}", list(shape), dt))

        wZ = sbb([128, 8, 2048], BF16, "wZ")
        wO = sbb([128, 16, 1024], BF16, "wO")
        for c in range(8):
            P.dma("pool", lambda e, c=c: e.dma_start(out=wZ[:, c, :], in_=w_in[c * 128:(c + 1) * 128, 4096:6144]), w=[wZ], sem="ld_w")
        for c in range(16):
            P.dma("pool", lambda e, c=c: e.dma_start(out=wO[:, c, :], in_=w_out[c * 128:(c + 1) * 128, :]), w=[wO], sem="ld_w")
        wR_, w1a1_, w2a2_ = self.wR, self.w1a1, self.w2a2
        for i in range(4):
            for c in range(8):
                P.dma("pool", lambda e, i=i, c=c: e.dma_start(out=wR_[:, c, i * 1024:(i + 1) * 1024],
                                                              in_=self.d_rkvz[i, c * 128:(c + 1) * 128, :]), w=[wR_], sem="x")
        for c in range(8):
            P.dma("pool", lambda e, c=c: e.dma_start(out=w1a1_[:, c, 0:64], in_=self.d_w1[c * 128:(c + 1) * 128, :]), w=[w1a1_], sem="x")
            P.dma("pool", lambda e, c=c: e.dma_start(out=w1a1_[:, c, 64:128], in_=self.d_a1[c * 128:(c + 1) * 128, :]), w=[w1a1_], sem="x")
        P.dma("pool", lambda e: e.dma_start(out=w2a2_[:, 0, :], in_=self.d_w2), w=[w2a2_], sem="x")
        P.dma("pool", lambda e: e.dma_start(out=w2a2_[:, 1, :], in_=self.d_a2), w=[w2a2_], sem="x")
        xtb = [sbb([128, 1024], F32, "xtb%d" % i) for i in range(2)]
        xnTb = sbb([128, 8, 128], BF16, "xnTb")
        ot = [sbb([128, 16, 128], F32, "ot%d" % i) for i in range(2)]
        osq = None
        orn = sbb([128, 16], F32, "orn")
        zs = sbb([128, 4, 128], F32, "zs")
        og = sbb([128, 16, 128], F32, "og")
        osq = og
        ogT = sbb([128, 16, 128], BF16, "ogT")
        x1 = [sbb([128, 1024], F32, "x1_%d" % i) for i in range(2)]
        for t in range(NT):
            x = xtb[t % 2]
            o = ot[t % 2]
            self.ld("sp", x[:], xin[t], [x], "ldb_x%d" % (t % 2))
            self.ld("sp", o[:].rearrange("p h f -> p (h f)"), oscr[t], [o], "ldb_o%d" % (t % 2))
            self.norm_T(x, nw0[:], xnTb)
            self.act(osq[:], o[:], AF.Square, [o], [osq])
            P.op("dve", lambda e: e.tensor_reduce(out=orn[:], in_=osq[:], axis=AX.X, op=ALU.add), r=[osq], w=[orn])
            self.ts("dve", orn[:], orn[:], 1.0 / 128, 1e-6, ALU.mult, ALU.add, [orn], [orn])
            self.act(orn[:], orn[:], AF.Ln, [orn], [orn])
            self.act(orn[:], orn[:], AF.Exp, [orn], [orn], scale=-0.5)
            self.tt("dve", og[:], o[:], V(orn, 0, [[1, 16], [0, 128]]), ALU.mult, [o, orn], [og])
            self.tt("pool", og[:], og[:], V(gnw, 0, [[0, 16], [1, 128]]), ALU.mult, [og, gnw], [og])
            for g in range(4):
                psZ = self.bank()
                for c in range(8):
                    self.mm(psZ, psZ[:], xnTb[:, c, :], wZ[:, c, g * 512:(g + 1) * 512], c == 0, c == 7, [xnTb, wZ])
                self.act(zs[:], psZ[:].rearrange("p (h f) -> p h f", h=4), AF.Silu, [], [psZ, zs])
                self.tt("dve", og[:, 4 * g:4 * g + 4, :], og[:, 4 * g:4 * g + 4, :], zs[:], ALU.mult, [zs, og], [og])
            for g in range(4):
                ps = self.bank()
                for hh in range(4):
                    self.tr(ps, ps[:, hh * 128:(hh + 1) * 128], og[:, 4 * g + hh, :], self.ident32[:], [og, self.ident32])
                self.cp(self.ev(), ogT[:, 4 * g:4 * g + 4, :], ps[:].rearrange("p (h f) -> p h f", h=4), [], [ps, ogT])
            xo = x1[t % 2]
            for half in range(2):
                ps = self.bank()
                for c in range(16):
                    self.mm(ps, ps[:], ogT[:, c, :], wO[:, c, half * 512:(half + 1) * 512], c == 0, c == 15, [ogT, wO])
                self.tt("dve", xo[:, half * 512:(half + 1) * 512], ps[:], x[:, half * 512:(half + 1) * 512], ALU.add, [x], [ps, xo])
            self.st("sp", x1scr[t], xo[:], [xo], "stb_x%d" % (t % 2))
        P.barrier()
        P.flush()
        es_b.close()

        self.layer1(x1scr, y, pbc)

    def layer1(self, x1scr, y, pbc):
        P = self.P
        nc = self.nc
        din, dout = self.din, self.dout
        srwkv = self.d_srwkv
        sshift = self.d_sshift
        mu_d = din("mu", [6, 1024])
        w0_d = din("w0", [1, 1024])
        a0_d = din("a0", [1, 1024])
        kk_d = din("k_k", [1, 1024]); ka_d = din("k_a", [1, 1024]); rk_d = din("r_k", [1, 1024])
        lw_d = din("lnx_w", [1, 1024]); lb_d = din("lnx_b", [1, 1024]); wo_d = din("w_o", [1024, 1024])
        p_rwkv = dout("p_rwkv", [16, 64, 64]); p_shift = dout("p_shift", [1, 1024])
        s_rwkv = dout("s_rwkv", [16, 16, 64, 64]); s_shift = dout("s_shift", [16, 1024])
        pm = self.pm
        es = ExitStack()
        cur = [es]

        def sb(shape, dt=F32, name=None):
            P.ntens += 1
            return cur[0].enter_context(nc.sbuf_tensor(name or f"c{P.ntens}", list(shape), dt))
        o32 = self.ones32
        wR, w1a1, w2a2 = self.wR, self.w1a1, self.w2a2
        xx = sb([128, 1024], F32, "r_xx")
        pst = xx
        self.ld("sp", pst[0:6, :], mu_d, [pst], "ld_c")
        for i, d in enumerate((w0_d, a0_d, kk_d, ka_d, rk_d)):
            self.ld("sp", pst[6 + i:7 + i, :], d, [pst], "ld_c")
        par = sb([128, 8, 16], F32, "par")
        ps = self.bank()
        for c in range(8):
            self.tr(ps, ps[:, c * 16:c * 16 + 11], pst[0:11, c * 128:(c + 1) * 128], self.ident32[0:11, 0:11], [pst, self.ident32])
        self.cp("dve", par[:, :, 0:11], ps[:, 0:128].rearrange("p (c i) -> p c i", i=16)[:, :, 0:11], [], [ps, par])
        self.ts("dve", par[:, :, 11:12], par[:, :, 6:7], -1.0, None, ALU.mult, None, [par], [par])
        self.ts("dve", par[:, :, 12:13], par[:, :, 9:10], -1.0, 1.0, ALU.mult, ALU.add, [par], [par])
        self.ts("dve", par[:, :, 13:14], par[:, :, 7:8], 0.5, None, ALU.mult, None, [par], [par])
        nw1 = self.nw0
        self.ld("sp", nw1[:], pbc(self.d_norm_w[1:2, :], 1024), [nw1], "ld_c")
        def sel(out, in_, pattern, op, base, cm, r, w, fill=0.0):
            P.op("pool", lambda e: e.affine_select(out=out, in_=in_, pattern=pattern, compare_op=op, fill=fill,
                                                    base=base, channel_multiplier=cm), r=r, w=w)
        shp = sb([128, 128], F32, "shp")
        sel(shp[:], o32[:], [[1, 128]], ALU.is_equal, -1, -1, [o32], [shp])
        nb0 = sb([128, 128], F32, "nb0")
        sel(nb0[:].rearrange("p (s t) -> p s t", t=8), o32[:].rearrange("p (s t) -> p s t", t=8), [[0, 16], [1, 8]], ALU.is_gt, 0, 0,
            [o32], [nb0])
        shs = sb([128, 128], F32, "shs")
        self.tt("pool", shs[:], shp[:], nb0[:], ALU.mult, [shp, nb0], [shs])
        elast = sb([128, 128], F32, "elast")
        sel(elast[:], o32[:], [[-1, 128]], ALU.is_equal, -127, 1, [o32], [elast])
        selS = sb([16, 128], F32, "selS")
        sel(selS[:], o32[0:16, :], [[1, 128]], ALU.is_equal, 0, -8, [o32], [selS])
        b64 = sb([128, 128], F32, "b64")
        v3 = b64[:].rearrange("p (s t) -> p s t", t=64)
        sel(v3, o32[:].rearrange("p (s t) -> p s t", t=64), [[-64, 2], [0, 64]], ALU.is_ge, 0, 1, [o32], [b64])
        sel(v3, v3, [[64, 2], [0, 64]], ALU.is_ge, 63, -1, [b64], [b64])
        mneg, msup, mup = {}, {}, {}
        t_tmpm = sb([128, 128], F32, "tmpm")
        for X in ("p", "s"):
            lt = self.lowT[X]
            mup[X] = pm[("mup", X)]
            self.cp("pool", mup[X][:], lt[:], [lt], [mup[X]])
            msup[X] = pm[("msup", X)]
            sel(msup[X][:], lt[:], [[1, 128]], ALU.is_gt, 0, -1, [lt], [msup[X]])
            mneg[X] = pm[("mneg", X)]
            tmpm = t_tmpm
            self.ts("pool", tmpm[:], self.addm[X][:, 0, :], -1.0 / BIG, 1.0, ALU.mult, ALU.add, [self.addm[X]], [tmpm])
            sel(tmpm[:], tmpm[:], [[-1, 128]], ALU.is_gt, 0, 1, [tmpm], [tmpm])
            self.ts("pool", mneg[X][:], tmpm[:], -1.0, None, ALU.mult, None, [tmpm], [mneg[X]])
        hsel = sb([128, 2], F32, "hsel")
        self.cp("pool", hsel[:, 0:1], b64[:, 0:1], [b64], [hsel])
        self.cp("pool", hsel[:, 1:2], b64[:, 127:128], [b64], [hsel])
        self.invw = {}
        for nm in ("MdT", "Md", "No", "X0", "X1", "P0", "P1", "PT0", "PT1", "IpPT0", "IpPT1", "XT"):
            self.invw[nm] = sb([128, 4, 128], BF16, "jw_" + nm)
        for a_, b_ in (("V", "P0"), ("VT", "P1"), ("IpVT", "PT0"), ("IpV2T", "PT1"), ("Y", "MdT")):
            self.invw[a_] = self.invw[b_]
        xn = [sb([128, 1024], F32, "r_xn%d" % i) for i in range(2)]
        P.op("pool", lambda e: e.memset(xn[1][:], 0.0), w=[xn[1]])
        x1t = [sb([128, 1024], F32, "r_x1")] * 2
        xnT = sb([128, 8, 128], BF16, "r_xnT")
        xxT = sb([128, 8, 128], BF16, "r_xxT")
        xs_all = sb([128, 4, 8, 128], BF16, "r_xs")

        class _XS:
            def __init__(s_, i):
                s_.i = i
                s_.name = "r_xs"

            def __getitem__(s_, idx):
                return xs_all[idx[0], s_.i, idx[1], idx[2]]
        xs = [_XS(i) for i in range(4)]
        class _AL:
            def __init__(s_, apf, name):
                s_.apf = apf
                s_.name = name

            def __getitem__(s_, idx):
                return s_.apf()[idx]

        hT = sb([64, 2, 128], BF16, "r_hT")
        fm = {nm: sb([128, 8, 128], BF16, "r_" + nm) for nm in ("kap", "rho", "kt", "bt")}
        kdj = sb([128, 128], BF16, "r_kdj")
        bdj = sb([128, 128], BF16, "r_bdj")
        vT = sb([128, 8, 128], BF16, "r_vT")
        Vb = sb([128, 1024], BF16, "r_Vb")
        kdT = sb([128, 8, 128], BF16, "r_kdT")
        bdT = sb([128, 8, 128], BF16, "r_bdT")
        Pc = sb([128, 8, 16], F32, "r_Pc")
        t_all = []
        for i_ in range(2):
            t_ = {nm: sb([128, 128], F32, "r_t%d_%s" % (i_, nm)) for nm in ("e", "ew", "a", "kk", "sq", "rn", "k2", "b", "cs", "x", "r", "k")}
            t_["dd"] = t_["sq"]
            t_["rk"] = t_["rn"]
            t_all.append(t_)
        kdjs = [kdj, sb([128, 128], BF16, "r_kdj2")]
        bdjs = [bdj, sb([128, 128], BF16, "r_bdj2")]
        Z = sb([128, 8, 64], F32, "r_Z")
        Zb = sb([128, 8, 64], BF16, "r_Zb")
        P.op("pool", lambda e: e.memset(Z[:], 0.0), w=[Z])
        P.op("pool", lambda e: e.memset(Zb[:], 0.0), w=[Zb])
        Mlow = sb([128, 4, 128], BF16, "r_Mlow")
        Tt = sb([128, 4, 128], BF16, "r_Tt")
        MkT = sb([128, 4, 128], BF16, "r_MkT")
        AkT = sb([128, 4, 128], BF16, "r_AkT")
        AbT = sb([128, 4, 128], BF16, "r_AbT")
        rhsS = sb([128, 4, 64], BF16, "r_rhsS")
        SA = sb([128, 4, 64], BF16, "r_SA")
        ytok = sb([128, 16, 64], F32, "r_ytok")
        ysq = xx
        st = {nm: sb([128, 16], F32, "r_st_" + nm) for nm in ("sum", "ssq", "mean", "var", "rstd", "rkb")}
        zs = self.nb["xn"]
        rec1 = nc.dram_tensor("rwrec1", [NT, 128, 56 * 128], BF16, kind="Internal").ap()
        frec1 = nc.dram_tensor("rwfrec1", [NT, 128, 1168], F32, kind="Internal").ap()

        x2 = zs
        ss2 = sb([128, 1], F32, "r_ss2")
        ygT = _AL(lambda: xs_all[:, 3], "r_xs")
        s0in = sb([64, 2, 64], F32, "r_s0in")
        Z0b2 = [_AL(lambda q=q: xs_all[:, 2 + q].rearrange("p c (a f) -> p (c a) f", a=2), "r_xs") for q in range(2)]
        mk = _AL(lambda: xs_all[:, 0:2].rearrange("p a c f -> p (a c) f"), "r_xs")
        Vm = sb([128, 128], BF16, "r_Vm")
        yz0 = sb([128, 4, 64], F32, "r_yz0")
        mk2 = _AL(lambda: xs_all[:, 0:2].rearrange("p a c f -> p (a c) f"), "r_xs")
        SAm = sb([128, 128], BF16, "r_SAm")
        Zn = sb([128, 64], F32, "r_Zn")
        Sout = sb([64, 128], F32, "r_Sout")

        def fmh(tn, h):
            b0 = (h % 2) * 64
            return tn[b0:b0 + 64, h // 2, :]

        for t in range(NT):
            X = "p" if t < 17 else "s"
            nseq, T = (1, 128) if X == "p" else (16, 8)
            x1 = x1t[t % 2]
            xc, xp = xn[t % 2], xn[(t + 1) % 2]
            self.ld("sp", x1[:], x1scr[t], [x1], "ld1_x")
            self.act(xc[:], x1[:], AF.Square, [x1], [xc, ss2], accum_out=ss2[:])
            self.ts("dve", ss2[:], ss2[:], 1.0 / 1024, 1e-6, ALU.mult, ALU.add, [ss2], [ss2])
            self.act(ss2[:], ss2[:], AF.Ln, [ss2], [ss2])
            self.act(ss2[:], ss2[:], AF.Exp, [ss2], [ss2], scale=-0.5)
            self.stt(xc[:], x1[:], ss2[:, 0:1], nw1[:], ALU.mult, ALU.mult, [x1, ss2, nw1], [xc])
            if t == 16:
                self.st("sp", p_shift, xc[127:128, :], [xc], "st_c")
            if t == 17:
                f = fsz(xc)
                self.st("sp", s_shift, bass.AP(xc, 7 * f, [[8 * f, 16], [1, 1024]]), [xc], "st_c")
            for half in range(2):
                ps = self.bank()
                cs_ = slice(half * 512, (half + 1) * 512)
                if X == "p":
                    self.mm(ps, ps[:], shp[:], xc[:, cs_], True, False, [shp, xc])
                    self.mm(ps, ps[:], elast[:], xp[:, cs_], False, True, [elast, xp])
                else:
                    if half == 0:
                        self.ld("sp", xp[0:16, :], sshift, [xp], "ld_c")
                    self.mm(ps, ps[:], shs[:], xc[:, cs_], True, False, [shs, xc])
                    self.mm(ps, ps[:], selS[:], xp[0:16, cs_], False, True, [selS, xp])
                self.tt("dve", xx[:, cs_], ps[:], xc[:, cs_], ALU.subtract, [xc], [ps, xx])
            for (src, dst) in ((xc, xnT), (xx, xxT)):
                for half in range(2):
                    ps = self.bank()
                    for c in range(4):
                        cc = half * 4 + c
                        self.tr(ps, ps[:, c * 128:(c + 1) * 128], src[:, cc * 128:(cc + 1) * 128], self.ident32[:], [src, self.ident32])
                    self.cp(self.ev(), dst[:, half * 4:half * 4 + 4, :], ps[:].rearrange("p (c f) -> p c f", c=4), [], [ps, dst])
            def mkxs(i, dst):
                for c in range(8):
                    self.stt(dst[:, c, :], xxT[:, c, :], par[:, c, i:i + 1], xnT[:, c, :], ALU.mult, ALU.add, [xxT, xnT, par], [dst])
            for i in range(3):
                mkxs(i, xs[i])
            ps = self.bank()
            mkxs(4, xs[3])
            for c in range(8):
                self.mm(ps, ps[0:64, 0:128], w1a1[:, c, 0:64], xs[3][:, c, :], c == 0, c == 7, [w1a1, xs[3]])
            mkxs(5, xs[3])
            for c in range(8):
                self.mm(ps, ps[0:64, 128:256], w1a1[:, c, 64:128], xs[3][:, c, :], c == 0, c == 7, [w1a1, xs[3]])
            self.act(hT[:, 0, :], ps[0:64, 0:128], AF.Tanh, [], [ps, hT])
            self.cp("dve", hT[:, 1, :], ps[0:64, 128:256], [], [ps, hT])
            mkxs(3, xs[3])
            for half in range(2):
                ps = self.bank()
                for c in range(8):
                    self.mm(ps, ps[:], xs[3][:, c, :], wR[:, c, 3072 + half * 512:3072 + (half + 1) * 512], c == 0, c == 7, [xs[3], wR])
                self.act(zs[:, half * 512:(half + 1) * 512], ps[:], AF.Silu, [], [ps, zs])
            psRK = self.pfx
            pjb = {}

            def proj1(j):
                psA_ = self.bank()
                psB_ = self.bank()
                pjb[j] = (psA_, psB_)
                for i in range(3):
                    for c in range(8):
                        self.mm(psA_, psA_[:, i * 128:(i + 1) * 128], wR[:, c, i * 1024 + j * 128:i * 1024 + (j + 1) * 128], xs[i][:, c, :],
                                c == 0, c == 7, [wR, xs[i]])
                self.mm(psA_, psA_[:, 384:512], w2a2[:, 0, j * 128:(j + 1) * 128], hT[:, 0, :], True, True, [w2a2, hT])
                self.mm(psB_, psB_[:, 0:128], w2a2[:, 1, j * 128:(j + 1) * 128], hT[:, 1, :], True, True, [w2a2, hT])
            def elem(j):
                yield
                psA_, psB_ = pjb[j]
                yield
                t_ = t_all[j % 2]
                yield
                kdj, bdj = kdjs[j % 2], bdjs[j % 2]
                yield
                pj = lambda i: par[:, j, i:i + 1]
                yield
                yield
                self.act(t_["e"][:], psA_[:, 384:512], AF.Exp, [par], [psA_, t_["e"]], scale=-1.0, bias=pj(11))
                yield
                self.act(t_["e"][:], t_["e"][:], AF.Ln, [t_["e"]], [t_["e"]], bias=1.0)
                yield
                self.act(t_["ew"][:], t_["e"][:], AF.Exp, [t_["e"]], [t_["ew"]], scale=-1.0, bias=-0.5)
                yield
                self.act(t_["a"][:], psB_[:, 0:128], AF.Tanh, [par], [psB_, t_["a"]], bias=pj(13), scale=0.5)
                yield
                self.ts("dve", t_["a"][:], t_["a"][:], 0.5, 0.5, ALU.mult, ALU.add, [t_["a"]], [t_["a"]])
                yield
                self.cp("act", t_["r"][:], psA_[:, 0:128], [], [psA_, t_["r"]])
                yield
                self.cp("dve", t_["k"][:], psA_[:, 128:256], [], [psA_, t_["k"]])
                yield
                self.cp("act", vT[:, j, :], psA_[:, 256:384], [], [psA_, vT])
                yield
                self.ts("dve", t_["kk"][:], t_["k"][:], pj(8), None, ALU.mult, None, [t_["k"], par], [t_["kk"]])
                yield
                self.act(t_["sq"][:], t_["kk"][:], AF.Square, [t_["kk"]], [t_["sq"]])
                yield
                psn = self.bank()
                yield
                self.mm(psn, psn[:, 0:128], b64[:], t_["sq"][:], True, True, [b64, t_["sq"]])
                yield
                self.act(t_["rn"][:], psn[:, 0:128], AF.Ln, [], [psn, t_["rn"]], bias=1e-6)
                yield "ps_done"
                self.act(t_["rn"][:], t_["rn"][:], AF.Exp, [t_["rn"]], [t_["rn"]], scale=-0.5)
                yield
                self.tt("dve", t_["kk"][:], t_["kk"][:], t_["rn"][:], ALU.mult, [t_["kk"], t_["rn"]], [t_["kk"]])
                yield
                self.ts("dve", t_["x"][:], t_["a"][:], pj(9), pj(12), ALU.mult, ALU.add, [t_["a"], par], [t_["x"]])
                yield
                self.tt("dve", t_["k2"][:], t_["k"][:], t_["x"][:], ALU.mult, [t_["k"], t_["x"]], [t_["k2"]])
                yield
                self.tt("pool", t_["b"][:], t_["kk"][:], t_["a"][:], ALU.mult, [t_["kk"], t_["a"]], [t_["b"]])
                yield
                yield
                self.stt(t_["rk"][:], t_["r"][:], pj(10), t_["k2"][:], ALU.mult, ALU.mult, [t_["r"], t_["k2"], par], [t_["rk"]])
                yield
                self.mm(psRK, psRK[:, 2 * j:2 * j + 2], t_["rk"][:], hsel[:], True, True, [t_["rk"], hsel])
                yield
                yield
                msk = o32 if X == "p" else nb0
                yield
                P.op("dve", lambda e, msk=msk, t_=t_: e.tensor_tensor_scan(out=t_["cs"][:], data0=msk[:], data1=t_["ew"][:], initial=0.0,
                                                                    op0=ALU.mult, op1=ALU.add), r=[msk, t_["ew"]], w=[t_["cs"]])
                yield
                cs = t_["cs"]
                yield
                yield
                self.act(Pc[:, j, 0:nseq], V(cs, T - 1, [[T, nseq]]), AF.Exp, [cs], [Pc], scale=-1.0)
                yield
                yield
                self.tt("dve", t_["dd"][:].rearrange("p (s t) -> p s t", t=T), cs[:].rearrange("p (s t) -> p s t", t=T),
                        V(cs, T - 1, [[T, nseq], [0, T]]), ALU.subtract, [cs], [t_["dd"]])
                yield
                self.act(t_["dd"][:], t_["dd"][:], AF.Exp, [t_["dd"]], [t_["dd"]])
                yield
                self.tt("dve", kdj[:], t_["dd"][:], t_["k2"][:], ALU.mult, [t_["dd"], t_["k2"]], [kdj])
                yield
                self.tt("pool", bdj[:], t_["dd"][:], t_["b"][:], ALU.mult, [t_["dd"], t_["b"]], [bdj])
                yield
                self.tr(self.pb[0], self.pb[0][:, j * 128:(j + 1) * 128], kdj[:], self.identb[:], [kdj, self.identb])
                yield
                self.tr(self.pb[1], self.pb[1][:, j * 128:(j + 1) * 128], bdj[:], self.identb[:], [bdj, self.identb])
                yield
                yield
                self.act(t_["x"][:], cs[:], AF.Exp, [cs], [t_["x"]])
                yield
                self.tt("dve", fm["kt"][:, j, :], t_["x"][:], t_["k2"][:], ALU.mult, [t_["x"], t_["k2"]], [fm["kt"]])
                yield
                self.tt("pool", fm["bt"][:, j, :], t_["x"][:], t_["b"][:], ALU.mult, [t_["x"], t_["b"]], [fm["bt"]])
                yield
                yield
                self.act(t_["x"][:], cs[:], AF.Exp, [cs], [t_["x"]], scale=-1.0)
                yield
                self.tt("dve", fm["rho"][:, j, :], t_["x"][:], t_["r"][:], ALU.mult, [t_["x"], t_["r"]], [fm["rho"]])
                yield
                self.tt("pool", t_["e"][:], cs[:], t_["ew"][:], ALU.subtract, [cs, t_["ew"]], [t_["e"]])
                yield
                self.act(t_["e"][:], t_["e"][:], AF.Exp, [t_["e"]], [t_["e"]], scale=-1.0)
                yield
                self.tt("dve", fm["kap"][:, j, :], t_["e"][:], t_["kk"][:], ALU.mult, [t_["e"], t_["kk"]], [fm["kap"]])

            proj1(0)
            proj1(1)
            for pr in range(4):
                gens = [elem(2 * pr), elem(2 * pr + 1)]
                ndone = 0
                projected = (pr == 3)
                while gens:
                    for g_ in list(gens):
                        try:
                            r_ = next(g_)
                            if r_ == "ps_done":
                                ndone += 1
                        except StopIteration:
                            gens.remove(g_)
                    if ndone == 2 and not projected:
                        proj1(2 * pr + 2)
                        proj1(2 * pr + 3)
                        projected = True
            self.cp("dve", st["rkb"][:], psRK[:, 0:16], [], [psRK, st["rkb"]])
            self.cp("act", kdT[:], self.pb[0][:].rearrange("p (c f) -> p c f", c=8), [], [self.pb[0], kdT])
            self.cp("dve", bdT[:], self.pb[1][:].rearrange("p (c f) -> p c f", c=8), [], [self.pb[1], bdT])
            pb = self.bbank()
            for c in range(8):
                self.tr(pb, pb[:, c * 128:(c + 1) * 128], vT[:, c, :], self.identb[:], [vT, self.identb])
            self.cp("act", Vb[:], pb[:], [], [pb, Vb])
            for i_, src_ in enumerate((fm["kap"], fm["rho"], fm["kt"], fm["bt"], kdT, bdT)):
                self.st("sp", rec1[t, :, i_ * 1024:(i_ + 1) * 1024], src_[:].rearrange("p c f -> p (c f)"), [src_], "st_r1%d" % i_)
            self.st("sp", rec1[t, :, 6144:7168], Vb[:], [Vb], "st_r16")
            self.st("sp", frec1[t, :, 0:1024], zs[:], [zs], "st_r17")
            self.st("sp", frec1[t, :, 1024:1152], Pc[:].rearrange("p c s -> p (c s)"), [Pc], "st_r18")
            self.st("sp", frec1[t, :, 1152:1168], st["rkb"][:], [st["rkb"]], "st_r19")
        P.barrier()
        P.flush()
        es.close()
        self.es_w1.close()

        es = ExitStack()
        cur[0] = es
        wO = sb([128, 8, 1024], BF16, "wO1")
        for c in range(8):
            P.dma("pool", lambda e, c=c: e.dma_start(out=wO[:, c, :], in_=wo_d[c * 128:(c + 1) * 128, :]), w=[wO], sem="ld_w")
        fnw = sb([128, 1024], F32, "fnw")
        lnw = sb([128, 1024], F32, "lnw")
        lnb = sb([128, 1024], F32, "lnb")
        self.ld("sp", fnw[:], pbc(self.d_fnorm_w, 1024), [fnw], "ld_c")
        self.ld("sp", lnw[:], pbc(lw_d, 1024), [lnw], "ld_c")
        self.ld("sp", lnb[:], pbc(lb_d, 1024), [lnb], "ld_c")
        recs = [sb([128, 56, 128], BF16, "b_rec%d" % i) for i in range(2)]
        frecs = [sb([128, 1168], F32, "b_frec%d" % i) for i in range(2)]
        x1t = [sb([128, 1024], F32, "b_x1")] * 2
        Z = sb([128, 8, 64], F32, "b_Z")
        Zb = sb([128, 8, 64], BF16, "b_Zb")
        P.op("pool", lambda e: e.memset(Z[:], 0.0), w=["Z%d" % i for i in range(8)])
        P.op("pool", lambda e: e.memset(Zb[:], 0.0), w=["Zb%d" % i for i in range(8)])
        sets = []
        for gi in range(4):
            B = {nm: sb([128, 4, 128], BF16, "h%d_%s" % (gi, nm)) for nm in ("Mlow", "Tt", "MkT", "AkT", "AbT")}
            B["rhsS"] = sb([128, 4, 64], BF16, "h%d_rhsS" % gi)
            B["SA"] = sb([128, 4, 64], BF16, "h%d_SA" % gi)
            B["yz0"] = sb([128, 4, 64], F32, "h%d_yz0" % gi)
            B["W"] = self.inv_bufs(sb, "h%d" % gi)
            sets.append(B)
        for gi in range(2):
            sh = {"s0in4": sb([64, 4, 2, 64], F32, "h%d_s0in4" % gi), "Vm4": sb([128, 4, 128], BF16, "h%d_Vm4" % gi),
                  "SAm4": sb([128, 4, 128], BF16, "h%d_SAm4" % gi), "Zn4": sb([128, 4, 64], F32, "h%d_Zn4" % gi),
                  "Sout4": sb([64, 4, 128], F32, "h%d_Sout4" % gi)}
            sets[gi].update(sh)
            sets[gi + 2].update(sh)
        ytoks = [sb([128, 16, 64], F32, "b_ytok%d" % i) for i in range(2)]
        ysq = self.nb["xn"]
        st = {nm: sb([128, 16], F32, "b_st_" + nm) for nm in ("sum", "ssq", "mean", "var", "rstd")}
        ss2 = sb([128, 1], F32, "b_ss2")
        ygT = sb([128, 8, 128], BF16, "b_ygT")
        x2 = ysq
        Z0b2 = [sb([128, 16, 64], BF16, "b_Z0b%d" % i) for i in range(2)]
        mk = sb([128, 16, 128], BF16, "b_mk")
        Sout = sb([64, 128], F32, "b_Sout")
        KAP, RHO, KT, BT, KD, BD, VB = 0, 8, 16, 24, 32, 40, 48
        v4 = lambda ps: ps[:, 0:512].rearrange("p (h f) -> p h f", h=4)

        def rwkv_group(t, g, B, rec, frec, ytok, ytn):
            X = "p" if t < 17 else "s"
            hs = [4 * g + hh for hh in range(4)]
            order = (0, 2, 1, 3)

            def fmh(off, h):
                b0 = (h % 2) * 64
                return rec[b0:b0 + 64, off + h // 2, :]
            Vh = lambda h: rec[:, VB + h // 2, (h % 2) * 64:(h % 2) * 64 + 64]
            Mlow, Tt, MkT, AkT, AbT, rhsS, SA, yz0 = (B[k_] for k_ in ("Mlow", "Tt", "MkT", "AkT", "AbT", "rhsS", "SA", "yz0"))

            def prod4(ps, lo, ro):
                for i_, hh in enumerate(order):
                    h = hs[hh]
                    self.mm(ps, ps[:, hh * 128:(hh + 1) * 128], fmh(lo, h), fmh(ro, h), True, True, [rec], sw=(i_ == 2))
            psM = self.bank()
            prod4(psM, KAP, BT)
            self.tt("dve", Mlow[:], v4(psM), self.bc4(mneg[X]), ALU.mult, [mneg[X]], [psM, Mlow])
            for (lo, ro, dst, msk) in ((KT, KAP, MkT, msup[X]), (KT, RHO, AkT, mup[X]), (BT, RHO, AbT, mup[X])):
                ps = self.bank()
                prod4(ps, lo, ro)
                self.tt("dve", dst[:], v4(ps), self.bc4(msk), ALU.mult, [msk], [ps, dst])
            yield
            yield from self.inverse_gen(B["W"], Mlow, Tt)
            s0in4, Vm4, SAm4, Zn4, Sout4 = (B[k_] for k_ in ("s0in4", "Vm4", "SAm4", "Zn4", "Sout4"))
            if X == "s":
                for q in range(2):
                    m = 2 * g + q
                    for sg in range(4):
                        for h2_ in range(2):
                            self.ld("sp", s0in4[:, :, h2_, :], srwkv[4 * sg:4 * sg + 4, 2 * m + h2_].rearrange("s v k -> v s k"), [s0in4],
                                    "ld_s0in%d" % (g % 2))
                        pz = self.bank()
                        for si in range(4):
                            self.tr(pz, pz[:, si * 64:(si + 1) * 64], s0in4[:, si].rearrange("v h k -> v (h k)"), self.ident32[0:64, 0:64],
                                    [s0in4, self.ident32])
                        self.cp(self.ev(), Z0b2[q][:, 4 * sg:4 * sg + 4, :], pz[:, 0:256].rearrange("p (s f) -> p s f", s=4), [],
                                [pz, Z0b2[q]])
            psR = self.bank()
            if X == "s":
                psY2 = self.bank()
            for hh, h in enumerate(hs):
                b0 = (h % 2) * 64
                m = h // 2
                if X == "s":
                    Z0b = Z0b2[hh // 2]
                    for (off_, psd, lastflag) in ((KAP, psR, False), (RHO, psY2, True)):
                        a_ = fmh(off_, h)
                        self.tt("pool", mk[b0:b0 + 64], bass.AP(a_.tensor, a_.offset, [list(a_.ap[0]), [0, 16], [1, 128]]),
                                self.cm[b0:b0 + 64], ALU.mult, [rec, self.cm], [mk])
                        for s_ in range(16):
                            self.mm(psd, psd[:, hh * 64:(hh + 1) * 64], mk[b0:b0 + 64, s_, :], Z0b[b0:b0 + 64, s_, :], s_ == 0,
                                    lastflag and s_ == 15, [mk, Z0b], sw=(s_ == 0))
                else:
                    self.mm(psR, psR[:, hh * 64:(hh + 1) * 64], fmh(KAP, h), Zb[b0:b0 + 64, m, :], True, False, [rec, "Zb%d" % m])
                self.mm(psR, psR[:, hh * 64:(hh + 1) * 64], MkT[:, hh, :], Vh(h), False, True, [MkT, rec])
            if X == "s":
                self.cp("dve", yz0[:].rearrange("p h f -> p (h f)"), psY2[:, 0:256], [], [psY2, yz0])
            self.act(rhsS[:].rearrange("p h f -> p (h f)"), psR[:, 0:256], AF.Copy, [], [psR, rhsS], scale=-1.0)
            yield
            psS = self.bank()
            for hh in range(4):
                self.mm(psS, psS[:, hh * 64:(hh + 1) * 64], Tt[:, hh, :], rhsS[:, hh, :], True, True, [Tt, rhsS])
            self.cp("act", SA[:].rearrange("p h f -> p (h f)"), psS[:, 0:256], [], [psS, SA])
            yield
            psY = self.bank()
            for hh, h in enumerate(hs):
                b0 = (h % 2) * 64
                m = h // 2
                if X == "p":
                    self.mm(psY, psY[:, hh * 64:(hh + 1) * 64], fmh(RHO, h), Zb[b0:b0 + 64, m, :], True, False, [rec, "Zb%d" % m])
                self.mm(psY, psY[:, hh * 64:(hh + 1) * 64], AkT[:, hh, :], Vh(h), X == "s", False, [AkT, rec])
                self.mm(psY, psY[:, hh * 64:(hh + 1) * 64], AbT[:, hh, :], SA[:, hh, :], False, True, [AbT, SA])
            yv = ytok[:, 4 * g:4 * g + 4, :].rearrange("p h f -> p (h f)")
            if X == "p":
                self.cp("dve", yv, psY[:, 0:256], [], [psY, ytn + str(g)])
            else:
                self.tt("dve", yv, psY[:, 0:256], yz0[:].rearrange("p h f -> p (h f)"), ALU.add, [yz0], [psY, ytn + str(g)])
            for mm_ in range(2):
                m = 2 * g + mm_
                SApair = SA[:, 2 * mm_:2 * mm_ + 2, :].rearrange("p h f -> p (h f)")
                if X == "p":
                    psZ = self.bank()
                    self.mm(psZ, psZ[:, 0:128], rec[:, KD + m, :], rec[:, VB + m, :], True, False, [rec])
                    self.mm(psZ, psZ[:, 0:128], rec[:, BD + m, :], SApair, False, True, [rec, SA])
                    for h2 in range(2):
                        b0 = h2 * 64
                        self.stt(Z[b0:b0 + 64, m, :], Z[b0:b0 + 64, m, :], V(frec, 1024 + m * 16, [[1, 1]], 64, b0),
                                 psZ[b0:b0 + 64, b0:b0 + 64], ALU.mult, ALU.add, ["Z%d" % m, frec], [psZ, "Z%d" % m])
                    self.cp("act", Zb[:, m, :], Z[:, m, :], ["Z%d" % m], ["Zb%d" % m])
                    if t == 16:
                        pz = self.bank()
                        self.tr(pz, pz[0:64, 0:128], Z[:, m, :], self.ident32[:], ["Z%d" % m, self.ident32])
                        self.cp("act", Sout[:], pz[0:64, 0:128], [], [pz, Sout])
                        self.st("sp", p_rwkv[2 * m:2 * m + 2].rearrange("h v k -> v h k"), Sout[:].rearrange("v (h k) -> v h k", h=2),
                                [Sout], "st_so")
                else:
                    for sg in range(4):
                        for h2_ in range(2):
                            self.ld("sp", s0in4[:, :, h2_, :], srwkv[4 * sg:4 * sg + 4, 2 * m + h2_].rearrange("s v k -> v s k"), [s0in4],
                                    "ld_s0in%d" % (g % 2))
                        pz = self.bank()
                        for si in range(4):
                            self.tr(pz, pz[:, si * 64:(si + 1) * 64], s0in4[:, si].rearrange("v h k -> v (h k)"), self.ident32[0:64, 0:64],
                                    [s0in4, self.ident32])
                        vpair = rec[:, VB + m, :]
                        smb = V(self.sm, 4 * sg, [[1, 4], [0, 128]])
                        self.tt("pool", Vm4[:], bass.AP(vpair.tensor, vpair.offset, [list(vpair.ap[0]), [0, 4], [1, 128]]), smb, ALU.mult,
                                [rec, self.sm], [Vm4])
                        sap = SA[:, 2 * mm_:2 * mm_ + 2, :]
                        self.tt("pool", SAm4[:], bass.AP(sap.tensor, sap.offset, [list(sap.ap[0]), [0, 4], [1, 128]]), smb, ALU.mult,
                                [SA, self.sm], [SAm4])
                        psZ = self.bank()
                        for si in range(4):
                            self.mm(psZ, psZ[:, si * 128:(si + 1) * 128], rec[:, KD + m, :], Vm4[:, si, :], True, False, [rec, Vm4])
                            self.mm(psZ, psZ[:, si * 128:(si + 1) * 128], rec[:, BD + m, :], SAm4[:, si, :], False, True, [rec, SAm4])
                        self.tt("dve", Zn4[:], pz[:, 0:256].rearrange("p (s f) -> p s f", s=4),
                                V(frec, 1024 + m * 16 + 4 * sg, [[1, 4], [0, 64]]), ALU.mult, [frec], [pz, Zn4])
                        for h2 in range(2):
                            b0 = h2 * 64
                            self.tt("dve", Zn4[b0:b0 + 64], Zn4[b0:b0 + 64],
                                    psZ[b0:b0 + 64, 0:512].rearrange("p (s f) -> p s f", s=4)[:, :, b0:b0 + 64], ALU.add, [Zn4], [psZ, Zn4])
                        pz2 = self.bank()
                        for si in range(4):
                            self.tr(pz2, pz2[0:64, si * 128:(si + 1) * 128], Zn4[:, si, :], self.ident32[:], [Zn4, self.ident32])
                        self.cp("act", Sout4[:].rearrange("v s f -> v (s f)"), pz2[0:64, 0:512], [], [pz2, Sout4])
                        for h2_ in range(2):
                            self.st("sp", s_rwkv[4 * sg:4 * sg + 4, 2 * m + h2_].rearrange("s v k -> v s k"),
                                    Sout4[:, :, h2_ * 64:(h2_ + 1) * 64], [Sout4], "st_so%d" % (g % 2))
                        yield
            yield

        def post_gen(t, rec, frec, ytok, ytn):
            x1 = x1t[0]
            ytokr = [ytn + str(g_) for g_ in range(4)]
            self.ld("sp", x1[:], x1scr[t], [x1], "x")
            P.op("dve", lambda e: e.tensor_reduce(out=st["sum"][:], in_=ytok[:], axis=AX.X, op=ALU.add), r=ytokr, w=[st["sum"]])
            ysq3 = ysq[:].rearrange("p (h f) -> p h f", h=16)
            self.act(ysq3, ytok[:], AF.Square, ytokr, [ysq])
            P.op("dve", lambda e: e.tensor_reduce(out=st["ssq"][:], in_=ysq3, axis=AX.X, op=ALU.add), r=[ysq], w=[st["ssq"]])
            yield
            self.ts("dve", st["mean"][:], st["sum"][:], 1.0 / 64, None, ALU.mult, None, [st["sum"]], [st["mean"]])
            self.tt("dve", st["var"][:], st["mean"][:], st["mean"][:], ALU.mult, [st["mean"]], [st["var"]])
            self.stt(st["var"][:], st["ssq"][:], 1.0 / 64, st["var"][:], ALU.mult, ALU.subtract, [st["ssq"], st["var"]], [st["var"]])
            self.ts("dve", st["var"][:], st["var"][:], 64e-5, None, ALU.add, None, [st["var"]], [st["var"]])
            self.act(st["rstd"][:], st["var"][:], AF.Ln, [st["var"]], [st["rstd"]])
            self.act(st["rstd"][:], st["rstd"][:], AF.Exp, [st["rstd"]], [st["rstd"]], scale=-0.5)
            yield
            self.tt("dve", ysq3, ytok[:], V(st["mean"], 0, [[1, 16], [0, 64]]), ALU.subtract, ytokr + [st["mean"]], [ysq])
            self.tt("dve", ysq3, ysq3, V(st["rstd"], 0, [[1, 16], [0, 64]]), ALU.mult, [ysq, st["rstd"]], [ysq])
            yield
            yf = ysq[:]
            self.tt("pool", yf, yf, lnw[:], ALU.mult, [ysq, lnw], [ysq])
            self.tt("pool", yf, yf, lnb[:], ALU.add, [ysq, lnb], [ysq])
            yield
            yt3 = ytok[:]
            self.tt("dve", yt3, rec[:, VB:VB + 8, :].rearrange("p c (a f) -> p (c a) f", a=2), V(frec, 1152, [[1, 16], [0, 64]]), ALU.mult,
                    [rec, frec] + ytokr, ytokr)
            self.tt("dve", yf, yf, ytok[:].rearrange("p h f -> p (h f)"), ALU.add, [ysq] + ytokr, [ysq])
            self.tt("dve", yf, yf, frec[:, 0:1024], ALU.mult, [ysq, frec], [ysq])
            yield
            for half in range(2):
                ps = self.bank()
                for c in range(4):
                    cc = half * 4 + c
                    self.tr(ps, ps[:, c * 128:(c + 1) * 128], yf[:, cc * 128:(cc + 1) * 128], self.ident32[:], [ysq, self.ident32])
                self.cp(self.ev(), ygT[:, half * 4:half * 4 + 4, :], ps[:].rearrange("p (c f) -> p c f", c=4), [], [ps, ygT])
                yield
            for half in range(2):
                ps = self.bank()
                cs_ = slice(half * 512, (half + 1) * 512)
                for c in range(8):
                    self.mm(ps, ps[:], ygT[:, c, :], wO[:, c, cs_], c == 0, c == 7, [ygT, wO])
                self.tt("dve", x2[:, cs_], ps[:], x1[:, cs_], ALU.add, [x1], [ps, x2])
                yield
            yy = ytok
            yyv = ytok[:].rearrange("p h f -> p (h f)")
            self.act(yyv, x2[:], AF.Square, [x2], ytokr + [ss2], accum_out=ss2[:])
            self.ts("dve", ss2[:], ss2[:], 1.0 / 1024, 1e-6, ALU.mult, ALU.add, [ss2], [ss2])
            self.act(ss2[:], ss2[:], AF.Ln, [ss2], [ss2])
            self.act(ss2[:], ss2[:], AF.Exp, [ss2], [ss2], scale=-0.5)
            yield
            self.stt(yyv, x2[:], ss2[:, 0:1], fnw[:], ALU.mult, ALU.mult, [x2, ss2, fnw], ytokr)
            self.st("sp", y[t], yyv, ytokr, "stc_y")

        pending = None
        for t in range(NT):
            rec, frec = recs[t % 2], frecs[t % 2]
            ytok, ytn = ytoks[t % 2], "yt%d_" % (t % 2)
            self.ld("sp", rec[:].rearrange("p a f -> p (a f)"), rec1[t], [rec], "x")
            self.ld("sp", frec[:], frec1[t], [frec], "x")
            gens = [rwkv_group(t, g, sets[g], rec, frec, ytok, ytn) for g in range(4)]
            if pending is not None:
                gens.append(pending)
            self.run_rr(gens)
            pending = post_gen(t, rec, frec, ytok, ytn)
        self.run_rr([pending])
        P.barrier()
        P.flush()
        es.close()
        P.es.close()


_CACHE = {}


def kernel(x_prompt, x_sample, state_gdn, state_gdn_conv, state_rwkv, state_rwkv_shift,
           meta_tokens, norm_w, final_norm_w,
           gdn_w_in, gdn_conv_w, gdn_a_log, gdn_dt_bias, gdn_norm_w, gdn_w_out,
           rwkv_mu, rwkv_w_rkvz, rwkv_w0, rwkv_w1, rwkv_w2, rwkv_a0, rwkv_a1, rwkv_a2,
           rwkv_k_k, rwkv_k_a, rwkv_r_k, rwkv_lnx_w, rwkv_lnx_b, rwkv_w_o):
    f = lambda a: np.ascontiguousarray(np.asarray(a, dtype=np.float32))
    if "nc" not in _CACHE:
        _CACHE["nc"] = K().nc
    nc = _CACHE["nc"]
    x_prompt, x_sample = f(x_prompt), f(x_sample)
    meta = f(meta_tokens)
    in_maps = []
    for c in range(8):
        xin = np.zeros((NT, 128, 1024), np.float32)
        xin[0, 112:128] = meta
        xin[1:17] = x_prompt[c].reshape(16, 128, 1024)
        xin[17] = x_sample[16 * c:16 * c + 16].reshape(128, 1024)
        in_maps.append({
            "xin": xin,
            "sgdn": f(state_gdn[0, 16 * c:16 * c + 16]),
            "sconv": f(state_gdn_conv[0, 16 * c:16 * c + 16]).reshape(48, 4096),
            "srwkv": f(state_rwkv[0, 16 * c:16 * c + 16]),
            "sshift": f(state_rwkv_shift[0, 16 * c:16 * c + 16]),
            "norm_w": f(norm_w), "fnorm_w": f(final_norm_w).reshape(1, 1024),
            "w_in": f(gdn_w_in[0]), "conv_w": f(gdn_conv_w[0]), "a_log": f(gdn_a_log), "dt_bias": f(gdn_dt_bias),
            "gnorm_w": f(gdn_norm_w), "w_out": f(gdn_w_out[0]),
            "mu": f(rwkv_mu[0]), "rkvz": f(rwkv_w_rkvz[0]), "w0": f(rwkv_w0), "w1": f(rwkv_w1[0]), "w2": f(rwkv_w2[0]),
            "a0": f(rwkv_a0), "a1": f(rwkv_a1[0]), "a2": f(rwkv_a2[0]), "k_k": f(rwkv_k_k), "k_a": f(rwkv_k_a),
            "r_k": f(rwkv_r_k).reshape(1, 1024), "lnx_w": f(rwkv_lnx_w), "lnx_b": f(rwkv_lnx_b), "w_o": f(rwkv_w_o[0]),
        })
    res = run_bass_kernel_spmd(nc, in_maps, core_ids=list(range(8)))
    R = res.results
    y_prompt = np.stack([R[c]["y"][1:17].reshape(2048, 1024) for c in range(8)])
    y_sample = np.concatenate([R[c]["y"][17].reshape(16, 8, 1024) for c in range(8)])
    p_gdn = np.stack([R[c]["p_gdn"] for c in range(8)])[None]
    p_conv = np.stack([R[c]["p_conv"] for c in range(8)])[None]
    s_gdn = np.concatenate([R[c]["s_gdn"] for c in range(8)])[None]
    s_conv = np.concatenate([R[c]["s_conv"].reshape(16, 3, 4096) for c in range(8)])[None]
    p_rwkv = np.stack([R[c]["p_rwkv"] for c in range(8)])[None]
    p_shift = np.stack([R[c]["p_shift"].reshape(1024) for c in range(8)])[None]
    s_rwkv = np.concatenate([R[c]["s_rwkv"] for c in range(8)])[None]
    s_shift = np.concatenate([R[c]["s_shift"] for c in range(8)])[None]
    return (y_prompt, y_sample, p_gdn, p_conv, p_rwkv, p_shift, s_gdn, s_conv, s_rwkv, s_shift)
```
